# Optimizing a Trainium2 kernel written in Bass

```python
import jax, jax.numpy as jnp
from jax import lax
import numpy as np

D_MODEL = 1024
BATCH = 4
SEQ = 8192
DEPTH = 1

HEAD_DIM = 64
N_HEADS = D_MODEL // HEAD_DIM
N_FOX_HEADS = N_HEADS // 2
N_SB_HEADS = N_HEADS - N_FOX_HEADS
FOX_WIDTH = N_FOX_HEADS * HEAD_DIM
SB_WIDTH = N_SB_HEADS * HEAD_DIM
MIX_WIDTH = FOX_WIDTH + SB_WIDTH
IN_COLS = 3 * FOX_WIDTH + N_FOX_HEADS + 3 * SB_WIDTH
D_FF = ((8 * D_MODEL // 3 + 127) // 128) * 128
CONV_WIDTH = 3
BLOCK_Q = 128
EPS = 1e-6

kernel_name = "hybrid_fox_stickbreaking_convffn"


def rmsnorm(x, g):
    xf = x.astype(jnp.float32)
    y = xf * lax.rsqrt(jnp.mean(xf * xf, axis=-1, keepdims=True) + EPS)
    return (y * g.astype(jnp.float32)).astype(x.dtype)


def split_heads(t, n_heads):
    b, s, _ = t.shape
    return t.reshape(b, s, n_heads, HEAD_DIM).transpose(0, 2, 1, 3)


def merge_heads(t):
    b, h, s, d = t.shape
    return t.transpose(0, 2, 1, 3).reshape(b, s, h * d)


def to_blocks(t):
    b, h, s = t.shape[:3]
    t = t.reshape(b, h, s // BLOCK_Q, BLOCK_Q, *t.shape[3:])
    return jnp.moveaxis(t, 2, 0)


def from_blocks(t):
    nb, b, h, bq, d = t.shape
    return jnp.moveaxis(t, 0, 2).reshape(b, h, nb * bq, d)


def fox_attention(q, k, v, log_f):
    s_len = q.shape[2]
    scale = HEAD_DIM ** -0.5
    kf = k.astype(jnp.float32)
    vf = v.astype(jnp.float32)
    F = jnp.cumsum(log_f, axis=-1)
    kpos = jnp.arange(s_len)

    def block(args):
        i, qi, Fi = args
        qpos = i * BLOCK_Q + jnp.arange(BLOCK_Q)
        logits = (jnp.einsum('bhqd,bhkd->bhqk', qi.astype(jnp.float32), kf) * scale
                  + Fi[..., :, None] - F[:, :, None, :])
        logits = jnp.where(kpos[None, :] <= qpos[:, None], logits, -jnp.inf)
        p = jax.nn.softmax(logits, axis=-1)
        return jnp.einsum('bhqk,bhkd->bhqd', p, vf)

    out = lax.map(block, (jnp.arange(s_len // BLOCK_Q), to_blocks(q), to_blocks(F)))
    return from_blocks(out).astype(v.dtype)


def stick_breaking_attention(q, k, v):
    s_len = q.shape[2]
    scale = HEAD_DIM ** -0.5
    kf = k.astype(jnp.float32)
    vf = v.astype(jnp.float32)
    kpos = jnp.arange(s_len)

    def block(args):
        i, qi = args
        qpos = i * BLOCK_Q + jnp.arange(BLOCK_Q)
        mask = kpos[None, :] < qpos[:, None]
        z = jnp.einsum('bhqd,bhkd->bhqk', qi.astype(jnp.float32), kf) * scale
        log_beta = jax.nn.log_sigmoid(z)
        log_one_minus = jnp.where(mask, jax.nn.log_sigmoid(-z), 0.0)
        after = lax.cumsum(log_one_minus, axis=3, reverse=True) - log_one_minus
        weights = jnp.where(mask, jnp.exp(log_beta + after), 0.0)
        return jnp.einsum('bhqk,bhkd->bhqd', weights, vf)

    out = lax.map(block, (jnp.arange(s_len // BLOCK_Q), to_blocks(q)))
    return from_blocks(out).astype(v.dtype)


def causal_dwconv(u, w, b):
    s_len = u.shape[1]
    up = jnp.pad(u, ((0, 0), (CONV_WIDTH - 1, 0), (0, 0)))
    out = b + w[0] * up[:, 0:s_len]
    for kk in range(1, CONV_WIDTH):
        out = out + w[kk] * up[:, kk:kk + s_len]
    return out


def setup_inputs(seed: int = 0) -> dict:
    key = jax.random.key(seed)
    ks = jax.random.split(key, 13)
    f32 = jnp.float32
    x = jax.random.normal(ks[0], (BATCH, SEQ, D_MODEL), f32)
    attn_norm_g = 1.0 + 0.05 * jax.random.normal(ks[1], (DEPTH, D_MODEL), f32)
    w_in = jax.random.normal(ks[2], (DEPTH, D_MODEL, IN_COLS), f32) * D_MODEL ** -0.5
    forget_bias = (jnp.linspace(1.0, 6.0, N_FOX_HEADS, dtype=f32)[None, :]
                   + 0.1 * jax.random.normal(ks[3], (DEPTH, N_FOX_HEADS), f32))
    fox_out_g = 1.0 + 0.05 * jax.random.normal(ks[4], (DEPTH, FOX_WIDTH), f32)
    sb_out_g = 1.0 + 0.05 * jax.random.normal(ks[5], (DEPTH, SB_WIDTH), f32)
    w_out = jax.random.normal(ks[6], (DEPTH, MIX_WIDTH, D_MODEL), f32) * MIX_WIDTH ** -0.5
    ffn_norm_g = 1.0 + 0.05 * jax.random.normal(ks[7], (DEPTH, D_MODEL), f32)
    w_up = jax.random.normal(ks[8], (DEPTH, D_MODEL, 2 * D_FF), f32) * D_MODEL ** -0.5
    conv_w = jax.random.normal(ks[9], (DEPTH, CONV_WIDTH, 2 * D_FF), f32) * CONV_WIDTH ** -0.5
    conv_b = 0.02 * jax.random.normal(ks[10], (DEPTH, 2 * D_FF), f32)
    w_down = jax.random.normal(ks[11], (DEPTH, D_FF, D_MODEL), f32) * D_FF ** -0.5
    final_norm_g = 1.0 + 0.05 * jax.random.normal(ks[12], (D_MODEL,), f32)
    return {"x": x, "attn_norm_g": attn_norm_g, "w_in": w_in, "forget_bias": forget_bias,
            "fox_out_g": fox_out_g, "sb_out_g": sb_out_g, "w_out": w_out,
            "ffn_norm_g": ffn_norm_g, "w_up": w_up, "conv_w": conv_w, "conv_b": conv_b,
            "w_down": w_down, "final_norm_g": final_norm_g}


def reference(x, attn_norm_g, w_in, forget_bias, fox_out_g, sb_out_g, w_out,
              ffn_norm_g, w_up, conv_w, conv_b, w_down, final_norm_g):
    b, s_len, _ = x.shape
    splits = [FOX_WIDTH, 2 * FOX_WIDTH, 3 * FOX_WIDTH, 3 * FOX_WIDTH + N_FOX_HEADS,
              3 * FOX_WIDTH + N_FOX_HEADS + SB_WIDTH,
              3 * FOX_WIDTH + N_FOX_HEADS + 2 * SB_WIDTH]
    for l in range(DEPTH):
        h = rmsnorm(x, attn_norm_g[l])
        proj = h @ w_in[l]
        fq, fk, fv, f_logit, sq, sk, sv = jnp.split(proj, splits, axis=-1)
        log_f = jax.nn.log_sigmoid(
            (f_logit + forget_bias[l]).astype(jnp.float32)).transpose(0, 2, 1)
        o_fox = fox_attention(split_heads(fq, N_FOX_HEADS), split_heads(fk, N_FOX_HEADS),
                              split_heads(fv, N_FOX_HEADS), log_f)
        o_sb = stick_breaking_attention(split_heads(sq, N_SB_HEADS), split_heads(sk, N_SB_HEADS),
                                        split_heads(sv, N_SB_HEADS))
        o = jnp.concatenate([rmsnorm(merge_heads(o_fox), fox_out_g[l]),
                             rmsnorm(merge_heads(o_sb), sb_out_g[l])], axis=-1)
        x = x + o @ w_out[l]
        h = rmsnorm(x, ffn_norm_g[l])
        u = causal_dwconv(h @ w_up[l], conv_w[l], conv_b[l])
        gate, val = jnp.split(u, 2, axis=-1)
        x = x + (jax.nn.silu(gate) * val) @ w_down[l]
    return rmsnorm(x, final_norm_g)
```

```python
import numpy as np
import ml_dtypes
from contextlib import ExitStack
import concourse.bass as bass
import concourse.mybir as mybir
from concourse.bass_utils import run_bass_kernel_spmd

F32 = mybir.dt.float32
BF16 = mybir.dt.bfloat16
AF = mybir.ActivationFunctionType
ALU = mybir.AluOpType

D = 1024
DH = 64
DFF = 2816
INC = 3080
EPS = 1e-6
NEG = -30000.0
CQ_F, CK_F, CV_F, CL, CQ_S, CK_S, CV_S = 0, 512, 1024, 1536, 1544, 2056, 2568
NCH = 2 * DFF // 128

import os
OPT = set(os.environ.get("KOPT", "").split(","))
COMPUTE = ("pe", "act", "dve", "pool")
CH = 16000
NDMA = 8


class Prog:
    def __init__(self, nc, gs, tag):
        self.nc = nc
        self.gs = gs
        self.tag = tag
        self.ops = []
        self.last_w = {}
        self.readers = {}

    def op(self, eng, fn, reads=(), writes=(), dma=False):
        j = len(self.ops)
        deps = set()
        for r in reads:
            if r in self.last_w:
                deps.add(self.last_w[r])
        for w in writes:
            if w in self.last_w:
                deps.add(self.last_w[w])
            for rd in self.readers.get(w, ()):
                deps.add(rd)
        deps.discard(j)
        self.ops.append(dict(eng=eng, fn=fn, deps=deps, dma=dma, sig=False))
        for r in reads:
            self.readers.setdefault(r, []).append(j)
        for w in writes:
            self.last_w[w] = j
            self.readers[w] = []
        return j

    def dma(self, q, fn, reads=(), writes=()):
        return self.op(q, fn, reads, writes, dma=True)

    def emit(self):
        nc, ops = self.nc, self.ops
        for j, o in enumerate(ops):
            nd = set()
            for d in o["deps"]:
                p = ops[d]
                if not p["dma"] and not o["dma"] and p["eng"] == o["eng"] == "pe":
                    continue
                nd.add(d)
            o["deps"] = nd
            for d in nd:
                ops[d]["sig"] = True
        cnt = {e: 0 for e in COMPUTE}
        sems = {}

        def getsem(key):
            if key not in sems:
                sems[key] = self.gs.enter_context(nc.semaphore("s%s_%s_%s" % (self.tag, key[0], key[1])))
            return sems[key]

        dcount = {}
        dma_i = {}
        for j, o in enumerate(ops):
            if o["dma"]:
                q = o["eng"]
                k = dma_i.get(q, 0)
                dma_i[q] = k + 1
                key = ("d" + q, k % NDMA)
                dcount[key] = dcount.get(key, 0) + 16
                o["semkey"], o["semval"] = key, dcount[key]
                o["prev"] = (key, dcount[key] - 16)
            elif o["sig"]:
                e = o["eng"]
                c = cnt[e]
                cnt[e] = c + 1
                o["semkey"], o["semval"] = (e, c // CH), c % CH + 1
        last_dma = {}
        for o in ops:
            if o["dma"]:
                last_dma[o["semkey"]] = max(last_dma.get(o["semkey"], 0), o["semval"])
            if "semkey" in o:
                getsem(o["semkey"])
        per = {e: [] for e in ("sp", "act", "dve", "pe", "pool")}
        for j, o in enumerate(ops):
            per[o["eng"]].append(j)
        nwc = [0]

        def run_engine(e, E):
            known = {}
            for j in per[e]:
                o = ops[j]
                need = {}
                for d in o["deps"]:
                    p = ops[d]
                    k, v = p["semkey"], p["semval"]
                    if need.get(k, 0) < v:
                        need[k] = v
                if o["dma"] and o["prev"][1] > 0:
                    k, v = o["prev"]
                    if need.get(k, 0) < v:
                        need[k] = v
                for k, v in need.items():
                    if known.get(k, 0) >= v:
                        continue
                    known[k] = v
                    E.wait_ge(sems[k], v)
                    nwc[0] += 1
                inst = o["fn"](E)
                if o["dma"]:
                    inst.then_inc(sems[o["semkey"]], 16)
                elif o["sig"]:
                    inst.then_inc(sems[o["semkey"]], 1)
            if e == "sp":
                for k, v in last_dma.items():
                    if known.get(k, 0) < v:
                        E.wait_ge(sems[k], v)

        with nc.Block() as block:
            @block.sync
            def _(E):
                run_engine("sp", E)

            @block.scalar
            def _(E):
                run_engine("act", E)

            @block.vector
            def _(E):
                run_engine("dve", E)

            @block.tensor
            def _(E):
                run_engine("pe", E)

            @block.gpsimd
            def _(E):
                run_engine("pool", E)
        self.stats = dict(tag=self.tag, nops=len(ops), nwaits=nwc[0], nsems=len(sems), cnt=cnt)


def MM(P, out, lhsT, rhs, start, stop, rd, wr, skip=False):
    P.op("pe", lambda E: E.matmul(out, lhsT=lhsT, rhs=rhs, start=start, stop=stop, skip_group_check=skip), rd, wr)


def TR(P, out, in_, ident, rd, wr):
    P.op("pe", lambda E: E.transpose(out, in_, ident), rd, wr)


def ACTF(P, out, in_, func, rd, wr, bias=None, scale=None, accum=None):
    kw = {}
    if bias is not None:
        kw["bias"] = bias
    if scale is not None:
        kw["scale"] = scale
    if accum is not None:
        kw["accum_out"] = accum
    P.op("act", lambda E: E.activation(out=out, in_=in_, func=func, **kw), rd, wr)


def DMA(P, q, out, in_, rd, wr):
    P.dma(q, lambda E: E.dma_start(out=out, in_=in_), rd, wr)


class RR:
    def __init__(self, engs):
        self.engs = engs
        self.i = 0

    def copy(self, P, out, in_, rd, wr, scale=None):
        e = self.engs[self.i % len(self.engs)]
        self.i += 1
        if e == "act":
            ACTF(P, out, in_, AF.Copy, rd, wr, scale=scale)
        elif scale is None:
            P.op(e, lambda E: E.tensor_copy(out=out, in_=in_), rd, wr)
        else:
            P.op(e, lambda E: E.tensor_scalar(out=out, in0=in_, scalar1=float(scale), scalar2=None, op0=ALU.mult), rd, wr)


def own_tiles(p, NSL):
    res = []
    for j in range(NSL):
        first = (j % 2 == 0) if p == 0 else (j % 2 == 1)
        res.append(2 * j if first else 2 * j + 1)
    return res


def build_program(S, debug=False, stop_after="d"):
    NT = S // 512
    NSL = NT // 2
    NB = S // 128
    NOWN = NSL * 512
    NQ = NOWN + 16
    nc = bass.Bass("TRN2", target_bir_lowering=False)
    T = {}

    def din(name, shape, dt=F32):
        T[name] = nc.dram_tensor(name, list(shape), dt, kind="ExternalInput").ap()

    def dscr(name, shape, dt=BF16):
        kind = "ExternalOutput" if debug else "Internal"
        T[name] = nc.dram_tensor(name, list(shape), dt, kind=kind).ap()

    din("xn", [S, D]); din("xo", [NQ, D])
    din("w_in", [D, INC]); din("w_out", [D, D]); din("w_up", [D, 2 * DFF]); din("w_down", [DFF, D])
    din("attn_g", [128, D]); din("ffn_g", [128, D]); din("fin_g", [128, D])
    din("fox_g", [128, 512]); din("sb_g", [128, 512]); din("fb", [128, 8])
    din("cw", [128, NCH, 3]); din("cb", [128, NCH])
    din("esel", [128, 2 * NSL]); din("eselh", [128, 2 * NSL]); din("hmask", [128, 16])
    din("idE", [128, NSL, 128], BF16); din("idNE", [128, NSL, 128], BF16); din("erow", [1, NSL, 128], BF16)
    din("mh_f", [128, NB, 16], BF16); din("mh_s", [128, NB, 16], BF16)
    din("ident_bf", [128, 128], BF16); din("ident_f", [128, 128]); din("triu_f", [128, 128]); din("ones_f", [128, 128])
    din("negtri", [128, 128], BF16); din("negones", [128, 128], BF16)
    din("mt_f", [128, 4, 512], BF16); din("mt_s", [128, 4, 512], BF16); din("negrow", [1, 512], BF16)
    din("cm1", [64, 3, 128], BF16); din("cp1", [64, 3, 128], BF16)
    T["out"] = nc.dram_tensor("out", [NOWN, D], F32, kind="ExternalOutput").ap()
    dscr("Kd", [16, 64, S]); dscr("Vd", [16, 128, NB, 65]); dscr("Qd", [16, 64, NQ])
    dscr("KA", [8, 6, S]); dscr("QA", [8, 6, S]); dscr("Od", [NQ + 112, D])
    dscr("Wup", [128, 8, 2 * DFF]); dscr("Wdn", [128, 22, D]); dscr("Wout", [128, 8, D])
    if debug:
        dscr("dbg_nF", [128, 8, NB], F32)

    stats = []
    with ExitStack() as gs:
        def sbt(es, name, shape, dt):
            return es.enter_context(nc.sbuf_tensor("sb_" + name, list(shape), dt))

        def pst(es, name, shape, dt):
            return es.enter_context(nc.psum_tensor("pp_" + name, list(shape), dt))

        with ExitStack() as es:
            P = Prog(nc, gs, "a")
            win = sbt(es, "win", [128, 8, INC], BF16)
            wstg = [sbt(es, "wstg%d" % i, [128, INC], F32) for i in range(2)]
            g_r = sbt(es, "g_r", [128, D], F32)
            fb_r = sbt(es, "fb_r", [128, 8], F32)
            ident = sbt(es, "ident", [128, 128], BF16)
            xb = [sbt(es, "xb%d" % i, [128, D], F32) for i in range(3)]
            junk = sbt(es, "junk", [128, D], BF16)
            stat = sbt(es, "stat", [128, 3, 8], F32)
            xnb = [sbt(es, "xnb%d" % i, [128, D], BF16) for i in range(2)]
            hT = [sbt(es, "hT%d" % i, [128, 8, 512], BF16) for i in range(2)]
            kts = [sbt(es, "kts%d" % i, [128, 512], BF16) for i in range(3)]
            vs = [sbt(es, "vs%d" % i, [128, 16, 4, 65], BF16) for i in range(2)]
            flog = sbt(es, "flog", [128, 8, NB], F32)
            cst = [sbt(es, "cst%d" % i, [128, 2816], F32) for i in range(2)]
            cbf = [sbt(es, "cbf%d" % i, [128, 2816], BF16) for i in range(2)]
            esp = ExitStack()
            pT = [pst(esp, "pT%d" % i, [128, 8, 128], BF16) for i in range(2)]
            psk = [pst(esp, "psk%d" % i, [128, 512], F32) for i in range(2)]
            psv = [pst(esp, "psv%d" % i, [128, 512], F32) for i in range(2)]
            psf = [pst(esp, "psf%d" % i, [128, 512], F32) for i in range(2)]
            rr = RR(["act", "dve"])

            DMA(P, "sp", g_r[:], T["attn_g"][:, :], [], ["g_r"])
            DMA(P, "sp", fb_r[:], T["fb"][:, :], [], ["fb_r"])
            DMA(P, "sp", ident[:], T["ident_bf"][:, :], [], ["ident"])
            for k in range(8):
                DMA(P, "sp", wstg[k % 2][:], T["w_in"][k * 128:(k + 1) * 128, :], [], ["wstg%d" % (k % 2)])
                P.op("pool", lambda E, k=k: E.tensor_copy(out=win[:, k, :], in_=wstg[k % 2][:]), ["wstg%d" % (k % 2)], ["win"])
            for i in range(2):
                P.op("pool", lambda E, i=i: E.memset(vs[i][:], 1.0), [], ["vs%d.%d.%d" % (i, b, g) for b in range(4) for g in range(2)])
                P.op("pool", lambda E, i=i: E.memset(xnb[i][:], 0.0), [], ["xnb%d" % i])
            jobs = []
            for k in range(8):
                for hf in range(2):
                    jobs.append((T["w_up"][k * 128:(k + 1) * 128, hf * 2816:(hf + 1) * 2816], T["Wup"][:, k, hf * 2816:(hf + 1) * 2816], 2816, None))
            for c in range(0, 22, 2):
                jobs.append((T["w_down"][c * 128:(c + 2) * 128, :].rearrange("(c p) n -> p c n", p=128), T["Wdn"][:, c:c + 2, :], 2048, 2))
            for k in range(0, 8, 2):
                jobs.append((T["w_out"][k * 128:(k + 2) * 128, :].rearrange("(c p) n -> p c n", p=128), T["Wout"][:, k:k + 2, :], 2048, 2))
            if "nojobs" in OPT:
                jobs = []
            for n, (src, dst, width, sub) in enumerate(jobs):
                b = n % 2
                if sub is None:
                    s_ap, b_ap = cst[b][:, :width], cbf[b][:, :width]
                else:
                    s_ap = cst[b][:, :width].rearrange("p (c n) -> p c n", c=sub)
                    b_ap = cbf[b][:, :width].rearrange("p (c n) -> p c n", c=sub)
                DMA(P, "pool", s_ap, src, [], ["cst%d" % b])
                P.op("pool", lambda E, b=b, width=width: E.tensor_copy(out=cbf[b][:, :width], in_=cst[b][:, :width]), ["cst%d" % b], ["cbf%d" % b])
                DMA(P, "pool", dst, b_ap, ["cbf%d" % b], ["Wscr"])

            blkn = [0]

            def norm_block(src_rows, rows, hbuf, col0):
                n = blkn[0]
                blkn[0] += 1
                x = xb[n % 3]
                xs = "xb%d" % (n % 3)
                st = stat[:, :, n % 8:n % 8 + 1]
                sk = "stat%d" % (n % 8)
                DMA(P, "sp", x[:rows, :], src_rows, [], [xs])
                ACTF(P, junk[:rows, :], x[:rows, :], AF.Square, [xs], ["junk", sk], accum=stat[:rows, 0, n % 8:n % 8 + 1])
                ACTF(P, stat[:rows, 1, n % 8:n % 8 + 1], stat[:rows, 0, n % 8:n % 8 + 1], AF.Ln, [sk], [sk], bias=EPS, scale=1.0 / D)
                ACTF(P, stat[:rows, 2, n % 8:n % 8 + 1], stat[:rows, 1, n % 8:n % 8 + 1], AF.Exp, [sk], [sk], scale=-0.5)
                xq = xnb[n % 2]
                qs = "xnb%d" % (n % 2)
                P.op("dve", lambda E: E.scalar_tensor_tensor(out=xq[:rows, :], in0=x[:rows, :], scalar=stat[:rows, 2, n % 8:n % 8 + 1],
                                                             in1=g_r[:rows, :], op0=ALU.mult, op1=ALU.mult), [xs, sk, "g_r"], [qs])
                pt = pT[n % 2]
                ps = "pT%d" % (n % 2)
                for k in range(8):
                    TR(P, pt[:, k, :], xq[:, k * 128:(k + 1) * 128], ident[:, :], [qs, "ident"], [ps])
                rr.copy(P, hT[hbuf][:, :, col0:col0 + rows], pt[:, :, :rows], [ps], ["hT%d.%d" % (hbuf, col0 // 128)])

            def proj_T(hbuf, width, col_w, dst, hkeys, scale=None):
                n = proj_T.n
                proj_T.n += 1
                ps = psk[n % 2]
                pk = "psk%d" % (n % 2)
                for k in range(8):
                    MM(P, ps[:, :width], win[:, k, col_w:col_w + 128], hT[hbuf][:, k, :width], k == 0, k == 7, ["win"] + hkeys, [pk])
                ks = kts[n % 3]
                kk = "kts%d" % (n % 3)
                rr.copy(P, ks[:, :width], ps[:, :width], [pk], [kk], scale=scale)
                DMA(P, "sp", dst, ks[:, :width], [kk], ["KQscr"])
            proj_T.n = 0

            tcount = [0]
            vcount = [0]

            def tile_A(Tn):
                hb = tcount[0] % 2
                tcount[0] += 1
                for b in range(4):
                    norm_block(T["xn"][Tn * 512 + b * 128:Tn * 512 + (b + 1) * 128, :], 128, hb, b * 128)
                hk = ["hT%d.%d" % (hb, b) for b in range(4)]
                for c in range(8):
                    col = (CK_F + c * 128) if c < 4 else (CK_S + (c - 4) * 128)
                    h0 = 2 * c if c < 4 else 8 + 2 * (c - 4)
                    proj_T(hb, 512, col, T["Kd"][h0:h0 + 2, :, Tn * 512:(Tn + 1) * 512].rearrange("h r t -> (h r) t"), hk)
                vb = Tn % 2
                for b in range(4):
                    for g in range(2):
                        n = vcount[0]
                        vcount[0] += 1
                        ps = psv[n % 2]
                        pk = "psv%d" % (n % 2)
                        vc = CV_F if g == 0 else CV_S
                        for k in range(8):
                            MM(P, ps[:, :], hT[hb][:, k, b * 128:(b + 1) * 128], win[:, k, vc:vc + 512], k == 0, k == 7,
                               ["win", "hT%d.%d" % (hb, b)], [pk])
                        rr.copy(P, vs[vb][:, g * 8:(g + 1) * 8, b, 0:64], ps[:, :].rearrange("p (h d) -> p h d", h=8), [pk],
                                ["vs%d.%d.%d" % (vb, b, g)])
                    pf = psf[b % 2]
                    for k in range(8):
                        MM(P, pf[:, 0:8], hT[hb][:, k, b * 128:(b + 1) * 128], win[:, k, CL:CL + 8], k == 0, k == 7,
                           ["win", "hT%d.%d" % (hb, b)], ["psf%d" % (b % 2)])
                    P.op("dve", lambda E, b=b, Tn=Tn, pf=pf: E.tensor_tensor(out=flog[:, :, Tn * 4 + b], in0=pf[:, 0:8], in1=fb_r[:, :], op=ALU.add),
                         ["psf%d" % (b % 2), "fb_r"], ["flog"])
                for h in range(16):
                    DMA(P, "sp", T["Vd"][h, :, Tn * 4:Tn * 4 + 4, :], vs[vb][:, h, :, :],
                        ["vs%d.%d.%d" % (vb, b, g) for b in range(4) for g in range(2)], ["Vscr"])

            def tile_Q(row0, width, col0):
                hb = tcount[0] % 2
                tcount[0] += 1
                nb = (width + 127) // 128
                for b in range(nb):
                    rows = min(128, width - b * 128)
                    norm_block(T["xo"][row0 + b * 128:row0 + b * 128 + rows, :], rows, hb, b * 128)
                hk = ["hT%d.%d" % (hb, b) for b in range(nb)]
                for c in range(8):
                    col = (CQ_F + c * 128) if c < 4 else (CQ_S + (c - 4) * 128)
                    h0 = 2 * c if c < 4 else 8 + 2 * (c - 4)
                    proj_T(hb, width, col, T["Qd"][h0:h0 + 2, :, col0:col0 + width].rearrange("h r t -> (h r) t"), hk, scale=0.125)

            qi = 0
            for Tn in range(NT if "noA" not in OPT else 0):
                tile_A(Tn)
                if Tn % 2 == 1 and qi < NSL:
                    tile_Q(qi * 512, 512, qi * 512)
                    qi += 1
            if "noQh" not in OPT:
                tile_Q(NOWN, 16, NOWN)
            P.emit()
            stats.append(P.stats)
            esp.close()
            if stop_after == "a":
                return nc, stats

            with ExitStack() as es2:
                P = Prog(nc, gs, "b")
                nlf = sbt(es2, "nlf", [128, 8 * NB], F32)
                ee = sbt(es2, "ee", [128, 8 * NB], F32)
                sc = [sbt(es2, "sc%d" % i, [128, 8, NB], F32) for i in range(2)]
                tot = sbt(es2, "tot", [128, 8, NB], F32)
                nF = sbt(es2, "nF", [128, 8, NB], F32)
                nFp = sbt(es2, "nFp", [128, 8, 128], F32)
                nFT = sbt(es2, "nFT", [NB, 8, 128], F32)
                r1 = sbt(es2, "r1", [NB, 8, 128], F32)
                parts = [sbt(es2, "part%d" % i, [NB, 8, 128], BF16) for i in range(3)]
                triu = sbt(es2, "triu", [128, 128], F32)
                ones = sbt(es2, "ones", [128, 128], F32)
                identf = sbt(es2, "identf", [128, 128], F32)
                cm1 = sbt(es2, "cm1", [64, 3, 128], BF16)
                cp1 = sbt(es2, "cp1", [64, 3, 128], BF16)
                ps_c = pst(es2, "ps_c", [128, 512], F32)
                ps_t = pst(es2, "ps_t", [128, 512], F32)
                ps_x = [pst(es2, "ps_x%d" % i, [128, 4, 128], F32) for i in range(2)]
                W8 = 8 * NB
                DMA(P, "sp", triu[:], T["triu_f"][:, :], [], ["triu"])
                DMA(P, "sp", ones[:], T["ones_f"][:, :], [], ["ones"])
                DMA(P, "sp", identf[:], T["ident_f"][:, :], [], ["identf"])
                DMA(P, "sp", cm1[:], T["cm1"][:, :, :], [], ["cm1"])
                DMA(P, "sp", cp1[:], T["cp1"][:, :, :], [], ["cp1"])
                fl2 = flog[:, :, :].rearrange("p h b -> p (h b)")
                ACTF(P, ee[:, :], fl2, AF.Exp, [], ["ee"], scale=-1.0)
                ACTF(P, nlf[:, :], ee[:, :], AF.Ln, ["ee"], ["nlf"], bias=1.0)
                MM(P, ps_c[:, :W8], triu[:, :], nlf[:, :], True, True, ["triu", "nlf"], ["ps_c"])
                MM(P, ps_t[:, :W8], ones[:, :], nlf[:, :], True, True, ["ones", "nlf"], ["ps_t"])
                P.op("dve", lambda E: E.tensor_copy(out=tot[:, :, :], in_=ps_t[:, :W8].rearrange("p (h b) -> p h b", h=8)), ["ps_t"], ["tot"])
                P.op("dve", lambda E: E.tensor_copy(out=sc[0][:, :, :], in_=tot[:, :, :]), ["tot"], ["sc0"])
                cur = 0
                d = 1
                while d < NB:
                    nxt = 1 - cur
                    P.op("dve", lambda E, cur=cur, nxt=nxt, d=d: E.tensor_copy(out=sc[nxt][:, :, 0:d], in_=sc[cur][:, :, 0:d]), ["sc%d" % cur], ["sc%d" % nxt])
                    P.op("dve", lambda E, cur=cur, nxt=nxt, d=d: E.tensor_tensor(out=sc[nxt][:, :, d:NB], in0=sc[cur][:, :, d:NB], in1=sc[cur][:, :, 0:NB - d], op=ALU.add),
                         ["sc%d" % cur], ["sc%d" % nxt])
                    cur = nxt
                    d *= 2
                P.op("dve", lambda E, cur=cur: E.tensor_tensor(out=tot[:, :, :], in0=sc[cur][:, :, :], in1=tot[:, :, :], op=ALU.subtract), ["sc%d" % cur, "tot"], ["tot"])
                P.op("dve", lambda E: E.tensor_tensor(out=nF[:, :, :], in0=ps_c[:, :W8].rearrange("p (h b) -> p h b", h=8), in1=tot[:, :, :], op=ALU.add), ["ps_c", "tot"], ["nF"])
                if debug:
                    DMA(P, "sp", T["dbg_nF"][:, :, :], nF[:, :, :], ["nF"], ["dbg"])
                P.op("dve", lambda E: E.memset(nFp[:], 0.0), [], ["nFp"])
                P.op("dve", lambda E: E.tensor_copy(out=nFp[:, :, 0:NB], in_=nF[:, :, :]), ["nF", "nFp"], ["nFp"])
                for h in range(8):
                    px = ps_x[h // 4]
                    P.op("pe", lambda E, h=h, px=px: E.transpose(px[:, h % 4, :], nFp[:, h, :], identf[:, :]), ["nFp", "identf"], ["ps_x%d" % (h // 4)])
                for q in range(2):
                    P.op("dve", lambda E, q=q: E.tensor_copy(out=nFT[:, q * 4:(q + 1) * 4, :], in_=ps_x[q][:NB, :, :]), ["ps_x%d" % q], ["nFT"])
                P.op("dve", lambda E: E.tensor_copy(out=parts[0][:, :, :], in_=nFT[:, :, :]), ["nFT"], ["part0"])
                P.op("dve", lambda E: E.tensor_tensor(out=r1[:, :, :], in0=nFT[:, :, :], in1=parts[0][:, :, :], op=ALU.subtract), ["nFT", "part0"], ["r1"])
                P.op("dve", lambda E: E.tensor_copy(out=parts[1][:, :, :], in_=r1[:, :, :]), ["r1"], ["part1"])
                P.op("dve", lambda E: E.tensor_tensor(out=nFT[:, :, :], in0=r1[:, :, :], in1=parts[1][:, :, :], op=ALU.subtract), ["r1", "part1"], ["nFT"])
                P.op("dve", lambda E: E.tensor_copy(out=parts[2][:, :, :], in_=nFT[:, :, :]), ["nFT"], ["part2"])
                for i in range(3):
                    DMA(P, "sp", T["KA"][:, i, :].rearrange("h (b t) -> b h t", t=128), parts[i][:, :, :], ["part%d" % i], ["KAs"])
                    DMA(P, "sp", T["QA"][:, 3 + i, :].rearrange("h (b t) -> b h t", t=128), parts[i][:, :, :], ["part%d" % i], ["QAs"])
                for h in range(8):
                    DMA(P, "sp", T["KA"][h, 3:6, :].rearrange("r (b t) -> b r t", t=128), cm1[:NB, :, :], ["cm1"], ["KAs"])
                    DMA(P, "sp", T["QA"][h, 0:3, :].rearrange("r (b t) -> b r t", t=128), cp1[:NB, :, :], ["cp1"], ["QAs"])
                P.emit()
                stats.append(P.stats)
        if stop_after == "b":
            return nc, stats

        with ExitStack() as es:
            P = Prog(nc, gs, "c")
            KT = [sbt(es, "KT%d" % i, [70, S], BF16) for i in range(2)]
            QT = [sbt(es, "QT%d" % i, [70, NQ], BF16) for i in range(2)]
            VV = [sbt(es, "VV%d" % i, [128, NB, 65], BF16) for i in range(2)]
            qa = sbt(es, "qa", [70, S], BF16)
            qtmp = sbt(es, "qtmp", [70, 512], BF16)
            esel = sbt(es, "esel", [128, 2 * NSL], F32)
            eselh = sbt(es, "eselh", [128, 2 * NSL], F32)
            idE = sbt(es, "idE", [128, NSL, 128], BF16)
            idNE = sbt(es, "idNE", [128, NSL, 128], BF16)
            erow = sbt(es, "erow", [1, NSL, 128], BF16)
            negrow = sbt(es, "negrow", [1, 512], BF16)
            allneg = sbt(es, "allneg", [128, 512], BF16)
            ident = sbt(es, "ident2", [128, 128], BF16)
            mt = [sbt(es, "mt%d" % i, [128, 4, 512], BF16) for i in range(2)]
            mh = [sbt(es, "mh%d" % i, [128, NB, 16], BF16) for i in range(2)]
            negtri = sbt(es, "negtri", [128, 128], BF16)
            negones = sbt(es, "negones", [128, 128], BF16)
            PT = [sbt(es, "PT%d" % i, [128, 512], BF16) for i in range(3)]
            UU = [sbt(es, "UU%d" % i, [128, 512], F32) for i in range(2)]
            LL = [sbt(es, "LL%d" % i, [128, 512], BF16) for i in range(3)]
            LS = [sbt(es, "LS%d" % i, [128, 512], BF16) for i in range(3)]
            rden = sbt(es, "rden", [128, 2, 4], F32)
            ob = [sbt(es, "ob%d" % i, [128, 4, 64], BF16) for i in range(3)]
            psZ = [pst(es, "psZ%d" % i, [128, 512], F32) for i in range(4)]
            psO = [pst(es, "psO%d" % i, [128, 512], F32) for i in range(2)]
            for nm, t_, src in (("esel", esel, T["esel"][:, :]), ("eselh", eselh, T["eselh"][:, :]), ("idE", idE, T["idE"][:, :, :]),
                                ("idNE", idNE, T["idNE"][:, :, :]), ("erow", erow, T["erow"][:, :, :]), ("negrow", negrow, T["negrow"][:, :]),
                                ("ident", ident, T["ident_bf"][:, :]), ("mt0", mt[0], T["mt_f"][:, :, :]), ("mt1", mt[1], T["mt_s"][:, :, :]),
                                ("mh0", mh[0], T["mh_f"][:, :, :]), ("mh1", mh[1], T["mh_s"][:, :, :]),
                                ("negtri", negtri, T["negtri"][:, :]), ("negones", negones, T["negones"][:, :])):
                DMA(P, "sp", t_[:], src, [], [nm])
            P.op("pool", lambda E: E.memset(allneg[:], NEG), [], ["allneg"])
            for i in range(3):
                P.op("pool", lambda E, i=i: E.memset(PT[i][:], 0.0), [], ["PT%d" % i])
            CONSTS = ["idE", "idNE", "erow", "negrow", "ident", "mt0", "mt1", "mh0", "mh1", "allneg"]

            def load_head(hh):
                hb = hh % 2
                fox = hh < 8
                DMA(P, "sp", KT[hb][0:64, :], T["Kd"][hh, :, :], [], ["KTm%d" % hb])
                DMA(P, "sp", QT[hb][0:64, :], T["Qd"][hh, :, :], [], ["QTm%d" % hb])
                DMA(P, "sp", VV[hb][:, :, :], T["Vd"][hh, :, :, :], [], ["VV%d" % hb])
                if not fox:
                    P.op("pool", lambda E: E.memset(KT[hb][64:70, :], 0.0), [], ["KTa%d" % hb])
                    P.op("pool", lambda E: E.memset(QT[hb][64:70, :], 0.0), [], ["QTa%d" % hb])
                if fox:
                    DMA(P, "sp", KT[hb][64:70, :], T["KA"][hh, :, :], [], ["KTa%d" % hb])
                    DMA(P, "sp", qa[64:70, :], T["QA"][hh, :, :], [], ["qa"])
                    for j in range(NSL + 1):
                        if j < NSL:
                            c0 = qa[64:70, 1024 * j:1024 * j + 512]
                            c1 = qa[64:70, 1024 * j + 512:1024 * j + 1024]
                            e0 = esel[64:70, 2 * j:2 * j + 1]
                            e1 = esel[64:70, 2 * j + 1:2 * j + 2]
                            dst = QT[hb][64:70, 512 * j:512 * j + 512]
                            tmp = qtmp[64:70, 0:512]
                            P.op("dve", lambda E, c0=c0, e0=e0, tmp=tmp: E.tensor_scalar(out=tmp, in0=c0, scalar1=e0, scalar2=None, op0=ALU.mult),
                                 ["qa", "esel"], ["qtmp"])
                            P.op("dve", lambda E, c1=c1, e1=e1, tmp=tmp, dst=dst: E.scalar_tensor_tensor(out=dst, in0=c1, scalar=e1, in1=tmp, op0=ALU.mult, op1=ALU.add),
                                 ["qa", "esel", "qtmp"], ["QTa%d" % hb])
                        else:
                            for jj in range(NSL):
                                a0 = max(0, 1024 * jj - 2)
                                c0 = qa[64:70, a0:a0 + 2]
                                c1 = qa[64:70, 1024 * jj + 510:1024 * jj + 512]
                                e0 = eselh[64:70, 2 * jj:2 * jj + 1]
                                e1 = eselh[64:70, 2 * jj + 1:2 * jj + 2]
                                dst = QT[hb][64:70, NOWN + 2 * jj:NOWN + 2 * jj + 2]
                                tmp = qtmp[64:70, 0:2]
                                P.op("dve", lambda E, c0=c0, e0=e0, tmp=tmp: E.tensor_scalar(out=tmp, in0=c0, scalar1=e0, scalar2=None, op0=ALU.mult),
                                     ["qa", "eselh"], ["qtmp"])
                                P.op("dve", lambda E, c1=c1, e1=e1, tmp=tmp, dst=dst: E.scalar_tensor_tensor(out=dst, in0=c1, scalar=e1, in1=tmp, op0=ALU.mult, op1=ALU.add),
                                     ["qa", "eselh", "qtmp"], ["QTa%d" % hb])

            units = []
            slot_ctr = 0
            for hh in range(16):
                fox = hh < 8
                if "noSB" in OPT and not fox:
                    continue
                if "noFOX" in OPT and fox:
                    continue
                for sl in range(NSL + 1):
                    halo = sl == NSL
                    if halo and "noHalo" in OPT:
                        continue
                    W = 16 if halo else 512
                    nkb = NB if halo else 8 * (sl + 1)
                    order = list(range(nkb)) if fox else list(range(nkb - 1, -1, -1))
                    for idx, kb in enumerate(order):
                        if halo:
                            mk = ("H", kb)
                        elif 8 * sl <= kb < 8 * sl + 4:
                            mk = ("E", kb - 8 * sl)
                        elif 8 * sl + 4 <= kb < 8 * sl + 8:
                            mk = ("NE", kb - 8 * sl - 4)
                        else:
                            mk = None
                        units.append(dict(hh=hh, fox=fox, sl=sl, W=W, kb=kb, first=idx == 0, last=idx == nkb - 1, mk=mk,
                                          so=slot_ctr, qc=NOWN if halo else 512 * sl))
                    slot_ctr += 1
            for i, u in enumerate(units):
                u["i"] = i
                u["head_first"] = (i == 0 or units[i - 1]["hh"] != u["hh"])
                u["head_last"] = (i == len(units) - 1 or units[i + 1]["hh"] != u["hh"])

            def stage_A(u):
                hb = u["hh"] % 2
                KD = 70
                W, kb, i = u["W"], u["kb"], u["i"]
                z = psZ[i % 4]
                zk = "psZ%d" % (i % 4)
                mi = 0 if u["fox"] else 1
                rd = ["KTm%d" % hb, "QTm%d" % hb, "KTa%d" % hb, "QTa%d" % hb]
                mk = u["mk"] if "noMask" not in OPT else None
                MM(P, z[:, :W], KT[hb][0:KD, kb * 128:(kb + 1) * 128], QT[hb][0:KD, u["qc"]:u["qc"] + W], True, mk is None and u["fox"], rd, [zk])
                if mk is not None:
                    typ, m = mk
                    sl = u["sl"]
                    if typ == "H":
                        MM(P, z[:, :W], ident[:, :], mh[mi][:, m, :], False, u["fox"], CONSTS, [zk])
                    elif typ == "E":
                        MM(P, z[:, :W], idE[:, sl, :], mt[mi][:, m, :], False, u["fox"], CONSTS, [zk])
                    else:
                        MM(P, z[:, :W], idNE[:, sl, :], mt[mi][:, m, :], False, False, CONSTS, [zk])
                        MM(P, z[:, :W], idE[:, sl, :], allneg[:, :W], False, u["fox"], CONSTS, [zk])

            def stage_B_fox(u):
                W, i = u["W"], u["i"]
                ACTF(P, PT[i % 3][:, :W], psZ[i % 4][:, :W], AF.Exp, ["psZ%d" % (i % 4)], ["PT%d" % (i % 3)])

            def stage_B1(u):
                W, i = u["W"], u["i"]
                ACTF(P, UU[i % 2][:, :W], psZ[i % 4][:, :W], AF.Exp, ["psZ%d" % (i % 4)], ["UU%d" % (i % 2)])
                ACTF(P, LL[i % 3][:, :W], UU[i % 2][:, :W], AF.Ln, ["UU%d" % (i % 2)], ["LL%d" % (i % 3)], bias=1.0)
                if not u["last"]:
                    if u["first"]:
                        P.op("pool", lambda E: E.tensor_copy(out=LS[(i + 1) % 3][:, :W], in_=LL[i % 3][:, :W]), ["LL%d" % (i % 3)], ["LS%d" % ((i + 1) % 3)])
                    else:
                        P.op("pool", lambda E: E.tensor_tensor(out=LS[(i + 1) % 3][:, :W], in0=LS[i % 3][:, :W], in1=LL[i % 3][:, :W], op=ALU.add),
                             ["LL%d" % (i % 3), "LS%d" % (i % 3)], ["LS%d" % ((i + 1) % 3)])

            def stage_C(u):
                W, i = u["W"], u["i"]
                z = psZ[i % 4]
                zk = "psZ%d" % (i % 4)
                MM(P, z[:, :W], negtri[:, :], LL[i % 3][:, :W], False, u["first"], ["negtri", "LL%d" % (i % 3)], [zk])
                if not u["first"]:
                    MM(P, z[:, :W], negones[:, :], LS[i % 3][:, :W], False, True, ["negones", "LS%d" % (i % 3)], [zk])

            def stage_B2(u):
                W, i = u["W"], u["i"]
                ACTF(P, PT[i % 3][:, :W], psZ[i % 4][:, :W], AF.Exp, ["psZ%d" % (i % 4)], ["PT%d" % (i % 3)])

            fin_ctr = [0]

            def stage_D(u):
                hb = u["hh"] % 2
                W, kb, i = u["W"], u["kb"], u["i"]
                o = psO[u["so"] % 2]
                ok = "psO%d" % (u["so"] % 2)
                NV = 65 if u["fox"] else 64
                nsub = max(1, W // 128)
                M = min(128, W)
                for s_ in range(nsub):
                    MM(P, o[:, s_ * NV:(s_ + 1) * NV], PT[i % 3][:, s_ * 128:s_ * 128 + 128], VV[hb][:, kb, 0:NV],
                       u["first"] and s_ == 0, u["last"] and s_ == nsub - 1, ["PT%d" % (i % 3), "VV%d" % hb], [ok], skip=True)
                if u["last"]:
                    f = fin_ctr[0]
                    fin_ctr[0] += 1
                    obuf = ob[f % 3]
                    obk = "ob%d" % (f % 3)
                    ov = o[:M, 0:nsub * NV].rearrange("p (s v) -> p s v", s=nsub)
                    if u["fox"]:
                        rk = "rden%d" % (f % 2)
                        P.op("dve", lambda E: E.reciprocal(out=rden[:M, f % 2, 0:nsub], in_=ov[:, :, 64]), [ok], [rk])
                        for s_ in range(nsub):
                            P.op("dve", lambda E, s_=s_: E.tensor_scalar(out=obuf[:M, s_, :], in0=ov[:, s_, 0:64], scalar1=rden[:M, f % 2, s_:s_ + 1],
                                                                         scalar2=None, op0=ALU.mult), [ok, rk], [obk])
                    else:
                        P.op("dve", lambda E: E.tensor_copy(out=obuf[:M, 0:nsub, :], in_=ov[:, :, 0:64]), [ok], [obk])
                    r0 = u["qc"]
                    hh = u["hh"]
                    if nsub == 4:
                        dst = T["Od"][r0:r0 + 512, hh * 64:(hh + 1) * 64].rearrange("(s p) d -> p s d", p=128)
                        DMA(P, "sp", dst, obuf[:, :, :], [obk], ["Od"])
                    else:
                        DMA(P, "sp", T["Od"][r0:r0 + M, hh * 64:(hh + 1) * 64], obuf[:M, 0, :], [obk], ["Od"])

            n = len(units)
            hh0 = units[0]["hh"]
            load_head(hh0)
            stage_A(units[0])
            for i in range(n):
                u = units[i]
                if u["head_first"] and u["hh"] + 1 < 16 and u["hh"] == hh0:
                    load_head(hh0 + 1)
                if i + 1 < n:
                    stage_A(units[i + 1])
                if u["fox"]:
                    stage_B_fox(u)
                else:
                    stage_B1(u)
                    stage_C(u)
                if i >= 1:
                    pu = units[i - 1]
                    if not pu["fox"]:
                        stage_B2(pu)
                    stage_D(pu)
                    if pu["head_last"] and pu["hh"] + 2 < 16:
                        load_head(pu["hh"] + 2)
            pu = units[n - 1]
            if not pu["fox"]:
                stage_B2(pu)
            stage_D(pu)
            P.emit()
            stats.append(P.stats)
        if stop_after == "c":
            return nc, stats

        with ExitStack() as es:
            P = Prog(nc, gs, "d")
            wout = sbt(es, "wout", [128, 8, D], BF16)
            fox_g = sbt(es, "fox_g", [128, 512], F32)
            sb_g = sbt(es, "sb_g", [128, 512], F32)
            ffn_g = sbt(es, "ffn_g", [128, D], F32)
            fin_g = sbt(es, "fin_g", [128, D], F32)
            cw = sbt(es, "cw", [128, NCH, 3], F32)
            cb = sbt(es, "cb", [128, NCH], F32)
            hmask = sbt(es, "hmask", [128, 16], F32)
            ident = sbt(es, "ident3", [128, 128], BF16)
            identf = sbt(es, "identf3", [128, 128], F32)
            o_s = [sbt(es, "o_s%d" % i, [128, 4, D], BF16) for i in range(2)]
            xr = [sbt(es, "xr%d" % i, [128, 4, D], F32) for i in range(2)]
            junk = sbt(es, "junk3", [128, D], BF16)
            stat = sbt(es, "stat3", [128, 3, 16], F32)
            on = sbt(es, "on", [128, 4, D], BF16)
            onT = sbt(es, "onT", [128, 8, 512], BF16)
            h2T = sbt(es, "h2T", [128, 8, 512], BF16)
            wu = [sbt(es, "wu%d" % i, [128, 8, 2, 128], BF16) for i in range(3)]
            ub = [sbt(es, "ub%d" % i, [128, 514], F32) for i in range(4)]
            cv = [sbt(es, "cv%d" % i, [128, 512], F32) for i in range(4)]
            sg = [sbt(es, "sg%d" % i, [128, 512], F32) for i in range(2)]
            aT = sbt(es, "aT", [128, 22, 512], BF16)
            uh = sbt(es, "uh", [128, NCH, 16], F32)
            wd = [sbt(es, "wd%d" % i, [128, 22, 128], BF16) for i in range(3)]
            yT = [sbt(es, "yT%d" % i, [128, 512], F32) for i in range(2)]
            pT = [pst(es, "p3T%d" % i, [128, 8, 128], BF16) for i in range(2)]
            psa = [pst(es, "psa%d" % i, [128, 512], F32) for i in range(2)]
            psu = [pst(es, "psu%d" % i, [128, 512], F32) for i in range(2)]
            psy = pst(es, "psy", [128, 512], F32)
            pst_ = pst(es, "pst", [128, 4, 128], F32)
            rr = RR(["act", "dve"])
            for nm, t_, src in (("wout", wout, T["Wout"][:, :, :]), ("fox_g", fox_g, T["fox_g"][:, :]), ("sb_g", sb_g, T["sb_g"][:, :]),
                                ("ffn_g", ffn_g, T["ffn_g"][:, :]), ("fin_g", fin_g, T["fin_g"][:, :]), ("cw", cw, T["cw"][:, :, :]),
                                ("cb", cb, T["cb"][:, :]), ("hmask", hmask, T["hmask"][:, :]), ("ident", ident, T["ident_bf"][:, :]),
                                ("identf", identf, T["ident_f"][:, :])):
                DMA(P, "sp", t_[:], src, [], [nm])

            P.op("pool", lambda E: E.memset(on[:], 0.0), [], ["on"])
            P.op("pool", lambda E: E.memset(onT[:], 0.0), [], ["onT.%d" % b for b in range(4)])
            P.op("pool", lambda E: E.memset(h2T[:], 0.0), [], ["h2T.%d" % b for b in range(4)])
            sctr = [0]
            pctr = [0]
            wuc = [0]
            wdc = [0]

            def rstd_of(src_ap, rows, width):
                c = sctr[0] % 16
                sctr[0] += 1
                sk = "st%d" % c
                ACTF(P, junk[:rows, :width], src_ap, AF.Square, src_ap_keys[0], ["junk", sk], accum=stat[:rows, 0, c:c + 1])
                ACTF(P, stat[:rows, 1, c:c + 1], stat[:rows, 0, c:c + 1], AF.Ln, [sk], [sk], bias=EPS, scale=1.0 / width)
                ACTF(P, stat[:rows, 2, c:c + 1], stat[:rows, 1, c:c + 1], AF.Exp, [sk], [sk], scale=-0.5)
                return stat[:rows, 2, c:c + 1], sk
            src_ap_keys = [None]

            def transpose_to(src_tile, src_key, blocks, dstT, dst_key):
                for (bi, rows) in blocks:
                    n = pctr[0]
                    pctr[0] += 1
                    pt = pT[n % 2]
                    pk = "p3T%d" % (n % 2)
                    for k in range(8):
                        TR(P, pt[:, k, :], src_tile[:, bi, k * 128:(k + 1) * 128], ident[:, :], [src_key, "ident"], [pk])
                    rr.copy(P, dstT[:, :, bi * 128:bi * 128 + rows], pt[:, :, :rows], [pk], [dst_key + ".%d" % bi])

            slots = [NSL] + list(range(NSL))

            def do_slot(si, sl):
                halo = sl == NSL
                W = 16 if halo else 512
                r0 = NOWN if halo else 512 * sl
                blocks = [(0, 16)] if halo else [(b, 128) for b in range(4)]
                sb_ = si % 2
                os_, xr_ = o_s[sb_], xr[sb_]
                osk, xrk = "o_s%d" % sb_, "xr%d" % sb_
                if halo:
                    DMA(P, "sp", os_[:16, 0, :], T["Od"][r0:r0 + 16, :], [], [osk])
                    DMA(P, "sp", xr_[:16, 0, :], T["xo"][r0:r0 + 16, :], [], [xrk])
                else:
                    DMA(P, "sp", os_[:, :, :], T["Od"][r0:r0 + 512, :].rearrange("(b p) d -> p b d", p=128), [], [osk])
                    DMA(P, "sp", xr_[:, :, :], T["xo"][r0:r0 + 512, :].rearrange("(b p) d -> p b d", p=128), [], [xrk])
                for (bi, rows) in blocks:
                    for g in range(2):
                        src = os_[:rows, bi, g * 512:(g + 1) * 512]
                        src_ap_keys[0] = [osk]
                        rs, sk = rstd_of(src, rows, 512)
                        gt = fox_g if g == 0 else sb_g
                        P.op("dve", lambda E, src=src, rs=rs, gt=gt, rows=rows, bi=bi, g=g: E.scalar_tensor_tensor(
                            out=on[:rows, bi, g * 512:(g + 1) * 512], in0=src, scalar=rs, in1=gt[:rows, :], op0=ALU.mult, op1=ALU.mult),
                            [osk, sk, "fox_g", "sb_g"], ["on"])
                transpose_to(on, "on", blocks, onT, "onT")
                onk = ["onT.%d" % bi for (bi, _) in blocks]
                for (bi, rows) in blocks:
                    for ch in range(2):
                        n = pctr[0]
                        pctr[0] += 1
                        pa = psa[n % 2]
                        pk = "psa%d" % (n % 2)
                        for k in range(8):
                            MM(P, pa[:, :], onT[:, k, bi * 128:bi * 128 + 128], wout[:, k, ch * 512:(ch + 1) * 512], k == 0, k == 7,
                               ["wout", "onT.%d" % bi], [pk])
                        P.op("dve", lambda E, pa=pa, rows=rows, bi=bi, ch=ch: E.tensor_tensor(
                            out=xr_[:rows, bi, ch * 512:(ch + 1) * 512], in0=pa[:rows, :], in1=xr_[:rows, bi, ch * 512:(ch + 1) * 512], op=ALU.add),
                            [pk, xrk], [xrk])
                for (bi, rows) in blocks:
                    src = xr_[:rows, bi, :]
                    src_ap_keys[0] = [xrk]
                    rs, sk = rstd_of(src, rows, D)
                    P.op("dve", lambda E, src=src, rs=rs, rows=rows, bi=bi: E.scalar_tensor_tensor(
                        out=on[:rows, bi, :], in0=src, scalar=rs, in1=ffn_g[:rows, :], op0=ALU.mult, op1=ALU.mult),
                        [xrk, sk, "ffn_g"] + onk, ["on"])
                transpose_to(on, "on", blocks, h2T, "h2T")
                h2k = ["h2T.%d" % bi for (bi, _) in blocks]
                for c in range(22):
                    wn = wuc[0]
                    wuc[0] += 1
                    wt = wu[wn % 3]
                    wk = "wu%d" % (wn % 3)
                    DMA(P, "sp", wt[:, :, 0, :], T["Wup"][:, :, c * 128:(c + 1) * 128], [], [wk + "g"])
                    DMA(P, "sp", wt[:, :, 1, :], T["Wup"][:, :, DFF + c * 128:DFF + (c + 1) * 128], [], [wk + "v"])
                    for gv in range(2):
                        pu_ = psu[gv]
                        pk = "psu%d" % gv
                        for k in range(8):
                            MM(P, pu_[:, :W], wt[:, k, gv, :], h2T[:, k, :W], k == 0, k == 7, [wk + ("g" if gv == 0 else "v")] + h2k, [pk])
                        cc = c + 22 * gv
                        if halo:
                            P.op("dve", lambda E, pu_=pu_, cc=cc: E.tensor_tensor(out=uh[:, cc, :], in0=pu_[:, :16], in1=hmask[:, :], op=ALU.mult),
                                 [pk, "hmask"], ["uh"])
                            continue
                        un = (2 * wn + gv) % 4
                        ut, uk = ub[un], "ub%d" % un
                        ct, ck = cv[un], "cv%d" % un
                        ACTF(P, ut[:, 2:514], pu_[:, :512], AF.Copy, [pk], [uk])
                        P.op("pool", lambda E, ut=ut, cc=cc, sl=sl: E.tensor_copy(out=ut[:, 0:2], in_=uh[:, cc, 2 * sl:2 * sl + 2]), ["uh"], [uk + "h"])
                        P.op("dve", lambda E, ut=ut, ct=ct, cc=cc: E.tensor_scalar(out=ct[:, :], in0=ut[:, 2:514], scalar1=cw[:, cc, 2:3], scalar2=cb[:, cc:cc + 1],
                                                                                op0=ALU.mult, op1=ALU.add), [uk, "cw", "cb"], [ck])
                        P.op("dve", lambda E, ut=ut, ct=ct, cc=cc: E.scalar_tensor_tensor(out=ct[:, :], in0=ut[:, 1:513], scalar=cw[:, cc, 1:2], in1=ct[:, :],
                                                                                       op0=ALU.mult, op1=ALU.add), [uk, uk + "h", "cw", ck], [ck])
                        P.op("dve", lambda E, ut=ut, ct=ct, cc=cc: E.scalar_tensor_tensor(out=ct[:, :], in0=ut[:, 0:512], scalar=cw[:, cc, 0:1], in1=ct[:, :],
                                                                                       op0=ALU.mult, op1=ALU.add), [uk, uk + "h", "cw", ck], [ck])
                        if gv == 0:
                            st_, stk = sg[wn % 2], "sg%d" % (wn % 2)
                            ACTF(P, st_[:, :], ct[:, :], AF.Silu, [ck], [stk])
                        else:
                            st_, stk = sg[wn % 2], "sg%d" % (wn % 2)
                            P.op("dve", lambda E, st_=st_, ct=ct, c=c: E.tensor_tensor(out=aT[:, c, :], in0=st_[:, :], in1=ct[:, :], op=ALU.mult),
                                 [stk, ck], ["aT.%d" % c])
                if halo:
                    return
                for cc in range(8):
                    wn = wdc[0]
                    wdc[0] += 1
                    wt = wd[wn % 3]
                    wk = "wd%d" % (wn % 3)
                    DMA(P, "sp", wt[:, :, :], T["Wdn"][:, :, cc * 128:(cc + 1) * 128], [], [wk])
                    for c in range(22):
                        MM(P, psy[:, :], wt[:, c, :], aT[:, c, :], c == 0, c == 21, [wk, "aT.%d" % c], ["psy"])
                    yt, yk = yT[wn % 2], "yT%d" % (wn % 2)
                    ACTF(P, yt[:, :], psy[:, :], AF.Copy, ["psy"], [yk])
                    for b in range(4):
                        P.op("pe", lambda E, yt=yt, b=b: E.transpose(pst_[:, b, :], yt[:, b * 128:(b + 1) * 128], identf[:, :]), [yk, "identf"], ["pst"])
                    P.op("dve", lambda E, cc=cc: E.tensor_tensor(out=xr_[:, :, cc * 128:(cc + 1) * 128], in0=pst_[:, :, :],
                                                               in1=xr_[:, :, cc * 128:(cc + 1) * 128], op=ALU.add), ["pst", xrk], [xrk])
                for (bi, rows) in blocks:
                    src = xr_[:rows, bi, :]
                    src_ap_keys[0] = [xrk]
                    rs, sk = rstd_of(src, rows, D)
                    P.op("dve", lambda E, src=src, rs=rs: E.scalar_tensor_tensor(out=src, in0=src, scalar=rs, in1=fin_g[:, :], op0=ALU.mult, op1=ALU.mult),
                         [xrk, sk, "fin_g"], [xrk])
                DMA(P, "pool", T["out"][r0:r0 + 512, :].rearrange("(b p) d -> p b d", p=128), xr_[:, :, :], [xrk], ["out"])

            for si, sl in enumerate(slots):
                do_slot(si, sl)
            P.emit()
            stats.append(P.stats)
    return nc, stats


_CACHE = {}


def _consts(S):
    NB = S // 128
    bf = ml_dtypes.bfloat16
    p = np.arange(128)
    c = {}
    c["ident_bf"] = np.eye(128, dtype=np.float32).astype(bf)
    c["ident_f"] = np.eye(128, dtype=np.float32)
    c["triu_f"] = (p[:, None] <= p[None, :]).astype(np.float32)
    c["ones_f"] = np.ones((128, 128), np.float32)
    c["negtri"] = (-(p[:, None] >= p[None, :]).astype(np.float32)).astype(bf)
    c["negones"] = (-np.ones((128, 128), np.float32)).astype(bf)
    t = np.arange(512)
    key = (np.arange(4)[None, :, None] * 128 + p[:, None, None])
    c["mt_f"] = np.where(key <= t[None, None, :], 0.0, NEG).astype(np.float32).astype(bf)
    c["mt_s"] = np.where(key < t[None, None, :], 0.0, NEG).astype(np.float32).astype(bf)
    c["negrow"] = np.full((1, 512), NEG, np.float32).astype(bf)
    c["cm1"] = np.full((64, 3, 128), -1.0, np.float32).astype(bf)
    c["cp1"] = np.full((64, 3, 128), 1.0, np.float32).astype(bf)
    return c


def _core_consts(S, par):
    NT = S // 512
    NSL = NT // 2
    NB = S // 128
    bf = ml_dtypes.bfloat16
    own = own_tiles(par, NSL)
    esel = np.zeros((128, 2 * NSL), np.float32)
    eselh = np.zeros((128, 2 * NSL), np.float32)
    hmask = np.ones((128, 16), np.float32)
    idE = np.zeros((128, NSL, 128), np.float32)
    idNE = np.zeros((128, NSL, 128), np.float32)
    erow = np.zeros((1, NSL, 128), np.float32)
    I = np.eye(128, dtype=np.float32)
    p = np.arange(128)
    keypos = (np.arange(NB)[None, :] * 128 + p[:, None])
    mh_f = np.zeros((128, NB, 16), np.float32)
    mh_s = np.zeros((128, NB, 16), np.float32)
    for j in range(NSL):
        e0 = 1.0 if own[j] == 2 * j else 0.0
        esel[:, 2 * j] = e0
        esel[:, 2 * j + 1] = 1.0 - e0
        eselh[:, 2 * j] = 1.0 if (own[j] == 2 * j and j > 0) else 0.0
        eselh[:, 2 * j + 1] = 1.0 if own[j] == 2 * j + 1 else 0.0
        idE[:, j, :] = e0 * I
        idNE[:, j, :] = (1.0 - e0) * I
        erow[0, j, :] = e0
        for r in range(2):
            col = 2 * j + r
            pos = 512 * own[j] - 2 + r
            if own[j] == 0:
                hmask[:, col] = 0.0
                mh_f[:, :, col] = np.where(keypos == 0, 0.0, NEG)
                mh_s[:, :, col] = np.where(keypos == 0, 0.0, NEG)
            else:
                mh_f[:, :, col] = np.where(keypos <= pos, 0.0, NEG)
                mh_s[:, :, col] = np.where(keypos < pos, 0.0, NEG)
    return dict(esel=esel, eselh=eselh, hmask=hmask, idE=idE.astype(bf), idNE=idNE.astype(bf), erow=erow.astype(bf),
                mh_f=mh_f.astype(bf), mh_s=mh_s.astype(bf)), own


def _prepare(inputs, debug=False):
    x = np.asarray(inputs["x"], np.float32)
    B, S, _ = x.shape
    NT = S // 512
    NSL = NT // 2
    rep = lambda v: np.ascontiguousarray(np.broadcast_to(np.asarray(v, np.float32).reshape(1, -1), (128, np.asarray(v).size)))
    shared = dict(_consts(S))
    shared["w_in"] = np.ascontiguousarray(np.asarray(inputs["w_in"], np.float32)[0])
    shared["w_out"] = np.ascontiguousarray(np.asarray(inputs["w_out"], np.float32)[0])
    shared["w_up"] = np.ascontiguousarray(np.asarray(inputs["w_up"], np.float32)[0])
    shared["w_down"] = np.ascontiguousarray(np.asarray(inputs["w_down"], np.float32)[0])
    shared["attn_g"] = rep(inputs["attn_norm_g"][0])
    shared["ffn_g"] = rep(inputs["ffn_norm_g"][0])
    shared["fin_g"] = rep(inputs["final_norm_g"])
    shared["fox_g"] = rep(inputs["fox_out_g"][0])
    shared["sb_g"] = rep(inputs["sb_out_g"][0])
    shared["fb"] = rep(inputs["forget_bias"][0])
    cwv = np.asarray(inputs["conv_w"], np.float32)[0]
    shared["cw"] = np.ascontiguousarray(cwv.reshape(3, NCH, 128).transpose(2, 1, 0))
    shared["cb"] = np.ascontiguousarray(np.asarray(inputs["conv_b"], np.float32)[0].reshape(NCH, 128).T)
    in_maps = []
    owns = []
    for c in range(8):
        b, par = c // 2, c % 2
        cc, own = _core_consts(S, par)
        owns.append(own)
        xo = np.zeros((NSL * 512 + 16, D), np.float32)
        for j, t in enumerate(own):
            xo[j * 512:(j + 1) * 512] = x[b, t * 512:(t + 1) * 512]
            if t > 0:
                xo[NSL * 512 + 2 * j:NSL * 512 + 2 * j + 2] = x[b, t * 512 - 2:t * 512]
        m = dict(shared)
        m.update(cc)
        m["xn"] = np.ascontiguousarray(x[b])
        m["xo"] = xo
        in_maps.append(m)
    return in_maps, owns, (B, S)


def kernel(**inputs):
    in_maps, owns, (B, S) = _prepare(inputs)
    if S not in _CACHE:
        _CACHE[S] = build_program(S)
    nc, _ = _CACHE[S]
    res = run_bass_kernel_spmd(nc, in_maps, core_ids=list(range(8)))
    out = np.zeros((B, S, D), np.float32)
    for c in range(8):
        b = c // 2
        o = np.asarray(res.results[c]["out"], np.float32)
        for j, t in enumerate(owns[c]):
            out[b, t * 512:(t + 1) * 512] = o[j * 512:(j + 1) * 512]
    return out
```

```python
import numpy as np
import ml_dtypes
from contextlib import ExitStack
import concourse.bass as bass
import concourse.mybir as mybir
from concourse.bass_utils import run_bass_kernel_spmd

F32 = mybir.dt.float32
BF16 = mybir.dt.bfloat16
AF = mybir.ActivationFunctionType
ALU = mybir.AluOpType

D = 1024
DH = 64
DFF = 2816
INC = 3080
EPS = 1e-6
NEG = -30000.0
CQ_F, CK_F, CV_F, CL, CQ_S, CK_S, CV_S = 0, 512, 1024, 1536, 1544, 2056, 2568
NCH = 2 * DFF // 128

import os
OPT = set(os.environ.get("KOPT", "").split(","))
COMPUTE = ("pe", "act", "dve", "pool")
CH = 16000
NDMA = 8


class Prog:
    def __init__(self, nc, gs, tag):
        self.nc = nc
        self.gs = gs
        self.tag = tag
        self.ops = []
        self.last_w = {}
        self.readers = {}

    def op(self, eng, fn, reads=(), writes=(), dma=False):
        j = len(self.ops)
        deps = set()
        for r in reads:
            if r in self.last_w:
                deps.add(self.last_w[r])
        for w in writes:
            if w in self.last_w:
                deps.add(self.last_w[w])
            for rd in self.readers.get(w, ()):
                deps.add(rd)
        deps.discard(j)
        self.ops.append(dict(eng=eng, fn=fn, deps=deps, dma=dma, sig=False))
        for r in reads:
            self.readers.setdefault(r, []).append(j)
        for w in writes:
            self.last_w[w] = j
            self.readers[w] = []
        return j

    def dma(self, q, fn, reads=(), writes=()):
        return self.op(q, fn, reads, writes, dma=True)

    def emit(self):
        nc, ops = self.nc, self.ops
        for j, o in enumerate(ops):
            nd = set()
            for d in o["deps"]:
                p = ops[d]
                if not p["dma"] and not o["dma"] and p["eng"] == o["eng"] == "pe":
                    continue
                nd.add(d)
            o["deps"] = nd
            for d in nd:
                ops[d]["sig"] = True
        cnt = {e: 0 for e in COMPUTE}
        sems = {}

        def getsem(key):
            if key not in sems:
                sems[key] = self.gs.enter_context(nc.semaphore("s%s_%s_%s" % (self.tag, key[0], key[1])))
            return sems[key]

        dcount = {}
        dma_i = {}
        for j, o in enumerate(ops):
            if o["dma"]:
                q = o["eng"]
                k = dma_i.get(q, 0)
                dma_i[q] = k + 1
                key = ("d" + q, k % NDMA)
                dcount[key] = dcount.get(key, 0) + 16
                o["semkey"], o["semval"] = key, dcount[key]
                o["prev"] = (key, dcount[key] - 16)
            elif o["sig"]:
                e = o["eng"]
                c = cnt[e]
                cnt[e] = c + 1
                o["semkey"], o["semval"] = (e, c // CH), c % CH + 1
        last_dma = {}
        for o in ops:
            if o["dma"]:
                last_dma[o["semkey"]] = max(last_dma.get(o["semkey"], 0), o["semval"])
            if "semkey" in o:
                getsem(o["semkey"])
        per = {e: [] for e in ("sp", "act", "dve", "pe", "pool")}
        for j, o in enumerate(ops):
            per[o["eng"]].append(j)
        nwc = [0]

        def run_engine(e, E):
            known = {}
            for j in per[e]:
                o = ops[j]
                need = {}
                for d in o["deps"]:
                    p = ops[d]
                    k, v = p["semkey"], p["semval"]
                    if need.get(k, 0) < v:
                        need[k] = v
                if o["dma"] and o["prev"][1] > 0:
                    k, v = o["prev"]
                    if need.get(k, 0) < v:
                        need[k] = v
                for k, v in need.items():
                    if known.get(k, 0) >= v:
                        continue
                    known[k] = v
                    E.wait_ge(sems[k], v)
                    nwc[0] += 1
                inst = o["fn"](E)
                if o["dma"]:
                    inst.then_inc(sems[o["semkey"]], 16)
                elif o["sig"]:
                    inst.then_inc(sems[o["semkey"]], 1)
            if e == "sp":
                for k, v in last_dma.items():
                    if known.get(k, 0) < v:
                        E.wait_ge(sems[k], v)

        with nc.Block() as block:
            @block.sync
            def _(E):
                run_engine("sp", E)

            @block.scalar
            def _(E):
                run_engine("act", E)

            @block.vector
            def _(E):
                run_engine("dve", E)

            @block.tensor
            def _(E):
                run_engine("pe", E)

            @block.gpsimd
            def _(E):
                run_engine("pool", E)
        self.stats = dict(tag=self.tag, nops=len(ops), nwaits=nwc[0], nsems=len(sems), cnt=cnt)


def MM(P, out, lhsT, rhs, start, stop, rd, wr, skip=False):
    P.op("pe", lambda E: E.matmul(out, lhsT=lhsT, rhs=rhs, start=start, stop=stop, skip_group_check=skip), rd, wr)


def TR(P, out, in_, ident, rd, wr):
    P.op("pe", lambda E: E.transpose(out, in_, ident), rd, wr)


def ACTF(P, out, in_, func, rd, wr, bias=None, scale=None, accum=None):
    kw = {}
    if bias is not None:
        kw["bias"] = bias
    if scale is not None:
        kw["scale"] = scale
    if accum is not None:
        kw["accum_out"] = accum
    P.op("act", lambda E: E.activation(out=out, in_=in_, func=func, **kw), rd, wr)


def DMA(P, q, out, in_, rd, wr):
    P.dma(q, lambda E: E.dma_start(out=out, in_=in_), rd, wr)


class RR:
    def __init__(self, engs):
        self.engs = engs
        self.i = 0

    def copy(self, P, out, in_, rd, wr, scale=None):
        e = self.engs[self.i % len(self.engs)]
        self.i += 1
        if e == "act":
            ACTF(P, out, in_, AF.Copy, rd, wr, scale=scale)
        elif scale is None:
            P.op(e, lambda E: E.tensor_copy(out=out, in_=in_), rd, wr)
        else:
            P.op(e, lambda E: E.tensor_scalar(out=out, in0=in_, scalar1=float(scale), scalar2=None, op0=ALU.mult), rd, wr)


def own_tiles(p, NSL):
    res = []
    for j in range(NSL):
        first = (j % 2 == 0) if p == 0 else (j % 2 == 1)
        res.append(2 * j if first else 2 * j + 1)
    return res


def build_program(S, debug=False, stop_after="d"):
    NT = S // 512
    NSL = NT // 2
    NB = S // 128
    NOWN = NSL * 512
    NQ = NOWN + 16
    nc = bass.Bass("TRN2", target_bir_lowering=False)
    T = {}

    def din(name, shape, dt=F32):
        T[name] = nc.dram_tensor(name, list(shape), dt, kind="ExternalInput").ap()

    def dscr(name, shape, dt=BF16):
        kind = "ExternalOutput" if debug else "Internal"
        T[name] = nc.dram_tensor(name, list(shape), dt, kind=kind).ap()

    din("xn", [S, D]); din("xo", [NQ, D])
    din("w_in", [D, INC]); din("w_out", [D, D]); din("w_up", [D, 2 * DFF]); din("w_down", [DFF, D])
    din("attn_g", [128, D]); din("ffn_g", [128, D]); din("fin_g", [128, D])
    din("fox_g", [128, 512]); din("sb_g", [128, 512]); din("fb", [128, 8])
    din("cw", [128, NCH, 3]); din("cb", [128, NCH])
    din("esel", [128, 2 * NSL]); din("eselh", [128, 2 * NSL]); din("hmask", [128, 16])
    din("idE", [128, NSL, 128], BF16); din("idNE", [128, NSL, 128], BF16); din("erow", [1, NSL, 128], BF16)
    din("mh_f", [128, NB, 16], BF16); din("mh_s", [128, NB, 16], BF16)
    din("ident_bf", [128, 128], BF16); din("ident_f", [128, 128]); din("triu_f", [128, 128]); din("ones_f", [128, 128])
    din("negtri", [128, 128], BF16); din("negones", [128, 128], BF16)
    din("mt_f", [128, 4, 512], BF16); din("mt_s", [128, 4, 512], BF16); din("negrow", [1, 512], BF16)
    din("cm1", [64, 3, 128], BF16); din("cp1", [64, 3, 128], BF16)
    T["out"] = nc.dram_tensor("out", [NOWN, D], F32, kind="ExternalOutput").ap()
    dscr("Kd", [16, 64, S]); dscr("Vd", [16, 128, NB, 65]); dscr("Qd", [16, 64, NQ])
    dscr("KA", [8, 6, S]); dscr("QA", [8, 6, S]); dscr("Od", [NQ + 112, D])
    dscr("Wup", [128, 8, 2 * DFF]); dscr("Wdn", [128, 22, D]); dscr("Wout", [128, 8, D])
    if debug:
        dscr("dbg_nF", [128, 8, NB], F32)

    stats = []
    with ExitStack() as gs:
        def sbt(es, name, shape, dt):
            return es.enter_context(nc.sbuf_tensor("sb_" + name, list(shape), dt))

        def pst(es, name, shape, dt):
            return es.enter_context(nc.psum_tensor("pp_" + name, list(shape), dt))

        with ExitStack() as es:
            P = Prog(nc, gs, "a")
            win = sbt(es, "win", [128, 8, INC], BF16)
            wstg = [sbt(es, "wstg%d" % i, [128, INC], F32) for i in range(2)]
            g_r = sbt(es, "g_r", [128, D], F32)
            fb_r = sbt(es, "fb_r", [128, 8], F32)
            ident = sbt(es, "ident", [128, 128], BF16)
            xb = [sbt(es, "xb%d" % i, [128, D], F32) for i in range(3)]
            junk = sbt(es, "junk", [128, D], BF16)
            stat = sbt(es, "stat", [128, 3, 8], F32)
            xnb = [sbt(es, "xnb%d" % i, [128, D], BF16) for i in range(2)]
            hT = [sbt(es, "hT%d" % i, [128, 8, 512], BF16) for i in range(2)]
            kts = [sbt(es, "kts%d" % i, [128, 512], BF16) for i in range(3)]
            vs = [sbt(es, "vs%d" % i, [128, 16, 4, 65], BF16) for i in range(2)]
            flog = sbt(es, "flog", [128, 8, NB], F32)
            cst = [sbt(es, "cst%d" % i, [128, 2816], F32) for i in range(2)]
            cbf = [sbt(es, "cbf%d" % i, [128, 2816], BF16) for i in range(2)]
            esp = ExitStack()
            pT = [pst(esp, "pT%d" % i, [128, 8, 128], BF16) for i in range(2)]
            psk = [pst(esp, "psk%d" % i, [128, 512], F32) for i in range(2)]
            psv = [pst(esp, "psv%d" % i, [128, 512], F32) for i in range(2)]
            psf = [pst(esp, "psf%d" % i, [128, 512], F32) for i in range(2)]
            rr = RR(["act", "dve"])

            DMA(P, "sp", g_r[:], T["attn_g"][:, :], [], ["g_r"])
            DMA(P, "sp", fb_r[:], T["fb"][:, :], [], ["fb_r"])
            DMA(P, "sp", ident[:], T["ident_bf"][:, :], [], ["ident"])
            for k in range(8):
                DMA(P, "sp", wstg[k % 2][:], T["w_in"][k * 128:(k + 1) * 128, :], [], ["wstg%d" % (k % 2)])
                P.op("pool", lambda E, k=k: E.tensor_copy(out=win[:, k, :], in_=wstg[k % 2][:]), ["wstg%d" % (k % 2)], ["win"])
            for i in range(2):
                P.op("pool", lambda E, i=i: E.memset(vs[i][:], 1.0), [], ["vs%d.%d.%d" % (i, b, g) for b in range(4) for g in range(2)])
                P.op("pool", lambda E, i=i: E.memset(xnb[i][:], 0.0), [], ["xnb%d" % i])
            jobs = []
            for k in range(8):
                for hf in range(2):
                    jobs.append((T["w_up"][k * 128:(k + 1) * 128, hf * 2816:(hf + 1) * 2816], T["Wup"][:, k, hf * 2816:(hf + 1) * 2816], 2816, None))
            for c in range(0, 22, 2):
                jobs.append((T["w_down"][c * 128:(c + 2) * 128, :].rearrange("(c p) n -> p c n", p=128), T["Wdn"][:, c:c + 2, :], 2048, 2))
            for k in range(0, 8, 2):
                jobs.append((T["w_out"][k * 128:(k + 2) * 128, :].rearrange("(c p) n -> p c n", p=128), T["Wout"][:, k:k + 2, :], 2048, 2))
            if "nojobs" in OPT:
                jobs = []
            for n, (src, dst, width, sub) in enumerate(jobs):
                b = n % 2
                if sub is None:
                    s_ap, b_ap = cst[b][:, :width], cbf[b][:, :width]
                else:
                    s_ap = cst[b][:, :width].rearrange("p (c n) -> p c n", c=sub)
                    b_ap = cbf[b][:, :width].rearrange("p (c n) -> p c n", c=sub)
                DMA(P, "pool", s_ap, src, [], ["cst%d" % b])
                P.op("pool", lambda E, b=b, width=width: E.tensor_copy(out=cbf[b][:, :width], in_=cst[b][:, :width]), ["cst%d" % b], ["cbf%d" % b])
                DMA(P, "pool", dst, b_ap, ["cbf%d" % b], ["Wscr"])

            blkn = [0]

            def norm_block(src_rows, rows, hbuf, col0):
                n = blkn[0]
                blkn[0] += 1
                x = xb[n % 3]
                xs = "xb%d" % (n % 3)
                st = stat[:, :, n % 8:n % 8 + 1]
                sk = "stat%d" % (n % 8)
                DMA(P, "sp", x[:rows, :], src_rows, [], [xs])
                ACTF(P, junk[:rows, :], x[:rows, :], AF.Square, [xs], ["junk", sk], accum=stat[:rows, 0, n % 8:n % 8 + 1])
                ACTF(P, stat[:rows, 1, n % 8:n % 8 + 1], stat[:rows, 0, n % 8:n % 8 + 1], AF.Ln, [sk], [sk], bias=EPS, scale=1.0 / D)
                ACTF(P, stat[:rows, 2, n % 8:n % 8 + 1], stat[:rows, 1, n % 8:n % 8 + 1], AF.Exp, [sk], [sk], scale=-0.5)
                xq = xnb[n % 2]
                qs = "xnb%d" % (n % 2)
                P.op("dve", lambda E: E.scalar_tensor_tensor(out=xq[:rows, :], in0=x[:rows, :], scalar=stat[:rows, 2, n % 8:n % 8 + 1],
                                                             in1=g_r[:rows, :], op0=ALU.mult, op1=ALU.mult), [xs, sk, "g_r"], [qs])
                pt = pT[n % 2]
                ps = "pT%d" % (n % 2)
                for k in range(8):
                    TR(P, pt[:, k, :], xq[:, k * 128:(k + 1) * 128], ident[:, :], [qs, "ident"], [ps])
                rr.copy(P, hT[hbuf][:, :, col0:col0 + rows], pt[:, :, :rows], [ps], ["hT%d.%d" % (hbuf, col0 // 128)])

            def proj_T(hbuf, width, col_w, dst, hkeys, scale=None):
                n = proj_T.n
                proj_T.n += 1
                ps = psk[n % 2]
                pk = "psk%d" % (n % 2)
                for k in range(8):
                    MM(P, ps[:, :width], win[:, k, col_w:col_w + 128], hT[hbuf][:, k, :width], k == 0, k == 7, ["win"] + hkeys, [pk])
                ks = kts[n % 3]
                kk = "kts%d" % (n % 3)
                rr.copy(P, ks[:, :width], ps[:, :width], [pk], [kk], scale=scale)
                DMA(P, "sp", dst, ks[:, :width], [kk], ["KQscr"])
            proj_T.n = 0

            vcount = [0]

            def make_tile_A(Tn, hb):
                blocks = [(lambda b=b: norm_block(T["xn"][Tn * 512 + b * 128:Tn * 512 + (b + 1) * 128, :], 128, hb, b * 128)) for b in range(4)]
                hk = ["hT%d.%d" % (hb, b) for b in range(4)]
                vb = Tn % 2
                groups = []

                def kgrp(c):
                    col = (CK_F + c * 128) if c < 4 else (CK_S + (c - 4) * 128)
                    h0 = 2 * c if c < 4 else 8 + 2 * (c - 4)
                    proj_T(hb, 512, col, T["Kd"][h0:h0 + 2, :, Tn * 512:(Tn + 1) * 512].rearrange("h r t -> (h r) t"), hk)

                def vgrp(b, g):
                    n = vcount[0]
                    vcount[0] += 1
                    ps = psv[n % 2]
                    pk = "psv%d" % (n % 2)
                    vc = CV_F if g == 0 else CV_S
                    for k in range(8):
                        MM(P, ps[:, :], hT[hb][:, k, b * 128:(b + 1) * 128], win[:, k, vc:vc + 512], k == 0, k == 7,
                           ["win", "hT%d.%d" % (hb, b)], [pk])
                    rr.copy(P, vs[vb][:, g * 8:(g + 1) * 8, b, 0:64], ps[:, :].rearrange("p (h d) -> p h d", h=8), [pk],
                            ["vs%d.%d.%d" % (vb, b, g)])
                    if g == 1:
                        pf = psf[b % 2]
                        for k in range(8):
                            MM(P, pf[:, 0:8], hT[hb][:, k, b * 128:(b + 1) * 128], win[:, k, CL:CL + 8], k == 0, k == 7,
                               ["win", "hT%d.%d" % (hb, b)], ["psf%d" % (b % 2)])
                        P.op("dve", lambda E: E.tensor_tensor(out=flog[:, :, Tn * 4 + b], in0=pf[:, 0:8], in1=fb_r[:, :], op=ALU.add),
                             ["psf%d" % (b % 2), "fb_r"], ["flog"])

                def vstore():
                    for h in range(16):
                        DMA(P, "sp", T["Vd"][h, :, Tn * 4:Tn * 4 + 4, :], vs[vb][:, h, :, :],
                            ["vs%d.%d.%d" % (vb, b, g) for b in range(4) for g in range(2)], ["Vscr"])
                for c in range(8):
                    groups.append(lambda c=c: kgrp(c))
                for b in range(4):
                    for g in range(2):
                        groups.append(lambda b=b, g=g: vgrp(b, g))
                groups.append(vstore)
                return blocks, groups

            def make_tile_Q(row0, width, col0, hb):
                nb = (width + 127) // 128
                blocks = []
                for b in range(nb):
                    rows = min(128, width - b * 128)
                    blocks.append(lambda b=b, rows=rows: norm_block(T["xo"][row0 + b * 128:row0 + b * 128 + rows, :], rows, hb, b * 128))
                hk = ["hT%d.%d" % (hb, b) for b in range(nb)]

                def qgrp(c):
                    col = (CQ_F + c * 128) if c < 4 else (CQ_S + (c - 4) * 128)
                    h0 = 2 * c if c < 4 else 8 + 2 * (c - 4)
                    proj_T(hb, width, col, T["Qd"][h0:h0 + 2, :, col0:col0 + width].rearrange("h r t -> (h r) t"), hk, scale=0.125)
                groups = [(lambda c=c: qgrp(c)) for c in range(8)]
                return blocks, groups

            seq = []
            qi = 0
            for Tn in range(NT):
                seq.append(("A", Tn))
                if Tn % 2 == 1 and qi < NSL:
                    seq.append(("Q", qi))
                    qi += 1
            seq.append(("H", 0))
            tiles = []
            for i, (kind, idx) in enumerate(seq):
                hb = i % 2
                if kind == "A":
                    tiles.append(make_tile_A(idx, hb))
                elif kind == "Q":
                    tiles.append(make_tile_Q(idx * 512, 512, idx * 512, hb))
                else:
                    tiles.append(make_tile_Q(NOWN, 16, NOWN, hb))
            for blk in tiles[0][0]:
                blk()
            for i, (blocks, groups) in enumerate(tiles):
                nxt = tiles[i + 1][0] if i + 1 < len(tiles) else []
                per = max(1, len(groups) // max(1, len(nxt)))
                bi = 0
                for gi, g in enumerate(groups):
                    g()
                    if bi < len(nxt) and (gi + 1) % per == 0:
                        nxt[bi]()
                        bi += 1
                while bi < len(nxt):
                    nxt[bi]()
                    bi += 1
            P.emit()
            stats.append(P.stats)
            esp.close()
            if stop_after == "a":
                return nc, stats

            with ExitStack() as es2:
                P = Prog(nc, gs, "b")
                nlf = sbt(es2, "nlf", [128, 8 * NB], F32)
                ee = sbt(es2, "ee", [128, 8 * NB], F32)
                sc = [sbt(es2, "sc%d" % i, [128, 8, NB], F32) for i in range(2)]
                tot = sbt(es2, "tot", [128, 8, NB], F32)
                nF = sbt(es2, "nF", [128, 8, NB], F32)
                nFp = sbt(es2, "nFp", [128, 8, 128], F32)
                nFT = sbt(es2, "nFT", [NB, 8, 128], F32)
                r1 = sbt(es2, "r1", [NB, 8, 128], F32)
                parts = [sbt(es2, "part%d" % i, [NB, 8, 128], BF16) for i in range(3)]
                triu = sbt(es2, "triu", [128, 128], F32)
                ones = sbt(es2, "ones", [128, 128], F32)
                identf = sbt(es2, "identf", [128, 128], F32)
                cm1 = sbt(es2, "cm1", [64, 3, 128], BF16)
                cp1 = sbt(es2, "cp1", [64, 3, 128], BF16)
                ps_c = pst(es2, "ps_c", [128, 512], F32)
                ps_t = pst(es2, "ps_t", [128, 512], F32)
                ps_x = [pst(es2, "ps_x%d" % i, [128, 4, 128], F32) for i in range(2)]
                W8 = 8 * NB
                DMA(P, "sp", triu[:], T["triu_f"][:, :], [], ["triu"])
                DMA(P, "sp", ones[:], T["ones_f"][:, :], [], ["ones"])
                DMA(P, "sp", identf[:], T["ident_f"][:, :], [], ["identf"])
                DMA(P, "sp", cm1[:], T["cm1"][:, :, :], [], ["cm1"])
                DMA(P, "sp", cp1[:], T["cp1"][:, :, :], [], ["cp1"])
                fl2 = flog[:, :, :].rearrange("p h b -> p (h b)")
                ACTF(P, ee[:, :], fl2, AF.Exp, [], ["ee"], scale=-1.0)
                ACTF(P, nlf[:, :], ee[:, :], AF.Ln, ["ee"], ["nlf"], bias=1.0)
                MM(P, ps_c[:, :W8], triu[:, :], nlf[:, :], True, True, ["triu", "nlf"], ["ps_c"])
                MM(P, ps_t[:, :W8], ones[:, :], nlf[:, :], True, True, ["ones", "nlf"], ["ps_t"])
                P.op("dve", lambda E: E.tensor_copy(out=tot[:, :, :], in_=ps_t[:, :W8].rearrange("p (h b) -> p h b", h=8)), ["ps_t"], ["tot"])
                P.op("dve", lambda E: E.tensor_copy(out=sc[0][:, :, :], in_=tot[:, :, :]), ["tot"], ["sc0"])
                cur = 0
                d = 1
                while d < NB:
                    nxt = 1 - cur
                    P.op("dve", lambda E, cur=cur, nxt=nxt, d=d: E.tensor_copy(out=sc[nxt][:, :, 0:d], in_=sc[cur][:, :, 0:d]), ["sc%d" % cur], ["sc%d" % nxt])
                    P.op("dve", lambda E, cur=cur, nxt=nxt, d=d: E.tensor_tensor(out=sc[nxt][:, :, d:NB], in0=sc[cur][:, :, d:NB], in1=sc[cur][:, :, 0:NB - d], op=ALU.add),
                         ["sc%d" % cur], ["sc%d" % nxt])
                    cur = nxt
                    d *= 2
                P.op("dve", lambda E, cur=cur: E.tensor_tensor(out=tot[:, :, :], in0=sc[cur][:, :, :], in1=tot[:, :, :], op=ALU.subtract), ["sc%d" % cur, "tot"], ["tot"])
                P.op("dve", lambda E: E.tensor_tensor(out=nF[:, :, :], in0=ps_c[:, :W8].rearrange("p (h b) -> p h b", h=8), in1=tot[:, :, :], op=ALU.add), ["ps_c", "tot"], ["nF"])
                if debug:
                    DMA(P, "sp", T["dbg_nF"][:, :, :], nF[:, :, :], ["nF"], ["dbg"])
                P.op("dve", lambda E: E.memset(nFp[:], 0.0), [], ["nFp"])
                P.op("dve", lambda E: E.tensor_copy(out=nFp[:, :, 0:NB], in_=nF[:, :, :]), ["nF", "nFp"], ["nFp"])
                for h in range(8):
                    px = ps_x[h // 4]
                    P.op("pe", lambda E, h=h, px=px: E.transpose(px[:, h % 4, :], nFp[:, h, :], identf[:, :]), ["nFp", "identf"], ["ps_x%d" % (h // 4)])
                for q in range(2):
                    P.op("dve", lambda E, q=q: E.tensor_copy(out=nFT[:, q * 4:(q + 1) * 4, :], in_=ps_x[q][:NB, :, :]), ["ps_x%d" % q], ["nFT"])
                P.op("dve", lambda E: E.tensor_copy(out=parts[0][:, :, :], in_=nFT[:, :, :]), ["nFT"], ["part0"])
                P.op("dve", lambda E: E.tensor_tensor(out=r1[:, :, :], in0=nFT[:, :, :], in1=parts[0][:, :, :], op=ALU.subtract), ["nFT", "part0"], ["r1"])
                P.op("dve", lambda E: E.tensor_copy(out=parts[1][:, :, :], in_=r1[:, :, :]), ["r1"], ["part1"])
                P.op("dve", lambda E: E.tensor_tensor(out=nFT[:, :, :], in0=r1[:, :, :], in1=parts[1][:, :, :], op=ALU.subtract), ["r1", "part1"], ["nFT"])
                P.op("dve", lambda E: E.tensor_copy(out=parts[2][:, :, :], in_=nFT[:, :, :]), ["nFT"], ["part2"])
                for i in range(3):
                    DMA(P, "sp", T["KA"][:, i, :].rearrange("h (b t) -> b h t", t=128), parts[i][:, :, :], ["part%d" % i], ["KAs"])
                    DMA(P, "sp", T["QA"][:, 3 + i, :].rearrange("h (b t) -> b h t", t=128), parts[i][:, :, :], ["part%d" % i], ["QAs"])
                for h in range(8):
                    DMA(P, "sp", T["KA"][h, 3:6, :].rearrange("r (b t) -> b r t", t=128), cm1[:NB, :, :], ["cm1"], ["KAs"])
                    DMA(P, "sp", T["QA"][h, 0:3, :].rearrange("r (b t) -> b r t", t=128), cp1[:NB, :, :], ["cp1"], ["QAs"])
                P.emit()
                stats.append(P.stats)
        if stop_after == "b":
            return nc, stats

        with ExitStack() as es:
            P = Prog(nc, gs, "c")
            KT = [sbt(es, "KT%d" % i, [70, S], BF16) for i in range(2)]
            QT = [sbt(es, "QT%d" % i, [70, NQ], BF16) for i in range(2)]
            VV = [sbt(es, "VV%d" % i, [128, NB, 65], BF16) for i in range(2)]
            qa = sbt(es, "qa", [70, S], BF16)
            qtmp = sbt(es, "qtmp", [70, 512], BF16)
            esel = sbt(es, "esel", [128, 2 * NSL], F32)
            eselh = sbt(es, "eselh", [128, 2 * NSL], F32)
            idE = sbt(es, "idE", [128, NSL, 128], BF16)
            idNE = sbt(es, "idNE", [128, NSL, 128], BF16)
            erow = sbt(es, "erow", [1, NSL, 128], BF16)
            negrow = sbt(es, "negrow", [1, 512], BF16)
            allneg = sbt(es, "allneg", [128, 512], BF16)
            ident = sbt(es, "ident2", [128, 128], BF16)
            mt = [sbt(es, "mt%d" % i, [128, 4, 512], BF16) for i in range(2)]
            mh = [sbt(es, "mh%d" % i, [128, NB, 16], BF16) for i in range(2)]
            negtri = sbt(es, "negtri", [128, 128], BF16)
            negones = sbt(es, "negones", [128, 128], BF16)
            PT = [sbt(es, "PT%d" % i, [128, 512], BF16) for i in range(3)]
            UU = [sbt(es, "UU%d" % i, [128, 512], F32) for i in range(2)]
            LL = [sbt(es, "LL%d" % i, [128, 512], BF16) for i in range(3)]
            LS = [sbt(es, "LS%d" % i, [128, 512], BF16) for i in range(3)]
            rden = sbt(es, "rden", [128, 2, 4], F32)
            ob = [sbt(es, "ob%d" % i, [128, 4, 64], BF16) for i in range(3)]
            psZ = [pst(es, "psZ%d" % i, [128, 512], F32) for i in range(4)]
            psO = [pst(es, "psO%d" % i, [128, 512], F32) for i in range(2)]
            for nm, t_, src in (("esel", esel, T["esel"][:, :]), ("eselh", eselh, T["eselh"][:, :]), ("idE", idE, T["idE"][:, :, :]),
                                ("idNE", idNE, T["idNE"][:, :, :]), ("erow", erow, T["erow"][:, :, :]), ("negrow", negrow, T["negrow"][:, :]),
                                ("ident", ident, T["ident_bf"][:, :]), ("mt0", mt[0], T["mt_f"][:, :, :]), ("mt1", mt[1], T["mt_s"][:, :, :]),
                                ("mh0", mh[0], T["mh_f"][:, :, :]), ("mh1", mh[1], T["mh_s"][:, :, :]),
                                ("negtri", negtri, T["negtri"][:, :]), ("negones", negones, T["negones"][:, :])):
                DMA(P, "sp", t_[:], src, [], [nm])
            P.op("pool", lambda E: E.memset(allneg[:], NEG), [], ["allneg"])
            for i in range(3):
                P.op("pool", lambda E, i=i: E.memset(PT[i][:], 0.0), [], ["PT%d" % i])
            CONSTS = ["idE", "idNE", "erow", "negrow", "ident", "mt0", "mt1", "mh0", "mh1", "allneg"]

            def load_head(hh):
                hb = hh % 2
                fox = hh < 8
                DMA(P, "sp", KT[hb][0:64, :], T["Kd"][hh, :, :], [], ["KTm%d" % hb])
                DMA(P, "sp", QT[hb][0:64, :], T["Qd"][hh, :, :], [], ["QTm%d" % hb])
                DMA(P, "sp", VV[hb][:, :, :], T["Vd"][hh, :, :, :], [], ["VV%d" % hb])
                if not fox:
                    P.op("pool", lambda E: E.memset(KT[hb][64:70, :], 0.0), [], ["KTa%d" % hb])
                    P.op("pool", lambda E: E.memset(QT[hb][64:70, :], 0.0), [], ["QTa%d" % hb])
                if fox:
                    DMA(P, "sp", KT[hb][64:70, :], T["KA"][hh, :, :], [], ["KTa%d" % hb])
                    DMA(P, "sp", qa[64:70, :], T["QA"][hh, :, :], [], ["qa"])
                    for j in range(NSL + 1):
                        if j < NSL:
                            c0 = qa[64:70, 1024 * j:1024 * j + 512]
                            c1 = qa[64:70, 1024 * j + 512:1024 * j + 1024]
                            e0 = esel[64:70, 2 * j:2 * j + 1]
                            e1 = esel[64:70, 2 * j + 1:2 * j + 2]
                            dst = QT[hb][64:70, 512 * j:512 * j + 512]
                            tmp = qtmp[64:70, 0:512]
                            P.op("dve", lambda E, c0=c0, e0=e0, tmp=tmp: E.tensor_scalar(out=tmp, in0=c0, scalar1=e0, scalar2=None, op0=ALU.mult),
                                 ["qa", "esel"], ["qtmp"])
                            P.op("dve", lambda E, c1=c1, e1=e1, tmp=tmp, dst=dst: E.scalar_tensor_tensor(out=dst, in0=c1, scalar=e1, in1=tmp, op0=ALU.mult, op1=ALU.add),
                                 ["qa", "esel", "qtmp"], ["QTa%d" % hb])
                        else:
                            for jj in range(NSL):
                                a0 = max(0, 1024 * jj - 2)
                                c0 = qa[64:70, a0:a0 + 2]
                                c1 = qa[64:70, 1024 * jj + 510:1024 * jj + 512]
                                e0 = eselh[64:70, 2 * jj:2 * jj + 1]
                                e1 = eselh[64:70, 2 * jj + 1:2 * jj + 2]
                                dst = QT[hb][64:70, NOWN + 2 * jj:NOWN + 2 * jj + 2]
                                tmp = qtmp[64:70, 0:2]
                                P.op("dve", lambda E, c0=c0, e0=e0, tmp=tmp: E.tensor_scalar(out=tmp, in0=c0, scalar1=e0, scalar2=None, op0=ALU.mult),
                                     ["qa", "eselh"], ["qtmp"])
                                P.op("dve", lambda E, c1=c1, e1=e1, tmp=tmp, dst=dst: E.scalar_tensor_tensor(out=dst, in0=c1, scalar=e1, in1=tmp, op0=ALU.mult, op1=ALU.add),
                                     ["qa", "eselh", "qtmp"], ["QTa%d" % hb])

            units = []
            slot_ctr = 0
            for hh in range(16):
                fox = hh < 8
                if "noSB" in OPT and not fox:
                    continue
                if "noFOX" in OPT and fox:
                    continue
                for sl in range(NSL + 1):
                    halo = sl == NSL
                    if halo and "noHalo" in OPT:
                        continue
                    W = 16 if halo else 512
                    nkb = NB if halo else 8 * (sl + 1)
                    order = list(range(nkb)) if fox else list(range(nkb - 1, -1, -1))
                    for idx, kb in enumerate(order):
                        if halo:
                            mk = ("H", kb)
                        elif 8 * sl <= kb < 8 * sl + 4:
                            mk = ("E", kb - 8 * sl)
                        elif 8 * sl + 4 <= kb < 8 * sl + 8:
                            mk = ("NE", kb - 8 * sl - 4)
                        else:
                            mk = None
                        units.append(dict(hh=hh, fox=fox, sl=sl, W=W, kb=kb, first=idx == 0, last=idx == nkb - 1, mk=mk,
                                          so=slot_ctr, qc=NOWN if halo else 512 * sl))
                    slot_ctr += 1
            for i, u in enumerate(units):
                u["i"] = i
                u["head_first"] = (i == 0 or units[i - 1]["hh"] != u["hh"])
                u["head_last"] = (i == len(units) - 1 or units[i + 1]["hh"] != u["hh"])

            def stage_A(u):
                hb = u["hh"] % 2
                KD = 70
                W, kb, i = u["W"], u["kb"], u["i"]
                z = psZ[i % 4]
                zk = "psZ%d" % (i % 4)
                mi = 0 if u["fox"] else 1
                rd = ["KTm%d" % hb, "QTm%d" % hb, "KTa%d" % hb, "QTa%d" % hb]
                mk = u["mk"] if "noMask" not in OPT else None
                MM(P, z[:, :W], KT[hb][0:KD, kb * 128:(kb + 1) * 128], QT[hb][0:KD, u["qc"]:u["qc"] + W], True, mk is None and u["fox"], rd, [zk])
                if mk is not None:
                    typ, m = mk
                    sl = u["sl"]
                    if typ == "H":
                        MM(P, z[:, :W], ident[:, :], mh[mi][:, m, :], False, u["fox"], CONSTS, [zk])
                    elif typ == "E":
                        MM(P, z[:, :W], idE[:, sl, :], mt[mi][:, m, :], False, u["fox"], CONSTS, [zk])
                    else:
                        MM(P, z[:, :W], idNE[:, sl, :], mt[mi][:, m, :], False, False, CONSTS, [zk])
                        MM(P, z[:, :W], idE[:, sl, :], allneg[:, :W], False, u["fox"], CONSTS, [zk])

            def stage_B_fox(u):
                W, i = u["W"], u["i"]
                ACTF(P, PT[i % 3][:, :W], psZ[i % 4][:, :W], AF.Exp, ["psZ%d" % (i % 4)], ["PT%d" % (i % 3)])

            def stage_B1(u):
                W, i = u["W"], u["i"]
                ACTF(P, UU[i % 2][:, :W], psZ[i % 4][:, :W], AF.Exp, ["psZ%d" % (i % 4)], ["UU%d" % (i % 2)])
                ACTF(P, LL[i % 3][:, :W], UU[i % 2][:, :W], AF.Ln, ["UU%d" % (i % 2)], ["LL%d" % (i % 3)], bias=1.0)
                if not u["last"]:
                    if u["first"]:
                        P.op("pool", lambda E: E.tensor_copy(out=LS[(i + 1) % 3][:, :W], in_=LL[i % 3][:, :W]), ["LL%d" % (i % 3)], ["LS%d" % ((i + 1) % 3)])
                    else:
                        P.op("pool", lambda E: E.tensor_tensor(out=LS[(i + 1) % 3][:, :W], in0=LS[i % 3][:, :W], in1=LL[i % 3][:, :W], op=ALU.add),
                             ["LL%d" % (i % 3), "LS%d" % (i % 3)], ["LS%d" % ((i + 1) % 3)])

            def stage_C(u):
                W, i = u["W"], u["i"]
                z = psZ[i % 4]
                zk = "psZ%d" % (i % 4)
                MM(P, z[:, :W], negtri[:, :], LL[i % 3][:, :W], False, u["first"], ["negtri", "LL%d" % (i % 3)], [zk])
                if not u["first"]:
                    MM(P, z[:, :W], negones[:, :], LS[i % 3][:, :W], False, True, ["negones", "LS%d" % (i % 3)], [zk])

            def stage_B2(u):
                W, i = u["W"], u["i"]
                ACTF(P, PT[i % 3][:, :W], psZ[i % 4][:, :W], AF.Exp, ["psZ%d" % (i % 4)], ["PT%d" % (i % 3)])

            fin_ctr = [0]

            def stage_D(u):
                hb = u["hh"] % 2
                W, kb, i = u["W"], u["kb"], u["i"]
                o = psO[u["so"] % 2]
                ok = "psO%d" % (u["so"] % 2)
                NV = 65 if u["fox"] else 64
                nsub = max(1, W // 128)
                M = min(128, W)
                for s_ in range(nsub):
                    MM(P, o[:, s_ * NV:(s_ + 1) * NV], PT[i % 3][:, s_ * 128:s_ * 128 + 128], VV[hb][:, kb, 0:NV],
                       u["first"] and s_ == 0, u["last"] and s_ == nsub - 1, ["PT%d" % (i % 3), "VV%d" % hb], [ok], skip=True)
                if u["last"]:
                    f = fin_ctr[0]
                    fin_ctr[0] += 1
                    obuf = ob[f % 3]
                    obk = "ob%d" % (f % 3)
                    ov = o[:M, 0:nsub * NV].rearrange("p (s v) -> p s v", s=nsub)
                    if u["fox"]:
                        rk = "rden%d" % (f % 2)
                        P.op("dve", lambda E: E.reciprocal(out=rden[:M, f % 2, 0:nsub], in_=ov[:, :, 64]), [ok], [rk])
                        for s_ in range(nsub):
                            P.op("dve", lambda E, s_=s_: E.tensor_scalar(out=obuf[:M, s_, :], in0=ov[:, s_, 0:64], scalar1=rden[:M, f % 2, s_:s_ + 1],
                                                                         scalar2=None, op0=ALU.mult), [ok, rk], [obk])
                    else:
                        P.op("dve", lambda E: E.tensor_copy(out=obuf[:M, 0:nsub, :], in_=ov[:, :, 0:64]), [ok], [obk])
                    r0 = u["qc"]
                    hh = u["hh"]
                    if nsub == 4:
                        dst = T["Od"][r0:r0 + 512, hh * 64:(hh + 1) * 64].rearrange("(s p) d -> p s d", p=128)
                        DMA(P, "sp", dst, obuf[:, :, :], [obk], ["Od"])
                    else:
                        DMA(P, "sp", T["Od"][r0:r0 + M, hh * 64:(hh + 1) * 64], obuf[:M, 0, :], [obk], ["Od"])

            n = len(units)
            hh0 = units[0]["hh"]
            load_head(hh0)
            stage_A(units[0])
            for i in range(n):
                u = units[i]
                if u["head_first"] and u["hh"] + 1 < 16 and u["hh"] == hh0:
                    load_head(hh0 + 1)
                if i + 1 < n:
                    stage_A(units[i + 1])
                if u["fox"]:
                    stage_B_fox(u)
                else:
                    stage_B1(u)
                    stage_C(u)
                if i >= 1:
                    pu = units[i - 1]
                    if not pu["fox"]:
                        stage_B2(pu)
                    stage_D(pu)
                    if pu["head_last"] and pu["hh"] + 2 < 16:
                        load_head(pu["hh"] + 2)
            pu = units[n - 1]
            if not pu["fox"]:
                stage_B2(pu)
            stage_D(pu)
            P.emit()
            stats.append(P.stats)
        if stop_after == "c":
            return nc, stats

        with ExitStack() as es:
            P = Prog(nc, gs, "d")
            wout = sbt(es, "wout", [128, 8, D], BF16)
            fox_g = sbt(es, "fox_g", [128, 512], F32)
            sb_g = sbt(es, "sb_g", [128, 512], F32)
            ffn_g = sbt(es, "ffn_g", [128, D], F32)
            fin_g = sbt(es, "fin_g", [128, D], F32)
            cw = sbt(es, "cw", [128, NCH, 3], F32)
            cb = sbt(es, "cb", [128, NCH], F32)
            hmask = sbt(es, "hmask", [128, 16], F32)
            ident = sbt(es, "ident3", [128, 128], BF16)
            identf = sbt(es, "identf3", [128, 128], F32)
            o_s = [sbt(es, "o_s%d" % i, [128, 4, D], BF16) for i in range(2)]
            xr = [sbt(es, "xr%d" % i, [128, 4, D], F32) for i in range(2)]
            junk = sbt(es, "junk3", [128, D], BF16)
            stat = sbt(es, "stat3", [128, 3, 16], F32)
            on = sbt(es, "on", [128, 4, D], BF16)
            onT = sbt(es, "onT", [128, 8, 512], BF16)
            h2T = sbt(es, "h2T", [128, 8, 512], BF16)
            wu = [sbt(es, "wu%d" % i, [128, 8, 2, 128], BF16) for i in range(3)]
            cv = [sbt(es, "cv%d" % i, [128, 512], F32) for i in range(4)]
            sg = [sbt(es, "sg%d" % i, [128, 512], F32) for i in range(2)]
            aT = sbt(es, "aT", [128, 22, 512], BF16)
            uh = sbt(es, "uh", [128, NCH, 16], F32)
            wd = [sbt(es, "wd%d" % i, [128, 22, 128], BF16) for i in range(3)]
            yT = [sbt(es, "yT%d" % i, [128, 512], F32) for i in range(2)]
            pT = [pst(es, "p3T%d" % i, [128, 8, 128], BF16) for i in range(2)]
            psa = [pst(es, "psa%d" % i, [128, 512], F32) for i in range(2)]
            psu = [pst(es, "psu%d" % i, [128, 512], F32) for i in range(2)]
            psy = pst(es, "psy", [128, 512], F32)
            pst_ = pst(es, "pst", [128, 4, 128], F32)
            rr = RR(["act", "dve"])
            PSU = [psa[0], psa[1], psu[0], psu[1]]
            PSUK = ["psa0", "psa1", "psu0", "psu1"]
            for nm, t_, src in (("wout", wout, T["Wout"][:, :, :]), ("fox_g", fox_g, T["fox_g"][:, :]), ("sb_g", sb_g, T["sb_g"][:, :]),
                                ("ffn_g", ffn_g, T["ffn_g"][:, :]), ("fin_g", fin_g, T["fin_g"][:, :]), ("cw", cw, T["cw"][:, :, :]),
                                ("cb", cb, T["cb"][:, :]), ("hmask", hmask, T["hmask"][:, :]), ("ident", ident, T["ident_bf"][:, :]),
                                ("identf", identf, T["ident_f"][:, :])):
                DMA(P, "sp", t_[:], src, [], [nm])

            P.op("pool", lambda E: E.memset(on[:], 0.0), [], ["on"])
            P.op("pool", lambda E: E.memset(onT[:], 0.0), [], ["onT.%d" % b for b in range(4)])
            P.op("pool", lambda E: E.memset(h2T[:], 0.0), [], ["h2T.%d" % b for b in range(4)])
            sctr = [0]
            pctr = [0]
            wuc = [0]
            wdc = [0]

            def rstd_of(src_ap, rows, width):
                c = sctr[0] % 16
                sctr[0] += 1
                sk = "st%d" % c
                ACTF(P, junk[:rows, :width], src_ap, AF.Square, src_ap_keys[0], ["junk", sk], accum=stat[:rows, 0, c:c + 1])
                ACTF(P, stat[:rows, 1, c:c + 1], stat[:rows, 0, c:c + 1], AF.Ln, [sk], [sk], bias=EPS, scale=1.0 / width)
                ACTF(P, stat[:rows, 2, c:c + 1], stat[:rows, 1, c:c + 1], AF.Exp, [sk], [sk], scale=-0.5)
                return stat[:rows, 2, c:c + 1], sk
            src_ap_keys = [None]

            def transpose_to(src_tile, src_key, blocks, dstT, dst_key):
                for (bi, rows) in blocks:
                    n = pctr[0]
                    pctr[0] += 1
                    pt = pT[n % 2]
                    pk = "p3T%d" % (n % 2)
                    for k in range(8):
                        TR(P, pt[:, k, :], src_tile[:, bi, k * 128:(k + 1) * 128], ident[:, :], [src_key, "ident"], [pk])
                    rr.copy(P, dstT[:, :, bi * 128:bi * 128 + rows], pt[:, :, :rows], [pk], [dst_key + ".%d" % bi])

            slots = [NSL] + list(range(NSL))

            def do_slot(si, sl):
                halo = sl == NSL
                W = 16 if halo else 512
                r0 = NOWN if halo else 512 * sl
                blocks = [(0, 16)] if halo else [(b, 128) for b in range(4)]
                sb_ = si % 2
                os_, xr_ = o_s[sb_], xr[sb_]
                osk, xrk = "o_s%d" % sb_, "xr%d" % sb_
                if halo:
                    DMA(P, "sp", os_[:16, 0, :], T["Od"][r0:r0 + 16, :], [], [osk])
                    DMA(P, "sp", xr_[:16, 0, :], T["xo"][r0:r0 + 16, :], [], [xrk])
                else:
                    DMA(P, "sp", os_[:, :, :], T["Od"][r0:r0 + 512, :].rearrange("(b p) d -> p b d", p=128), [], [osk])
                    DMA(P, "sp", xr_[:, :, :], T["xo"][r0:r0 + 512, :].rearrange("(b p) d -> p b d", p=128), [], [xrk])
                for (bi, rows) in blocks:
                    for g in range(2):
                        src = os_[:rows, bi, g * 512:(g + 1) * 512]
                        src_ap_keys[0] = [osk]
                        rs, sk = rstd_of(src, rows, 512)
                        gt = fox_g if g == 0 else sb_g
                        P.op("dve", lambda E, src=src, rs=rs, gt=gt, rows=rows, bi=bi, g=g: E.scalar_tensor_tensor(
                            out=on[:rows, bi, g * 512:(g + 1) * 512], in0=src, scalar=rs, in1=gt[:rows, :], op0=ALU.mult, op1=ALU.mult),
                            [osk, sk, "fox_g", "sb_g"], ["on"])
                transpose_to(on, "on", blocks, onT, "onT")
                onk = ["onT.%d" % bi for (bi, _) in blocks]
                for (bi, rows) in blocks:
                    for ch in range(2):
                        n = pctr[0]
                        pctr[0] += 1
                        pa = psa[n % 2]
                        pk = "psa%d" % (n % 2)
                        for k in range(8):
                            MM(P, pa[:, :], onT[:, k, bi * 128:bi * 128 + 128], wout[:, k, ch * 512:(ch + 1) * 512], k == 0, k == 7,
                               ["wout", "onT.%d" % bi], [pk])
                        P.op("dve", lambda E, pa=pa, rows=rows, bi=bi, ch=ch: E.tensor_tensor(
                            out=xr_[:rows, bi, ch * 512:(ch + 1) * 512], in0=pa[:rows, :], in1=xr_[:rows, bi, ch * 512:(ch + 1) * 512], op=ALU.add),
                            [pk, xrk], [xrk])
                for (bi, rows) in blocks:
                    src = xr_[:rows, bi, :]
                    src_ap_keys[0] = [xrk]
                    rs, sk = rstd_of(src, rows, D)
                    P.op("dve", lambda E, src=src, rs=rs, rows=rows, bi=bi: E.scalar_tensor_tensor(
                        out=on[:rows, bi, :], in0=src, scalar=rs, in1=ffn_g[:rows, :], op0=ALU.mult, op1=ALU.mult),
                        [xrk, sk, "ffn_g"] + onk, ["on"])
                transpose_to(on, "on", blocks, h2T, "h2T")
                h2k = ["h2T.%d" % bi for (bi, _) in blocks]
                for c in range(22):
                    wn = wuc[0]
                    wuc[0] += 1
                    wt = wu[wn % 3]
                    wk = "wu%d" % (wn % 3)
                    DMA(P, "sp", wt[:, :, 0, :], T["Wup"][:, :, c * 128:(c + 1) * 128], [], [wk + "g"])
                    DMA(P, "sp", wt[:, :, 1, :], T["Wup"][:, :, DFF + c * 128:DFF + (c + 1) * 128], [], [wk + "v"])
                    pend = []
                    for gv in range(2):
                        un = (2 * wn + gv) % 4
                        pu_ = PSU[un]
                        pk = PSUK[un]
                        for k in range(8):
                            MM(P, pu_[:, :W], wt[:, k, gv, :], h2T[:, k, :W], k == 0, k == 7, [wk + ("g" if gv == 0 else "v")] + h2k, [pk])
                        cc = c + 22 * gv
                        if halo:
                            P.op("dve", lambda E, pu_=pu_, cc=cc: E.tensor_tensor(out=uh[:, cc, :], in0=pu_[:, :16], in1=hmask[:, :], op=ALU.mult),
                                 [pk, "hmask"], ["uh"])
                            continue
                        ct, ck = cv[un], "cv%d" % un
                        ACTF(P, ct[:, :], pu_[:, :512], AF.Identity, [pk, "cw", "cb"], [ck], bias=cb[:, cc:cc + 1], scale=cw[:, cc, 2:3])
                        pend.append((pu_, pk, ct, ck, cc))
                    if halo:
                        continue
                    for (pu_, pk, ct, ck, cc) in pend:
                        P.op("dve", lambda E, pu_=pu_, ct=ct, cc=cc: E.scalar_tensor_tensor(out=ct[:, 1:512], in0=pu_[:, 0:511], scalar=cw[:, cc, 1:2], in1=ct[:, 1:512],
                                                                                         op0=ALU.mult, op1=ALU.add), [pk, "cw", ck], [ck])
                    for (pu_, pk, ct, ck, cc) in pend:
                        P.op("dve", lambda E, pu_=pu_, ct=ct, cc=cc: E.scalar_tensor_tensor(out=ct[:, 2:512], in0=pu_[:, 0:510], scalar=cw[:, cc, 0:1], in1=ct[:, 2:512],
                                                                                         op0=ALU.mult, op1=ALU.add), [pk, "cw", ck], [ck])
                    for (pu_, pk, ct, ck, cc) in pend:
                        P.op("dve", lambda E, ct=ct, cc=cc: E.scalar_tensor_tensor(out=ct[:, 0:1], in0=uh[:, cc, 2 * sl + 1:2 * sl + 2], scalar=cw[:, cc, 1:2], in1=ct[:, 0:1],
                                                                                op0=ALU.mult, op1=ALU.add), ["uh", "cw", ck], [ck])
                    for (pu_, pk, ct, ck, cc) in pend:
                        P.op("dve", lambda E, ct=ct, cc=cc: E.scalar_tensor_tensor(out=ct[:, 0:2], in0=uh[:, cc, 2 * sl:2 * sl + 2], scalar=cw[:, cc, 0:1], in1=ct[:, 0:2],
                                                                                op0=ALU.mult, op1=ALU.add), ["uh", "cw", ck], [ck])
                    (_, _, ctg, ckg, _), (_, _, ctv, ckv, _) = pend
                    st_, stk = sg[wn % 2], "sg%d" % (wn % 2)
                    ACTF(P, st_[:, :], ctg[:, :], AF.Silu, [ckg], [stk])
                    P.op("dve", lambda E, st_=st_, ctv=ctv, c=c: E.tensor_tensor(out=aT[:, c, :], in0=st_[:, :], in1=ctv[:, :], op=ALU.mult),
                         [stk, ckv], ["aT.%d" % c])
                if halo:
                    return
                for cc in range(8):
                    wn = wdc[0]
                    wdc[0] += 1
                    wt = wd[wn % 3]
                    wk = "wd%d" % (wn % 3)
                    DMA(P, "sp", wt[:, :, :], T["Wdn"][:, :, cc * 128:(cc + 1) * 128], [], [wk])
                    for c in range(22):
                        MM(P, psy[:, :], wt[:, c, :], aT[:, c, :], c == 0, c == 21, [wk, "aT.%d" % c], ["psy"])
                    yt, yk = yT[wn % 2], "yT%d" % (wn % 2)
                    ACTF(P, yt[:, :], psy[:, :], AF.Copy, ["psy"], [yk])
                    for b in range(4):
                        P.op("pe", lambda E, yt=yt, b=b: E.transpose(pst_[:, b, :], yt[:, b * 128:(b + 1) * 128], identf[:, :]), [yk, "identf"], ["pst"])
                    P.op("dve", lambda E, cc=cc: E.tensor_tensor(out=xr_[:, :, cc * 128:(cc + 1) * 128], in0=pst_[:, :, :],
                                                               in1=xr_[:, :, cc * 128:(cc + 1) * 128], op=ALU.add), ["pst", xrk], [xrk])
                for (bi, rows) in blocks:
                    src = xr_[:rows, bi, :]
                    src_ap_keys[0] = [xrk]
                    rs, sk = rstd_of(src, rows, D)
                    P.op("dve", lambda E, src=src, rs=rs: E.scalar_tensor_tensor(out=src, in0=src, scalar=rs, in1=fin_g[:, :], op0=ALU.mult, op1=ALU.mult),
                         [xrk, sk, "fin_g"], [xrk])
                DMA(P, "pool", T["out"][r0:r0 + 512, :].rearrange("(b p) d -> p b d", p=128), xr_[:, :, :], [xrk], ["out"])

            for si, sl in enumerate(slots):
                do_slot(si, sl)
            P.emit()
            stats.append(P.stats)
    return nc, stats


_CACHE = {}


def _consts(S):
    NB = S // 128
    bf = ml_dtypes.bfloat16
    p = np.arange(128)
    c = {}
    c["ident_bf"] = np.eye(128, dtype=np.float32).astype(bf)
    c["ident_f"] = np.eye(128, dtype=np.float32)
    c["triu_f"] = (p[:, None] <= p[None, :]).astype(np.float32)
    c["ones_f"] = np.ones((128, 128), np.float32)
    c["negtri"] = (-(p[:, None] >= p[None, :]).astype(np.float32)).astype(bf)
    c["negones"] = (-np.ones((128, 128), np.float32)).astype(bf)
    t = np.arange(512)
    key = (np.arange(4)[None, :, None] * 128 + p[:, None, None])
    c["mt_f"] = np.where(key <= t[None, None, :], 0.0, NEG).astype(np.float32).astype(bf)
    c["mt_s"] = np.where(key < t[None, None, :], 0.0, NEG).astype(np.float32).astype(bf)
    c["negrow"] = np.full((1, 512), NEG, np.float32).astype(bf)
    c["cm1"] = np.full((64, 3, 128), -1.0, np.float32).astype(bf)
    c["cp1"] = np.full((64, 3, 128), 1.0, np.float32).astype(bf)
    return c


def _core_consts(S, par):
    NT = S // 512
    NSL = NT // 2
    NB = S // 128
    bf = ml_dtypes.bfloat16
    own = own_tiles(par, NSL)
    esel = np.zeros((128, 2 * NSL), np.float32)
    eselh = np.zeros((128, 2 * NSL), np.float32)
    hmask = np.ones((128, 16), np.float32)
    idE = np.zeros((128, NSL, 128), np.float32)
    idNE = np.zeros((128, NSL, 128), np.float32)
    erow = np.zeros((1, NSL, 128), np.float32)
    I = np.eye(128, dtype=np.float32)
    p = np.arange(128)
    keypos = (np.arange(NB)[None, :] * 128 + p[:, None])
    mh_f = np.zeros((128, NB, 16), np.float32)
    mh_s = np.zeros((128, NB, 16), np.float32)
    for j in range(NSL):
        e0 = 1.0 if own[j] == 2 * j else 0.0
        esel[:, 2 * j] = e0
        esel[:, 2 * j + 1] = 1.0 - e0
        eselh[:, 2 * j] = 1.0 if (own[j] == 2 * j and j > 0) else 0.0
        eselh[:, 2 * j + 1] = 1.0 if own[j] == 2 * j + 1 else 0.0
        idE[:, j, :] = e0 * I
        idNE[:, j, :] = (1.0 - e0) * I
        erow[0, j, :] = e0
        for r in range(2):
            col = 2 * j + r
            pos = 512 * own[j] - 2 + r
            if own[j] == 0:
                hmask[:, col] = 0.0
                mh_f[:, :, col] = np.where(keypos == 0, 0.0, NEG)
                mh_s[:, :, col] = np.where(keypos == 0, 0.0, NEG)
            else:
                mh_f[:, :, col] = np.where(keypos <= pos, 0.0, NEG)
                mh_s[:, :, col] = np.where(keypos < pos, 0.0, NEG)
    return dict(esel=esel, eselh=eselh, hmask=hmask, idE=idE.astype(bf), idNE=idNE.astype(bf), erow=erow.astype(bf),
                mh_f=mh_f.astype(bf), mh_s=mh_s.astype(bf)), own


def _prepare(inputs, debug=False):
    x = np.asarray(inputs["x"], np.float32)
    B, S, _ = x.shape
    NT = S // 512
    NSL = NT // 2
    rep = lambda v: np.ascontiguousarray(np.broadcast_to(np.asarray(v, np.float32).reshape(1, -1), (128, np.asarray(v).size)))
    shared = dict(_consts(S))
    shared["w_in"] = np.ascontiguousarray(np.asarray(inputs["w_in"], np.float32)[0])
    shared["w_out"] = np.ascontiguousarray(np.asarray(inputs["w_out"], np.float32)[0])
    shared["w_up"] = np.ascontiguousarray(np.asarray(inputs["w_up"], np.float32)[0])
    shared["w_down"] = np.ascontiguousarray(np.asarray(inputs["w_down"], np.float32)[0])
    shared["attn_g"] = rep(inputs["attn_norm_g"][0])
    shared["ffn_g"] = rep(inputs["ffn_norm_g"][0])
    shared["fin_g"] = rep(inputs["final_norm_g"])
    shared["fox_g"] = rep(inputs["fox_out_g"][0])
    shared["sb_g"] = rep(inputs["sb_out_g"][0])
    shared["fb"] = rep(inputs["forget_bias"][0])
    cwv = np.asarray(inputs["conv_w"], np.float32)[0]
    shared["cw"] = np.ascontiguousarray(cwv.reshape(3, NCH, 128).transpose(2, 1, 0))
    shared["cb"] = np.ascontiguousarray(np.asarray(inputs["conv_b"], np.float32)[0].reshape(NCH, 128).T)
    in_maps = []
    owns = []
    for c in range(8):
        b, par = c // 2, c % 2
        cc, own = _core_consts(S, par)
        owns.append(own)
        xo = np.zeros((NSL * 512 + 16, D), np.float32)
        for j, t in enumerate(own):
            xo[j * 512:(j + 1) * 512] = x[b, t * 512:(t + 1) * 512]
            if t > 0:
                xo[NSL * 512 + 2 * j:NSL * 512 + 2 * j + 2] = x[b, t * 512 - 2:t * 512]
        m = dict(shared)
        m.update(cc)
        m["xn"] = np.ascontiguousarray(x[b])
        m["xo"] = xo
        in_maps.append(m)
    return in_maps, owns, (B, S)


def kernel(**inputs):
    in_maps, owns, (B, S) = _prepare(inputs)
    if S not in _CACHE:
        _CACHE[S] = build_program(S)
    nc, _ = _CACHE[S]
    res = run_bass_kernel_spmd(nc, in_maps, core_ids=list(range(8)))
    out = np.zeros((B, S, D), np.float32)
    for c in range(8):
        b = c // 2
        o = np.asarray(res.results[c]["out"], np.float32)
        for j, t in enumerate(owns[c]):
            out[b, t * 512:(t + 1) * 512] = o[j * 512:(j + 1) * 512]
    return out
```

```python
import numpy as np
import ml_dtypes
from contextlib import ExitStack
import concourse.bass as bass
import concourse.mybir as mybir
from concourse.bass_utils import run_bass_kernel_spmd

F32 = mybir.dt.float32
BF16 = mybir.dt.bfloat16
AF = mybir.ActivationFunctionType
ALU = mybir.AluOpType

D = 1024
DH = 64
DFF = 2816
INC = 3080
EPS = 1e-6
NEG = -30000.0
CQ_F, CK_F, CV_F, CL, CQ_S, CK_S, CV_S = 0, 512, 1024, 1536, 1544, 2056, 2568
NCH = 2 * DFF // 128

import os
OPT = set(os.environ.get("KOPT", "").split(","))
COMPUTE = ("pe", "act", "dve", "pool")
CH = 16000
NDMA = 8


class Prog:
    def __init__(self, nc, gs, tag):
        self.nc = nc
        self.gs = gs
        self.tag = tag
        self.ops = []
        self.last_w = {}
        self.readers = {}

    def op(self, eng, fn, reads=(), writes=(), dma=False):
        j = len(self.ops)
        deps = set()
        for r in reads:
            if r in self.last_w:
                deps.add(self.last_w[r])
        for w in writes:
            if w in self.last_w:
                deps.add(self.last_w[w])
            for rd in self.readers.get(w, ()):
                deps.add(rd)
        deps.discard(j)
        self.ops.append(dict(eng=eng, fn=fn, deps=deps, dma=dma, sig=False))
        for r in reads:
            self.readers.setdefault(r, []).append(j)
        for w in writes:
            self.last_w[w] = j
            self.readers[w] = []
        return j

    def dma(self, q, fn, reads=(), writes=()):
        return self.op(q, fn, reads, writes, dma=True)

    def emit(self):
        nc, ops = self.nc, self.ops
        for j, o in enumerate(ops):
            nd = set()
            for d in o["deps"]:
                p = ops[d]
                if not p["dma"] and not o["dma"] and p["eng"] == o["eng"] == "pe":
                    continue
                nd.add(d)
            o["deps"] = nd
            for d in nd:
                ops[d]["sig"] = True
        cnt = {e: 0 for e in COMPUTE}
        sems = {}

        def getsem(key):
            if key not in sems:
                sems[key] = self.gs.enter_context(nc.semaphore("s%s_%s_%s" % (self.tag, key[0], key[1])))
            return sems[key]

        dcount = {}
        dma_i = {}
        for j, o in enumerate(ops):
            if o["dma"]:
                q = o["eng"]
                k = dma_i.get(q, 0)
                dma_i[q] = k + 1
                key = ("d" + q, k % NDMA)
                dcount[key] = dcount.get(key, 0) + 16
                o["semkey"], o["semval"] = key, dcount[key]
                o["prev"] = (key, dcount[key] - 16)
            elif o["sig"]:
                e = o["eng"]
                c = cnt[e]
                cnt[e] = c + 1
                o["semkey"], o["semval"] = (e, c // CH), c % CH + 1
        last_dma = {}
        for o in ops:
            if o["dma"]:
                last_dma[o["semkey"]] = max(last_dma.get(o["semkey"], 0), o["semval"])
            if "semkey" in o:
                getsem(o["semkey"])
        per = {e: [] for e in ("sp", "act", "dve", "pe", "pool")}
        for j, o in enumerate(ops):
            per[o["eng"]].append(j)
        nwc = [0]

        def run_engine(e, E):
            known = {}
            for j in per[e]:
                o = ops[j]
                need = {}
                for d in o["deps"]:
                    p = ops[d]
                    k, v = p["semkey"], p["semval"]
                    if need.get(k, 0) < v:
                        need[k] = v
                if o["dma"] and o["prev"][1] > 0:
                    k, v = o["prev"]
                    if need.get(k, 0) < v:
                        need[k] = v
                for k, v in need.items():
                    if known.get(k, 0) >= v:
                        continue
                    known[k] = v
                    E.wait_ge(sems[k], v)
                    nwc[0] += 1
                inst = o["fn"](E)
                if o["dma"]:
                    inst.then_inc(sems[o["semkey"]], 16)
                elif o["sig"]:
                    inst.then_inc(sems[o["semkey"]], 1)
            if e == "sp":
                for k, v in last_dma.items():
                    if known.get(k, 0) < v:
                        E.wait_ge(sems[k], v)

        with nc.Block() as block:
            @block.sync
            def _(E):
                run_engine("sp", E)

            @block.scalar
            def _(E):
                run_engine("act", E)

            @block.vector
            def _(E):
                run_engine("dve", E)

            @block.tensor
            def _(E):
                run_engine("pe", E)

            @block.gpsimd
            def _(E):
                run_engine("pool", E)
        self.stats = dict(tag=self.tag, nops=len(ops), nwaits=nwc[0], nsems=len(sems), cnt=cnt)


def MM(P, out, lhsT, rhs, start, stop, rd, wr, skip=False):
    P.op("pe", lambda E: E.matmul(out, lhsT=lhsT, rhs=rhs, start=start, stop=stop, skip_group_check=skip), rd, wr)


def TR(P, out, in_, ident, rd, wr):
    P.op("pe", lambda E: E.transpose(out, in_, ident), rd, wr)


def ACTF(P, out, in_, func, rd, wr, bias=None, scale=None, accum=None):
    kw = {}
    if bias is not None:
        kw["bias"] = bias
    if scale is not None:
        kw["scale"] = scale
    if accum is not None:
        kw["accum_out"] = accum
    P.op("act", lambda E: E.activation(out=out, in_=in_, func=func, **kw), rd, wr)


def DMA(P, q, out, in_, rd, wr):
    P.dma(q, lambda E: E.dma_start(out=out, in_=in_), rd, wr)


class RR:
    def __init__(self, engs):
        self.engs = engs
        self.i = 0

    def copy(self, P, out, in_, rd, wr, scale=None):
        e = self.engs[self.i % len(self.engs)]
        self.i += 1
        if e == "act":
            ACTF(P, out, in_, AF.Copy, rd, wr, scale=scale)
        elif scale is None:
            P.op(e, lambda E: E.tensor_copy(out=out, in_=in_), rd, wr)
        else:
            P.op(e, lambda E: E.tensor_scalar(out=out, in0=in_, scalar1=float(scale), scalar2=None, op0=ALU.mult), rd, wr)


def own_tiles(p, NSL):
    res = []
    for j in range(NSL):
        first = (j % 2 == 0) if p == 0 else (j % 2 == 1)
        res.append(2 * j if first else 2 * j + 1)
    return res


def build_program(S, debug=False, stop_after="d"):
    NT = S // 512
    NSL = NT // 2
    NB = S // 128
    NOWN = NSL * 512
    NQ = NOWN + 16
    nc = bass.Bass("TRN2", target_bir_lowering=False)
    T = {}

    def din(name, shape, dt=F32):
        T[name] = nc.dram_tensor(name, list(shape), dt, kind="ExternalInput").ap()

    def dscr(name, shape, dt=BF16):
        kind = "ExternalOutput" if debug else "Internal"
        T[name] = nc.dram_tensor(name, list(shape), dt, kind=kind).ap()

    din("xn", [S, D]); din("xo", [NQ, D])
    din("w_in", [D, INC]); din("w_out", [D, D]); din("w_up", [D, 2 * DFF]); din("w_down", [DFF, D])
    din("attn_g", [128, D]); din("ffn_g", [128, D]); din("fin_g", [128, D])
    din("fox_g", [128, 512]); din("sb_g", [128, 512]); din("fb", [128, 8])
    din("cw", [128, NCH, 3]); din("cb", [128, NCH])
    din("esel", [128, 2 * NSL]); din("eselh", [128, 2 * NSL]); din("hmask", [128, 16])
    din("idE", [128, NSL, 128], BF16); din("idNE", [128, NSL, 128], BF16); din("erow", [1, NSL, 128], BF16)
    din("mh_f", [128, NB, 16], BF16); din("mh_s", [128, NB, 16], BF16)
    din("ident_bf", [128, 128], BF16); din("ident_f", [128, 128]); din("triu_f", [128, 128]); din("ones_f", [128, 128])
    din("negtri", [128, 128], BF16); din("negones", [128, 128], BF16)
    din("mt_f", [128, 4, 512], BF16); din("mt_s", [128, 4, 512], BF16); din("negrow", [1, 512], BF16)
    din("cm1", [64, 3, 128], BF16); din("cp1", [64, 3, 128], BF16)
    T["out"] = nc.dram_tensor("out", [NOWN, D], F32, kind="ExternalOutput").ap()
    dscr("Kd", [16, 64, S]); dscr("Vd", [16, 128, NB, 65]); dscr("Qd", [16, 64, NQ])
    dscr("KA", [8, 6, S]); dscr("QA", [8, 6, S]); dscr("Od", [NQ + 112, D])
    dscr("Wup", [128, 8, 2 * DFF]); dscr("Wdn", [128, 22, D]); dscr("Wout", [128, 8, D])
    if debug:
        dscr("dbg_nF", [128, 8, NB], F32)

    stats = []
    with ExitStack() as gs:
        def sbt(es, name, shape, dt):
            return es.enter_context(nc.sbuf_tensor("sb_" + name, list(shape), dt))

        def pst(es, name, shape, dt):
            return es.enter_context(nc.psum_tensor("pp_" + name, list(shape), dt))

        with ExitStack() as es:
            P = Prog(nc, gs, "a")
            win = sbt(es, "win", [128, 8, INC], BF16)
            wstg = [sbt(es, "wstg%d" % i, [128, INC], F32) for i in range(2)]
            g_r = sbt(es, "g_r", [128, D], F32)
            fb_r = sbt(es, "fb_r", [128, 8], F32)
            ident = sbt(es, "ident", [128, 128], BF16)
            xb = [sbt(es, "xb%d" % i, [128, D], F32) for i in range(3)]
            junk = sbt(es, "junk", [128, D], BF16)
            stat = sbt(es, "stat", [128, 3, 8], F32)
            xnb = [sbt(es, "xnb%d" % i, [128, D], BF16) for i in range(2)]
            hT = [sbt(es, "hT%d" % i, [128, 8, 512], BF16) for i in range(2)]
            kts = [sbt(es, "kts%d" % i, [128, 512], BF16) for i in range(3)]
            vs = [sbt(es, "vs%d" % i, [128, 16, 4, 65], BF16) for i in range(2)]
            flog = sbt(es, "flog", [128, 8, NB], F32)
            esp = ExitStack()
            pT = [pst(esp, "pT%d" % i, [128, 8, 128], BF16) for i in range(2)]
            psk = [pst(esp, "psk%d" % i, [128, 512], F32) for i in range(2)]
            psv = [pst(esp, "psv%d" % i, [128, 512], F32) for i in range(2)]
            psf = [pst(esp, "psf%d" % i, [128, 512], F32) for i in range(2)]
            rr = RR(["act", "dve"])

            DMA(P, "sp", g_r[:], T["attn_g"][:, :], [], ["g_r"])
            DMA(P, "sp", fb_r[:], T["fb"][:, :], [], ["fb_r"])
            DMA(P, "sp", ident[:], T["ident_bf"][:, :], [], ["ident"])
            for k in range(8):
                DMA(P, "sp", wstg[k % 2][:], T["w_in"][k * 128:(k + 1) * 128, :], [], ["wstg%d" % (k % 2)])
                P.op("pool", lambda E, k=k: E.tensor_copy(out=win[:, k, :], in_=wstg[k % 2][:]), ["wstg%d" % (k % 2)], ["win"])
            for i in range(2):
                P.op("pool", lambda E, i=i: E.memset(vs[i][:], 1.0), [], ["vs%d.%d.%d" % (i, b, g) for b in range(4) for g in range(2)])
                P.op("pool", lambda E, i=i: E.memset(xnb[i][:], 0.0), [], ["xnb%d" % i])
            blkn = [0]

            def norm_block(src_rows, rows, hbuf, col0):
                n = blkn[0]
                blkn[0] += 1
                x = xb[n % 3]
                xs = "xb%d" % (n % 3)
                st = stat[:, :, n % 8:n % 8 + 1]
                sk = "stat%d" % (n % 8)
                DMA(P, "sp", x[:rows, :], src_rows, [], [xs])
                ACTF(P, junk[:rows, :], x[:rows, :], AF.Square, [xs], ["junk", sk], accum=stat[:rows, 0, n % 8:n % 8 + 1])
                ACTF(P, stat[:rows, 1, n % 8:n % 8 + 1], stat[:rows, 0, n % 8:n % 8 + 1], AF.Ln, [sk], [sk], bias=EPS, scale=1.0 / D)
                ACTF(P, stat[:rows, 2, n % 8:n % 8 + 1], stat[:rows, 1, n % 8:n % 8 + 1], AF.Exp, [sk], [sk], scale=-0.5)
                xq = xnb[n % 2]
                qs = "xnb%d" % (n % 2)
                P.op("dve", lambda E: E.scalar_tensor_tensor(out=xq[:rows, :], in0=x[:rows, :], scalar=stat[:rows, 2, n % 8:n % 8 + 1],
                                                             in1=g_r[:rows, :], op0=ALU.mult, op1=ALU.mult), [xs, sk, "g_r"], [qs])
                pt = pT[n % 2]
                ps = "pT%d" % (n % 2)
                for k in range(8):
                    TR(P, pt[:, k, :], xq[:, k * 128:(k + 1) * 128], ident[:, :], [qs, "ident"], [ps])
                rr.copy(P, hT[hbuf][:, :, col0:col0 + rows], pt[:, :, :rows], [ps], ["hT%d.%d" % (hbuf, col0 // 128)])

            def proj_T(hbuf, width, col_w, dst, hkeys, scale=None):
                n = proj_T.n
                proj_T.n += 1
                ps = psk[n % 2]
                pk = "psk%d" % (n % 2)
                for k in range(8):
                    MM(P, ps[:, :width], win[:, k, col_w:col_w + 128], hT[hbuf][:, k, :width], k == 0, k == 7, ["win"] + hkeys, [pk])
                ks = kts[n % 3]
                kk = "kts%d" % (n % 3)
                rr.copy(P, ks[:, :width], ps[:, :width], [pk], [kk], scale=scale)
                DMA(P, "pool", dst, ks[:, :width], [kk], ["KQscr"])
            proj_T.n = 0

            vcount = [0]

            def make_tile_A(Tn, hb):
                blocks = [(lambda b=b: norm_block(T["xn"][Tn * 512 + b * 128:Tn * 512 + (b + 1) * 128, :], 128, hb, b * 128)) for b in range(4)]
                hk = ["hT%d.%d" % (hb, b) for b in range(4)]
                vb = Tn % 2
                groups = []

                def kgrp(c):
                    col = (CK_F + c * 128) if c < 4 else (CK_S + (c - 4) * 128)
                    h0 = 2 * c if c < 4 else 8 + 2 * (c - 4)
                    proj_T(hb, 512, col, T["Kd"][h0:h0 + 2, :, Tn * 512:(Tn + 1) * 512].rearrange("h r t -> (h r) t"), hk)

                def vgrp(b, g):
                    n = vcount[0]
                    vcount[0] += 1
                    ps = psv[n % 2]
                    pk = "psv%d" % (n % 2)
                    vc = CV_F if g == 0 else CV_S
                    for k in range(8):
                        MM(P, ps[:, :], hT[hb][:, k, b * 128:(b + 1) * 128], win[:, k, vc:vc + 512], k == 0, k == 7,
                           ["win", "hT%d.%d" % (hb, b)], [pk])
                    rr.copy(P, vs[vb][:, g * 8:(g + 1) * 8, b, 0:64], ps[:, :].rearrange("p (h d) -> p h d", h=8), [pk],
                            ["vs%d.%d.%d" % (vb, b, g)])
                    if g == 1:
                        pf = psf[b % 2]
                        for k in range(8):
                            MM(P, pf[:, 0:8], hT[hb][:, k, b * 128:(b + 1) * 128], win[:, k, CL:CL + 8], k == 0, k == 7,
                               ["win", "hT%d.%d" % (hb, b)], ["psf%d" % (b % 2)])
                        P.op("dve", lambda E: E.tensor_tensor(out=flog[:, :, Tn * 4 + b], in0=pf[:, 0:8], in1=fb_r[:, :], op=ALU.add),
                             ["psf%d" % (b % 2), "fb_r"], ["flog"])

                def vstore():
                    for h in range(16):
                        DMA(P, "pool", T["Vd"][h, :, Tn * 4:Tn * 4 + 4, :], vs[vb][:, h, :, :],
                            ["vs%d.%d.%d" % (vb, b, g) for b in range(4) for g in range(2)], ["Vscr"])
                for c in range(8):
                    groups.append(lambda c=c: kgrp(c))
                for b in range(4):
                    for g in range(2):
                        groups.append(lambda b=b, g=g: vgrp(b, g))
                groups.append(vstore)
                return blocks, groups

            def make_tile_Q(row0, width, col0, hb):
                nb = (width + 127) // 128
                blocks = []
                for b in range(nb):
                    rows = min(128, width - b * 128)
                    blocks.append(lambda b=b, rows=rows: norm_block(T["xo"][row0 + b * 128:row0 + b * 128 + rows, :], rows, hb, b * 128))
                hk = ["hT%d.%d" % (hb, b) for b in range(nb)]

                def qgrp(c):
                    col = (CQ_F + c * 128) if c < 4 else (CQ_S + (c - 4) * 128)
                    h0 = 2 * c if c < 4 else 8 + 2 * (c - 4)
                    proj_T(hb, width, col, T["Qd"][h0:h0 + 2, :, col0:col0 + width].rearrange("h r t -> (h r) t"), hk, scale=0.125)
                groups = [(lambda c=c: qgrp(c)) for c in range(8)]
                return blocks, groups

            seq = []
            qi = 0
            for Tn in range(NT):
                seq.append(("A", Tn))
                if Tn % 2 == 1 and qi < NSL:
                    seq.append(("Q", qi))
                    qi += 1
            seq.append(("H", 0))
            tiles = []
            for i, (kind, idx) in enumerate(seq):
                hb = i % 2
                if kind == "A":
                    tiles.append(make_tile_A(idx, hb))
                elif kind == "Q":
                    tiles.append(make_tile_Q(idx * 512, 512, idx * 512, hb))
                else:
                    tiles.append(make_tile_Q(NOWN, 16, NOWN, hb))
            for blk in tiles[0][0]:
                blk()
            for i, (blocks, groups) in enumerate(tiles):
                nxt = tiles[i + 1][0] if i + 1 < len(tiles) else []
                per = max(1, len(groups) // max(1, len(nxt)))
                bi = 0
                for gi, g in enumerate(groups):
                    g()
                    if bi < len(nxt) and (gi + 1) % per == 0:
                        nxt[bi]()
                        bi += 1
                while bi < len(nxt):
                    nxt[bi]()
                    bi += 1
            P.emit()
            stats.append(P.stats)
            esp.close()
            if stop_after == "a":
                return nc, stats

            with ExitStack() as es2:
                P = Prog(nc, gs, "b")
                nlf = sbt(es2, "nlf", [128, 8 * NB], F32)
                ee = sbt(es2, "ee", [128, 8 * NB], F32)
                sc = [sbt(es2, "sc%d" % i, [128, 8, NB], F32) for i in range(2)]
                tot = sbt(es2, "tot", [128, 8, NB], F32)
                nF = sbt(es2, "nF", [128, 8, NB], F32)
                nFp = sbt(es2, "nFp", [128, 8, 128], F32)
                nFT = sbt(es2, "nFT", [NB, 8, 128], F32)
                r1 = sbt(es2, "r1", [NB, 8, 128], F32)
                parts = [sbt(es2, "part%d" % i, [NB, 8, 128], BF16) for i in range(3)]
                triu = sbt(es2, "triu", [128, 128], F32)
                ones = sbt(es2, "ones", [128, 128], F32)
                identf = sbt(es2, "identf", [128, 128], F32)
                cm1 = sbt(es2, "cm1", [64, 3, 128], BF16)
                cp1 = sbt(es2, "cp1", [64, 3, 128], BF16)
                ps_c = pst(es2, "ps_c", [128, 512], F32)
                ps_t = pst(es2, "ps_t", [128, 512], F32)
                ps_x = [pst(es2, "ps_x%d" % i, [128, 4, 128], F32) for i in range(2)]
                W8 = 8 * NB
                DMA(P, "sp", triu[:], T["triu_f"][:, :], [], ["triu"])
                DMA(P, "sp", ones[:], T["ones_f"][:, :], [], ["ones"])
                DMA(P, "sp", identf[:], T["ident_f"][:, :], [], ["identf"])
                DMA(P, "sp", cm1[:], T["cm1"][:, :, :], [], ["cm1"])
                DMA(P, "sp", cp1[:], T["cp1"][:, :, :], [], ["cp1"])
                fl2 = flog[:, :, :].rearrange("p h b -> p (h b)")
                ACTF(P, ee[:, :], fl2, AF.Exp, [], ["ee"], scale=-1.0)
                ACTF(P, nlf[:, :], ee[:, :], AF.Ln, ["ee"], ["nlf"], bias=1.0)
                MM(P, ps_c[:, :W8], triu[:, :], nlf[:, :], True, True, ["triu", "nlf"], ["ps_c"])
                MM(P, ps_t[:, :W8], ones[:, :], nlf[:, :], True, True, ["ones", "nlf"], ["ps_t"])
                P.op("dve", lambda E: E.tensor_copy(out=tot[:, :, :], in_=ps_t[:, :W8].rearrange("p (h b) -> p h b", h=8)), ["ps_t"], ["tot"])
                P.op("dve", lambda E: E.tensor_copy(out=sc[0][:, :, :], in_=tot[:, :, :]), ["tot"], ["sc0"])
                cur = 0
                d = 1
                while d < NB:
                    nxt = 1 - cur
                    P.op("dve", lambda E, cur=cur, nxt=nxt, d=d: E.tensor_copy(out=sc[nxt][:, :, 0:d], in_=sc[cur][:, :, 0:d]), ["sc%d" % cur], ["sc%d" % nxt])
                    P.op("dve", lambda E, cur=cur, nxt=nxt, d=d: E.tensor_tensor(out=sc[nxt][:, :, d:NB], in0=sc[cur][:, :, d:NB], in1=sc[cur][:, :, 0:NB - d], op=ALU.add),
                         ["sc%d" % cur], ["sc%d" % nxt])
                    cur = nxt
                    d *= 2
                P.op("dve", lambda E, cur=cur: E.tensor_tensor(out=tot[:, :, :], in0=sc[cur][:, :, :], in1=tot[:, :, :], op=ALU.subtract), ["sc%d" % cur, "tot"], ["tot"])
                P.op("dve", lambda E: E.tensor_tensor(out=nF[:, :, :], in0=ps_c[:, :W8].rearrange("p (h b) -> p h b", h=8), in1=tot[:, :, :], op=ALU.add), ["ps_c", "tot"], ["nF"])
                if debug:
                    DMA(P, "sp", T["dbg_nF"][:, :, :], nF[:, :, :], ["nF"], ["dbg"])
                P.op("dve", lambda E: E.memset(nFp[:], 0.0), [], ["nFp"])
                P.op("dve", lambda E: E.tensor_copy(out=nFp[:, :, 0:NB], in_=nF[:, :, :]), ["nF", "nFp"], ["nFp"])
                for h in range(8):
                    px = ps_x[h // 4]
                    P.op("pe", lambda E, h=h, px=px: E.transpose(px[:, h % 4, :], nFp[:, h, :], identf[:, :]), ["nFp", "identf"], ["ps_x%d" % (h // 4)])
                for q in range(2):
                    P.op("dve", lambda E, q=q: E.tensor_copy(out=nFT[:, q * 4:(q + 1) * 4, :], in_=ps_x[q][:NB, :, :]), ["ps_x%d" % q], ["nFT"])
                P.op("dve", lambda E: E.tensor_copy(out=parts[0][:, :, :], in_=nFT[:, :, :]), ["nFT"], ["part0"])
                P.op("dve", lambda E: E.tensor_tensor(out=r1[:, :, :], in0=nFT[:, :, :], in1=parts[0][:, :, :], op=ALU.subtract), ["nFT", "part0"], ["r1"])
                P.op("dve", lambda E: E.tensor_copy(out=parts[1][:, :, :], in_=r1[:, :, :]), ["r1"], ["part1"])
                P.op("dve", lambda E: E.tensor_tensor(out=nFT[:, :, :], in0=r1[:, :, :], in1=parts[1][:, :, :], op=ALU.subtract), ["r1", "part1"], ["nFT"])
                P.op("dve", lambda E: E.tensor_copy(out=parts[2][:, :, :], in_=nFT[:, :, :]), ["nFT"], ["part2"])
                for i in range(3):
                    DMA(P, "sp", T["KA"][:, i, :].rearrange("h (b t) -> b h t", t=128), parts[i][:, :, :], ["part%d" % i], ["KAs"])
                    DMA(P, "sp", T["QA"][:, 3 + i, :].rearrange("h (b t) -> b h t", t=128), parts[i][:, :, :], ["part%d" % i], ["QAs"])
                for h in range(8):
                    DMA(P, "sp", T["KA"][h, 3:6, :].rearrange("r (b t) -> b r t", t=128), cm1[:NB, :, :], ["cm1"], ["KAs"])
                    DMA(P, "sp", T["QA"][h, 0:3, :].rearrange("r (b t) -> b r t", t=128), cp1[:NB, :, :], ["cp1"], ["QAs"])
                P.emit()
                stats.append(P.stats)
        if stop_after == "b":
            return nc, stats

        with ExitStack() as es:
            P = Prog(nc, gs, "c")
            KT = [sbt(es, "KT%d" % i, [70, S], BF16) for i in range(2)]
            QT = [sbt(es, "QT%d" % i, [70, NQ], BF16) for i in range(2)]
            VV = [sbt(es, "VV%d" % i, [128, NB, 65], BF16) for i in range(2)]
            qa = sbt(es, "qa", [70, S], BF16)
            qtmp = sbt(es, "qtmp", [70, 512], BF16)
            esel = sbt(es, "esel", [128, 2 * NSL], F32)
            eselh = sbt(es, "eselh", [128, 2 * NSL], F32)
            idE = sbt(es, "idE", [128, NSL, 128], BF16)
            idNE = sbt(es, "idNE", [128, NSL, 128], BF16)
            erow = sbt(es, "erow", [1, NSL, 128], BF16)
            negrow = sbt(es, "negrow", [1, 512], BF16)
            allneg = sbt(es, "allneg", [128, 512], BF16)
            ident = sbt(es, "ident2", [128, 128], BF16)
            mt = [sbt(es, "mt%d" % i, [128, 4, 512], BF16) for i in range(2)]
            mh = [sbt(es, "mh%d" % i, [128, NB, 16], BF16) for i in range(2)]
            negtri = sbt(es, "negtri", [128, 128], BF16)
            negones = sbt(es, "negones", [128, 128], BF16)
            PT = [sbt(es, "PT%d" % i, [128, 512], BF16) for i in range(3)]
            UU = [sbt(es, "UU%d" % i, [128, 512], F32) for i in range(2)]
            LL = [sbt(es, "LL%d" % i, [128, 512], BF16) for i in range(3)]
            LS = [sbt(es, "LS%d" % i, [128, 512], BF16) for i in range(3)]
            rden = sbt(es, "rden", [128, 2, 4], F32)
            ob = [sbt(es, "ob%d" % i, [128, 4, 64], BF16) for i in range(3)]
            cst = [sbt(es, "cst%d" % i, [128, 2816], F32) for i in range(2)]
            cbf = [sbt(es, "cbf%d" % i, [128, 2816], BF16) for i in range(2)]
            jobs = []
            for k in range(8):
                for hf in range(2):
                    jobs.append((T["w_up"][k * 128:(k + 1) * 128, hf * 2816:(hf + 1) * 2816], T["Wup"][:, k, hf * 2816:(hf + 1) * 2816], 2816, None))
            for c in range(0, 22, 2):
                jobs.append((T["w_down"][c * 128:(c + 2) * 128, :].rearrange("(c p) n -> p c n", p=128), T["Wdn"][:, c:c + 2, :], 2048, 2))
            for k in range(0, 8, 2):
                jobs.append((T["w_out"][k * 128:(k + 2) * 128, :].rearrange("(c p) n -> p c n", p=128), T["Wout"][:, k:k + 2, :], 2048, 2))
            jobn = [0]

            def conv_job():
                n = jobn[0]
                if n >= len(jobs):
                    return
                jobn[0] += 1
                src, dst, width, sub = jobs[n]
                b = n % 2
                if sub is None:
                    s_ap, b_ap = cst[b][:, :width], cbf[b][:, :width]
                else:
                    s_ap = cst[b][:, :width].rearrange("p (c n) -> p c n", c=sub)
                    b_ap = cbf[b][:, :width].rearrange("p (c n) -> p c n", c=sub)
                DMA(P, "sp", s_ap, src, [], ["cst%d" % b])
                P.op("dve", lambda E: E.tensor_copy(out=cbf[b][:, :width], in_=cst[b][:, :width]), ["cst%d" % b], ["cbf%d" % b])
                DMA(P, "sp", dst, b_ap, ["cbf%d" % b], ["Wscr"])
            psZ = [pst(es, "psZ%d" % i, [128, 512], F32) for i in range(4)]
            psO = [pst(es, "psO%d" % i, [128, 512], F32) for i in range(2)]
            for nm, t_, src in (("esel", esel, T["esel"][:, :]), ("eselh", eselh, T["eselh"][:, :]), ("idE", idE, T["idE"][:, :, :]),
                                ("idNE", idNE, T["idNE"][:, :, :]), ("erow", erow, T["erow"][:, :, :]), ("negrow", negrow, T["negrow"][:, :]),
                                ("ident", ident, T["ident_bf"][:, :]), ("mt0", mt[0], T["mt_f"][:, :, :]), ("mt1", mt[1], T["mt_s"][:, :, :]),
                                ("mh0", mh[0], T["mh_f"][:, :, :]), ("mh1", mh[1], T["mh_s"][:, :, :]),
                                ("negtri", negtri, T["negtri"][:, :]), ("negones", negones, T["negones"][:, :])):
                DMA(P, "sp", t_[:], src, [], [nm])
            P.op("pool", lambda E: E.memset(allneg[:], NEG), [], ["allneg"])
            for i in range(3):
                P.op("pool", lambda E, i=i: E.memset(PT[i][:], 0.0), [], ["PT%d" % i])
            CONSTS = ["idE", "idNE", "erow", "negrow", "ident", "mt0", "mt1", "mh0", "mh1", "allneg"]

            def load_head(hh):
                hb = hh % 2
                fox = hh < 8
                DMA(P, "sp", KT[hb][0:64, :], T["Kd"][hh, :, :], [], ["KTm%d" % hb])
                DMA(P, "sp", QT[hb][0:64, :], T["Qd"][hh, :, :], [], ["QTm%d" % hb])
                DMA(P, "sp", VV[hb][:, :, :], T["Vd"][hh, :, :, :], [], ["VV%d" % hb])
                if not fox:
                    P.op("pool", lambda E: E.memset(KT[hb][64:70, :], 0.0), [], ["KTa%d" % hb])
                    P.op("pool", lambda E: E.memset(QT[hb][64:70, :], 0.0), [], ["QTa%d" % hb])
                if fox:
                    DMA(P, "sp", KT[hb][64:70, :], T["KA"][hh, :, :], [], ["KTa%d" % hb])
                    DMA(P, "sp", qa[64:70, :], T["QA"][hh, :, :], [], ["qa"])
                    for j in range(NSL + 1):
                        if j < NSL:
                            c0 = qa[64:70, 1024 * j:1024 * j + 512]
                            c1 = qa[64:70, 1024 * j + 512:1024 * j + 1024]
                            e0 = esel[64:70, 2 * j:2 * j + 1]
                            e1 = esel[64:70, 2 * j + 1:2 * j + 2]
                            dst = QT[hb][64:70, 512 * j:512 * j + 512]
                            tmp = qtmp[64:70, 0:512]
                            P.op("dve", lambda E, c0=c0, e0=e0, tmp=tmp: E.tensor_scalar(out=tmp, in0=c0, scalar1=e0, scalar2=None, op0=ALU.mult),
                                 ["qa", "esel"], ["qtmp"])
                            P.op("dve", lambda E, c1=c1, e1=e1, tmp=tmp, dst=dst: E.scalar_tensor_tensor(out=dst, in0=c1, scalar=e1, in1=tmp, op0=ALU.mult, op1=ALU.add),
                                 ["qa", "esel", "qtmp"], ["QTa%d" % hb])
                        else:
                            for jj in range(NSL):
                                a0 = max(0, 1024 * jj - 2)
                                c0 = qa[64:70, a0:a0 + 2]
                                c1 = qa[64:70, 1024 * jj + 510:1024 * jj + 512]
                                e0 = eselh[64:70, 2 * jj:2 * jj + 1]
                                e1 = eselh[64:70, 2 * jj + 1:2 * jj + 2]
                                dst = QT[hb][64:70, NOWN + 2 * jj:NOWN + 2 * jj + 2]
                                tmp = qtmp[64:70, 0:2]
                                P.op("dve", lambda E, c0=c0, e0=e0, tmp=tmp: E.tensor_scalar(out=tmp, in0=c0, scalar1=e0, scalar2=None, op0=ALU.mult),
                                     ["qa", "eselh"], ["qtmp"])
                                P.op("dve", lambda E, c1=c1, e1=e1, tmp=tmp, dst=dst: E.scalar_tensor_tensor(out=dst, in0=c1, scalar=e1, in1=tmp, op0=ALU.mult, op1=ALU.add),
                                     ["qa", "eselh", "qtmp"], ["QTa%d" % hb])

            units = []
            slot_ctr = 0
            for hh in range(16):
                fox = hh < 8
                if "noSB" in OPT and not fox:
                    continue
                if "noFOX" in OPT and fox:
                    continue
                for sl in range(NSL + 1):
                    halo = sl == NSL
                    if halo and "noHalo" in OPT:
                        continue
                    W = 16 if halo else 512
                    nkb = NB if halo else 8 * (sl + 1)
                    order = list(range(nkb)) if fox else list(range(nkb - 1, -1, -1))
                    for idx, kb in enumerate(order):
                        if halo:
                            mk = ("H", kb)
                        elif 8 * sl <= kb < 8 * sl + 4:
                            mk = ("E", kb - 8 * sl)
                        elif 8 * sl + 4 <= kb < 8 * sl + 8:
                            mk = ("NE", kb - 8 * sl - 4)
                        else:
                            mk = None
                        units.append(dict(hh=hh, fox=fox, sl=sl, W=W, kb=kb, first=idx == 0, last=idx == nkb - 1, mk=mk,
                                          so=slot_ctr, qc=NOWN if halo else 512 * sl))
                    slot_ctr += 1
            for i, u in enumerate(units):
                u["i"] = i
                u["head_first"] = (i == 0 or units[i - 1]["hh"] != u["hh"])
                u["head_last"] = (i == len(units) - 1 or units[i + 1]["hh"] != u["hh"])

            def stage_A(u):
                hb = u["hh"] % 2
                KD = 70
                W, kb, i = u["W"], u["kb"], u["i"]
                z = psZ[i % 4]
                zk = "psZ%d" % (i % 4)
                mi = 0 if u["fox"] else 1
                rd = ["KTm%d" % hb, "QTm%d" % hb, "KTa%d" % hb, "QTa%d" % hb]
                mk = u["mk"] if "noMask" not in OPT else None
                MM(P, z[:, :W], KT[hb][0:KD, kb * 128:(kb + 1) * 128], QT[hb][0:KD, u["qc"]:u["qc"] + W], True, mk is None and u["fox"], rd, [zk])
                if mk is not None:
                    typ, m = mk
                    sl = u["sl"]
                    if typ == "H":
                        MM(P, z[:, :W], ident[:, :], mh[mi][:, m, :], False, u["fox"], CONSTS, [zk])
                    elif typ == "E":
                        MM(P, z[:, :W], idE[:, sl, :], mt[mi][:, m, :], False, u["fox"], CONSTS, [zk])
                    else:
                        MM(P, z[:, :W], idNE[:, sl, :], mt[mi][:, m, :], False, False, CONSTS, [zk])
                        MM(P, z[:, :W], idE[:, sl, :], allneg[:, :W], False, u["fox"], CONSTS, [zk])

            def stage_B_fox(u):
                W, i = u["W"], u["i"]
                ACTF(P, PT[i % 3][:, :W], psZ[i % 4][:, :W], AF.Exp, ["psZ%d" % (i % 4)], ["PT%d" % (i % 3)])

            def stage_B1(u):
                W, i = u["W"], u["i"]
                ACTF(P, UU[i % 2][:, :W], psZ[i % 4][:, :W], AF.Exp, ["psZ%d" % (i % 4)], ["UU%d" % (i % 2)])
                ACTF(P, LL[i % 3][:, :W], UU[i % 2][:, :W], AF.Ln, ["UU%d" % (i % 2)], ["LL%d" % (i % 3)], bias=1.0)
                if not u["last"]:
                    if u["first"]:
                        P.op("pool", lambda E: E.tensor_copy(out=LS[(i + 1) % 3][:, :W], in_=LL[i % 3][:, :W]), ["LL%d" % (i % 3)], ["LS%d" % ((i + 1) % 3)])
                    else:
                        P.op("pool", lambda E: E.tensor_tensor(out=LS[(i + 1) % 3][:, :W], in0=LS[i % 3][:, :W], in1=LL[i % 3][:, :W], op=ALU.add),
                             ["LL%d" % (i % 3), "LS%d" % (i % 3)], ["LS%d" % ((i + 1) % 3)])

            def stage_C(u):
                W, i = u["W"], u["i"]
                z = psZ[i % 4]
                zk = "psZ%d" % (i % 4)
                MM(P, z[:, :W], negtri[:, :], LL[i % 3][:, :W], False, u["first"], ["negtri", "LL%d" % (i % 3)], [zk])
                if not u["first"]:
                    MM(P, z[:, :W], negones[:, :], LS[i % 3][:, :W], False, True, ["negones", "LS%d" % (i % 3)], [zk])

            def stage_B2(u):
                W, i = u["W"], u["i"]
                ACTF(P, PT[i % 3][:, :W], psZ[i % 4][:, :W], AF.Exp, ["psZ%d" % (i % 4)], ["PT%d" % (i % 3)])

            fin_ctr = [0]

            def stage_D(u):
                hb = u["hh"] % 2
                W, kb, i = u["W"], u["kb"], u["i"]
                o = psO[u["so"] % 2]
                ok = "psO%d" % (u["so"] % 2)
                NV = 65 if u["fox"] else 64
                nsub = max(1, W // 128)
                M = min(128, W)
                for s_ in range(nsub):
                    MM(P, o[:, s_ * NV:(s_ + 1) * NV], PT[i % 3][:, s_ * 128:s_ * 128 + 128], VV[hb][:, kb, 0:NV],
                       u["first"] and s_ == 0, u["last"] and s_ == nsub - 1, ["PT%d" % (i % 3), "VV%d" % hb], [ok], skip=True)
                if u["last"]:
                    f = fin_ctr[0]
                    fin_ctr[0] += 1
                    obuf = ob[f % 3]
                    obk = "ob%d" % (f % 3)
                    ov = o[:M, 0:nsub * NV].rearrange("p (s v) -> p s v", s=nsub)
                    if u["fox"]:
                        rk = "rden%d" % (f % 2)
                        P.op("dve", lambda E: E.reciprocal(out=rden[:M, f % 2, 0:nsub], in_=ov[:, :, 64]), [ok], [rk])
                        for s_ in range(nsub):
                            P.op("dve", lambda E, s_=s_: E.tensor_scalar(out=obuf[:M, s_, :], in0=ov[:, s_, 0:64], scalar1=rden[:M, f % 2, s_:s_ + 1],
                                                                         scalar2=None, op0=ALU.mult), [ok, rk], [obk])
                    else:
                        P.op("dve", lambda E: E.tensor_copy(out=obuf[:M, 0:nsub, :], in_=ov[:, :, 0:64]), [ok], [obk])
                    r0 = u["qc"]
                    hh = u["hh"]
                    if nsub == 4:
                        dst = T["Od"][r0:r0 + 512, hh * 64:(hh + 1) * 64].rearrange("(s p) d -> p s d", p=128)
                        DMA(P, "sp", dst, obuf[:, :, :], [obk], ["Od"])
                    else:
                        DMA(P, "sp", T["Od"][r0:r0 + M, hh * 64:(hh + 1) * 64], obuf[:M, 0, :], [obk], ["Od"])

            n = len(units)
            hh0 = units[0]["hh"]
            load_head(hh0)
            stage_A(units[0])
            job_every = max(1, n // (len(jobs) + 2))
            for i in range(n):
                u = units[i]
                if u["head_first"] and u["hh"] + 1 < 16 and u["hh"] == hh0:
                    load_head(hh0 + 1)
                if i % job_every == job_every - 1:
                    conv_job()
                if i + 1 < n:
                    stage_A(units[i + 1])
                if u["fox"]:
                    stage_B_fox(u)
                else:
                    stage_B1(u)
                    stage_C(u)
                if i >= 1:
                    pu = units[i - 1]
                    if not pu["fox"]:
                        stage_B2(pu)
                    stage_D(pu)
                    if pu["head_last"] and pu["hh"] + 2 < 16:
                        load_head(pu["hh"] + 2)
            pu = units[n - 1]
            if not pu["fox"]:
                stage_B2(pu)
            stage_D(pu)
            while jobn[0] < len(jobs):
                conv_job()
            P.emit()
            stats.append(P.stats)
        if stop_after == "c":
            return nc, stats

        with ExitStack() as es:
            P = Prog(nc, gs, "d")
            wout = sbt(es, "wout", [128, 8, D], BF16)
            fox_g = sbt(es, "fox_g", [128, 512], F32)
            sb_g = sbt(es, "sb_g", [128, 512], F32)
            ffn_g = sbt(es, "ffn_g", [128, D], F32)
            fin_g = sbt(es, "fin_g", [128, D], F32)
            cw = sbt(es, "cw", [128, NCH, 3], F32)
            cb = sbt(es, "cb", [128, NCH], F32)
            hmask = sbt(es, "hmask", [128, 16], F32)
            ident = sbt(es, "ident3", [128, 128], BF16)
            identf = sbt(es, "identf3", [128, 128], F32)
            o_s = [sbt(es, "o_s%d" % i, [128, 4, D], BF16) for i in range(2)]
            xr = [sbt(es, "xr%d" % i, [128, 4, D], F32) for i in range(2)]
            junk = sbt(es, "junk3", [128, D], BF16)
            stat = sbt(es, "stat3", [128, 3, 16], F32)
            on = sbt(es, "on", [128, 4, D], BF16)
            onT = sbt(es, "onT", [128, 8, 512], BF16)
            h2T = sbt(es, "h2T", [128, 8, 512], BF16)
            wu = [sbt(es, "wu%d" % i, [128, 8, 2, 128], BF16) for i in range(3)]
            cv = [sbt(es, "cv%d" % i, [128, 512], F32) for i in range(4)]
            sg = [sbt(es, "sg%d" % i, [128, 512], F32) for i in range(2)]
            aT = sbt(es, "aT", [128, 22, 512], BF16)
            uh = sbt(es, "uh", [128, NCH, 16], F32)
            wd = [sbt(es, "wd%d" % i, [128, 22, 128], BF16) for i in range(3)]
            yT = [sbt(es, "yT%d" % i, [128, 512], F32) for i in range(2)]
            pT = [pst(es, "p3T%d" % i, [128, 8, 128], BF16) for i in range(2)]
            psa = [pst(es, "psa%d" % i, [128, 512], F32) for i in range(2)]
            psu = [pst(es, "psu%d" % i, [128, 512], F32) for i in range(2)]
            psy = pst(es, "psy", [128, 512], F32)
            pst_ = pst(es, "pst", [128, 4, 128], F32)
            rr = RR(["act", "dve"])
            PSU = [psa[0], psa[1], psu[0], psu[1]]
            PSUK = ["psa0", "psa1", "psu0", "psu1"]
            for nm, t_, src in (("wout", wout, T["Wout"][:, :, :]), ("fox_g", fox_g, T["fox_g"][:, :]), ("sb_g", sb_g, T["sb_g"][:, :]),
                                ("ffn_g", ffn_g, T["ffn_g"][:, :]), ("fin_g", fin_g, T["fin_g"][:, :]), ("cw", cw, T["cw"][:, :, :]),
                                ("cb", cb, T["cb"][:, :]), ("hmask", hmask, T["hmask"][:, :]), ("ident", ident, T["ident_bf"][:, :]),
                                ("identf", identf, T["ident_f"][:, :])):
                DMA(P, "sp", t_[:], src, [], [nm])

            P.op("pool", lambda E: E.memset(on[:], 0.0), [], ["on"])
            P.op("pool", lambda E: E.memset(onT[:], 0.0), [], ["onT.%d" % b for b in range(4)])
            P.op("pool", lambda E: E.memset(h2T[:], 0.0), [], ["h2T.%d" % b for b in range(4)])
            sctr = [0]
            pctr = [0]
            wuc = [0]
            wdc = [0]

            def rstd_of(src_ap, rows, width):
                c = sctr[0] % 16
                sctr[0] += 1
                sk = "st%d" % c
                ACTF(P, junk[:rows, :width], src_ap, AF.Square, src_ap_keys[0], ["junk", sk], accum=stat[:rows, 0, c:c + 1])
                ACTF(P, stat[:rows, 1, c:c + 1], stat[:rows, 0, c:c + 1], AF.Ln, [sk], [sk], bias=EPS, scale=1.0 / width)
                ACTF(P, stat[:rows, 2, c:c + 1], stat[:rows, 1, c:c + 1], AF.Exp, [sk], [sk], scale=-0.5)
                return stat[:rows, 2, c:c + 1], sk
            src_ap_keys = [None]

            def transpose_to(src_tile, src_key, blocks, dstT, dst_key):
                for (bi, rows) in blocks:
                    n = pctr[0]
                    pctr[0] += 1
                    pt = pT[n % 2]
                    pk = "p3T%d" % (n % 2)
                    for k in range(8):
                        TR(P, pt[:, k, :], src_tile[:, bi, k * 128:(k + 1) * 128], ident[:, :], [src_key, "ident"], [pk])
                    rr.copy(P, dstT[:, :, bi * 128:bi * 128 + rows], pt[:, :, :rows], [pk], [dst_key + ".%d" % bi])

            slots = [NSL] + list(range(NSL))

            def load_slot(si):
                sl = slots[si]
                halo = sl == NSL
                r0 = NOWN if halo else 512 * sl
                sb_ = si % 2
                os_, xr_ = o_s[sb_], xr[sb_]
                osk, xrk = "o_s%d" % sb_, "xr%d" % sb_
                if halo:
                    DMA(P, "sp", os_[:16, 0, :], T["Od"][r0:r0 + 16, :], [], [osk])
                    DMA(P, "sp", xr_[:16, 0, :], T["xo"][r0:r0 + 16, :], [], [xrk])
                else:
                    DMA(P, "sp", os_[:, :, :], T["Od"][r0:r0 + 512, :].rearrange("(b p) d -> p b d", p=128), [], [osk])
                    DMA(P, "sp", xr_[:, :, :], T["xo"][r0:r0 + 512, :].rearrange("(b p) d -> p b d", p=128), [], [xrk])

            def do_slot(si, sl):
                halo = sl == NSL
                W = 16 if halo else 512
                r0 = NOWN if halo else 512 * sl
                blocks = [(0, 16)] if halo else [(b, 128) for b in range(4)]
                sb_ = si % 2
                os_, xr_ = o_s[sb_], xr[sb_]
                osk, xrk = "o_s%d" % sb_, "xr%d" % sb_
                if si == 0:
                    load_slot(si)
                if si + 1 < len(slots):
                    load_slot(si + 1)
                for (bi, rows) in blocks:
                    for g in range(2):
                        src = os_[:rows, bi, g * 512:(g + 1) * 512]
                        src_ap_keys[0] = [osk]
                        rs, sk = rstd_of(src, rows, 512)
                        gt = fox_g if g == 0 else sb_g
                        P.op("dve", lambda E, src=src, rs=rs, gt=gt, rows=rows, bi=bi, g=g: E.scalar_tensor_tensor(
                            out=on[:rows, bi, g * 512:(g + 1) * 512], in0=src, scalar=rs, in1=gt[:rows, :], op0=ALU.mult, op1=ALU.mult),
                            [osk, sk, "fox_g", "sb_g"], ["on"])
                transpose_to(on, "on", blocks, onT, "onT")
                onk = ["onT.%d" % bi for (bi, _) in blocks]
                for (bi, rows) in blocks:
                    for ch in range(2):
                        n = pctr[0]
                        pctr[0] += 1
                        pa = psa[n % 2]
                        pk = "psa%d" % (n % 2)
                        for k in range(8):
                            MM(P, pa[:, :], onT[:, k, bi * 128:bi * 128 + 128], wout[:, k, ch * 512:(ch + 1) * 512], k == 0, k == 7,
                               ["wout", "onT.%d" % bi], [pk])
                        P.op("dve", lambda E, pa=pa, rows=rows, bi=bi, ch=ch: E.tensor_tensor(
                            out=xr_[:rows, bi, ch * 512:(ch + 1) * 512], in0=pa[:rows, :], in1=xr_[:rows, bi, ch * 512:(ch + 1) * 512], op=ALU.add),
                            [pk, xrk], [xrk])
                for (bi, rows) in blocks:
                    src = xr_[:rows, bi, :]
                    src_ap_keys[0] = [xrk]
                    rs, sk = rstd_of(src, rows, D)
                    P.op("dve", lambda E, src=src, rs=rs, rows=rows, bi=bi: E.scalar_tensor_tensor(
                        out=on[:rows, bi, :], in0=src, scalar=rs, in1=ffn_g[:rows, :], op0=ALU.mult, op1=ALU.mult),
                        [xrk, sk, "ffn_g"] + onk, ["on"])
                transpose_to(on, "on", blocks, h2T, "h2T")
                h2k = ["h2T.%d" % bi for (bi, _) in blocks]
                for c in range(22):
                    wn = wuc[0]
                    wuc[0] += 1
                    wt = wu[wn % 3]
                    wk = "wu%d" % (wn % 3)
                    DMA(P, "sp", wt[:, :, 0, :], T["Wup"][:, :, c * 128:(c + 1) * 128], [], [wk + "g"])
                    DMA(P, "sp", wt[:, :, 1, :], T["Wup"][:, :, DFF + c * 128:DFF + (c + 1) * 128], [], [wk + "v"])
                    pend = []
                    for gv in range(2):
                        un = (2 * wn + gv) % 4
                        pu_ = PSU[un]
                        pk = PSUK[un]
                        for k in range(8):
                            MM(P, pu_[:, :W], wt[:, k, gv, :], h2T[:, k, :W], k == 0, k == 7, [wk + ("g" if gv == 0 else "v")] + h2k, [pk])
                        cc = c + 22 * gv
                        if halo:
                            P.op("dve", lambda E, pu_=pu_, cc=cc: E.tensor_tensor(out=uh[:, cc, :], in0=pu_[:, :16], in1=hmask[:, :], op=ALU.mult),
                                 [pk, "hmask"], ["uh"])
                            continue
                        ct, ck = cv[un], "cv%d" % un
                        ACTF(P, ct[:, :], pu_[:, :512], AF.Identity, [pk, "cw", "cb"], [ck], bias=cb[:, cc:cc + 1], scale=cw[:, cc, 2:3])
                        pend.append((pu_, pk, ct, ck, cc))
                    if halo:
                        continue
                    for (pu_, pk, ct, ck, cc) in pend:
                        P.op("dve", lambda E, pu_=pu_, ct=ct, cc=cc: E.scalar_tensor_tensor(out=ct[:, 1:512], in0=pu_[:, 0:511], scalar=cw[:, cc, 1:2], in1=ct[:, 1:512],
                                                                                         op0=ALU.mult, op1=ALU.add), [pk, "cw", ck], [ck])
                    for (pu_, pk, ct, ck, cc) in pend:
                        P.op("dve", lambda E, pu_=pu_, ct=ct, cc=cc: E.scalar_tensor_tensor(out=ct[:, 2:512], in0=pu_[:, 0:510], scalar=cw[:, cc, 0:1], in1=ct[:, 2:512],
                                                                                         op0=ALU.mult, op1=ALU.add), [pk, "cw", ck], [ck])
                    for (pu_, pk, ct, ck, cc) in pend:
                        P.op("dve", lambda E, ct=ct, cc=cc: E.scalar_tensor_tensor(out=ct[:, 0:1], in0=uh[:, cc, 2 * sl + 1:2 * sl + 2], scalar=cw[:, cc, 1:2], in1=ct[:, 0:1],
                                                                                op0=ALU.mult, op1=ALU.add), ["uh", "cw", ck], [ck])
                    for (pu_, pk, ct, ck, cc) in pend:
                        P.op("dve", lambda E, ct=ct, cc=cc: E.scalar_tensor_tensor(out=ct[:, 0:2], in0=uh[:, cc, 2 * sl:2 * sl + 2], scalar=cw[:, cc, 0:1], in1=ct[:, 0:2],
                                                                                op0=ALU.mult, op1=ALU.add), ["uh", "cw", ck], [ck])
                    (_, _, ctg, ckg, _), (_, _, ctv, ckv, _) = pend
                    st_, stk = sg[wn % 2], "sg%d" % (wn % 2)
                    ACTF(P, st_[:, :], ctg[:, :], AF.Silu, [ckg], [stk])
                    P.op("pool", lambda E, st_=st_, ctv=ctv, c=c: E.tensor_tensor(out=aT[:, c, :], in0=st_[:, :], in1=ctv[:, :], op=ALU.mult),
                         [stk, ckv], ["aT.%d" % c])
                if halo:
                    return
                for cc in range(8):
                    wn = wdc[0]
                    wdc[0] += 1
                    wt = wd[wn % 3]
                    wk = "wd%d" % (wn % 3)
                    DMA(P, "sp", wt[:, :, :], T["Wdn"][:, :, cc * 128:(cc + 1) * 128], [], [wk])
                    for c in range(22):
                        MM(P, psy[:, :], wt[:, c, :], aT[:, c, :], c == 0, c == 21, [wk, "aT.%d" % c], ["psy"])
                    yt, yk = yT[wn % 2], "yT%d" % (wn % 2)
                    ACTF(P, yt[:, :], psy[:, :], AF.Copy, ["psy"], [yk])
                    for b in range(4):
                        P.op("pe", lambda E, yt=yt, b=b: E.transpose(pst_[:, b, :], yt[:, b * 128:(b + 1) * 128], identf[:, :]), [yk, "identf"], ["pst"])
                    P.op("dve", lambda E, cc=cc: E.tensor_tensor(out=xr_[:, :, cc * 128:(cc + 1) * 128], in0=pst_[:, :, :],
                                                               in1=xr_[:, :, cc * 128:(cc + 1) * 128], op=ALU.add), ["pst", xrk], [xrk])
                for (bi, rows) in blocks:
                    src = xr_[:rows, bi, :]
                    src_ap_keys[0] = [xrk]
                    rs, sk = rstd_of(src, rows, D)
                    P.op("dve", lambda E, src=src, rs=rs: E.scalar_tensor_tensor(out=src, in0=src, scalar=rs, in1=fin_g[:, :], op0=ALU.mult, op1=ALU.mult),
                         [xrk, sk, "fin_g"], [xrk])
                DMA(P, "pool", T["out"][r0:r0 + 512, :].rearrange("(b p) d -> p b d", p=128), xr_[:, :, :], [xrk], ["out"])

            for si, sl in enumerate(slots):
                do_slot(si, sl)
            P.emit()
            stats.append(P.stats)
    return nc, stats


_CACHE = {}


def _consts(S):
    NB = S // 128
    bf = ml_dtypes.bfloat16
    p = np.arange(128)
    c = {}
    c["ident_bf"] = np.eye(128, dtype=np.float32).astype(bf)
    c["ident_f"] = np.eye(128, dtype=np.float32)
    c["triu_f"] = (p[:, None] <= p[None, :]).astype(np.float32)
    c["ones_f"] = np.ones((128, 128), np.float32)
    c["negtri"] = (-(p[:, None] >= p[None, :]).astype(np.float32)).astype(bf)
    c["negones"] = (-np.ones((128, 128), np.float32)).astype(bf)
    t = np.arange(512)
    key = (np.arange(4)[None, :, None] * 128 + p[:, None, None])
    c["mt_f"] = np.where(key <= t[None, None, :], 0.0, NEG).astype(np.float32).astype(bf)
    c["mt_s"] = np.where(key < t[None, None, :], 0.0, NEG).astype(np.float32).astype(bf)
    c["negrow"] = np.full((1, 512), NEG, np.float32).astype(bf)
    c["cm1"] = np.full((64, 3, 128), -1.0, np.float32).astype(bf)
    c["cp1"] = np.full((64, 3, 128), 1.0, np.float32).astype(bf)
    return c


def _core_consts(S, par):
    NT = S // 512
    NSL = NT // 2
    NB = S // 128
    bf = ml_dtypes.bfloat16
    own = own_tiles(par, NSL)
    esel = np.zeros((128, 2 * NSL), np.float32)
    eselh = np.zeros((128, 2 * NSL), np.float32)
    hmask = np.ones((128, 16), np.float32)
    idE = np.zeros((128, NSL, 128), np.float32)
    idNE = np.zeros((128, NSL, 128), np.float32)
    erow = np.zeros((1, NSL, 128), np.float32)
    I = np.eye(128, dtype=np.float32)
    p = np.arange(128)
    keypos = (np.arange(NB)[None, :] * 128 + p[:, None])
    mh_f = np.zeros((128, NB, 16), np.float32)
    mh_s = np.zeros((128, NB, 16), np.float32)
    for j in range(NSL):
        e0 = 1.0 if own[j] == 2 * j else 0.0
        esel[:, 2 * j] = e0
        esel[:, 2 * j + 1] = 1.0 - e0
        eselh[:, 2 * j] = 1.0 if (own[j] == 2 * j and j > 0) else 0.0
        eselh[:, 2 * j + 1] = 1.0 if own[j] == 2 * j + 1 else 0.0
        idE[:, j, :] = e0 * I
        idNE[:, j, :] = (1.0 - e0) * I
        erow[0, j, :] = e0
        for r in range(2):
            col = 2 * j + r
            pos = 512 * own[j] - 2 + r
            if own[j] == 0:
                hmask[:, col] = 0.0
                mh_f[:, :, col] = np.where(keypos == 0, 0.0, NEG)
                mh_s[:, :, col] = np.where(keypos == 0, 0.0, NEG)
            else:
                mh_f[:, :, col] = np.where(keypos <= pos, 0.0, NEG)
                mh_s[:, :, col] = np.where(keypos < pos, 0.0, NEG)
    return dict(esel=esel, eselh=eselh, hmask=hmask, idE=idE.astype(bf), idNE=idNE.astype(bf), erow=erow.astype(bf),
                mh_f=mh_f.astype(bf), mh_s=mh_s.astype(bf)), own


def _prepare(inputs, debug=False):
    x = np.asarray(inputs["x"], np.float32)
    B, S, _ = x.shape
    NT = S // 512
    NSL = NT // 2
    rep = lambda v: np.ascontiguousarray(np.broadcast_to(np.asarray(v, np.float32).reshape(1, -1), (128, np.asarray(v).size)))
    shared = dict(_consts(S))
    shared["w_in"] = np.ascontiguousarray(np.asarray(inputs["w_in"], np.float32)[0])
    shared["w_out"] = np.ascontiguousarray(np.asarray(inputs["w_out"], np.float32)[0])
    shared["w_up"] = np.ascontiguousarray(np.asarray(inputs["w_up"], np.float32)[0])
    shared["w_down"] = np.ascontiguousarray(np.asarray(inputs["w_down"], np.float32)[0])
    shared["attn_g"] = rep(inputs["attn_norm_g"][0])
    shared["ffn_g"] = rep(inputs["ffn_norm_g"][0])
    shared["fin_g"] = rep(inputs["final_norm_g"])
    shared["fox_g"] = rep(inputs["fox_out_g"][0])
    shared["sb_g"] = rep(inputs["sb_out_g"][0])
    shared["fb"] = rep(inputs["forget_bias"][0])
    cwv = np.asarray(inputs["conv_w"], np.float32)[0]
    shared["cw"] = np.ascontiguousarray(cwv.reshape(3, NCH, 128).transpose(2, 1, 0))
    shared["cb"] = np.ascontiguousarray(np.asarray(inputs["conv_b"], np.float32)[0].reshape(NCH, 128).T)
    in_maps = []
    owns = []
    for c in range(8):
        b, par = c // 2, c % 2
        cc, own = _core_consts(S, par)
        owns.append(own)
        xo = np.zeros((NSL * 512 + 16, D), np.float32)
        for j, t in enumerate(own):
            xo[j * 512:(j + 1) * 512] = x[b, t * 512:(t + 1) * 512]
            if t > 0:
                xo[NSL * 512 + 2 * j:NSL * 512 + 2 * j + 2] = x[b, t * 512 - 2:t * 512]
        m = dict(shared)
        m.update(cc)
        m["xn"] = np.ascontiguousarray(x[b])
        m["xo"] = xo
        in_maps.append(m)
    return in_maps, owns, (B, S)


def kernel(**inputs):
    in_maps, owns, (B, S) = _prepare(inputs)
    if S not in _CACHE:
        _CACHE[S] = build_program(S)
    nc, _ = _CACHE[S]
    res = run_bass_kernel_spmd(nc, in_maps, core_ids=list(range(8)))
    out = np.zeros((B, S, D), np.float32)
    for c in range(8):
        b = c // 2
        o = np.asarray(res.results[c]["out"], np.float32)
        for j, t in enumerate(owns[c]):
            out[b, t * 512:(t + 1) * 512] = o[j * 512:(j + 1) * 512]
    return out
```

```python
import numpy as np
import ml_dtypes
from contextlib import ExitStack
import concourse.bass as bass
import concourse.mybir as mybir
from concourse.bass_utils import run_bass_kernel_spmd

F32 = mybir.dt.float32
BF16 = mybir.dt.bfloat16
AF = mybir.ActivationFunctionType
ALU = mybir.AluOpType

D = 1024
DH = 64
DFF = 2816
INC = 3080
EPS = 1e-6
NEG = -30000.0
CQ_F, CK_F, CV_F, CL, CQ_S, CK_S, CV_S = 0, 512, 1024, 1536, 1544, 2056, 2568
NCH = 2 * DFF // 128

import os
OPT = set(os.environ.get("KOPT", "").split(","))
COMPUTE = ("pe", "act", "dve", "pool")
CH = 16000
NDMA = 8


class Prog:
    def __init__(self, nc, gs, tag):
        self.nc = nc
        self.gs = gs
        self.tag = tag
        self.ops = []
        self.last_w = {}
        self.readers = {}

    def op(self, eng, fn, reads=(), writes=(), dma=False):
        j = len(self.ops)
        deps = set()
        for r in reads:
            if r in self.last_w:
                deps.add(self.last_w[r])
        for w in writes:
            if w in self.last_w:
                deps.add(self.last_w[w])
            for rd in self.readers.get(w, ()):
                deps.add(rd)
        deps.discard(j)
        self.ops.append(dict(eng=eng, fn=fn, deps=deps, dma=dma, sig=False))
        for r in reads:
            self.readers.setdefault(r, []).append(j)
        for w in writes:
            self.last_w[w] = j
            self.readers[w] = []
        return j

    def dma(self, q, fn, reads=(), writes=()):
        return self.op(q, fn, reads, writes, dma=True)

    def emit(self):
        nc, ops = self.nc, self.ops
        for j, o in enumerate(ops):
            nd = set()
            for d in o["deps"]:
                p = ops[d]
                if not p["dma"] and not o["dma"] and p["eng"] == o["eng"] == "pe":
                    continue
                nd.add(d)
            o["deps"] = nd
            for d in nd:
                ops[d]["sig"] = True
        cnt = {e: 0 for e in COMPUTE}
        sems = {}

        def getsem(key):
            if key not in sems:
                sems[key] = self.gs.enter_context(nc.semaphore("s%s_%s_%s" % (self.tag, key[0], key[1])))
            return sems[key]

        dcount = {}
        dma_i = {}
        for j, o in enumerate(ops):
            if o["dma"]:
                q = o["eng"]
                k = dma_i.get(q, 0)
                dma_i[q] = k + 1
                key = ("d" + q, k % NDMA)
                dcount[key] = dcount.get(key, 0) + 16
                o["semkey"], o["semval"] = key, dcount[key]
                o["prev"] = (key, dcount[key] - 16)
            elif o["sig"]:
                e = o["eng"]
                c = cnt[e]
                cnt[e] = c + 1
                o["semkey"], o["semval"] = (e, c // CH), c % CH + 1
        last_dma = {}
        for o in ops:
            if o["dma"]:
                last_dma[o["semkey"]] = max(last_dma.get(o["semkey"], 0), o["semval"])
            if "semkey" in o:
                getsem(o["semkey"])
        per = {e: [] for e in ("sp", "act", "dve", "pe", "pool")}
        for j, o in enumerate(ops):
            per[o["eng"]].append(j)
        nwc = [0]

        def run_engine(e, E):
            known = {}
            for j in per[e]:
                o = ops[j]
                need = {}
                for d in o["deps"]:
                    p = ops[d]
                    k, v = p["semkey"], p["semval"]
                    if need.get(k, 0) < v:
                        need[k] = v
                if o["dma"] and o["prev"][1] > 0:
                    k, v = o["prev"]
                    if need.get(k, 0) < v:
                        need[k] = v
                for k, v in need.items():
                    if known.get(k, 0) >= v:
                        continue
                    known[k] = v
                    E.wait_ge(sems[k], v)
                    nwc[0] += 1
                inst = o["fn"](E)
                if o["dma"]:
                    inst.then_inc(sems[o["semkey"]], 16)
                elif o["sig"]:
                    inst.then_inc(sems[o["semkey"]], 1)
            if e == "sp":
                for k, v in last_dma.items():
                    if known.get(k, 0) < v:
                        E.wait_ge(sems[k], v)

        with nc.Block() as block:
            @block.sync
            def _(E):
                run_engine("sp", E)

            @block.scalar
            def _(E):
                run_engine("act", E)

            @block.vector
            def _(E):
                run_engine("dve", E)

            @block.tensor
            def _(E):
                run_engine("pe", E)

            @block.gpsimd
            def _(E):
                run_engine("pool", E)
        self.stats = dict(tag=self.tag, nops=len(ops), nwaits=nwc[0], nsems=len(sems), cnt=cnt)


def MM(P, out, lhsT, rhs, start, stop, rd, wr, skip=False):
    P.op("pe", lambda E: E.matmul(out, lhsT=lhsT, rhs=rhs, start=start, stop=stop, skip_group_check=skip), rd, wr)


def TR(P, out, in_, ident, rd, wr):
    P.op("pe", lambda E: E.transpose(out, in_, ident), rd, wr)


def ACTF(P, out, in_, func, rd, wr, bias=None, scale=None, accum=None):
    kw = {}
    if bias is not None:
        kw["bias"] = bias
    if scale is not None:
        kw["scale"] = scale
    if accum is not None:
        kw["accum_out"] = accum
    P.op("act", lambda E: E.activation(out=out, in_=in_, func=func, **kw), rd, wr)


def DMA(P, q, out, in_, rd, wr):
    P.dma(q, lambda E: E.dma_start(out=out, in_=in_), rd, wr)


class RR:
    def __init__(self, engs):
        self.engs = engs
        self.i = 0

    def copy(self, P, out, in_, rd, wr, scale=None):
        e = self.engs[self.i % len(self.engs)]
        self.i += 1
        if e == "act":
            ACTF(P, out, in_, AF.Copy, rd, wr, scale=scale)
        elif scale is None:
            P.op(e, lambda E: E.tensor_copy(out=out, in_=in_), rd, wr)
        else:
            P.op(e, lambda E: E.tensor_scalar(out=out, in0=in_, scalar1=float(scale), scalar2=None, op0=ALU.mult), rd, wr)


def own_tiles(p, NSL):
    res = []
    for j in range(NSL):
        first = (j % 2 == 0) if p == 0 else (j % 2 == 1)
        res.append(2 * j if first else 2 * j + 1)
    return res


def build_program(S, debug=False, stop_after="d"):
    NT = S // 512
    NSL = NT // 2
    NB = S // 128
    NOWN = NSL * 512
    NQ = NOWN + 16
    nc = bass.Bass("TRN2", target_bir_lowering=False)
    T = {}

    def din(name, shape, dt=F32):
        T[name] = nc.dram_tensor(name, list(shape), dt, kind="ExternalInput").ap()

    def dscr(name, shape, dt=BF16):
        kind = "ExternalOutput" if debug else "Internal"
        T[name] = nc.dram_tensor(name, list(shape), dt, kind=kind).ap()

    din("xn", [S, D]); din("xo", [NQ, D])
    din("w_in", [D, INC]); din("w_out", [D, D]); din("w_up", [D, 2 * DFF]); din("w_down", [DFF, D])
    din("attn_g", [128, D]); din("ffn_g", [128, D]); din("fin_g", [128, D])
    din("fox_g", [128, 512]); din("sb_g", [128, 512]); din("fb", [128, 8])
    din("cw", [128, NCH, 3]); din("cb", [128, NCH])
    din("esel", [128, 2 * NSL]); din("eselh", [128, 2 * NSL]); din("hmask", [128, 16])
    din("idE", [128, NSL, 128], BF16); din("idNE", [128, NSL, 128], BF16); din("erow", [1, NSL, 128], BF16)
    din("mh_f", [128, NB, 16], BF16); din("mh_s", [128, NB, 16], BF16)
    din("ident_bf", [128, 128], BF16); din("ident_f", [128, 128]); din("triu_f", [128, 128]); din("ones_f", [128, 128])
    din("negtri", [128, 128], BF16); din("negones", [128, 128], BF16)
    din("mt_f", [128, 4, 512], BF16); din("mt_s", [128, 4, 512], BF16); din("negrow", [1, 512], BF16)
    din("cm1", [64, 3, 128], BF16); din("cp1", [64, 3, 128], BF16)
    T["out"] = nc.dram_tensor("out", [NOWN, D], F32, kind="ExternalOutput").ap()
    dscr("Kd", [16, 64, S]); dscr("Vd", [16, 128, NB, 65]); dscr("Qd", [16, 64, NQ])
    dscr("KA", [8, 6, S]); dscr("QA", [8, 6, S]); dscr("Od", [NQ + 112, D])
    dscr("Wup", [128, 8, 2 * DFF]); dscr("Wdn", [128, 22, D]); dscr("Wout", [128, 8, D])
    if debug:
        dscr("dbg_nF", [128, 8, NB], F32)

    stats = []
    with ExitStack() as gs:
        def sbt(es, name, shape, dt):
            return es.enter_context(nc.sbuf_tensor("sb_" + name, list(shape), dt))

        def pst(es, name, shape, dt):
            return es.enter_context(nc.psum_tensor("pp_" + name, list(shape), dt))

        with ExitStack() as es:
            P = Prog(nc, gs, "a")
            win = sbt(es, "win", [128, 8, INC], BF16)
            wstg = [sbt(es, "wstg%d" % i, [128, INC], F32) for i in range(2)]
            g_r = sbt(es, "g_r", [128, D], F32)
            fb_r = sbt(es, "fb_r", [128, 8], F32)
            ident = sbt(es, "ident", [128, 128], BF16)
            xb = [sbt(es, "xb%d" % i, [128, D], F32) for i in range(3)]
            junk = sbt(es, "junk", [128, D], BF16)
            stat = sbt(es, "stat", [128, 3, 8], F32)
            xnb = [sbt(es, "xnb%d" % i, [128, D], BF16) for i in range(2)]
            hT = [sbt(es, "hT%d" % i, [128, 8, 512], BF16) for i in range(2)]
            kts = [sbt(es, "kts%d" % i, [128, 512], BF16) for i in range(3)]
            vs = [sbt(es, "vs%d" % i, [128, 16, 4, 65], BF16) for i in range(2)]
            flog = sbt(es, "flog", [128, 8, NB], F32)
            esp = ExitStack()
            pT = [pst(esp, "pT%d" % i, [128, 8, 128], BF16) for i in range(2)]
            psk = [pst(esp, "psk%d" % i, [128, 512], F32) for i in range(2)]
            psv = [pst(esp, "psv%d" % i, [128, 512], F32) for i in range(2)]
            psf = [pst(esp, "psf%d" % i, [128, 512], F32) for i in range(2)]
            rr = RR(["act", "dve"])

            DMA(P, "sp", g_r[:], T["attn_g"][:, :], [], ["g_r"])
            DMA(P, "sp", fb_r[:], T["fb"][:, :], [], ["fb_r"])
            DMA(P, "sp", ident[:], T["ident_bf"][:, :], [], ["ident"])
            for k in range(8):
                DMA(P, "sp", wstg[k % 2][:], T["w_in"][k * 128:(k + 1) * 128, :], [], ["wstg%d" % (k % 2)])
                P.op("pool", lambda E, k=k: E.tensor_copy(out=win[:, k, :], in_=wstg[k % 2][:]), ["wstg%d" % (k % 2)], ["win"])
            for i in range(2):
                P.op("pool", lambda E, i=i: E.memset(vs[i][:], 1.0), [], ["vs%d.%d.%d" % (i, b, g) for b in range(4) for g in range(2)])
                P.op("pool", lambda E, i=i: E.memset(xnb[i][:], 0.0), [], ["xnb%d" % i])
            blkn = [0]

            def norm_block(src_rows, rows, hbuf, col0):
                st = {}

                def part1():
                    n = blkn[0]
                    blkn[0] += 1
                    st["n"] = n
                    x = xb[n % 3]
                    xs = "xb%d" % (n % 3)
                    sk = "stat%d" % (n % 8)
                    DMA(P, "sp", x[:rows, :], src_rows, [], [xs])
                    ACTF(P, junk[:rows, :], x[:rows, :], AF.Square, [xs], ["junk", sk], accum=stat[:rows, 0, n % 8:n % 8 + 1])
                    ACTF(P, stat[:rows, 1, n % 8:n % 8 + 1], stat[:rows, 0, n % 8:n % 8 + 1], AF.Ln, [sk], [sk], bias=EPS, scale=1.0 / D)
                    ACTF(P, stat[:rows, 2, n % 8:n % 8 + 1], stat[:rows, 1, n % 8:n % 8 + 1], AF.Exp, [sk], [sk], scale=-0.5)
                    xq = xnb[n % 2]
                    qs = "xnb%d" % (n % 2)
                    P.op("dve", lambda E: E.scalar_tensor_tensor(out=xq[:rows, :], in0=x[:rows, :], scalar=stat[:rows, 2, n % 8:n % 8 + 1],
                                                                 in1=g_r[:rows, :], op0=ALU.mult, op1=ALU.mult), [xs, sk, "g_r"], [qs])

                def part2():
                    n = st["n"]
                    xq = xnb[n % 2]
                    qs = "xnb%d" % (n % 2)
                    pt = pT[n % 2]
                    ps = "pT%d" % (n % 2)
                    for k in range(8):
                        TR(P, pt[:, k, :], xq[:, k * 128:(k + 1) * 128], ident[:, :], [qs, "ident"], [ps])
                    rr.copy(P, hT[hbuf][:, :, col0:col0 + rows], pt[:, :, :rows], [ps], ["hT%d.%d" % (hbuf, col0 // 128)])
                return part1, part2

            def proj_T(hbuf, width, col_w, dst, hkeys, scale=None):
                n = proj_T.n
                proj_T.n += 1
                ps = psk[n % 2]
                pk = "psk%d" % (n % 2)
                for k in range(8):
                    MM(P, ps[:, :width], win[:, k, col_w:col_w + 128], hT[hbuf][:, k, :width], k == 0, k == 7, ["win"] + hkeys, [pk])
                ks = kts[n % 3]
                kk = "kts%d" % (n % 3)
                rr.copy(P, ks[:, :width], ps[:, :width], [pk], [kk], scale=scale)
                DMA(P, "pool", dst, ks[:, :width], [kk], ["KQscr"])
            proj_T.n = 0

            vcount = [0]

            def make_tile_A(Tn, hb):
                blocks = [norm_block(T["xn"][Tn * 512 + b * 128:Tn * 512 + (b + 1) * 128, :], 128, hb, b * 128) for b in range(4)]
                hk = ["hT%d.%d" % (hb, b) for b in range(4)]
                vb = Tn % 2
                groups = []

                def kgrp(c):
                    col = (CK_F + c * 128) if c < 4 else (CK_S + (c - 4) * 128)
                    h0 = 2 * c if c < 4 else 8 + 2 * (c - 4)
                    proj_T(hb, 512, col, T["Kd"][h0:h0 + 2, :, Tn * 512:(Tn + 1) * 512].rearrange("h r t -> (h r) t"), hk)

                def vgrp(b, g):
                    n = vcount[0]
                    vcount[0] += 1
                    ps = psv[n % 2]
                    pk = "psv%d" % (n % 2)
                    vc = CV_F if g == 0 else CV_S
                    for k in range(8):
                        MM(P, ps[:, :], hT[hb][:, k, b * 128:(b + 1) * 128], win[:, k, vc:vc + 512], k == 0, k == 7,
                           ["win", "hT%d.%d" % (hb, b)], [pk])
                    rr.copy(P, vs[vb][:, g * 8:(g + 1) * 8, b, 0:64], ps[:, :].rearrange("p (h d) -> p h d", h=8), [pk],
                            ["vs%d.%d.%d" % (vb, b, g)])
                    if g == 1:
                        pf = psf[b % 2]
                        for k in range(8):
                            MM(P, pf[:, 0:8], hT[hb][:, k, b * 128:(b + 1) * 128], win[:, k, CL:CL + 8], k == 0, k == 7,
                               ["win", "hT%d.%d" % (hb, b)], ["psf%d" % (b % 2)])
                        P.op("dve", lambda E: E.tensor_tensor(out=flog[:, :, Tn * 4 + b], in0=pf[:, 0:8], in1=fb_r[:, :], op=ALU.add),
                             ["psf%d" % (b % 2), "fb_r"], ["flog"])

                def vstore():
                    for h in range(16):
                        DMA(P, "pool", T["Vd"][h, :, Tn * 4:Tn * 4 + 4, :], vs[vb][:, h, :, :],
                            ["vs%d.%d.%d" % (vb, b, g) for b in range(4) for g in range(2)], ["Vscr"])
                for c in range(8):
                    groups.append(lambda c=c: kgrp(c))
                for b in range(4):
                    for g in range(2):
                        groups.append(lambda b=b, g=g: vgrp(b, g))
                groups.append(vstore)
                return blocks, groups

            def make_tile_Q(row0, width, col0, hb):
                nb = (width + 127) // 128
                blocks = []
                for b in range(nb):
                    rows = min(128, width - b * 128)
                    blocks.append(norm_block(T["xo"][row0 + b * 128:row0 + b * 128 + rows, :], rows, hb, b * 128))
                hk = ["hT%d.%d" % (hb, b) for b in range(nb)]

                def qgrp(c):
                    col = (CQ_F + c * 128) if c < 4 else (CQ_S + (c - 4) * 128)
                    h0 = 2 * c if c < 4 else 8 + 2 * (c - 4)
                    proj_T(hb, width, col, T["Qd"][h0:h0 + 2, :, col0:col0 + width].rearrange("h r t -> (h r) t"), hk, scale=0.125)
                groups = [(lambda c=c: qgrp(c)) for c in range(8)]
                return blocks, groups

            seq = []
            qi = 0
            for Tn in range(NT):
                seq.append(("A", Tn))
                if Tn % 2 == 1 and qi < NSL:
                    seq.append(("Q", qi))
                    qi += 1
            seq.append(("H", 0))
            tiles = []
            for i, (kind, idx) in enumerate(seq):
                hb = i % 2
                if kind == "A":
                    tiles.append(make_tile_A(idx, hb))
                elif kind == "Q":
                    tiles.append(make_tile_Q(idx * 512, 512, idx * 512, hb))
                else:
                    tiles.append(make_tile_Q(NOWN, 16, NOWN, hb))
            for (p1, p2) in tiles[0][0]:
                p1()
                p2()
            for i, (blocks, groups) in enumerate(tiles):
                nxt = tiles[i + 1][0] if i + 1 < len(tiles) else []
                G = len(groups)
                nb_ = max(1, len(nxt))
                ev = {}
                for b, (p1, p2) in enumerate(nxt):
                    ev.setdefault(int(b * G / nb_), []).append(p1)
                    ev.setdefault(min(G - 1, int((b + 0.7) * G / nb_)), []).append(p2)
                for gi, g in enumerate(groups):
                    g()
                    for f_ in ev.get(gi, []):
                        f_()
            P.emit()
            stats.append(P.stats)
            esp.close()
            if stop_after == "a":
                return nc, stats

            with ExitStack() as es2:
                P = Prog(nc, gs, "b")
                nlf = sbt(es2, "nlf", [128, 8 * NB], F32)
                ee = sbt(es2, "ee", [128, 8 * NB], F32)
                sc = [sbt(es2, "sc%d" % i, [128, 8, NB], F32) for i in range(2)]
                tot = sbt(es2, "tot", [128, 8, NB], F32)
                nF = sbt(es2, "nF", [128, 8, NB], F32)
                nFp = sbt(es2, "nFp", [128, 8, 128], F32)
                nFT = sbt(es2, "nFT", [NB, 8, 128], F32)
                r1 = sbt(es2, "r1", [NB, 8, 128], F32)
                parts = [sbt(es2, "part%d" % i, [NB, 8, 128], BF16) for i in range(3)]
                triu = sbt(es2, "triu", [128, 128], F32)
                ones = sbt(es2, "ones", [128, 128], F32)
                identf = sbt(es2, "identf", [128, 128], F32)
                cm1 = sbt(es2, "cm1", [64, 3, 128], BF16)
                cp1 = sbt(es2, "cp1", [64, 3, 128], BF16)
                ps_c = pst(es2, "ps_c", [128, 512], F32)
                ps_t = pst(es2, "ps_t", [128, 512], F32)
                ps_x = [pst(es2, "ps_x%d" % i, [128, 4, 128], F32) for i in range(2)]
                W8 = 8 * NB
                DMA(P, "sp", triu[:], T["triu_f"][:, :], [], ["triu"])
                DMA(P, "sp", ones[:], T["ones_f"][:, :], [], ["ones"])
                DMA(P, "sp", identf[:], T["ident_f"][:, :], [], ["identf"])
                DMA(P, "sp", cm1[:], T["cm1"][:, :, :], [], ["cm1"])
                DMA(P, "sp", cp1[:], T["cp1"][:, :, :], [], ["cp1"])
                fl2 = flog[:, :, :].rearrange("p h b -> p (h b)")
                ACTF(P, ee[:, :], fl2, AF.Exp, [], ["ee"], scale=-1.0)
                ACTF(P, nlf[:, :], ee[:, :], AF.Ln, ["ee"], ["nlf"], bias=1.0)
                MM(P, ps_c[:, :W8], triu[:, :], nlf[:, :], True, True, ["triu", "nlf"], ["ps_c"])
                MM(P, ps_t[:, :W8], ones[:, :], nlf[:, :], True, True, ["ones", "nlf"], ["ps_t"])
                P.op("dve", lambda E: E.tensor_copy(out=tot[:, :, :], in_=ps_t[:, :W8].rearrange("p (h b) -> p h b", h=8)), ["ps_t"], ["tot"])
                P.op("dve", lambda E: E.tensor_copy(out=sc[0][:, :, :], in_=tot[:, :, :]), ["tot"], ["sc0"])
                cur = 0
                d = 1
                while d < NB:
                    nxt = 1 - cur
                    P.op("dve", lambda E, cur=cur, nxt=nxt, d=d: E.tensor_copy(out=sc[nxt][:, :, 0:d], in_=sc[cur][:, :, 0:d]), ["sc%d" % cur], ["sc%d" % nxt])
                    P.op("dve", lambda E, cur=cur, nxt=nxt, d=d: E.tensor_tensor(out=sc[nxt][:, :, d:NB], in0=sc[cur][:, :, d:NB], in1=sc[cur][:, :, 0:NB - d], op=ALU.add),
                         ["sc%d" % cur], ["sc%d" % nxt])
                    cur = nxt
                    d *= 2
                P.op("dve", lambda E, cur=cur: E.tensor_tensor(out=tot[:, :, :], in0=sc[cur][:, :, :], in1=tot[:, :, :], op=ALU.subtract), ["sc%d" % cur, "tot"], ["tot"])
                P.op("dve", lambda E: E.tensor_tensor(out=nF[:, :, :], in0=ps_c[:, :W8].rearrange("p (h b) -> p h b", h=8), in1=tot[:, :, :], op=ALU.add), ["ps_c", "tot"], ["nF"])
                if debug:
                    DMA(P, "sp", T["dbg_nF"][:, :, :], nF[:, :, :], ["nF"], ["dbg"])
                P.op("dve", lambda E: E.memset(nFp[:], 0.0), [], ["nFp"])
                P.op("dve", lambda E: E.tensor_copy(out=nFp[:, :, 0:NB], in_=nF[:, :, :]), ["nF", "nFp"], ["nFp"])
                for h in range(8):
                    px = ps_x[h // 4]
                    P.op("pe", lambda E, h=h, px=px: E.transpose(px[:, h % 4, :], nFp[:, h, :], identf[:, :]), ["nFp", "identf"], ["ps_x%d" % (h // 4)])
                for q in range(2):
                    P.op("dve", lambda E, q=q: E.tensor_copy(out=nFT[:, q * 4:(q + 1) * 4, :], in_=ps_x[q][:NB, :, :]), ["ps_x%d" % q], ["nFT"])
                P.op("dve", lambda E: E.tensor_copy(out=parts[0][:, :, :], in_=nFT[:, :, :]), ["nFT"], ["part0"])
                P.op("dve", lambda E: E.tensor_tensor(out=r1[:, :, :], in0=nFT[:, :, :], in1=parts[0][:, :, :], op=ALU.subtract), ["nFT", "part0"], ["r1"])
                P.op("dve", lambda E: E.tensor_copy(out=parts[1][:, :, :], in_=r1[:, :, :]), ["r1"], ["part1"])
                P.op("dve", lambda E: E.tensor_tensor(out=nFT[:, :, :], in0=r1[:, :, :], in1=parts[1][:, :, :], op=ALU.subtract), ["r1", "part1"], ["nFT"])
                P.op("dve", lambda E: E.tensor_copy(out=parts[2][:, :, :], in_=nFT[:, :, :]), ["nFT"], ["part2"])
                for i in range(3):
                    DMA(P, "sp", T["KA"][:, i, :].rearrange("h (b t) -> b h t", t=128), parts[i][:, :, :], ["part%d" % i], ["KAs"])
                    DMA(P, "sp", T["QA"][:, 3 + i, :].rearrange("h (b t) -> b h t", t=128), parts[i][:, :, :], ["part%d" % i], ["QAs"])
                for h in range(8):
                    DMA(P, "sp", T["KA"][h, 3:6, :].rearrange("r (b t) -> b r t", t=128), cm1[:NB, :, :], ["cm1"], ["KAs"])
                    DMA(P, "sp", T["QA"][h, 0:3, :].rearrange("r (b t) -> b r t", t=128), cp1[:NB, :, :], ["cp1"], ["QAs"])
                P.emit()
                stats.append(P.stats)
        if stop_after == "b":
            return nc, stats

        with ExitStack() as es:
            P = Prog(nc, gs, "c")
            KT = [sbt(es, "KT%d" % i, [70, S], BF16) for i in range(2)]
            QT = [sbt(es, "QT%d" % i, [70, NQ], BF16) for i in range(2)]
            VV = [sbt(es, "VV%d" % i, [128, NB, 65], BF16) for i in range(2)]
            qa = sbt(es, "qa", [70, S], BF16)
            qtmp = sbt(es, "qtmp", [70, 512], BF16)
            esel = sbt(es, "esel", [128, 2 * NSL], F32)
            eselh = sbt(es, "eselh", [128, 2 * NSL], F32)
            idE = sbt(es, "idE", [128, NSL, 128], BF16)
            idNE = sbt(es, "idNE", [128, NSL, 128], BF16)
            erow = sbt(es, "erow", [1, NSL, 128], BF16)
            negrow = sbt(es, "negrow", [1, 512], BF16)
            allneg = sbt(es, "allneg", [128, 512], BF16)
            ident = sbt(es, "ident2", [128, 128], BF16)
            mt = [sbt(es, "mt%d" % i, [128, 4, 512], BF16) for i in range(2)]
            mh = [sbt(es, "mh%d" % i, [128, NB, 16], BF16) for i in range(2)]
            negtri = sbt(es, "negtri", [128, 128], BF16)
            negones = sbt(es, "negones", [128, 128], BF16)
            PT = [sbt(es, "PT%d" % i, [128, 512], BF16) for i in range(3)]
            UU = [sbt(es, "UU%d" % i, [128, 512], F32) for i in range(2)]
            LL = [sbt(es, "LL%d" % i, [128, 512], BF16) for i in range(3)]
            LS = [sbt(es, "LS%d" % i, [128, 512], BF16) for i in range(3)]
            rden = sbt(es, "rden", [128, 2, 4], F32)
            ob = [sbt(es, "ob%d" % i, [128, 4, 64], BF16) for i in range(3)]
            cst = [sbt(es, "cst%d" % i, [128, 2816], F32) for i in range(2)]
            cbf = [sbt(es, "cbf%d" % i, [128, 2816], BF16) for i in range(2)]
            jobs = []
            for k in range(8):
                for hf in range(2):
                    jobs.append((T["w_up"][k * 128:(k + 1) * 128, hf * 2816:(hf + 1) * 2816], T["Wup"][:, k, hf * 2816:(hf + 1) * 2816], 2816, None))
            for c in range(0, 22, 2):
                jobs.append((T["w_down"][c * 128:(c + 2) * 128, :].rearrange("(c p) n -> p c n", p=128), T["Wdn"][:, c:c + 2, :], 2048, 2))
            for k in range(0, 8, 2):
                jobs.append((T["w_out"][k * 128:(k + 2) * 128, :].rearrange("(c p) n -> p c n", p=128), T["Wout"][:, k:k + 2, :], 2048, 2))
            jobn = [0]

            def conv_job():
                n = jobn[0]
                if n >= len(jobs):
                    return
                jobn[0] += 1
                src, dst, width, sub = jobs[n]
                b = n % 2
                if sub is None:
                    s_ap, b_ap = cst[b][:, :width], cbf[b][:, :width]
                else:
                    s_ap = cst[b][:, :width].rearrange("p (c n) -> p c n", c=sub)
                    b_ap = cbf[b][:, :width].rearrange("p (c n) -> p c n", c=sub)
                DMA(P, "sp", s_ap, src, [], ["cst%d" % b])
                P.op("dve", lambda E: E.tensor_copy(out=cbf[b][:, :width], in_=cst[b][:, :width]), ["cst%d" % b], ["cbf%d" % b])
                DMA(P, "sp", dst, b_ap, ["cbf%d" % b], ["Wscr"])
            psZ = [pst(es, "psZ%d" % i, [128, 512], F32) for i in range(4)]
            psO = [pst(es, "psO%d" % i, [128, 512], F32) for i in range(2)]
            for nm, t_, src in (("esel", esel, T["esel"][:, :]), ("eselh", eselh, T["eselh"][:, :]), ("idE", idE, T["idE"][:, :, :]),
                                ("idNE", idNE, T["idNE"][:, :, :]), ("erow", erow, T["erow"][:, :, :]), ("negrow", negrow, T["negrow"][:, :]),
                                ("ident", ident, T["ident_bf"][:, :]), ("mt0", mt[0], T["mt_f"][:, :, :]), ("mt1", mt[1], T["mt_s"][:, :, :]),
                                ("mh0", mh[0], T["mh_f"][:, :, :]), ("mh1", mh[1], T["mh_s"][:, :, :]),
                                ("negtri", negtri, T["negtri"][:, :]), ("negones", negones, T["negones"][:, :])):
                DMA(P, "sp", t_[:], src, [], [nm])
            P.op("pool", lambda E: E.memset(allneg[:], NEG), [], ["allneg"])
            for i in range(3):
                P.op("pool", lambda E, i=i: E.memset(PT[i][:], 0.0), [], ["PT%d" % i])
            CONSTS = ["idE", "idNE", "erow", "negrow", "ident", "mt0", "mt1", "mh0", "mh1", "allneg"]

            def load_head(hh):
                hb = hh % 2
                fox = hh < 8
                DMA(P, "sp", KT[hb][0:64, :], T["Kd"][hh, :, :], [], ["KTm%d" % hb])
                DMA(P, "sp", QT[hb][0:64, :], T["Qd"][hh, :, :], [], ["QTm%d" % hb])
                DMA(P, "sp", VV[hb][:, :, :], T["Vd"][hh, :, :, :], [], ["VV%d" % hb])
                if not fox:
                    P.op("pool", lambda E: E.memset(KT[hb][64:70, :], 0.0), [], ["KTa%d" % hb])
                    P.op("pool", lambda E: E.memset(QT[hb][64:70, :], 0.0), [], ["QTa%d" % hb])
                if fox:
                    DMA(P, "sp", KT[hb][64:70, :], T["KA"][hh, :, :], [], ["KTa%d" % hb])
                    DMA(P, "sp", qa[64:70, :], T["QA"][hh, :, :], [], ["qa"])
                    for j in range(NSL + 1):
                        if j < NSL:
                            c0 = qa[64:70, 1024 * j:1024 * j + 512]
                            c1 = qa[64:70, 1024 * j + 512:1024 * j + 1024]
                            e0 = esel[64:70, 2 * j:2 * j + 1]
                            e1 = esel[64:70, 2 * j + 1:2 * j + 2]
                            dst = QT[hb][64:70, 512 * j:512 * j + 512]
                            tmp = qtmp[64:70, 0:512]
                            P.op("dve", lambda E, c0=c0, e0=e0, tmp=tmp: E.tensor_scalar(out=tmp, in0=c0, scalar1=e0, scalar2=None, op0=ALU.mult),
                                 ["qa", "esel"], ["qtmp"])
                            P.op("dve", lambda E, c1=c1, e1=e1, tmp=tmp, dst=dst: E.scalar_tensor_tensor(out=dst, in0=c1, scalar=e1, in1=tmp, op0=ALU.mult, op1=ALU.add),
                                 ["qa", "esel", "qtmp"], ["QTa%d" % hb])
                        else:
                            for jj in range(NSL):
                                a0 = max(0, 1024 * jj - 2)
                                c0 = qa[64:70, a0:a0 + 2]
                                c1 = qa[64:70, 1024 * jj + 510:1024 * jj + 512]
                                e0 = eselh[64:70, 2 * jj:2 * jj + 1]
                                e1 = eselh[64:70, 2 * jj + 1:2 * jj + 2]
                                dst = QT[hb][64:70, NOWN + 2 * jj:NOWN + 2 * jj + 2]
                                tmp = qtmp[64:70, 0:2]
                                P.op("dve", lambda E, c0=c0, e0=e0, tmp=tmp: E.tensor_scalar(out=tmp, in0=c0, scalar1=e0, scalar2=None, op0=ALU.mult),
                                     ["qa", "eselh"], ["qtmp"])
                                P.op("dve", lambda E, c1=c1, e1=e1, tmp=tmp, dst=dst: E.scalar_tensor_tensor(out=dst, in0=c1, scalar=e1, in1=tmp, op0=ALU.mult, op1=ALU.add),
                                     ["qa", "eselh", "qtmp"], ["QTa%d" % hb])

            units = []
            slot_ctr = 0
            for hh in range(16):
                fox = hh < 8
                if "noSB" in OPT and not fox:
                    continue
                if "noFOX" in OPT and fox:
                    continue
                for sl in range(NSL + 1):
                    halo = sl == NSL
                    if halo and "noHalo" in OPT:
                        continue
                    W = 16 if halo else 512
                    nkb = NB if halo else 8 * (sl + 1)
                    order = list(range(nkb)) if fox else list(range(nkb - 1, -1, -1))
                    for idx, kb in enumerate(order):
                        if halo:
                            mk = ("H", kb)
                        elif 8 * sl <= kb < 8 * sl + 4:
                            mk = ("E", kb - 8 * sl)
                        elif 8 * sl + 4 <= kb < 8 * sl + 8:
                            mk = ("NE", kb - 8 * sl - 4)
                        else:
                            mk = None
                        units.append(dict(hh=hh, fox=fox, sl=sl, W=W, kb=kb, first=idx == 0, last=idx == nkb - 1, mk=mk,
                                          so=slot_ctr, qc=NOWN if halo else 512 * sl))
                    slot_ctr += 1
            for i, u in enumerate(units):
                u["i"] = i
                u["head_first"] = (i == 0 or units[i - 1]["hh"] != u["hh"])
                u["head_last"] = (i == len(units) - 1 or units[i + 1]["hh"] != u["hh"])

            def stage_A(u):
                hb = u["hh"] % 2
                KD = 70
                W, kb, i = u["W"], u["kb"], u["i"]
                z = psZ[i % 4]
                zk = "psZ%d" % (i % 4)
                mi = 0 if u["fox"] else 1
                rd = ["KTm%d" % hb, "QTm%d" % hb, "KTa%d" % hb, "QTa%d" % hb]
                mk = u["mk"] if "noMask" not in OPT else None
                MM(P, z[:, :W], KT[hb][0:KD, kb * 128:(kb + 1) * 128], QT[hb][0:KD, u["qc"]:u["qc"] + W], True, mk is None and u["fox"], rd, [zk])
                if mk is not None:
                    typ, m = mk
                    sl = u["sl"]
                    if typ == "H":
                        MM(P, z[:, :W], ident[:, :], mh[mi][:, m, :], False, u["fox"], CONSTS, [zk])
                    elif typ == "E":
                        MM(P, z[:, :W], idE[:, sl, :], mt[mi][:, m, :], False, u["fox"], CONSTS, [zk])
                    else:
                        MM(P, z[:, :W], idNE[:, sl, :], mt[mi][:, m, :], False, False, CONSTS, [zk])
                        MM(P, z[:, :W], idE[:, sl, :], allneg[:, :W], False, u["fox"], CONSTS, [zk])

            def stage_B_fox(u):
                W, i = u["W"], u["i"]
                ACTF(P, PT[i % 3][:, :W], psZ[i % 4][:, :W], AF.Exp, ["psZ%d" % (i % 4)], ["PT%d" % (i % 3)])

            def stage_B1(u):
                W, i = u["W"], u["i"]
                ACTF(P, UU[i % 2][:, :W], psZ[i % 4][:, :W], AF.Exp, ["psZ%d" % (i % 4)], ["UU%d" % (i % 2)])
                ACTF(P, LL[i % 3][:, :W], UU[i % 2][:, :W], AF.Ln, ["UU%d" % (i % 2)], ["LL%d" % (i % 3)], bias=1.0)
                if not u["last"]:
                    if u["first"]:
                        P.op("pool", lambda E: E.tensor_copy(out=LS[(i + 1) % 3][:, :W], in_=LL[i % 3][:, :W]), ["LL%d" % (i % 3)], ["LS%d" % ((i + 1) % 3)])
                    else:
                        P.op("pool", lambda E: E.tensor_tensor(out=LS[(i + 1) % 3][:, :W], in0=LS[i % 3][:, :W], in1=LL[i % 3][:, :W], op=ALU.add),
                             ["LL%d" % (i % 3), "LS%d" % (i % 3)], ["LS%d" % ((i + 1) % 3)])

            def stage_C(u):
                W, i = u["W"], u["i"]
                z = psZ[i % 4]
                zk = "psZ%d" % (i % 4)
                MM(P, z[:, :W], negtri[:, :], LL[i % 3][:, :W], False, u["first"], ["negtri", "LL%d" % (i % 3)], [zk])
                if not u["first"]:
                    MM(P, z[:, :W], negones[:, :], LS[i % 3][:, :W], False, True, ["negones", "LS%d" % (i % 3)], [zk])

            def stage_B2(u):
                W, i = u["W"], u["i"]
                ACTF(P, PT[i % 3][:, :W], psZ[i % 4][:, :W], AF.Exp, ["psZ%d" % (i % 4)], ["PT%d" % (i % 3)])

            fin_ctr = [0]

            def stage_D(u):
                hb = u["hh"] % 2
                W, kb, i = u["W"], u["kb"], u["i"]
                o = psO[u["so"] % 2]
                ok = "psO%d" % (u["so"] % 2)
                NV = 65 if u["fox"] else 64
                nsub = max(1, W // 128)
                M = min(128, W)
                for s_ in range(nsub):
                    MM(P, o[:, s_ * NV:(s_ + 1) * NV], PT[i % 3][:, s_ * 128:s_ * 128 + 128], VV[hb][:, kb, 0:NV],
                       u["first"] and s_ == 0, u["last"] and s_ == nsub - 1, ["PT%d" % (i % 3), "VV%d" % hb], [ok], skip=True)
                if u["last"]:
                    f = fin_ctr[0]
                    fin_ctr[0] += 1
                    obuf = ob[f % 3]
                    obk = "ob%d" % (f % 3)
                    ov = o[:M, 0:nsub * NV].rearrange("p (s v) -> p s v", s=nsub)
                    if u["fox"]:
                        rk = "rden%d" % (f % 2)
                        P.op("dve", lambda E: E.reciprocal(out=rden[:M, f % 2, 0:nsub], in_=ov[:, :, 64]), [ok], [rk])
                        for s_ in range(nsub):
                            P.op("dve", lambda E, s_=s_: E.tensor_scalar(out=obuf[:M, s_, :], in0=ov[:, s_, 0:64], scalar1=rden[:M, f % 2, s_:s_ + 1],
                                                                         scalar2=None, op0=ALU.mult), [ok, rk], [obk])
                    else:
                        P.op("dve", lambda E: E.tensor_copy(out=obuf[:M, 0:nsub, :], in_=ov[:, :, 0:64]), [ok], [obk])
                    r0 = u["qc"]
                    hh = u["hh"]
                    if nsub == 4:
                        dst = T["Od"][r0:r0 + 512, hh * 64:(hh + 1) * 64].rearrange("(s p) d -> p s d", p=128)
                        DMA(P, "sp", dst, obuf[:, :, :], [obk], ["Od"])
                    else:
                        DMA(P, "sp", T["Od"][r0:r0 + M, hh * 64:(hh + 1) * 64], obuf[:M, 0, :], [obk], ["Od"])

            n = len(units)
            hh0 = units[0]["hh"]
            load_head(hh0)
            stage_A(units[0])
            job_every = max(1, n // (len(jobs) + 2))
            for i in range(n):
                u = units[i]
                if u["head_first"] and u["hh"] + 1 < 16 and u["hh"] == hh0:
                    load_head(hh0 + 1)
                if i % job_every == job_every - 1:
                    conv_job()
                if i + 1 < n:
                    stage_A(units[i + 1])
                if u["fox"]:
                    stage_B_fox(u)
                else:
                    stage_B1(u)
                    stage_C(u)
                if i >= 1:
                    pu = units[i - 1]
                    if not pu["fox"]:
                        stage_B2(pu)
                    stage_D(pu)
                    if pu["head_last"] and pu["hh"] + 2 < 16:
                        load_head(pu["hh"] + 2)
            pu = units[n - 1]
            if not pu["fox"]:
                stage_B2(pu)
            stage_D(pu)
            while jobn[0] < len(jobs):
                conv_job()
            P.emit()
            stats.append(P.stats)
        if stop_after == "c":
            return nc, stats

        with ExitStack() as es:
            P = Prog(nc, gs, "d")
            wout = sbt(es, "wout", [128, 8, D], BF16)
            fox_g = sbt(es, "fox_g", [128, 512], F32)
            sb_g = sbt(es, "sb_g", [128, 512], F32)
            ffn_g = sbt(es, "ffn_g", [128, D], F32)
            fin_g = sbt(es, "fin_g", [128, D], F32)
            cw = sbt(es, "cw", [128, NCH, 3], F32)
            cb = sbt(es, "cb", [128, NCH], F32)
            hmask = sbt(es, "hmask", [128, 16], F32)
            ident = sbt(es, "ident3", [128, 128], BF16)
            identf = sbt(es, "identf3", [128, 128], F32)
            o_s = [sbt(es, "o_s%d" % i, [128, 4, D], BF16) for i in range(2)]
            xr = [sbt(es, "xr%d" % i, [128, 4, D], F32) for i in range(2)]
            junk = sbt(es, "junk3", [128, D], BF16)
            stat = sbt(es, "stat3", [128, 3, 16], F32)
            on = sbt(es, "on", [128, 4, D], BF16)
            onT = sbt(es, "onT", [128, 8, 512], BF16)
            h2T = sbt(es, "h2T", [128, 8, 512], BF16)
            wu = [sbt(es, "wu%d" % i, [128, 8, 2, 128], BF16) for i in range(3)]
            cv = [sbt(es, "cv%d" % i, [128, 512], F32) for i in range(6)]
            sg = [sbt(es, "sg%d" % i, [128, 512], F32) for i in range(2)]
            aT = sbt(es, "aT", [128, 22, 512], BF16)
            uh = sbt(es, "uh", [128, NCH, 16], F32)
            wd = [sbt(es, "wd%d" % i, [128, 22, 128], BF16) for i in range(3)]
            yT = [sbt(es, "yT%d" % i, [128, 512], F32) for i in range(2)]
            pT = [pst(es, "p3T%d" % i, [128, 8, 128], BF16) for i in range(2)]
            psa = [pst(es, "psa%d" % i, [128, 512], F32) for i in range(2)]
            psu = [pst(es, "psu%d" % i, [128, 512], F32) for i in range(2)]
            psy = pst(es, "psy", [128, 512], F32)
            pst_ = pst(es, "pst", [128, 4, 128], F32)
            rr = RR(["act", "dve"])
            PSU = [psa[0], psa[1], psu[0], psu[1]]
            PSUK = ["psa0", "psa1", "psu0", "psu1"]
            for nm, t_, src in (("wout", wout, T["Wout"][:, :, :]), ("fox_g", fox_g, T["fox_g"][:, :]), ("sb_g", sb_g, T["sb_g"][:, :]),
                                ("ffn_g", ffn_g, T["ffn_g"][:, :]), ("fin_g", fin_g, T["fin_g"][:, :]), ("cw", cw, T["cw"][:, :, :]),
                                ("cb", cb, T["cb"][:, :]), ("hmask", hmask, T["hmask"][:, :]), ("ident", ident, T["ident_bf"][:, :]),
                                ("identf", identf, T["ident_f"][:, :])):
                DMA(P, "sp", t_[:], src, [], [nm])

            P.op("pool", lambda E: E.memset(on[:], 0.0), [], ["on"])
            P.op("pool", lambda E: E.memset(onT[:], 0.0), [], ["onT.%d" % b for b in range(4)])
            P.op("pool", lambda E: E.memset(h2T[:], 0.0), [], ["h2T.%d" % b for b in range(4)])
            sctr = [0]
            pctr = [0]
            wuc = [0]
            wdc = [0]

            def rstd_of(src_ap, rows, width):
                c = sctr[0] % 16
                sctr[0] += 1
                sk = "st%d" % c
                ACTF(P, junk[:rows, :width], src_ap, AF.Square, src_ap_keys[0], ["junk", sk], accum=stat[:rows, 0, c:c + 1])
                ACTF(P, stat[:rows, 1, c:c + 1], stat[:rows, 0, c:c + 1], AF.Ln, [sk], [sk], bias=EPS, scale=1.0 / width)
                ACTF(P, stat[:rows, 2, c:c + 1], stat[:rows, 1, c:c + 1], AF.Exp, [sk], [sk], scale=-0.5)
                return stat[:rows, 2, c:c + 1], sk
            src_ap_keys = [None]

            def transpose_to(src_tile, src_key, blocks, dstT, dst_key):
                for (bi, rows) in blocks:
                    n = pctr[0]
                    pctr[0] += 1
                    pt = pT[n % 2]
                    pk = "p3T%d" % (n % 2)
                    for k in range(8):
                        TR(P, pt[:, k, :], src_tile[:, bi, k * 128:(k + 1) * 128], ident[:, :], [src_key, "ident"], [pk])
                    rr.copy(P, dstT[:, :, bi * 128:bi * 128 + rows], pt[:, :, :rows], [pk], [dst_key + ".%d" % bi])

            slots = [NSL] + list(range(NSL))

            def load_slot(si):
                sl = slots[si]
                halo = sl == NSL
                r0 = NOWN if halo else 512 * sl
                sb_ = si % 2
                os_, xr_ = o_s[sb_], xr[sb_]
                osk, xrk = "o_s%d" % sb_, "xr%d" % sb_
                if halo:
                    DMA(P, "sp", os_[:16, 0, :], T["Od"][r0:r0 + 16, :], [], [osk])
                    DMA(P, "sp", xr_[:16, 0, :], T["xo"][r0:r0 + 16, :], [], [xrk])
                else:
                    DMA(P, "sp", os_[:, :, :], T["Od"][r0:r0 + 512, :].rearrange("(b p) d -> p b d", p=128), [], [osk])
                    DMA(P, "sp", xr_[:, :, :], T["xo"][r0:r0 + 512, :].rearrange("(b p) d -> p b d", p=128), [], [xrk])

            deferred = []

            def do_slot(si, sl):
                halo = sl == NSL
                W = 16 if halo else 512
                r0 = NOWN if halo else 512 * sl
                blocks = [(0, 16)] if halo else [(b, 128) for b in range(4)]
                sb_ = si % 2
                os_, xr_ = o_s[sb_], xr[sb_]
                osk, xrk = "o_s%d" % sb_, "xr%d" % sb_
                if si == 0:
                    load_slot(si)
                if si + 1 < len(slots):
                    load_slot(si + 1)
                for (bi, rows) in blocks:
                    for g in range(2):
                        src = os_[:rows, bi, g * 512:(g + 1) * 512]
                        src_ap_keys[0] = [osk]
                        rs, sk = rstd_of(src, rows, 512)
                        gt = fox_g if g == 0 else sb_g
                        P.op("dve", lambda E, src=src, rs=rs, gt=gt, rows=rows, bi=bi, g=g: E.scalar_tensor_tensor(
                            out=on[:rows, bi, g * 512:(g + 1) * 512], in0=src, scalar=rs, in1=gt[:rows, :], op0=ALU.mult, op1=ALU.mult),
                            [osk, sk, "fox_g", "sb_g"], ["on"])
                transpose_to(on, "on", blocks, onT, "onT")
                onk = ["onT.%d" % bi for (bi, _) in blocks]
                for (bi, rows) in blocks:
                    for ch in range(2):
                        n = pctr[0]
                        pctr[0] += 1
                        pa = psa[n % 2]
                        pk = "psa%d" % (n % 2)
                        for k in range(8):
                            MM(P, pa[:, :], onT[:, k, bi * 128:bi * 128 + 128], wout[:, k, ch * 512:(ch + 1) * 512], k == 0, k == 7,
                               ["wout", "onT.%d" % bi], [pk])
                        P.op("dve", lambda E, pa=pa, rows=rows, bi=bi, ch=ch: E.tensor_tensor(
                            out=xr_[:rows, bi, ch * 512:(ch + 1) * 512], in0=pa[:rows, :], in1=xr_[:rows, bi, ch * 512:(ch + 1) * 512], op=ALU.add),
                            [pk, xrk], [xrk])
                for (bi, rows) in blocks:
                    src = xr_[:rows, bi, :]
                    src_ap_keys[0] = [xrk]
                    rs, sk = rstd_of(src, rows, D)
                    P.op("dve", lambda E, src=src, rs=rs, rows=rows, bi=bi: E.scalar_tensor_tensor(
                        out=on[:rows, bi, :], in0=src, scalar=rs, in1=ffn_g[:rows, :], op0=ALU.mult, op1=ALU.mult),
                        [xrk, sk, "ffn_g"] + onk, ["on"])
                transpose_to(on, "on", blocks, h2T, "h2T")
                h2k = ["h2T.%d" % bi for (bi, _) in blocks]
                for c in range(22):
                    wn = wuc[0]
                    wuc[0] += 1
                    wt = wu[wn % 3]
                    wk = "wu%d" % (wn % 3)
                    DMA(P, "sp", wt[:, :, 0, :], T["Wup"][:, :, c * 128:(c + 1) * 128], [], [wk + "g"])
                    DMA(P, "sp", wt[:, :, 1, :], T["Wup"][:, :, DFF + c * 128:DFF + (c + 1) * 128], [], [wk + "v"])
                    pend = []
                    prev_fin = deferred[:]
                    del deferred[:]
                    for gv in range(2):
                        un = (2 * wn + gv) % 4
                        pu_ = PSU[un]
                        pk = PSUK[un]
                        for k in range(8):
                            MM(P, pu_[:, :W], wt[:, k, gv, :], h2T[:, k, :W], k == 0, k == 7, [wk + ("g" if gv == 0 else "v")] + h2k, [pk])
                        cc = c + 22 * gv
                        if halo:
                            P.op("dve", lambda E, pu_=pu_, cc=cc: E.tensor_tensor(out=uh[:, cc, :], in0=pu_[:, :16], in1=hmask[:, :], op=ALU.mult),
                                 [pk, "hmask"], ["uh"])
                            continue
                        cn = (2 * wn + gv) % 6
                        ct, ck = cv[cn], "cv%d" % cn
                        ACTF(P, ct[:, :], pu_[:, :512], AF.Identity, [pk, "cw", "cb"], [ck], bias=cb[:, cc:cc + 1], scale=cw[:, cc, 2:3])
                        pend.append((pu_, pk, ct, ck, cc))
                    if halo:
                        continue
                    for f_ in prev_fin:
                        f_()
                    for (pu_, pk, ct, ck, cc) in pend:
                        P.op("dve", lambda E, pu_=pu_, ct=ct, cc=cc: E.scalar_tensor_tensor(out=ct[:, 1:512], in0=pu_[:, 0:511], scalar=cw[:, cc, 1:2], in1=ct[:, 1:512],
                                                                                         op0=ALU.mult, op1=ALU.add), [pk, "cw", ck], [ck])
                    for (pu_, pk, ct, ck, cc) in pend:
                        P.op("dve", lambda E, pu_=pu_, ct=ct, cc=cc: E.scalar_tensor_tensor(out=ct[:, 2:512], in0=pu_[:, 0:510], scalar=cw[:, cc, 0:1], in1=ct[:, 2:512],
                                                                                         op0=ALU.mult, op1=ALU.add), [pk, "cw", ck], [ck])
                    for (pu_, pk, ct, ck, cc) in pend:
                        P.op("dve", lambda E, ct=ct, cc=cc: E.scalar_tensor_tensor(out=ct[:, 0:1], in0=uh[:, cc, 2 * sl + 1:2 * sl + 2], scalar=cw[:, cc, 1:2], in1=ct[:, 0:1],
                                                                                op0=ALU.mult, op1=ALU.add), ["uh", "cw", ck], [ck])
                    for (pu_, pk, ct, ck, cc) in pend:
                        P.op("dve", lambda E, ct=ct, cc=cc: E.scalar_tensor_tensor(out=ct[:, 0:2], in0=uh[:, cc, 2 * sl:2 * sl + 2], scalar=cw[:, cc, 0:1], in1=ct[:, 0:2],
                                                                                op0=ALU.mult, op1=ALU.add), ["uh", "cw", ck], [ck])
                    (_, _, ctg, ckg, _), (_, _, ctv, ckv, _) = pend
                    st_, stk = sg[wn % 2], "sg%d" % (wn % 2)

                    def fin(ctg=ctg, ckg=ckg, ctv=ctv, ckv=ckv, st_=st_, stk=stk, c=c):
                        ACTF(P, st_[:, :], ctg[:, :], AF.Silu, [ckg], [stk])
                        P.op("pool", lambda E: E.tensor_tensor(out=aT[:, c, :], in0=st_[:, :], in1=ctv[:, :], op=ALU.mult),
                             [stk, ckv], ["aT.%d" % c])
                    deferred.append(fin)
                for f_ in deferred:
                    f_()
                del deferred[:]
                if halo:
                    return
                for cc in range(8):
                    wn = wdc[0]
                    wdc[0] += 1
                    wt = wd[wn % 3]
                    wk = "wd%d" % (wn % 3)
                    DMA(P, "sp", wt[:, :, :], T["Wdn"][:, :, cc * 128:(cc + 1) * 128], [], [wk])
                    for c in range(22):
                        MM(P, psy[:, :], wt[:, c, :], aT[:, c, :], c == 0, c == 21, [wk, "aT.%d" % c], ["psy"])
                    yt, yk = yT[wn % 2], "yT%d" % (wn % 2)
                    ACTF(P, yt[:, :], psy[:, :], AF.Copy, ["psy"], [yk])
                    for b in range(4):
                        P.op("pe", lambda E, yt=yt, b=b: E.transpose(pst_[:, b, :], yt[:, b * 128:(b + 1) * 128], identf[:, :]), [yk, "identf"], ["pst"])
                    P.op("dve", lambda E, cc=cc: E.tensor_tensor(out=xr_[:, :, cc * 128:(cc + 1) * 128], in0=pst_[:, :, :],
                                                               in1=xr_[:, :, cc * 128:(cc + 1) * 128], op=ALU.add), ["pst", xrk], [xrk])
                for (bi, rows) in blocks:
                    src = xr_[:rows, bi, :]
                    src_ap_keys[0] = [xrk]
                    rs, sk = rstd_of(src, rows, D)
                    P.op("dve", lambda E, src=src, rs=rs: E.scalar_tensor_tensor(out=src, in0=src, scalar=rs, in1=fin_g[:, :], op0=ALU.mult, op1=ALU.mult),
                         [xrk, sk, "fin_g"], [xrk])
                DMA(P, "pool", T["out"][r0:r0 + 512, :].rearrange("(b p) d -> p b d", p=128), xr_[:, :, :], [xrk], ["out"])

            for si, sl in enumerate(slots):
                do_slot(si, sl)
            P.emit()
            stats.append(P.stats)
    return nc, stats


_CACHE = {}


def _consts(S):
    NB = S // 128
    bf = ml_dtypes.bfloat16
    p = np.arange(128)
    c = {}
    c["ident_bf"] = np.eye(128, dtype=np.float32).astype(bf)
    c["ident_f"] = np.eye(128, dtype=np.float32)
    c["triu_f"] = (p[:, None] <= p[None, :]).astype(np.float32)
    c["ones_f"] = np.ones((128, 128), np.float32)
    c["negtri"] = (-(p[:, None] >= p[None, :]).astype(np.float32)).astype(bf)
    c["negones"] = (-np.ones((128, 128), np.float32)).astype(bf)
    t = np.arange(512)
    key = (np.arange(4)[None, :, None] * 128 + p[:, None, None])
    c["mt_f"] = np.where(key <= t[None, None, :], 0.0, NEG).astype(np.float32).astype(bf)
    c["mt_s"] = np.where(key < t[None, None, :], 0.0, NEG).astype(np.float32).astype(bf)
    c["negrow"] = np.full((1, 512), NEG, np.float32).astype(bf)
    c["cm1"] = np.full((64, 3, 128), -1.0, np.float32).astype(bf)
    c["cp1"] = np.full((64, 3, 128), 1.0, np.float32).astype(bf)
    return c


def _core_consts(S, par):
    NT = S // 512
    NSL = NT // 2
    NB = S // 128
    bf = ml_dtypes.bfloat16
    own = own_tiles(par, NSL)
    esel = np.zeros((128, 2 * NSL), np.float32)
    eselh = np.zeros((128, 2 * NSL), np.float32)
    hmask = np.ones((128, 16), np.float32)
    idE = np.zeros((128, NSL, 128), np.float32)
    idNE = np.zeros((128, NSL, 128), np.float32)
    erow = np.zeros((1, NSL, 128), np.float32)
    I = np.eye(128, dtype=np.float32)
    p = np.arange(128)
    keypos = (np.arange(NB)[None, :] * 128 + p[:, None])
    mh_f = np.zeros((128, NB, 16), np.float32)
    mh_s = np.zeros((128, NB, 16), np.float32)
    for j in range(NSL):
        e0 = 1.0 if own[j] == 2 * j else 0.0
        esel[:, 2 * j] = e0
        esel[:, 2 * j + 1] = 1.0 - e0
        eselh[:, 2 * j] = 1.0 if (own[j] == 2 * j and j > 0) else 0.0
        eselh[:, 2 * j + 1] = 1.0 if own[j] == 2 * j + 1 else 0.0
        idE[:, j, :] = e0 * I
        idNE[:, j, :] = (1.0 - e0) * I
        erow[0, j, :] = e0
        for r in range(2):
            col = 2 * j + r
            pos = 512 * own[j] - 2 + r
            if own[j] == 0:
                hmask[:, col] = 0.0
                mh_f[:, :, col] = np.where(keypos == 0, 0.0, NEG)
                mh_s[:, :, col] = np.where(keypos == 0, 0.0, NEG)
            else:
                mh_f[:, :, col] = np.where(keypos <= pos, 0.0, NEG)
                mh_s[:, :, col] = np.where(keypos < pos, 0.0, NEG)
    return dict(esel=esel, eselh=eselh, hmask=hmask, idE=idE.astype(bf), idNE=idNE.astype(bf), erow=erow.astype(bf),
                mh_f=mh_f.astype(bf), mh_s=mh_s.astype(bf)), own


def _prepare(inputs, debug=False):
    x = np.asarray(inputs["x"], np.float32)
    B, S, _ = x.shape
    NT = S // 512
    NSL = NT // 2
    rep = lambda v: np.ascontiguousarray(np.broadcast_to(np.asarray(v, np.float32).reshape(1, -1), (128, np.asarray(v).size)))
    shared = dict(_consts(S))
    shared["w_in"] = np.ascontiguousarray(np.asarray(inputs["w_in"], np.float32)[0])
    shared["w_out"] = np.ascontiguousarray(np.asarray(inputs["w_out"], np.float32)[0])
    shared["w_up"] = np.ascontiguousarray(np.asarray(inputs["w_up"], np.float32)[0])
    shared["w_down"] = np.ascontiguousarray(np.asarray(inputs["w_down"], np.float32)[0])
    shared["attn_g"] = rep(inputs["attn_norm_g"][0])
    shared["ffn_g"] = rep(inputs["ffn_norm_g"][0])
    shared["fin_g"] = rep(inputs["final_norm_g"])
    shared["fox_g"] = rep(inputs["fox_out_g"][0])
    shared["sb_g"] = rep(inputs["sb_out_g"][0])
    shared["fb"] = rep(inputs["forget_bias"][0])
    cwv = np.asarray(inputs["conv_w"], np.float32)[0]
    shared["cw"] = np.ascontiguousarray(cwv.reshape(3, NCH, 128).transpose(2, 1, 0))
    shared["cb"] = np.ascontiguousarray(np.asarray(inputs["conv_b"], np.float32)[0].reshape(NCH, 128).T)
    in_maps = []
    owns = []
    for c in range(8):
        b, par = c // 2, c % 2
        cc, own = _core_consts(S, par)
        owns.append(own)
        xo = np.zeros((NSL * 512 + 16, D), np.float32)
        for j, t in enumerate(own):
            xo[j * 512:(j + 1) * 512] = x[b, t * 512:(t + 1) * 512]
            if t > 0:
                xo[NSL * 512 + 2 * j:NSL * 512 + 2 * j + 2] = x[b, t * 512 - 2:t * 512]
        m = dict(shared)
        m.update(cc)
        m["xn"] = np.ascontiguousarray(x[b])
        m["xo"] = xo
        in_maps.append(m)
    return in_maps, owns, (B, S)


def kernel(**inputs):
    in_maps, owns, (B, S) = _prepare(inputs)
    if S not in _CACHE:
        _CACHE[S] = build_program(S)
    nc, _ = _CACHE[S]
    res = run_bass_kernel_spmd(nc, in_maps, core_ids=list(range(8)))
    out = np.zeros((B, S, D), np.float32)
    for c in range(8):
        b = c // 2
        o = np.asarray(res.results[c]["out"], np.float32)
        for j, t in enumerate(owns[c]):
            out[b, t * 512:(t + 1) * 512] = o[j * 512:(j + 1) * 512]
    return out
```

```python
import numpy as np
import ml_dtypes
from contextlib import ExitStack
import concourse.bass as bass
import concourse.mybir as mybir
from concourse.bass_utils import run_bass_kernel_spmd

F32 = mybir.dt.float32
BF16 = mybir.dt.bfloat16
AF = mybir.ActivationFunctionType
ALU = mybir.AluOpType

D = 1024
DH = 64
DFF = 2816
INC = 3080
EPS = 1e-6
NEG = -30000.0
CQ_F, CK_F, CV_F, CL, CQ_S, CK_S, CV_S = 0, 512, 1024, 1536, 1544, 2056, 2568
NCH = 2 * DFF // 128

import os
OPT = set(os.environ.get("KOPT", "").split(","))
COMPUTE = ("pe", "act", "dve", "pool")
CH = 16000
NDMA = 8


class Prog:
    def __init__(self, nc, gs, tag):
        self.nc = nc
        self.gs = gs
        self.tag = tag
        self.ops = []
        self.last_w = {}
        self.readers = {}

    def op(self, eng, fn, reads=(), writes=(), dma=False):
        j = len(self.ops)
        deps = set()
        for r in reads:
            if r in self.last_w:
                deps.add(self.last_w[r])
        for w in writes:
            if w in self.last_w:
                deps.add(self.last_w[w])
            for rd in self.readers.get(w, ()):
                deps.add(rd)
        deps.discard(j)
        self.ops.append(dict(eng=eng, fn=fn, deps=deps, dma=dma, sig=False))
        for r in reads:
            self.readers.setdefault(r, []).append(j)
        for w in writes:
            self.last_w[w] = j
            self.readers[w] = []
        return j

    def dma(self, q, fn, reads=(), writes=()):
        return self.op(q, fn, reads, writes, dma=True)

    def emit(self):
        nc, ops = self.nc, self.ops
        for j, o in enumerate(ops):
            nd = set()
            for d in o["deps"]:
                p = ops[d]
                if not p["dma"] and not o["dma"] and p["eng"] == o["eng"] == "pe":
                    continue
                nd.add(d)
            o["deps"] = nd
            for d in nd:
                ops[d]["sig"] = True
        cnt = {e: 0 for e in COMPUTE}
        sems = {}

        def getsem(key):
            if key not in sems:
                sems[key] = self.gs.enter_context(nc.semaphore("s%s_%s_%s" % (self.tag, key[0], key[1])))
            return sems[key]

        dcount = {}
        dma_i = {}
        for j, o in enumerate(ops):
            if o["dma"]:
                q = o["eng"]
                k = dma_i.get(q, 0)
                dma_i[q] = k + 1
                key = ("d" + q, k % NDMA)
                dcount[key] = dcount.get(key, 0) + 16
                o["semkey"], o["semval"] = key, dcount[key]
                o["prev"] = (key, dcount[key] - 16)
            elif o["sig"]:
                e = o["eng"]
                c = cnt[e]
                cnt[e] = c + 1
                o["semkey"], o["semval"] = (e, c // CH), c % CH + 1
        last_dma = {}
        for o in ops:
            if o["dma"]:
                last_dma[o["semkey"]] = max(last_dma.get(o["semkey"], 0), o["semval"])
            if "semkey" in o:
                getsem(o["semkey"])
        per = {e: [] for e in ("sp", "act", "dve", "pe", "pool")}
        for j, o in enumerate(ops):
            per[o["eng"]].append(j)
        nwc = [0]

        def run_engine(e, E):
            known = {}
            for j in per[e]:
                o = ops[j]
                need = {}
                for d in o["deps"]:
                    p = ops[d]
                    k, v = p["semkey"], p["semval"]
                    if need.get(k, 0) < v:
                        need[k] = v
                if o["dma"] and o["prev"][1] > 0:
                    k, v = o["prev"]
                    if need.get(k, 0) < v:
                        need[k] = v
                for k, v in need.items():
                    if known.get(k, 0) >= v:
                        continue
                    known[k] = v
                    E.wait_ge(sems[k], v)
                    nwc[0] += 1
                inst = o["fn"](E)
                if o["dma"]:
                    inst.then_inc(sems[o["semkey"]], 16)
                elif o["sig"]:
                    inst.then_inc(sems[o["semkey"]], 1)
            if e == "sp":
                for k, v in last_dma.items():
                    if known.get(k, 0) < v:
                        E.wait_ge(sems[k], v)

        with nc.Block() as block:
            @block.sync
            def _(E):
                run_engine("sp", E)

            @block.scalar
            def _(E):
                run_engine("act", E)

            @block.vector
            def _(E):
                run_engine("dve", E)

            @block.tensor
            def _(E):
                run_engine("pe", E)

            @block.gpsimd
            def _(E):
                run_engine("pool", E)
        self.stats = dict(tag=self.tag, nops=len(ops), nwaits=nwc[0], nsems=len(sems), cnt=cnt)


def MM(P, out, lhsT, rhs, start, stop, rd, wr, skip=False):
    P.op("pe", lambda E: E.matmul(out, lhsT=lhsT, rhs=rhs, start=start, stop=stop, skip_group_check=skip), rd, wr)


def TR(P, out, in_, ident, rd, wr):
    P.op("pe", lambda E: E.transpose(out, in_, ident), rd, wr)


def ACTF(P, out, in_, func, rd, wr, bias=None, scale=None, accum=None):
    kw = {}
    if bias is not None:
        kw["bias"] = bias
    if scale is not None:
        kw["scale"] = scale
    if accum is not None:
        kw["accum_out"] = accum
    P.op("act", lambda E: E.activation(out=out, in_=in_, func=func, **kw), rd, wr)


def DMA(P, q, out, in_, rd, wr):
    P.dma(q, lambda E: E.dma_start(out=out, in_=in_), rd, wr)


class RR:
    def __init__(self, engs):
        self.engs = engs
        self.i = 0

    def copy(self, P, out, in_, rd, wr, scale=None):
        e = self.engs[self.i % len(self.engs)]
        self.i += 1
        if e == "act":
            ACTF(P, out, in_, AF.Copy, rd, wr, scale=scale)
        elif scale is None:
            P.op(e, lambda E: E.tensor_copy(out=out, in_=in_), rd, wr)
        else:
            P.op(e, lambda E: E.tensor_scalar(out=out, in0=in_, scalar1=float(scale), scalar2=None, op0=ALU.mult), rd, wr)


def own_tiles(p, NSL):
    res = []
    for j in range(NSL):
        first = (j % 2 == 0) if p == 0 else (j % 2 == 1)
        res.append(2 * j if first else 2 * j + 1)
    return res


def build_program(S, debug=False, stop_after="d"):
    NT = S // 512
    NSL = NT // 2
    NB = S // 128
    NOWN = NSL * 512
    NQ = NOWN + 16
    nc = bass.Bass("TRN2", target_bir_lowering=False)
    T = {}

    def din(name, shape, dt=F32):
        T[name] = nc.dram_tensor(name, list(shape), dt, kind="ExternalInput").ap()

    def dscr(name, shape, dt=BF16):
        kind = "ExternalOutput" if debug else "Internal"
        T[name] = nc.dram_tensor(name, list(shape), dt, kind=kind).ap()

    din("xn", [S, D]); din("xo", [NQ, D])
    din("w_in", [D, INC]); din("w_out", [D, D]); din("w_up", [D, 2 * DFF]); din("w_down", [DFF, D])
    din("attn_g", [128, D]); din("ffn_g", [128, D]); din("fin_g", [128, D])
    din("fox_g", [128, 512]); din("sb_g", [128, 512]); din("fb", [128, 8])
    din("cw", [128, NCH, 3]); din("cb", [128, NCH])
    din("esel", [128, 2 * NSL]); din("eselh", [128, 2 * NSL]); din("hmask", [128, 16])
    din("idE", [128, NSL, 128], BF16); din("idNE", [128, NSL, 128], BF16); din("erow", [1, NSL, 128], BF16)
    din("mh_f", [128, NB, 16], BF16); din("mh_s", [128, NB, 16], BF16)
    din("ident_bf", [128, 128], BF16); din("ident_f", [128, 128]); din("triu_f", [128, 128]); din("ones_f", [128, 128])
    din("negtri", [128, 128], BF16); din("negones", [128, 128], BF16)
    din("mt_f", [128, 4, 512], BF16); din("mt_s", [128, 4, 512], BF16); din("negrow", [1, 512], BF16)
    din("cm1", [64, 3, 128], BF16); din("cp1", [64, 3, 128], BF16)
    T["out"] = nc.dram_tensor("out", [NOWN, D], F32, kind="ExternalOutput").ap()
    dscr("Kd", [16, 64, S]); dscr("Vd", [NT, 128, 16, 4, 65]); dscr("Qd", [16, 64, NQ])
    dscr("KA", [8, 6, S]); dscr("QA", [8, 6, S]); dscr("Od", [NQ + 112, D])
    dscr("Wup", [128, 8, 2 * DFF]); dscr("Wdn", [128, 22, D]); dscr("Wout", [128, 8, D])
    if debug:
        dscr("dbg_nF", [128, 8, NB], F32)

    stats = []
    with ExitStack() as gs:
        def sbt(es, name, shape, dt):
            return es.enter_context(nc.sbuf_tensor("sb_" + name, list(shape), dt))

        def pst(es, name, shape, dt):
            return es.enter_context(nc.psum_tensor("pp_" + name, list(shape), dt))

        with ExitStack() as es:
            P = Prog(nc, gs, "a")
            win = sbt(es, "win", [128, 8, INC], BF16)
            wstg = [sbt(es, "wstg%d" % i, [128, INC], F32) for i in range(2)]
            g_r = sbt(es, "g_r", [128, D], F32)
            fb_r = sbt(es, "fb_r", [128, 8], F32)
            ident = sbt(es, "ident", [128, 128], BF16)
            xb = [sbt(es, "xb%d" % i, [128, D], F32) for i in range(3)]
            junk = sbt(es, "junk", [128, D], BF16)
            stat = sbt(es, "stat", [128, 3, 8], F32)
            xnb = [sbt(es, "xnb%d" % i, [128, D], BF16) for i in range(2)]
            hT = [sbt(es, "hT%d" % i, [128, 8, 512], BF16) for i in range(2)]
            kts = [sbt(es, "kts%d" % i, [128, 512], BF16) for i in range(3)]
            vs = [sbt(es, "vs%d" % i, [128, 16, 4, 65], BF16) for i in range(2)]
            flog = sbt(es, "flog", [128, 8, NB], F32)
            esp = ExitStack()
            pT = [pst(esp, "pT%d" % i, [128, 8, 128], BF16) for i in range(2)]
            psk = [pst(esp, "psk%d" % i, [128, 512], F32) for i in range(2)]
            psv = [pst(esp, "psv%d" % i, [128, 512], F32) for i in range(2)]
            psf = [pst(esp, "psf%d" % i, [128, 512], F32) for i in range(2)]
            rr = RR(["act", "dve"])

            DMA(P, "sp", g_r[:], T["attn_g"][:, :], [], ["g_r"])
            DMA(P, "sp", fb_r[:], T["fb"][:, :], [], ["fb_r"])
            DMA(P, "sp", ident[:], T["ident_bf"][:, :], [], ["ident"])
            for k in range(8):
                DMA(P, "sp", wstg[k % 2][:], T["w_in"][k * 128:(k + 1) * 128, :], [], ["wstg%d" % (k % 2)])
                P.op("pool" if k % 2 == 0 else "dve", lambda E, k=k: E.tensor_copy(out=win[:, k, :], in_=wstg[k % 2][:]), ["wstg%d" % (k % 2)], ["win.%d" % k])
            for i in range(2):
                P.op("pool", lambda E, i=i: E.memset(vs[i][:], 1.0), [], ["vs%d.%d.%d" % (i, b, g) for b in range(4) for g in range(2)])
                P.op("pool", lambda E, i=i: E.memset(xnb[i][:], 0.0), [], ["xnb%d" % i])
            blkn = [0]

            def norm_block(src_rows, rows, hbuf, col0):
                st = {}

                def part1():
                    n = blkn[0]
                    blkn[0] += 1
                    st["n"] = n
                    x = xb[n % 3]
                    xs = "xb%d" % (n % 3)
                    sk = "stat%d" % (n % 8)
                    DMA(P, "sp", x[:rows, :], src_rows, [], [xs])
                    ACTF(P, junk[:rows, :], x[:rows, :], AF.Square, [xs], ["junk", sk], accum=stat[:rows, 0, n % 8:n % 8 + 1])
                    ACTF(P, stat[:rows, 1, n % 8:n % 8 + 1], stat[:rows, 0, n % 8:n % 8 + 1], AF.Ln, [sk], [sk], bias=EPS, scale=1.0 / D)
                    ACTF(P, stat[:rows, 2, n % 8:n % 8 + 1], stat[:rows, 1, n % 8:n % 8 + 1], AF.Exp, [sk], [sk], scale=-0.5)
                    xq = xnb[n % 2]
                    qs = "xnb%d" % (n % 2)
                    P.op("dve", lambda E: E.scalar_tensor_tensor(out=xq[:rows, :], in0=x[:rows, :], scalar=stat[:rows, 2, n % 8:n % 8 + 1],
                                                                 in1=g_r[:rows, :], op0=ALU.mult, op1=ALU.mult), [xs, sk, "g_r"], [qs])

                def part2():
                    n = st["n"]
                    xq = xnb[n % 2]
                    qs = "xnb%d" % (n % 2)
                    pt = pT[n % 2]
                    ps = "pT%d" % (n % 2)
                    for k in range(8):
                        TR(P, pt[:, k, :], xq[:, k * 128:(k + 1) * 128], ident[:, :], [qs, "ident"], [ps])
                    rr.copy(P, hT[hbuf][:, :, col0:col0 + rows], pt[:, :, :rows], [ps], ["hT%d.%d" % (hbuf, col0 // 128)])
                return part1, part2

            def proj_T(hbuf, width, col_w, dst, hkeys, scale=None):
                n = proj_T.n
                proj_T.n += 1
                ps = psk[n % 2]
                pk = "psk%d" % (n % 2)
                for k in range(8):
                    MM(P, ps[:, :width], win[:, k, col_w:col_w + 128], hT[hbuf][:, k, :width], k == 0, k == 7, ["win.%d" % k] + hkeys, [pk])
                ks = kts[n % 3]
                kk = "kts%d" % (n % 3)
                rr.copy(P, ks[:, :width], ps[:, :width], [pk], [kk], scale=scale)
                DMA(P, "pool", dst, ks[:, :width], [kk], ["KQscr"])
            proj_T.n = 0

            vcount = [0]

            def make_tile_A(Tn, hb):
                blocks = [norm_block(T["xn"][Tn * 512 + b * 128:Tn * 512 + (b + 1) * 128, :], 128, hb, b * 128) for b in range(4)]
                hk = ["hT%d.%d" % (hb, b) for b in range(4)]
                vb = Tn % 2
                groups = []

                def kgrp(c):
                    col = (CK_F + c * 128) if c < 4 else (CK_S + (c - 4) * 128)
                    h0 = 2 * c if c < 4 else 8 + 2 * (c - 4)
                    proj_T(hb, 512, col, T["Kd"][h0:h0 + 2, :, Tn * 512:(Tn + 1) * 512].rearrange("h r t -> (h r) t"), hk)

                def vgrp(b, g):
                    n = vcount[0]
                    vcount[0] += 1
                    ps = psv[n % 2]
                    pk = "psv%d" % (n % 2)
                    vc = CV_F if g == 0 else CV_S
                    for k in range(8):
                        MM(P, ps[:, :], hT[hb][:, k, b * 128:(b + 1) * 128], win[:, k, vc:vc + 512], k == 0, k == 7,
                           ["win.%d" % k, "hT%d.%d" % (hb, b)], [pk])
                    rr.copy(P, vs[vb][:, g * 8:(g + 1) * 8, b, 0:64], ps[:, :].rearrange("p (h d) -> p h d", h=8), [pk],
                            ["vs%d.%d.%d" % (vb, b, g)])
                    if g == 1:
                        pf = psf[b % 2]
                        for k in range(8):
                            MM(P, pf[:, 0:8], hT[hb][:, k, b * 128:(b + 1) * 128], win[:, k, CL:CL + 8], k == 0, k == 7,
                               ["win.%d" % k, "hT%d.%d" % (hb, b)], ["psf%d" % (b % 2)])
                        P.op("dve", lambda E: E.tensor_tensor(out=flog[:, :, Tn * 4 + b], in0=pf[:, 0:8], in1=fb_r[:, :], op=ALU.add),
                             ["psf%d" % (b % 2), "fb_r"], ["flog"])

                def vstore():
                    DMA(P, "sp", T["Vd"][Tn, :, :, :, :], vs[vb][:, :, :, :],
                        ["vs%d.%d.%d" % (vb, b, g) for b in range(4) for g in range(2)], ["Vscr"])
                for c in range(8):
                    groups.append(lambda c=c: kgrp(c))
                for b in range(4):
                    for g in range(2):
                        groups.append(lambda b=b, g=g: vgrp(b, g))
                groups.append(vstore)
                return blocks, groups

            def make_tile_Q(row0, width, col0, hb):
                nb = (width + 127) // 128
                blocks = []
                for b in range(nb):
                    rows = min(128, width - b * 128)
                    blocks.append(norm_block(T["xo"][row0 + b * 128:row0 + b * 128 + rows, :], rows, hb, b * 128))
                hk = ["hT%d.%d" % (hb, b) for b in range(nb)]

                def qgrp(c):
                    col = (CQ_F + c * 128) if c < 4 else (CQ_S + (c - 4) * 128)
                    h0 = 2 * c if c < 4 else 8 + 2 * (c - 4)
                    proj_T(hb, width, col, T["Qd"][h0:h0 + 2, :, col0:col0 + width].rearrange("h r t -> (h r) t"), hk, scale=0.125)
                groups = [(lambda c=c: qgrp(c)) for c in range(8)]
                return blocks, groups

            seq = []
            qi = 0
            for Tn in range(NT):
                seq.append(("A", Tn))
                if Tn % 2 == 1 and qi < NSL:
                    seq.append(("Q", qi))
                    qi += 1
            seq.append(("H", 0))
            tiles = []
            for i, (kind, idx) in enumerate(seq):
                hb = i % 2
                if kind == "A":
                    tiles.append(make_tile_A(idx, hb))
                elif kind == "Q":
                    tiles.append(make_tile_Q(idx * 512, 512, idx * 512, hb))
                else:
                    tiles.append(make_tile_Q(NOWN, 16, NOWN, hb))
            for (p1, p2) in tiles[0][0]:
                p1()
                p2()
            for i, (blocks, groups) in enumerate(tiles):
                nxt = tiles[i + 1][0] if i + 1 < len(tiles) else []
                G = len(groups)
                nb_ = max(1, len(nxt))
                ev = {}
                for b, (p1, p2) in enumerate(nxt):
                    ev.setdefault(int(b * G / nb_), []).append(p1)
                    ev.setdefault(min(G - 1, int((b + 0.7) * G / nb_)), []).append(p2)
                for gi, g in enumerate(groups):
                    g()
                    for f_ in ev.get(gi, []):
                        f_()
            P.emit()
            stats.append(P.stats)
            esp.close()
            if stop_after == "a":
                return nc, stats

            with ExitStack() as es2:
                P = Prog(nc, gs, "b")
                nlf = sbt(es2, "nlf", [128, 8 * NB], F32)
                ee = sbt(es2, "ee", [128, 8 * NB], F32)
                sc = [sbt(es2, "sc%d" % i, [128, 8, NB], F32) for i in range(2)]
                tot = sbt(es2, "tot", [128, 8, NB], F32)
                nF = sbt(es2, "nF", [128, 8, NB], F32)
                nFp = sbt(es2, "nFp", [128, 8, 128], F32)
                nFT = sbt(es2, "nFT", [NB, 8, 128], F32)
                r1 = sbt(es2, "r1", [NB, 8, 128], F32)
                parts = [sbt(es2, "part%d" % i, [NB, 8, 128], BF16) for i in range(3)]
                triu = sbt(es2, "triu", [128, 128], F32)
                ones = sbt(es2, "ones", [128, 128], F32)
                identf = sbt(es2, "identf", [128, 128], F32)
                cm1 = sbt(es2, "cm1", [64, 3, 128], BF16)
                cp1 = sbt(es2, "cp1", [64, 3, 128], BF16)
                ps_c = pst(es2, "ps_c", [128, 512], F32)
                ps_t = pst(es2, "ps_t", [128, 512], F32)
                ps_x = [pst(es2, "ps_x%d" % i, [128, 4, 128], F32) for i in range(2)]
                W8 = 8 * NB
                DMA(P, "sp", triu[:], T["triu_f"][:, :], [], ["triu"])
                DMA(P, "sp", ones[:], T["ones_f"][:, :], [], ["ones"])
                DMA(P, "sp", identf[:], T["ident_f"][:, :], [], ["identf"])
                DMA(P, "sp", cm1[:], T["cm1"][:, :, :], [], ["cm1"])
                DMA(P, "sp", cp1[:], T["cp1"][:, :, :], [], ["cp1"])
                fl2 = flog[:, :, :].rearrange("p h b -> p (h b)")
                ACTF(P, ee[:, :], fl2, AF.Exp, [], ["ee"], scale=-1.0)
                ACTF(P, nlf[:, :], ee[:, :], AF.Ln, ["ee"], ["nlf"], bias=1.0)
                MM(P, ps_c[:, :W8], triu[:, :], nlf[:, :], True, True, ["triu", "nlf"], ["ps_c"])
                MM(P, ps_t[:, :W8], ones[:, :], nlf[:, :], True, True, ["ones", "nlf"], ["ps_t"])
                P.op("dve", lambda E: E.tensor_copy(out=tot[:, :, :], in_=ps_t[:, :W8].rearrange("p (h b) -> p h b", h=8)), ["ps_t"], ["tot"])
                P.op("dve", lambda E: E.tensor_copy(out=sc[0][:, :, :], in_=tot[:, :, :]), ["tot"], ["sc0"])
                cur = 0
                d = 1
                while d < NB:
                    nxt = 1 - cur
                    P.op("dve", lambda E, cur=cur, nxt=nxt, d=d: E.tensor_copy(out=sc[nxt][:, :, 0:d], in_=sc[cur][:, :, 0:d]), ["sc%d" % cur], ["sc%d" % nxt])
                    P.op("dve", lambda E, cur=cur, nxt=nxt, d=d: E.tensor_tensor(out=sc[nxt][:, :, d:NB], in0=sc[cur][:, :, d:NB], in1=sc[cur][:, :, 0:NB - d], op=ALU.add),
                         ["sc%d" % cur], ["sc%d" % nxt])
                    cur = nxt
                    d *= 2
                P.op("dve", lambda E, cur=cur: E.tensor_tensor(out=tot[:, :, :], in0=sc[cur][:, :, :], in1=tot[:, :, :], op=ALU.subtract), ["sc%d" % cur, "tot"], ["tot"])
                P.op("dve", lambda E: E.tensor_tensor(out=nF[:, :, :], in0=ps_c[:, :W8].rearrange("p (h b) -> p h b", h=8), in1=tot[:, :, :], op=ALU.add), ["ps_c", "tot"], ["nF"])
                if debug:
                    DMA(P, "sp", T["dbg_nF"][:, :, :], nF[:, :, :], ["nF"], ["dbg"])
                P.op("dve", lambda E: E.memset(nFp[:], 0.0), [], ["nFp"])
                P.op("dve", lambda E: E.tensor_copy(out=nFp[:, :, 0:NB], in_=nF[:, :, :]), ["nF", "nFp"], ["nFp"])
                for h in range(8):
                    px = ps_x[h // 4]
                    P.op("pe", lambda E, h=h, px=px: E.transpose(px[:, h % 4, :], nFp[:, h, :], identf[:, :]), ["nFp", "identf"], ["ps_x%d" % (h // 4)])
                for q in range(2):
                    P.op("dve", lambda E, q=q: E.tensor_copy(out=nFT[:, q * 4:(q + 1) * 4, :], in_=ps_x[q][:NB, :, :]), ["ps_x%d" % q], ["nFT"])
                P.op("dve", lambda E: E.tensor_copy(out=parts[0][:, :, :], in_=nFT[:, :, :]), ["nFT"], ["part0"])
                P.op("dve", lambda E: E.tensor_tensor(out=r1[:, :, :], in0=nFT[:, :, :], in1=parts[0][:, :, :], op=ALU.subtract), ["nFT", "part0"], ["r1"])
                P.op("dve", lambda E: E.tensor_copy(out=parts[1][:, :, :], in_=r1[:, :, :]), ["r1"], ["part1"])
                P.op("dve", lambda E: E.tensor_tensor(out=nFT[:, :, :], in0=r1[:, :, :], in1=parts[1][:, :, :], op=ALU.subtract), ["r1", "part1"], ["nFT"])
                P.op("dve", lambda E: E.tensor_copy(out=parts[2][:, :, :], in_=nFT[:, :, :]), ["nFT"], ["part2"])
                for i in range(3):
                    DMA(P, "sp", T["KA"][:, i, :].rearrange("h (b t) -> b h t", t=128), parts[i][:, :, :], ["part%d" % i], ["KAs"])
                    DMA(P, "sp", T["QA"][:, 3 + i, :].rearrange("h (b t) -> b h t", t=128), parts[i][:, :, :], ["part%d" % i], ["QAs"])
                for h in range(8):
                    DMA(P, "sp", T["KA"][h, 3:6, :].rearrange("r (b t) -> b r t", t=128), cm1[:NB, :, :], ["cm1"], ["KAs"])
                    DMA(P, "sp", T["QA"][h, 0:3, :].rearrange("r (b t) -> b r t", t=128), cp1[:NB, :, :], ["cp1"], ["QAs"])
                P.emit()
                stats.append(P.stats)
        if stop_after == "b":
            return nc, stats

        with ExitStack() as es:
            P = Prog(nc, gs, "c")
            KT = [sbt(es, "KT%d" % i, [70, S], BF16) for i in range(2)]
            QT = [sbt(es, "QT%d" % i, [70, NQ], BF16) for i in range(2)]
            VV = [sbt(es, "VV%d" % i, [128, NB, 65], BF16) for i in range(2)]
            qa = sbt(es, "qa", [70, S], BF16)
            qtmp = sbt(es, "qtmp", [70, 512], BF16)
            esel = sbt(es, "esel", [128, 2 * NSL], F32)
            eselh = sbt(es, "eselh", [128, 2 * NSL], F32)
            idE = sbt(es, "idE", [128, NSL, 128], BF16)
            idNE = sbt(es, "idNE", [128, NSL, 128], BF16)
            erow = sbt(es, "erow", [1, NSL, 128], BF16)
            negrow = sbt(es, "negrow", [1, 512], BF16)
            allneg = sbt(es, "allneg", [128, 512], BF16)
            ident = sbt(es, "ident2", [128, 128], BF16)
            mt = [sbt(es, "mt%d" % i, [128, 4, 512], BF16) for i in range(2)]
            mh = [sbt(es, "mh%d" % i, [128, NB, 16], BF16) for i in range(2)]
            negtri = sbt(es, "negtri", [128, 128], BF16)
            negones = sbt(es, "negones", [128, 128], BF16)
            PT = [sbt(es, "PT%d" % i, [128, 512], BF16) for i in range(3)]
            UU = [sbt(es, "UU%d" % i, [128, 512], F32) for i in range(2)]
            LL = [sbt(es, "LL%d" % i, [128, 512], BF16) for i in range(3)]
            LS = [sbt(es, "LS%d" % i, [128, 512], BF16) for i in range(3)]
            rden = sbt(es, "rden", [128, 2, 4], F32)
            ob = [sbt(es, "ob%d" % i, [128, 4, 64], BF16) for i in range(3)]
            cst = [sbt(es, "cst%d" % i, [128, 2816], F32) for i in range(2)]
            cbf = [sbt(es, "cbf%d" % i, [128, 2816], BF16) for i in range(2)]
            jobs = []
            for k in range(8):
                for hf in range(2):
                    jobs.append((T["w_up"][k * 128:(k + 1) * 128, hf * 2816:(hf + 1) * 2816], T["Wup"][:, k, hf * 2816:(hf + 1) * 2816], 2816, None))
            for c in range(0, 22, 2):
                jobs.append((T["w_down"][c * 128:(c + 2) * 128, :].rearrange("(c p) n -> p c n", p=128), T["Wdn"][:, c:c + 2, :], 2048, 2))
            for k in range(0, 8, 2):
                jobs.append((T["w_out"][k * 128:(k + 2) * 128, :].rearrange("(c p) n -> p c n", p=128), T["Wout"][:, k:k + 2, :], 2048, 2))
            jobn = [0]

            def conv_job():
                n = jobn[0]
                if n >= len(jobs):
                    return
                jobn[0] += 1
                src, dst, width, sub = jobs[n]
                b = n % 2
                if sub is None:
                    s_ap, b_ap = cst[b][:, :width], cbf[b][:, :width]
                else:
                    s_ap = cst[b][:, :width].rearrange("p (c n) -> p c n", c=sub)
                    b_ap = cbf[b][:, :width].rearrange("p (c n) -> p c n", c=sub)
                DMA(P, "sp", s_ap, src, [], ["cst%d" % b])
                P.op("dve", lambda E: E.tensor_copy(out=cbf[b][:, :width], in_=cst[b][:, :width]), ["cst%d" % b], ["cbf%d" % b])
                DMA(P, "sp", dst, b_ap, ["cbf%d" % b], ["Wscr"])
            psZ = [pst(es, "psZ%d" % i, [128, 512], F32) for i in range(4)]
            psO = [pst(es, "psO%d" % i, [128, 512], F32) for i in range(2)]
            for nm, t_, src in (("esel", esel, T["esel"][:, :]), ("eselh", eselh, T["eselh"][:, :]), ("idE", idE, T["idE"][:, :, :]),
                                ("idNE", idNE, T["idNE"][:, :, :]), ("erow", erow, T["erow"][:, :, :]), ("negrow", negrow, T["negrow"][:, :]),
                                ("ident", ident, T["ident_bf"][:, :]), ("mt0", mt[0], T["mt_f"][:, :, :]), ("mt1", mt[1], T["mt_s"][:, :, :]),
                                ("mh0", mh[0], T["mh_f"][:, :, :]), ("mh1", mh[1], T["mh_s"][:, :, :]),
                                ("negtri", negtri, T["negtri"][:, :]), ("negones", negones, T["negones"][:, :])):
                DMA(P, "sp", t_[:], src, [], [nm])
            P.op("pool", lambda E: E.memset(allneg[:], NEG), [], ["allneg"])
            for i in range(3):
                P.op("pool", lambda E, i=i: E.memset(PT[i][:], 0.0), [], ["PT%d" % i])
            CONSTS = ["idE", "idNE", "erow", "negrow", "ident", "mt0", "mt1", "mh0", "mh1", "allneg"]

            def load_head(hh):
                hb = hh % 2
                fox = hh < 8
                DMA(P, "sp", KT[hb][0:64, :], T["Kd"][hh, :, :], [], ["KTm%d" % hb])
                DMA(P, "sp", QT[hb][0:64, :], T["Qd"][hh, :, :], [], ["QTm%d" % hb])
                DMA(P, "sp", VV[hb][:, :, :].rearrange("p (t b) d -> p t b d", b=4), T["Vd"][:, :, hh, :, :].rearrange("t p b d -> p t b d"), [], ["VV%d" % hb])
                if not fox:
                    P.op("pool", lambda E: E.memset(KT[hb][64:70, :], 0.0), [], ["KTa%d" % hb])
                    P.op("pool", lambda E: E.memset(QT[hb][64:70, :], 0.0), [], ["QTa%d" % hb])
                if fox:
                    DMA(P, "sp", KT[hb][64:70, :], T["KA"][hh, :, :], [], ["KTa%d" % hb])
                    DMA(P, "sp", qa[64:70, :], T["QA"][hh, :, :], [], ["qa"])
                    for j in range(NSL + 1):
                        if j < NSL:
                            c0 = qa[64:70, 1024 * j:1024 * j + 512]
                            c1 = qa[64:70, 1024 * j + 512:1024 * j + 1024]
                            e0 = esel[64:70, 2 * j:2 * j + 1]
                            e1 = esel[64:70, 2 * j + 1:2 * j + 2]
                            dst = QT[hb][64:70, 512 * j:512 * j + 512]
                            tmp = qtmp[64:70, 0:512]
                            P.op("dve", lambda E, c0=c0, e0=e0, tmp=tmp: E.tensor_scalar(out=tmp, in0=c0, scalar1=e0, scalar2=None, op0=ALU.mult),
                                 ["qa", "esel"], ["qtmp"])
                            P.op("dve", lambda E, c1=c1, e1=e1, tmp=tmp, dst=dst: E.scalar_tensor_tensor(out=dst, in0=c1, scalar=e1, in1=tmp, op0=ALU.mult, op1=ALU.add),
                                 ["qa", "esel", "qtmp"], ["QTa%d" % hb])
                        else:
                            for jj in range(NSL):
                                a0 = max(0, 1024 * jj - 2)
                                c0 = qa[64:70, a0:a0 + 2]
                                c1 = qa[64:70, 1024 * jj + 510:1024 * jj + 512]
                                e0 = eselh[64:70, 2 * jj:2 * jj + 1]
                                e1 = eselh[64:70, 2 * jj + 1:2 * jj + 2]
                                dst = QT[hb][64:70, NOWN + 2 * jj:NOWN + 2 * jj + 2]
                                tmp = qtmp[64:70, 0:2]
                                P.op("dve", lambda E, c0=c0, e0=e0, tmp=tmp: E.tensor_scalar(out=tmp, in0=c0, scalar1=e0, scalar2=None, op0=ALU.mult),
                                     ["qa", "eselh"], ["qtmp"])
                                P.op("dve", lambda E, c1=c1, e1=e1, tmp=tmp, dst=dst: E.scalar_tensor_tensor(out=dst, in0=c1, scalar=e1, in1=tmp, op0=ALU.mult, op1=ALU.add),
                                     ["qa", "eselh", "qtmp"], ["QTa%d" % hb])

            units = []
            slot_ctr = 0
            for hh in range(16):
                fox = hh < 8
                if "noSB" in OPT and not fox:
                    continue
                if "noFOX" in OPT and fox:
                    continue
                for sl in range(NSL + 1):
                    halo = sl == NSL
                    if halo and "noHalo" in OPT:
                        continue
                    W = 16 if halo else 512
                    nkb = NB if halo else 8 * (sl + 1)
                    order = list(range(nkb)) if fox else list(range(nkb - 1, -1, -1))
                    for idx, kb in enumerate(order):
                        if halo:
                            mk = ("H", kb)
                        elif 8 * sl <= kb < 8 * sl + 4:
                            mk = ("E", kb - 8 * sl)
                        elif 8 * sl + 4 <= kb < 8 * sl + 8:
                            mk = ("NE", kb - 8 * sl - 4)
                        else:
                            mk = None
                        units.append(dict(hh=hh, fox=fox, sl=sl, W=W, kb=kb, first=idx == 0, last=idx == nkb - 1, mk=mk,
                                          so=slot_ctr, qc=NOWN if halo else 512 * sl))
                    slot_ctr += 1
            for i, u in enumerate(units):
                u["i"] = i
                u["head_first"] = (i == 0 or units[i - 1]["hh"] != u["hh"])
                u["head_last"] = (i == len(units) - 1 or units[i + 1]["hh"] != u["hh"])

            def stage_A(u):
                hb = u["hh"] % 2
                KD = 70
                W, kb, i = u["W"], u["kb"], u["i"]
                z = psZ[i % 4]
                zk = "psZ%d" % (i % 4)
                mi = 0 if u["fox"] else 1
                rd = ["KTm%d" % hb, "QTm%d" % hb, "KTa%d" % hb, "QTa%d" % hb]
                mk = u["mk"] if "noMask" not in OPT else None
                MM(P, z[:, :W], KT[hb][0:KD, kb * 128:(kb + 1) * 128], QT[hb][0:KD, u["qc"]:u["qc"] + W], True, mk is None and u["fox"], rd, [zk])
                if mk is not None:
                    typ, m = mk
                    sl = u["sl"]
                    if typ == "H":
                        MM(P, z[:, :W], ident[:, :], mh[mi][:, m, :], False, u["fox"], CONSTS, [zk])
                    elif typ == "E":
                        MM(P, z[:, :W], idE[:, sl, :], mt[mi][:, m, :], False, u["fox"], CONSTS, [zk])
                    else:
                        MM(P, z[:, :W], idNE[:, sl, :], mt[mi][:, m, :], False, False, CONSTS, [zk])
                        MM(P, z[:, :W], idE[:, sl, :], allneg[:, :W], False, u["fox"], CONSTS, [zk])

            def stage_B_fox(u):
                W, i = u["W"], u["i"]
                ACTF(P, PT[i % 3][:, :W], psZ[i % 4][:, :W], AF.Exp, ["psZ%d" % (i % 4)], ["PT%d" % (i % 3)])

            def stage_B1(u):
                W, i = u["W"], u["i"]
                ACTF(P, UU[i % 2][:, :W], psZ[i % 4][:, :W], AF.Exp, ["psZ%d" % (i % 4)], ["UU%d" % (i % 2)])
                ACTF(P, LL[i % 3][:, :W], UU[i % 2][:, :W], AF.Ln, ["UU%d" % (i % 2)], ["LL%d" % (i % 3)], bias=1.0)
                if not u["last"]:
                    if u["first"]:
                        P.op("pool", lambda E: E.tensor_copy(out=LS[(i + 1) % 3][:, :W], in_=LL[i % 3][:, :W]), ["LL%d" % (i % 3)], ["LS%d" % ((i + 1) % 3)])
                    else:
                        P.op("pool", lambda E: E.tensor_tensor(out=LS[(i + 1) % 3][:, :W], in0=LS[i % 3][:, :W], in1=LL[i % 3][:, :W], op=ALU.add),
                             ["LL%d" % (i % 3), "LS%d" % (i % 3)], ["LS%d" % ((i + 1) % 3)])

            def stage_C(u):
                W, i = u["W"], u["i"]
                z = psZ[i % 4]
                zk = "psZ%d" % (i % 4)
                MM(P, z[:, :W], negtri[:, :], LL[i % 3][:, :W], False, u["first"], ["negtri", "LL%d" % (i % 3)], [zk])
                if not u["first"]:
                    MM(P, z[:, :W], negones[:, :], LS[i % 3][:, :W], False, True, ["negones", "LS%d" % (i % 3)], [zk])

            def stage_B2(u):
                W, i = u["W"], u["i"]
                ACTF(P, PT[i % 3][:, :W], psZ[i % 4][:, :W], AF.Exp, ["psZ%d" % (i % 4)], ["PT%d" % (i % 3)])

            fin_ctr = [0]

            def stage_D(u):
                hb = u["hh"] % 2
                W, kb, i = u["W"], u["kb"], u["i"]
                o = psO[u["so"] % 2]
                ok = "psO%d" % (u["so"] % 2)
                NV = 65 if u["fox"] else 64
                nsub = max(1, W // 128)
                M = min(128, W)
                for s_ in range(nsub):
                    MM(P, o[:, s_ * NV:(s_ + 1) * NV], PT[i % 3][:, s_ * 128:s_ * 128 + 128], VV[hb][:, kb, 0:NV],
                       u["first"] and s_ == 0, u["last"] and s_ == nsub - 1, ["PT%d" % (i % 3), "VV%d" % hb], [ok], skip=True)
                if u["last"]:
                    f = fin_ctr[0]
                    fin_ctr[0] += 1
                    obuf = ob[f % 3]
                    obk = "ob%d" % (f % 3)
                    ov = o[:M, 0:nsub * NV].rearrange("p (s v) -> p s v", s=nsub)
                    if u["fox"]:
                        rk = "rden%d" % (f % 2)
                        P.op("dve", lambda E: E.reciprocal(out=rden[:M, f % 2, 0:nsub], in_=ov[:, :, 64]), [ok], [rk])
                        for s_ in range(nsub):
                            P.op("dve", lambda E, s_=s_: E.tensor_scalar(out=obuf[:M, s_, :], in0=ov[:, s_, 0:64], scalar1=rden[:M, f % 2, s_:s_ + 1],
                                                                         scalar2=None, op0=ALU.mult), [ok, rk], [obk])
                    else:
                        P.op("dve", lambda E: E.tensor_copy(out=obuf[:M, 0:nsub, :], in_=ov[:, :, 0:64]), [ok], [obk])
                    r0 = u["qc"]
                    hh = u["hh"]
                    if nsub == 4:
                        dst = T["Od"][r0:r0 + 512, hh * 64:(hh + 1) * 64].rearrange("(s p) d -> p s d", p=128)
                        DMA(P, "sp", dst, obuf[:, :, :], [obk], ["Od"])
                    else:
                        DMA(P, "sp", T["Od"][r0:r0 + M, hh * 64:(hh + 1) * 64], obuf[:M, 0, :], [obk], ["Od"])

            n = len(units)
            hh0 = units[0]["hh"]
            load_head(hh0)
            stage_A(units[0])
            job_every = max(1, n // (len(jobs) + 2))
            for i in range(n):
                u = units[i]
                if u["head_first"] and u["hh"] + 1 < 16 and u["hh"] == hh0:
                    load_head(hh0 + 1)
                if i % job_every == job_every - 1:
                    conv_job()
                if i + 1 < n:
                    stage_A(units[i + 1])
                if u["fox"]:
                    stage_B_fox(u)
                else:
                    stage_B1(u)
                    stage_C(u)
                if i >= 1:
                    pu = units[i - 1]
                    if not pu["fox"]:
                        stage_B2(pu)
                    stage_D(pu)
                    if pu["head_last"] and pu["hh"] + 2 < 16:
                        load_head(pu["hh"] + 2)
            pu = units[n - 1]
            if not pu["fox"]:
                stage_B2(pu)
            stage_D(pu)
            while jobn[0] < len(jobs):
                conv_job()
            P.emit()
            stats.append(P.stats)
        if stop_after == "c":
            return nc, stats

        with ExitStack() as es:
            P = Prog(nc, gs, "d")
            wout = sbt(es, "wout", [128, 8, D], BF16)
            fox_g = sbt(es, "fox_g", [128, 512], F32)
            sb_g = sbt(es, "sb_g", [128, 512], F32)
            ffn_g = sbt(es, "ffn_g", [128, D], F32)
            fin_g = sbt(es, "fin_g", [128, D], F32)
            cw = sbt(es, "cw", [128, NCH, 3], F32)
            cb = sbt(es, "cb", [128, NCH], F32)
            hmask = sbt(es, "hmask", [128, 16], F32)
            ident = sbt(es, "ident3", [128, 128], BF16)
            identf = sbt(es, "identf3", [128, 128], F32)
            o_s = [sbt(es, "o_s%d" % i, [128, 4, D], BF16) for i in range(2)]
            xr = [sbt(es, "xr%d" % i, [128, 4, D], F32) for i in range(2)]
            junk = sbt(es, "junk3", [128, D], BF16)
            stat = sbt(es, "stat3", [128, 3, 16], F32)
            on = sbt(es, "on", [128, 4, D], BF16)
            onT = sbt(es, "onT", [128, 8, 512], BF16)
            h2T = sbt(es, "h2T", [128, 8, 512], BF16)
            wu = [sbt(es, "wu%d" % i, [128, 8, 2, 128], BF16) for i in range(3)]
            cv = [sbt(es, "cv%d" % i, [128, 512], F32) for i in range(6)]
            sg = [sbt(es, "sg%d" % i, [128, 512], F32) for i in range(2)]
            aT = sbt(es, "aT", [128, 22, 512], BF16)
            uh = sbt(es, "uh", [128, NCH, 16], F32)
            wd = [sbt(es, "wd%d" % i, [128, 22, 128], BF16) for i in range(3)]
            yT = [sbt(es, "yT%d" % i, [128, 512], F32) for i in range(2)]
            pT = [pst(es, "p3T%d" % i, [128, 8, 128], BF16) for i in range(2)]
            psa = [pst(es, "psa%d" % i, [128, 512], F32) for i in range(2)]
            psu = [pst(es, "psu%d" % i, [128, 512], F32) for i in range(2)]
            psy = pst(es, "psy", [128, 512], F32)
            pst_ = pst(es, "pst", [128, 4, 128], F32)
            rr = RR(["act", "dve"])
            PSU = [psa[0], psa[1], psu[0], psu[1]]
            PSUK = ["psa0", "psa1", "psu0", "psu1"]
            for nm, t_, src in (("wout", wout, T["Wout"][:, :, :]), ("fox_g", fox_g, T["fox_g"][:, :]), ("sb_g", sb_g, T["sb_g"][:, :]),
                                ("ffn_g", ffn_g, T["ffn_g"][:, :]), ("fin_g", fin_g, T["fin_g"][:, :]), ("cw", cw, T["cw"][:, :, :]),
                                ("cb", cb, T["cb"][:, :]), ("hmask", hmask, T["hmask"][:, :]), ("ident", ident, T["ident_bf"][:, :]),
                                ("identf", identf, T["ident_f"][:, :])):
                DMA(P, "sp", t_[:], src, [], [nm])

            P.op("pool", lambda E: E.memset(on[:], 0.0), [], ["on"])
            P.op("pool", lambda E: E.memset(onT[:], 0.0), [], ["onT.%d" % b for b in range(4)])
            P.op("pool", lambda E: E.memset(h2T[:], 0.0), [], ["h2T.%d" % b for b in range(4)])
            sctr = [0]
            pctr = [0]
            wuc = [0]
            wdc = [0]

            def rstd_of(src_ap, rows, width):
                c = sctr[0] % 16
                sctr[0] += 1
                sk = "st%d" % c
                ACTF(P, junk[:rows, :width], src_ap, AF.Square, src_ap_keys[0], ["junk", sk], accum=stat[:rows, 0, c:c + 1])
                ACTF(P, stat[:rows, 1, c:c + 1], stat[:rows, 0, c:c + 1], AF.Ln, [sk], [sk], bias=EPS, scale=1.0 / width)
                ACTF(P, stat[:rows, 2, c:c + 1], stat[:rows, 1, c:c + 1], AF.Exp, [sk], [sk], scale=-0.5)
                return stat[:rows, 2, c:c + 1], sk
            src_ap_keys = [None]

            def transpose_to(src_tile, src_key, blocks, dstT, dst_key):
                for (bi, rows) in blocks:
                    n = pctr[0]
                    pctr[0] += 1
                    pt = pT[n % 2]
                    pk = "p3T%d" % (n % 2)
                    for k in range(8):
                        TR(P, pt[:, k, :], src_tile[:, bi, k * 128:(k + 1) * 128], ident[:, :], [src_key, "ident"], [pk])
                    rr.copy(P, dstT[:, :, bi * 128:bi * 128 + rows], pt[:, :, :rows], [pk], [dst_key + ".%d" % bi])

            slots = [NSL] + list(range(NSL))

            def load_slot(si):
                sl = slots[si]
                halo = sl == NSL
                r0 = NOWN if halo else 512 * sl
                sb_ = si % 2
                os_, xr_ = o_s[sb_], xr[sb_]
                osk, xrk = "o_s%d" % sb_, "xr%d" % sb_
                if halo:
                    DMA(P, "sp", os_[:16, 0, :], T["Od"][r0:r0 + 16, :], [], [osk])
                    DMA(P, "sp", xr_[:16, 0, :], T["xo"][r0:r0 + 16, :], [], [xrk])
                else:
                    DMA(P, "sp", os_[:, :, :], T["Od"][r0:r0 + 512, :].rearrange("(b p) d -> p b d", p=128), [], [osk])
                    DMA(P, "sp", xr_[:, :, :], T["xo"][r0:r0 + 512, :].rearrange("(b p) d -> p b d", p=128), [], [xrk])

            deferred = []

            def do_slot(si, sl):
                halo = sl == NSL
                W = 16 if halo else 512
                r0 = NOWN if halo else 512 * sl
                blocks = [(0, 16)] if halo else [(b, 128) for b in range(4)]
                sb_ = si % 2
                os_, xr_ = o_s[sb_], xr[sb_]
                osk, xrk = "o_s%d" % sb_, "xr%d" % sb_
                if si == 0:
                    load_slot(si)
                if si + 1 < len(slots):
                    load_slot(si + 1)
                for (bi, rows) in blocks:
                    for g in range(2):
                        src = os_[:rows, bi, g * 512:(g + 1) * 512]
                        src_ap_keys[0] = [osk]
                        rs, sk = rstd_of(src, rows, 512)
                        gt = fox_g if g == 0 else sb_g
                        P.op("dve", lambda E, src=src, rs=rs, gt=gt, rows=rows, bi=bi, g=g: E.scalar_tensor_tensor(
                            out=on[:rows, bi, g * 512:(g + 1) * 512], in0=src, scalar=rs, in1=gt[:rows, :], op0=ALU.mult, op1=ALU.mult),
                            [osk, sk, "fox_g", "sb_g"], ["on"])
                transpose_to(on, "on", blocks, onT, "onT")
                onk = ["onT.%d" % bi for (bi, _) in blocks]
                for (bi, rows) in blocks:
                    for ch in range(2):
                        n = pctr[0]
                        pctr[0] += 1
                        pa = psa[n % 2]
                        pk = "psa%d" % (n % 2)
                        for k in range(8):
                            MM(P, pa[:, :], onT[:, k, bi * 128:bi * 128 + 128], wout[:, k, ch * 512:(ch + 1) * 512], k == 0, k == 7,
                               ["wout", "onT.%d" % bi], [pk])
                        P.op("dve", lambda E, pa=pa, rows=rows, bi=bi, ch=ch: E.tensor_tensor(
                            out=xr_[:rows, bi, ch * 512:(ch + 1) * 512], in0=pa[:rows, :], in1=xr_[:rows, bi, ch * 512:(ch + 1) * 512], op=ALU.add),
                            [pk, xrk], [xrk])
                for (bi, rows) in blocks:
                    src = xr_[:rows, bi, :]
                    src_ap_keys[0] = [xrk]
                    rs, sk = rstd_of(src, rows, D)
                    P.op("dve", lambda E, src=src, rs=rs, rows=rows, bi=bi: E.scalar_tensor_tensor(
                        out=on[:rows, bi, :], in0=src, scalar=rs, in1=ffn_g[:rows, :], op0=ALU.mult, op1=ALU.mult),
                        [xrk, sk, "ffn_g"] + onk, ["on"])
                transpose_to(on, "on", blocks, h2T, "h2T")
                h2k = ["h2T.%d" % bi for (bi, _) in blocks]
                for c in range(22):
                    wn = wuc[0]
                    wuc[0] += 1
                    wt = wu[wn % 3]
                    wk = "wu%d" % (wn % 3)
                    DMA(P, "sp", wt[:, :, 0, :], T["Wup"][:, :, c * 128:(c + 1) * 128], [], [wk + "g"])
                    DMA(P, "sp", wt[:, :, 1, :], T["Wup"][:, :, DFF + c * 128:DFF + (c + 1) * 128], [], [wk + "v"])
                    pend = []
                    prev_fin = deferred[:]
                    del deferred[:]
                    for gv in range(2):
                        un = (2 * wn + gv) % 4
                        pu_ = PSU[un]
                        pk = PSUK[un]
                        for k in range(8):
                            MM(P, pu_[:, :W], wt[:, k, gv, :], h2T[:, k, :W], k == 0, k == 7, [wk + ("g" if gv == 0 else "v")] + h2k, [pk])
                        cc = c + 22 * gv
                        if halo:
                            P.op("dve", lambda E, pu_=pu_, cc=cc: E.tensor_tensor(out=uh[:, cc, :], in0=pu_[:, :16], in1=hmask[:, :], op=ALU.mult),
                                 [pk, "hmask"], ["uh"])
                            continue
                        cn = (2 * wn + gv) % 6
                        ct, ck = cv[cn], "cv%d" % cn
                        ACTF(P, ct[:, :], pu_[:, :512], AF.Identity, [pk, "cw", "cb"], [ck], bias=cb[:, cc:cc + 1], scale=cw[:, cc, 2:3])
                        pend.append((pu_, pk, ct, ck, cc))
                    if halo:
                        continue
                    for f_ in prev_fin:
                        f_()
                    for (pu_, pk, ct, ck, cc) in pend:
                        P.op("dve", lambda E, pu_=pu_, ct=ct, cc=cc: E.scalar_tensor_tensor(out=ct[:, 1:512], in0=pu_[:, 0:511], scalar=cw[:, cc, 1:2], in1=ct[:, 1:512],
                                                                                         op0=ALU.mult, op1=ALU.add), [pk, "cw", ck], [ck])
                    for (pu_, pk, ct, ck, cc) in pend:
                        P.op("dve", lambda E, pu_=pu_, ct=ct, cc=cc: E.scalar_tensor_tensor(out=ct[:, 2:512], in0=pu_[:, 0:510], scalar=cw[:, cc, 0:1], in1=ct[:, 2:512],
                                                                                         op0=ALU.mult, op1=ALU.add), [pk, "cw", ck], [ck])
                    for (pu_, pk, ct, ck, cc) in pend:
                        P.op("dve", lambda E, ct=ct, cc=cc: E.scalar_tensor_tensor(out=ct[:, 0:1], in0=uh[:, cc, 2 * sl + 1:2 * sl + 2], scalar=cw[:, cc, 1:2], in1=ct[:, 0:1],
                                                                                op0=ALU.mult, op1=ALU.add), ["uh", "cw", ck], [ck])
                    for (pu_, pk, ct, ck, cc) in pend:
                        P.op("dve", lambda E, ct=ct, cc=cc: E.scalar_tensor_tensor(out=ct[:, 0:2], in0=uh[:, cc, 2 * sl:2 * sl + 2], scalar=cw[:, cc, 0:1], in1=ct[:, 0:2],
                                                                                op0=ALU.mult, op1=ALU.add), ["uh", "cw", ck], [ck])
                    (_, _, ctg, ckg, _), (_, _, ctv, ckv, _) = pend
                    st_, stk = sg[wn % 2], "sg%d" % (wn % 2)

                    def fin(ctg=ctg, ckg=ckg, ctv=ctv, ckv=ckv, st_=st_, stk=stk, c=c):
                        ACTF(P, st_[:, :], ctg[:, :], AF.Silu, [ckg], [stk])
                        P.op("pool", lambda E: E.tensor_tensor(out=aT[:, c, :], in0=st_[:, :], in1=ctv[:, :], op=ALU.mult),
                             [stk, ckv], ["aT.%d" % c])
                    deferred.append(fin)
                for f_ in deferred:
                    f_()
                del deferred[:]
                if halo:
                    return
                for cc in range(8):
                    wn = wdc[0]
                    wdc[0] += 1
                    wt = wd[wn % 3]
                    wk = "wd%d" % (wn % 3)
                    DMA(P, "sp", wt[:, :, :], T["Wdn"][:, :, cc * 128:(cc + 1) * 128], [], [wk])
                    for c in range(22):
                        MM(P, psy[:, :], wt[:, c, :], aT[:, c, :], c == 0, c == 21, [wk, "aT.%d" % c], ["psy"])
                    yt, yk = yT[wn % 2], "yT%d" % (wn % 2)
                    ACTF(P, yt[:, :], psy[:, :], AF.Copy, ["psy"], [yk])
                    for b in range(4):
                        P.op("pe", lambda E, yt=yt, b=b: E.transpose(pst_[:, b, :], yt[:, b * 128:(b + 1) * 128], identf[:, :]), [yk, "identf"], ["pst"])
                    P.op("dve", lambda E, cc=cc: E.tensor_tensor(out=xr_[:, :, cc * 128:(cc + 1) * 128], in0=pst_[:, :, :],
                                                               in1=xr_[:, :, cc * 128:(cc + 1) * 128], op=ALU.add), ["pst", xrk], [xrk])
                for (bi, rows) in blocks:
                    src = xr_[:rows, bi, :]
                    src_ap_keys[0] = [xrk]
                    rs, sk = rstd_of(src, rows, D)
                    P.op("dve", lambda E, src=src, rs=rs: E.scalar_tensor_tensor(out=src, in0=src, scalar=rs, in1=fin_g[:, :], op0=ALU.mult, op1=ALU.mult),
                         [xrk, sk, "fin_g"], [xrk])
                DMA(P, "pool", T["out"][r0:r0 + 512, :].rearrange("(b p) d -> p b d", p=128), xr_[:, :, :], [xrk], ["out"])

            for si, sl in enumerate(slots):
                do_slot(si, sl)
            P.emit()
            stats.append(P.stats)
    return nc, stats


_CACHE = {}


def _consts(S):
    NB = S // 128
    bf = ml_dtypes.bfloat16
    p = np.arange(128)
    c = {}
    c["ident_bf"] = np.eye(128, dtype=np.float32).astype(bf)
    c["ident_f"] = np.eye(128, dtype=np.float32)
    c["triu_f"] = (p[:, None] <= p[None, :]).astype(np.float32)
    c["ones_f"] = np.ones((128, 128), np.float32)
    c["negtri"] = (-(p[:, None] >= p[None, :]).astype(np.float32)).astype(bf)
    c["negones"] = (-np.ones((128, 128), np.float32)).astype(bf)
    t = np.arange(512)
    key = (np.arange(4)[None, :, None] * 128 + p[:, None, None])
    c["mt_f"] = np.where(key <= t[None, None, :], 0.0, NEG).astype(np.float32).astype(bf)
    c["mt_s"] = np.where(key < t[None, None, :], 0.0, NEG).astype(np.float32).astype(bf)
    c["negrow"] = np.full((1, 512), NEG, np.float32).astype(bf)
    c["cm1"] = np.full((64, 3, 128), -1.0, np.float32).astype(bf)
    c["cp1"] = np.full((64, 3, 128), 1.0, np.float32).astype(bf)
    return c


def _core_consts(S, par):
    NT = S // 512
    NSL = NT // 2
    NB = S // 128
    bf = ml_dtypes.bfloat16
    own = own_tiles(par, NSL)
    esel = np.zeros((128, 2 * NSL), np.float32)
    eselh = np.zeros((128, 2 * NSL), np.float32)
    hmask = np.ones((128, 16), np.float32)
    idE = np.zeros((128, NSL, 128), np.float32)
    idNE = np.zeros((128, NSL, 128), np.float32)
    erow = np.zeros((1, NSL, 128), np.float32)
    I = np.eye(128, dtype=np.float32)
    p = np.arange(128)
    keypos = (np.arange(NB)[None, :] * 128 + p[:, None])
    mh_f = np.zeros((128, NB, 16), np.float32)
    mh_s = np.zeros((128, NB, 16), np.float32)
    for j in range(NSL):
        e0 = 1.0 if own[j] == 2 * j else 0.0
        esel[:, 2 * j] = e0
        esel[:, 2 * j + 1] = 1.0 - e0
        eselh[:, 2 * j] = 1.0 if (own[j] == 2 * j and j > 0) else 0.0
        eselh[:, 2 * j + 1] = 1.0 if own[j] == 2 * j + 1 else 0.0
        idE[:, j, :] = e0 * I
        idNE[:, j, :] = (1.0 - e0) * I
        erow[0, j, :] = e0
        for r in range(2):
            col = 2 * j + r
            pos = 512 * own[j] - 2 + r
            if own[j] == 0:
                hmask[:, col] = 0.0
                mh_f[:, :, col] = np.where(keypos == 0, 0.0, NEG)
                mh_s[:, :, col] = np.where(keypos == 0, 0.0, NEG)
            else:
                mh_f[:, :, col] = np.where(keypos <= pos, 0.0, NEG)
                mh_s[:, :, col] = np.where(keypos < pos, 0.0, NEG)
    return dict(esel=esel, eselh=eselh, hmask=hmask, idE=idE.astype(bf), idNE=idNE.astype(bf), erow=erow.astype(bf),
                mh_f=mh_f.astype(bf), mh_s=mh_s.astype(bf)), own


def _prepare(inputs, debug=False):
    x = np.asarray(inputs["x"], np.float32)
    B, S, _ = x.shape
    NT = S // 512
    NSL = NT // 2
    rep = lambda v: np.ascontiguousarray(np.broadcast_to(np.asarray(v, np.float32).reshape(1, -1), (128, np.asarray(v).size)))
    shared = dict(_consts(S))
    shared["w_in"] = np.ascontiguousarray(np.asarray(inputs["w_in"], np.float32)[0])
    shared["w_out"] = np.ascontiguousarray(np.asarray(inputs["w_out"], np.float32)[0])
    shared["w_up"] = np.ascontiguousarray(np.asarray(inputs["w_up"], np.float32)[0])
    shared["w_down"] = np.ascontiguousarray(np.asarray(inputs["w_down"], np.float32)[0])
    shared["attn_g"] = rep(inputs["attn_norm_g"][0])
    shared["ffn_g"] = rep(inputs["ffn_norm_g"][0])
    shared["fin_g"] = rep(inputs["final_norm_g"])
    shared["fox_g"] = rep(inputs["fox_out_g"][0])
    shared["sb_g"] = rep(inputs["sb_out_g"][0])
    shared["fb"] = rep(inputs["forget_bias"][0])
    cwv = np.asarray(inputs["conv_w"], np.float32)[0]
    shared["cw"] = np.ascontiguousarray(cwv.reshape(3, NCH, 128).transpose(2, 1, 0))
    shared["cb"] = np.ascontiguousarray(np.asarray(inputs["conv_b"], np.float32)[0].reshape(NCH, 128).T)
    in_maps = []
    owns = []
    for c in range(8):
        b, par = c // 2, c % 2
        cc, own = _core_consts(S, par)
        owns.append(own)
        xo = np.zeros((NSL * 512 + 16, D), np.float32)
        for j, t in enumerate(own):
            xo[j * 512:(j + 1) * 512] = x[b, t * 512:(t + 1) * 512]
            if t > 0:
                xo[NSL * 512 + 2 * j:NSL * 512 + 2 * j + 2] = x[b, t * 512 - 2:t * 512]
        m = dict(shared)
        m.update(cc)
        m["xn"] = np.ascontiguousarray(x[b])
        m["xo"] = xo
        in_maps.append(m)
    return in_maps, owns, (B, S)


def kernel(**inputs):
    in_maps, owns, (B, S) = _prepare(inputs)
    if S not in _CACHE:
        _CACHE[S] = build_program(S)
    nc, _ = _CACHE[S]
    res = run_bass_kernel_spmd(nc, in_maps, core_ids=list(range(8)))
    out = np.zeros((B, S, D), np.float32)
    for c in range(8):
        b = c // 2
        o = np.asarray(res.results[c]["out"], np.float32)
        for j, t in enumerate(owns[c]):
            out[b, t * 512:(t + 1) * 512] = o[j * 512:(j + 1) * 512]
    return out
```

```python
import numpy as np
import ml_dtypes
from contextlib import ExitStack
import concourse.bass as bass
import concourse.mybir as mybir
from concourse.bass_utils import run_bass_kernel_spmd

F32 = mybir.dt.float32
BF16 = mybir.dt.bfloat16
AF = mybir.ActivationFunctionType
ALU = mybir.AluOpType

D = 1024
DH = 64
DFF = 2816
INC = 3080
EPS = 1e-6
NEG = -30000.0
CQ_F, CK_F, CV_F, CL, CQ_S, CK_S, CV_S = 0, 512, 1024, 1536, 1544, 2056, 2568
NCH = 2 * DFF // 128

import os
OPT = set(os.environ.get("KOPT", "").split(","))
COMPUTE = ("pe", "act", "dve", "pool")
CH = 16000
NDMA = 8


class Prog:
    def __init__(self, nc, gs, tag):
        self.nc = nc
        self.gs = gs
        self.tag = tag
        self.ops = []
        self.last_w = {}
        self.readers = {}

    def op(self, eng, fn, reads=(), writes=(), dma=False):
        j = len(self.ops)
        deps = set()
        for r in reads:
            if r in self.last_w:
                deps.add(self.last_w[r])
        for w in writes:
            if w in self.last_w:
                deps.add(self.last_w[w])
            for rd in self.readers.get(w, ()):
                deps.add(rd)
        deps.discard(j)
        self.ops.append(dict(eng=eng, fn=fn, deps=deps, dma=dma, sig=False))
        for r in reads:
            self.readers.setdefault(r, []).append(j)
        for w in writes:
            self.last_w[w] = j
            self.readers[w] = []
        return j

    def dma(self, q, fn, reads=(), writes=()):
        return self.op(q, fn, reads, writes, dma=True)

    def emit(self):
        nc, ops = self.nc, self.ops
        for j, o in enumerate(ops):
            nd = set()
            for d in o["deps"]:
                p = ops[d]
                if not p["dma"] and not o["dma"] and p["eng"] == o["eng"] == "pe":
                    continue
                nd.add(d)
            o["deps"] = nd
            for d in nd:
                ops[d]["sig"] = True
        cnt = {e: 0 for e in COMPUTE}
        sems = {}

        def getsem(key):
            if key not in sems:
                sems[key] = self.gs.enter_context(nc.semaphore("s%s_%s_%s" % (self.tag, key[0], key[1])))
            return sems[key]

        dcount = {}
        dma_i = {}
        for j, o in enumerate(ops):
            if o["dma"]:
                q = o["eng"]
                k = dma_i.get(q, 0)
                dma_i[q] = k + 1
                key = ("d" + q, k % NDMA)
                dcount[key] = dcount.get(key, 0) + 16
                o["semkey"], o["semval"] = key, dcount[key]
                o["prev"] = (key, dcount[key] - 16)
            elif o["sig"]:
                e = o["eng"]
                c = cnt[e]
                cnt[e] = c + 1
                o["semkey"], o["semval"] = (e, c // CH), c % CH + 1
        last_dma = {}
        for o in ops:
            if o["dma"]:
                last_dma[o["semkey"]] = max(last_dma.get(o["semkey"], 0), o["semval"])
            if "semkey" in o:
                getsem(o["semkey"])
        per = {e: [] for e in ("sp", "act", "dve", "pe", "pool")}
        for j, o in enumerate(ops):
            per[o["eng"]].append(j)
        nwc = [0]

        def run_engine(e, E):
            known = {}
            for j in per[e]:
                o = ops[j]
                need = {}
                for d in o["deps"]:
                    p = ops[d]
                    k, v = p["semkey"], p["semval"]
                    if need.get(k, 0) < v:
                        need[k] = v
                if o["dma"] and o["prev"][1] > 0:
                    k, v = o["prev"]
                    if need.get(k, 0) < v:
                        need[k] = v
                for k, v in need.items():
                    if known.get(k, 0) >= v:
                        continue
                    known[k] = v
                    E.wait_ge(sems[k], v)
                    nwc[0] += 1
                inst = o["fn"](E)
                if o["dma"]:
                    inst.then_inc(sems[o["semkey"]], 16)
                elif o["sig"]:
                    inst.then_inc(sems[o["semkey"]], 1)
            if e == "sp":
                for k, v in last_dma.items():
                    if known.get(k, 0) < v:
                        E.wait_ge(sems[k], v)

        with nc.Block() as block:
            @block.sync
            def _(E):
                run_engine("sp", E)

            @block.scalar
            def _(E):
                run_engine("act", E)

            @block.vector
            def _(E):
                run_engine("dve", E)

            @block.tensor
            def _(E):
                run_engine("pe", E)

            @block.gpsimd
            def _(E):
                run_engine("pool", E)
        self.stats = dict(tag=self.tag, nops=len(ops), nwaits=nwc[0], nsems=len(sems), cnt=cnt)


def MM(P, out, lhsT, rhs, start, stop, rd, wr, skip=False):
    P.op("pe", lambda E: E.matmul(out, lhsT=lhsT, rhs=rhs, start=start, stop=stop, skip_group_check=skip), rd, wr)


def TR(P, out, in_, ident, rd, wr):
    P.op("pe", lambda E: E.transpose(out, in_, ident), rd, wr)


def ACTF(P, out, in_, func, rd, wr, bias=None, scale=None, accum=None):
    kw = {}
    if bias is not None:
        kw["bias"] = bias
    if scale is not None:
        kw["scale"] = scale
    if accum is not None:
        kw["accum_out"] = accum
    P.op("act", lambda E: E.activation(out=out, in_=in_, func=func, **kw), rd, wr)


def DMA(P, q, out, in_, rd, wr):
    P.dma(q, lambda E: E.dma_start(out=out, in_=in_), rd, wr)


class RR:
    def __init__(self, engs):
        self.engs = engs
        self.i = 0

    def copy(self, P, out, in_, rd, wr, scale=None):
        e = self.engs[self.i % len(self.engs)]
        self.i += 1
        if e == "act":
            ACTF(P, out, in_, AF.Copy, rd, wr, scale=scale)
        elif scale is None:
            P.op(e, lambda E: E.tensor_copy(out=out, in_=in_), rd, wr)
        else:
            P.op(e, lambda E: E.tensor_scalar(out=out, in0=in_, scalar1=float(scale), scalar2=None, op0=ALU.mult), rd, wr)


def own_tiles(p, NSL):
    res = []
    for j in range(NSL):
        first = (j % 2 == 0) if p == 0 else (j % 2 == 1)
        res.append(2 * j if first else 2 * j + 1)
    return res


def build_program(S, debug=False, stop_after="d"):
    NT = S // 512
    NSL = NT // 2
    NB = S // 128
    NOWN = NSL * 512
    NQ = NOWN + 16
    nc = bass.Bass("TRN2", target_bir_lowering=False)
    T = {}

    def din(name, shape, dt=F32):
        T[name] = nc.dram_tensor(name, list(shape), dt, kind="ExternalInput").ap()

    def dscr(name, shape, dt=BF16):
        kind = "ExternalOutput" if debug else "Internal"
        T[name] = nc.dram_tensor(name, list(shape), dt, kind=kind).ap()

    din("xn", [S, D]); din("xo", [NQ, D])
    din("w_in", [D, INC]); din("w_out", [D, D]); din("w_up", [D, 2 * DFF]); din("w_down", [DFF, D])
    din("attn_g", [128, D]); din("ffn_g", [128, D]); din("fin_g", [128, D])
    din("fox_g", [128, 512]); din("sb_g", [128, 512]); din("fb", [128, 8])
    din("cw", [128, NCH, 3]); din("cb", [128, NCH])
    din("esel", [128, 2 * NSL]); din("eselh", [128, 2 * NSL]); din("hmask", [128, 16])
    din("idE", [128, NSL, 128], BF16); din("idNE", [128, NSL, 128], BF16); din("erow", [1, NSL, 128], BF16)
    din("mh_f", [128, NB, 16], BF16); din("mh_s", [128, NB, 16], BF16)
    din("ident_bf", [128, 128], BF16); din("ident_f", [128, 128]); din("triu_f", [128, 128]); din("ones_f", [128, 128])
    din("negtri", [128, 128], BF16); din("negones", [128, 128], BF16)
    din("mt_f", [128, 4, 512], BF16); din("mt_s", [128, 4, 512], BF16); din("negrow", [1, 512], BF16)
    din("cm1", [64, 3, 128], BF16); din("cp1", [64, 3, 128], BF16)
    T["out"] = nc.dram_tensor("out", [NOWN, D], F32, kind="ExternalOutput").ap()
    dscr("Kd", [16, 64, S]); dscr("Vd", [NT, 128, 16, 4, 65]); dscr("Qd", [16, 64, NQ])
    dscr("KA", [8, 6, S]); dscr("QA", [8, 6, S]); dscr("Od", [NQ + 112, D])
    dscr("Wup", [128, 8, 2 * DFF]); dscr("Wdn", [128, 22, D]); dscr("Wout", [128, 8, D])
    if debug:
        dscr("dbg_nF", [128, 8, NB], F32)

    stats = []
    with ExitStack() as gs:
        def sbt(es, name, shape, dt):
            return es.enter_context(nc.sbuf_tensor("sb_" + name, list(shape), dt))

        def pst(es, name, shape, dt):
            return es.enter_context(nc.psum_tensor("pp_" + name, list(shape), dt))

        with ExitStack() as es:
            P = Prog(nc, gs, "a")
            win = sbt(es, "win", [128, 8, INC], BF16)
            wstg = [sbt(es, "wstg%d" % i, [128, INC], F32) for i in range(2)]
            g_r = sbt(es, "g_r", [128, D], F32)
            fb_r = sbt(es, "fb_r", [128, 8], F32)
            ident = sbt(es, "ident", [128, 128], BF16)
            xb = [sbt(es, "xb%d" % i, [128, D], F32) for i in range(3)]
            junk = sbt(es, "junk", [128, D], BF16)
            stat = sbt(es, "stat", [128, 3, 8], F32)
            xnb = [sbt(es, "xnb%d" % i, [128, D], BF16) for i in range(2)]
            hT = [sbt(es, "hT%d" % i, [128, 8, 512], BF16) for i in range(2)]
            kts = [sbt(es, "kts%d" % i, [128, 512], BF16) for i in range(3)]
            vs = [sbt(es, "vs%d" % i, [128, 16, 4, 65], BF16) for i in range(2)]
            flog = sbt(es, "flog", [128, 8, NB], F32)
            esp = ExitStack()
            pT = [pst(esp, "pT%d" % i, [128, 8, 128], BF16) for i in range(2)]
            psk = [pst(esp, "psk%d" % i, [128, 512], F32) for i in range(2)]
            psv = [pst(esp, "psv%d" % i, [128, 512], F32) for i in range(2)]
            psf = [pst(esp, "psf%d" % i, [128, 512], F32) for i in range(2)]
            rr = RR(["act", "dve"])

            DMA(P, "sp", g_r[:], T["attn_g"][:, :], [], ["g_r"])
            DMA(P, "sp", fb_r[:], T["fb"][:, :], [], ["fb_r"])
            DMA(P, "sp", ident[:], T["ident_bf"][:, :], [], ["ident"])
            for k in range(8):
                DMA(P, "sp", wstg[k % 2][:], T["w_in"][k * 128:(k + 1) * 128, :], [], ["wstg%d" % (k % 2)])
                P.op("pool" if k % 2 == 0 else "dve", lambda E, k=k: E.tensor_copy(out=win[:, k, :], in_=wstg[k % 2][:]), ["wstg%d" % (k % 2)], ["win.%d" % k])
            for i in range(2):
                P.op("pool", lambda E, i=i: E.memset(vs[i][:], 1.0), [], ["vs%d.%d.%d" % (i, b, g) for b in range(4) for g in range(2)])
                P.op("pool", lambda E, i=i: E.memset(xnb[i][:], 0.0), [], ["xnb%d" % i])
            blkn = [0]

            def norm_block(src_rows, rows, hbuf, col0):
                st = {}

                def part1():
                    n = blkn[0]
                    blkn[0] += 1
                    st["n"] = n
                    x = xb[n % 3]
                    xs = "xb%d" % (n % 3)
                    sk = "stat%d" % (n % 8)
                    DMA(P, "sp", x[:rows, :], src_rows, [], [xs])
                    ACTF(P, junk[:rows, :], x[:rows, :], AF.Square, [xs], ["junk", sk], accum=stat[:rows, 0, n % 8:n % 8 + 1])
                    ACTF(P, stat[:rows, 1, n % 8:n % 8 + 1], stat[:rows, 0, n % 8:n % 8 + 1], AF.Ln, [sk], [sk], bias=EPS, scale=1.0 / D)
                    ACTF(P, stat[:rows, 2, n % 8:n % 8 + 1], stat[:rows, 1, n % 8:n % 8 + 1], AF.Exp, [sk], [sk], scale=-0.5)
                    xq = xnb[n % 2]
                    qs = "xnb%d" % (n % 2)
                    P.op("dve", lambda E: E.scalar_tensor_tensor(out=xq[:rows, :], in0=x[:rows, :], scalar=stat[:rows, 2, n % 8:n % 8 + 1],
                                                                 in1=g_r[:rows, :], op0=ALU.mult, op1=ALU.mult), [xs, sk, "g_r"], [qs])

                def part2():
                    n = st["n"]
                    xq = xnb[n % 2]
                    qs = "xnb%d" % (n % 2)
                    pt = pT[n % 2]
                    ps = "pT%d" % (n % 2)
                    for k in range(8):
                        TR(P, pt[:, k, :], xq[:, k * 128:(k + 1) * 128], ident[:, :], [qs, "ident"], [ps])
                    rr.copy(P, hT[hbuf][:, :, col0:col0 + rows], pt[:, :, :rows], [ps], ["hT%d.%d" % (hbuf, col0 // 128)])
                return part1, part2

            def proj_T(hbuf, width, col_w, dst, hkeys, scale=None):
                n = proj_T.n
                proj_T.n += 1
                ps = psk[n % 2]
                pk = "psk%d" % (n % 2)
                for k in range(8):
                    MM(P, ps[:, :width], win[:, k, col_w:col_w + 128], hT[hbuf][:, k, :width], k == 0, k == 7, ["win.%d" % k] + hkeys, [pk])
                ks = kts[n % 3]
                kk = "kts%d" % (n % 3)
                rr.copy(P, ks[:, :width], ps[:, :width], [pk], [kk], scale=scale)
                DMA(P, "pool", dst, ks[:, :width], [kk], ["KQscr"])
            proj_T.n = 0

            vcount = [0]

            def make_tile_A(Tn, hb):
                blocks = [norm_block(T["xn"][Tn * 512 + b * 128:Tn * 512 + (b + 1) * 128, :], 128, hb, b * 128) for b in range(4)]
                hk = ["hT%d.%d" % (hb, b) for b in range(4)]
                vb = Tn % 2
                groups = []

                def kgrp(c):
                    col = (CK_F + c * 128) if c < 4 else (CK_S + (c - 4) * 128)
                    h0 = 2 * c if c < 4 else 8 + 2 * (c - 4)
                    proj_T(hb, 512, col, T["Kd"][h0:h0 + 2, :, Tn * 512:(Tn + 1) * 512].rearrange("h r t -> (h r) t"), hk)

                def vgrp(b, g):
                    n = vcount[0]
                    vcount[0] += 1
                    ps = psv[n % 2]
                    pk = "psv%d" % (n % 2)
                    vc = CV_F if g == 0 else CV_S
                    for k in range(8):
                        MM(P, ps[:, :], hT[hb][:, k, b * 128:(b + 1) * 128], win[:, k, vc:vc + 512], k == 0, k == 7,
                           ["win.%d" % k, "hT%d.%d" % (hb, b)], [pk])
                    rr.copy(P, vs[vb][:, g * 8:(g + 1) * 8, b, 0:64], ps[:, :].rearrange("p (h d) -> p h d", h=8), [pk],
                            ["vs%d.%d.%d" % (vb, b, g)])
                    if g == 1:
                        pf = psf[b % 2]
                        for k in range(8):
                            MM(P, pf[:, 0:8], hT[hb][:, k, b * 128:(b + 1) * 128], win[:, k, CL:CL + 8], k == 0, k == 7,
                               ["win.%d" % k, "hT%d.%d" % (hb, b)], ["psf%d" % (b % 2)])
                        P.op("dve", lambda E: E.tensor_tensor(out=flog[:, :, Tn * 4 + b], in0=pf[:, 0:8], in1=fb_r[:, :], op=ALU.add),
                             ["psf%d" % (b % 2), "fb_r"], ["flog"])

                def vstore():
                    DMA(P, "sp", T["Vd"][Tn, :, :, :, :], vs[vb][:, :, :, :],
                        ["vs%d.%d.%d" % (vb, b, g) for b in range(4) for g in range(2)], ["Vscr"])
                for c in range(8):
                    groups.append(lambda c=c: kgrp(c))
                for b in range(4):
                    for g in range(2):
                        groups.append(lambda b=b, g=g: vgrp(b, g))
                groups.append(vstore)
                return blocks, groups

            def make_tile_Q(row0, width, col0, hb):
                nb = (width + 127) // 128
                blocks = []
                for b in range(nb):
                    rows = min(128, width - b * 128)
                    blocks.append(norm_block(T["xo"][row0 + b * 128:row0 + b * 128 + rows, :], rows, hb, b * 128))
                hk = ["hT%d.%d" % (hb, b) for b in range(nb)]

                def qgrp(c):
                    col = (CQ_F + c * 128) if c < 4 else (CQ_S + (c - 4) * 128)
                    h0 = 2 * c if c < 4 else 8 + 2 * (c - 4)
                    proj_T(hb, width, col, T["Qd"][h0:h0 + 2, :, col0:col0 + width].rearrange("h r t -> (h r) t"), hk, scale=0.125)
                groups = [(lambda c=c: qgrp(c)) for c in range(8)]
                return blocks, groups

            seq = []
            qi = 0
            for Tn in range(NT):
                seq.append(("A", Tn))
                if Tn % 2 == 1 and qi < NSL:
                    seq.append(("Q", qi))
                    qi += 1
            seq.append(("H", 0))
            tiles = []
            for i, (kind, idx) in enumerate(seq):
                hb = i % 2
                if kind == "A":
                    tiles.append(make_tile_A(idx, hb))
                elif kind == "Q":
                    tiles.append(make_tile_Q(idx * 512, 512, idx * 512, hb))
                else:
                    tiles.append(make_tile_Q(NOWN, 16, NOWN, hb))
            for (p1, p2) in tiles[0][0]:
                p1()
                p2()
            for i, (blocks, groups) in enumerate(tiles):
                nxt = tiles[i + 1][0] if i + 1 < len(tiles) else []
                G = len(groups)
                nb_ = max(1, len(nxt))
                ev = {}
                for b, (p1, p2) in enumerate(nxt):
                    ev.setdefault(int(b * G / nb_), []).append(p1)
                    ev.setdefault(min(G - 1, int((b + 0.7) * G / nb_)), []).append(p2)
                for gi, g in enumerate(groups):
                    g()
                    for f_ in ev.get(gi, []):
                        f_()
            P.emit()
            stats.append(P.stats)
            esp.close()
            if stop_after == "a":
                return nc, stats

            with ExitStack() as es2:
                P = Prog(nc, gs, "b")
                nlf = sbt(es2, "nlf", [128, 8 * NB], F32)
                ee = sbt(es2, "ee", [128, 8 * NB], F32)
                sc = [sbt(es2, "sc%d" % i, [128, 8, NB], F32) for i in range(2)]
                tot = sbt(es2, "tot", [128, 8, NB], F32)
                nF = sbt(es2, "nF", [128, 8, NB], F32)
                nFp = sbt(es2, "nFp", [128, 8, 128], F32)
                nFT = sbt(es2, "nFT", [NB, 8, 128], F32)
                r1 = sbt(es2, "r1", [NB, 8, 128], F32)
                parts = [sbt(es2, "part%d" % i, [NB, 8, 128], BF16) for i in range(3)]
                triu = sbt(es2, "triu", [128, 128], F32)
                ones = sbt(es2, "ones", [128, 128], F32)
                identf = sbt(es2, "identf", [128, 128], F32)
                cm1 = sbt(es2, "cm1", [64, 3, 128], BF16)
                cp1 = sbt(es2, "cp1", [64, 3, 128], BF16)
                ps_c = pst(es2, "ps_c", [128, 512], F32)
                ps_t = pst(es2, "ps_t", [128, 512], F32)
                ps_x = [pst(es2, "ps_x%d" % i, [128, 4, 128], F32) for i in range(2)]
                W8 = 8 * NB
                DMA(P, "sp", triu[:], T["triu_f"][:, :], [], ["triu"])
                DMA(P, "sp", ones[:], T["ones_f"][:, :], [], ["ones"])
                DMA(P, "sp", identf[:], T["ident_f"][:, :], [], ["identf"])
                DMA(P, "sp", cm1[:], T["cm1"][:, :, :], [], ["cm1"])
                DMA(P, "sp", cp1[:], T["cp1"][:, :, :], [], ["cp1"])
                fl2 = flog[:, :, :].rearrange("p h b -> p (h b)")
                ACTF(P, ee[:, :], fl2, AF.Exp, [], ["ee"], scale=-1.0)
                ACTF(P, nlf[:, :], ee[:, :], AF.Ln, ["ee"], ["nlf"], bias=1.0)
                MM(P, ps_c[:, :W8], triu[:, :], nlf[:, :], True, True, ["triu", "nlf"], ["ps_c"])
                MM(P, ps_t[:, :W8], ones[:, :], nlf[:, :], True, True, ["ones", "nlf"], ["ps_t"])
                P.op("dve", lambda E: E.tensor_copy(out=tot[:, :, :], in_=ps_t[:, :W8].rearrange("p (h b) -> p h b", h=8)), ["ps_t"], ["tot"])
                P.op("dve", lambda E: E.tensor_copy(out=sc[0][:, :, :], in_=tot[:, :, :]), ["tot"], ["sc0"])
                cur = 0
                d = 1
                while d < NB:
                    nxt = 1 - cur
                    P.op("dve", lambda E, cur=cur, nxt=nxt, d=d: E.tensor_copy(out=sc[nxt][:, :, 0:d], in_=sc[cur][:, :, 0:d]), ["sc%d" % cur], ["sc%d" % nxt])
                    P.op("dve", lambda E, cur=cur, nxt=nxt, d=d: E.tensor_tensor(out=sc[nxt][:, :, d:NB], in0=sc[cur][:, :, d:NB], in1=sc[cur][:, :, 0:NB - d], op=ALU.add),
                         ["sc%d" % cur], ["sc%d" % nxt])
                    cur = nxt
                    d *= 2
                P.op("dve", lambda E, cur=cur: E.tensor_tensor(out=tot[:, :, :], in0=sc[cur][:, :, :], in1=tot[:, :, :], op=ALU.subtract), ["sc%d" % cur, "tot"], ["tot"])
                P.op("dve", lambda E: E.tensor_tensor(out=nF[:, :, :], in0=ps_c[:, :W8].rearrange("p (h b) -> p h b", h=8), in1=tot[:, :, :], op=ALU.add), ["ps_c", "tot"], ["nF"])
                if debug:
                    DMA(P, "sp", T["dbg_nF"][:, :, :], nF[:, :, :], ["nF"], ["dbg"])
                P.op("dve", lambda E: E.memset(nFp[:], 0.0), [], ["nFp"])
                P.op("dve", lambda E: E.tensor_copy(out=nFp[:, :, 0:NB], in_=nF[:, :, :]), ["nF", "nFp"], ["nFp"])
                for h in range(8):
                    px = ps_x[h // 4]
                    P.op("pe", lambda E, h=h, px=px: E.transpose(px[:, h % 4, :], nFp[:, h, :], identf[:, :]), ["nFp", "identf"], ["ps_x%d" % (h // 4)])
                for q in range(2):
                    P.op("dve", lambda E, q=q: E.tensor_copy(out=nFT[:, q * 4:(q + 1) * 4, :], in_=ps_x[q][:NB, :, :]), ["ps_x%d" % q], ["nFT"])
                P.op("dve", lambda E: E.tensor_copy(out=parts[0][:, :, :], in_=nFT[:, :, :]), ["nFT"], ["part0"])
                P.op("dve", lambda E: E.tensor_tensor(out=r1[:, :, :], in0=nFT[:, :, :], in1=parts[0][:, :, :], op=ALU.subtract), ["nFT", "part0"], ["r1"])
                P.op("dve", lambda E: E.tensor_copy(out=parts[1][:, :, :], in_=r1[:, :, :]), ["r1"], ["part1"])
                P.op("dve", lambda E: E.tensor_tensor(out=nFT[:, :, :], in0=r1[:, :, :], in1=parts[1][:, :, :], op=ALU.subtract), ["r1", "part1"], ["nFT"])
                P.op("dve", lambda E: E.tensor_copy(out=parts[2][:, :, :], in_=nFT[:, :, :]), ["nFT"], ["part2"])
                for i in range(3):
                    DMA(P, "sp", T["KA"][:, i, :].rearrange("h (b t) -> b h t", t=128), parts[i][:, :, :], ["part%d" % i], ["KAs"])
                    DMA(P, "sp", T["QA"][:, 3 + i, :].rearrange("h (b t) -> b h t", t=128), parts[i][:, :, :], ["part%d" % i], ["QAs"])
                for h in range(8):
                    DMA(P, "sp", T["KA"][h, 3:6, :].rearrange("r (b t) -> b r t", t=128), cm1[:NB, :, :], ["cm1"], ["KAs"])
                    DMA(P, "sp", T["QA"][h, 0:3, :].rearrange("r (b t) -> b r t", t=128), cp1[:NB, :, :], ["cp1"], ["QAs"])
                P.emit()
                stats.append(P.stats)
        if stop_after == "b":
            return nc, stats

        with ExitStack() as es:
            P = Prog(nc, gs, "c")
            KT = [sbt(es, "KT%d" % i, [70, S], BF16) for i in range(2)]
            QT = [sbt(es, "QT%d" % i, [70, NQ], BF16) for i in range(2)]
            VV = [sbt(es, "VV%d" % i, [128, NB, 65], BF16) for i in range(2)]
            qa = sbt(es, "qa", [70, S], BF16)
            qtmp = sbt(es, "qtmp", [70, 512], BF16)
            esel = sbt(es, "esel", [128, 2 * NSL], F32)
            eselh = sbt(es, "eselh", [128, 2 * NSL], F32)
            idE = sbt(es, "idE", [128, NSL, 128], BF16)
            idNE = sbt(es, "idNE", [128, NSL, 128], BF16)
            erow = sbt(es, "erow", [1, NSL, 128], BF16)
            negrow = sbt(es, "negrow", [1, 512], BF16)
            allneg = sbt(es, "allneg", [128, 512], BF16)
            ident = sbt(es, "ident2", [128, 128], BF16)
            mt = [sbt(es, "mt%d" % i, [128, 4, 512], BF16) for i in range(2)]
            mh = [sbt(es, "mh%d" % i, [128, NB, 16], BF16) for i in range(2)]
            negtri = sbt(es, "negtri", [128, 128], BF16)
            negones = sbt(es, "negones", [128, 128], BF16)
            PT = [sbt(es, "PT%d" % i, [128, 2, 512], BF16) for i in range(3)]
            UU = [sbt(es, "UU%d" % i, [128, 2, 512], F32) for i in range(2)]
            LL = [sbt(es, "LL%d" % i, [128, 2, 512], BF16) for i in range(3)]
            LS = [sbt(es, "LS%d" % i, [128, 512], BF16) for i in range(3)]
            rden = sbt(es, "rden", [128, 2, 4], F32)
            ob = [sbt(es, "ob%d" % i, [128, 4, 64], BF16) for i in range(3)]
            cst = [sbt(es, "cst%d" % i, [128, 2816], F32) for i in range(2)]
            cbf = [sbt(es, "cbf%d" % i, [128, 2816], BF16) for i in range(2)]
            jobs = []
            for k in range(8):
                for hf in range(2):
                    jobs.append((T["w_up"][k * 128:(k + 1) * 128, hf * 2816:(hf + 1) * 2816], T["Wup"][:, k, hf * 2816:(hf + 1) * 2816], 2816, None))
            for c in range(0, 22, 2):
                jobs.append((T["w_down"][c * 128:(c + 2) * 128, :].rearrange("(c p) n -> p c n", p=128), T["Wdn"][:, c:c + 2, :], 2048, 2))
            for k in range(0, 8, 2):
                jobs.append((T["w_out"][k * 128:(k + 2) * 128, :].rearrange("(c p) n -> p c n", p=128), T["Wout"][:, k:k + 2, :], 2048, 2))
            jobn = [0]

            def conv_job():
                n = jobn[0]
                if n >= len(jobs):
                    return
                jobn[0] += 1
                src, dst, width, sub = jobs[n]
                b = n % 2
                if sub is None:
                    s_ap, b_ap = cst[b][:, :width], cbf[b][:, :width]
                else:
                    s_ap = cst[b][:, :width].rearrange("p (c n) -> p c n", c=sub)
                    b_ap = cbf[b][:, :width].rearrange("p (c n) -> p c n", c=sub)
                DMA(P, "sp", s_ap, src, [], ["cst%d" % b])
                P.op("dve", lambda E: E.tensor_copy(out=cbf[b][:, :width], in_=cst[b][:, :width]), ["cst%d" % b], ["cbf%d" % b])
                DMA(P, "sp", dst, b_ap, ["cbf%d" % b], ["Wscr"])
            psZ = [pst(es, "psZ%d" % i, [128, 2, 512], F32) for i in range(3)]
            psO = [pst(es, "psO%d" % i, [128, 512], F32) for i in range(2)]
            for nm, t_, src in (("esel", esel, T["esel"][:, :]), ("eselh", eselh, T["eselh"][:, :]), ("idE", idE, T["idE"][:, :, :]),
                                ("idNE", idNE, T["idNE"][:, :, :]), ("erow", erow, T["erow"][:, :, :]), ("negrow", negrow, T["negrow"][:, :]),
                                ("ident", ident, T["ident_bf"][:, :]), ("mt0", mt[0], T["mt_f"][:, :, :]), ("mt1", mt[1], T["mt_s"][:, :, :]),
                                ("mh0", mh[0], T["mh_f"][:, :, :]), ("mh1", mh[1], T["mh_s"][:, :, :]),
                                ("negtri", negtri, T["negtri"][:, :]), ("negones", negones, T["negones"][:, :])):
                DMA(P, "sp", t_[:], src, [], [nm])
            P.op("pool", lambda E: E.memset(allneg[:], NEG), [], ["allneg"])
            for i in range(3):
                P.op("pool", lambda E, i=i: E.memset(PT[i][:], 0.0), [], ["PT%d" % i])
            CONSTS = ["idE", "idNE", "erow", "negrow", "ident", "mt0", "mt1", "mh0", "mh1", "allneg"]

            def load_head(hh):
                hb = hh % 2
                fox = hh < 8
                DMA(P, "sp", KT[hb][0:64, :], T["Kd"][hh, :, :], [], ["KTm%d" % hb])
                DMA(P, "sp", QT[hb][0:64, :], T["Qd"][hh, :, :], [], ["QTm%d" % hb])
                DMA(P, "sp", VV[hb][:, :, :].rearrange("p (t b) d -> p t b d", b=4), T["Vd"][:, :, hh, :, :].rearrange("t p b d -> p t b d"), [], ["VV%d" % hb])
                if not fox:
                    P.op("pool", lambda E: E.memset(KT[hb][64:70, :], 0.0), [], ["KTa%d" % hb])
                    P.op("pool", lambda E: E.memset(QT[hb][64:70, :], 0.0), [], ["QTa%d" % hb])
                if fox:
                    DMA(P, "sp", KT[hb][64:70, :], T["KA"][hh, :, :], [], ["KTa%d" % hb])
                    DMA(P, "sp", qa[64:70, :], T["QA"][hh, :, :], [], ["qa"])
                    for j in range(NSL + 1):
                        if j < NSL:
                            c0 = qa[64:70, 1024 * j:1024 * j + 512]
                            c1 = qa[64:70, 1024 * j + 512:1024 * j + 1024]
                            e0 = esel[64:70, 2 * j:2 * j + 1]
                            e1 = esel[64:70, 2 * j + 1:2 * j + 2]
                            dst = QT[hb][64:70, 512 * j:512 * j + 512]
                            tmp = qtmp[64:70, 0:512]
                            P.op("dve", lambda E, c0=c0, e0=e0, tmp=tmp: E.tensor_scalar(out=tmp, in0=c0, scalar1=e0, scalar2=None, op0=ALU.mult),
                                 ["qa", "esel"], ["qtmp"])
                            P.op("dve", lambda E, c1=c1, e1=e1, tmp=tmp, dst=dst: E.scalar_tensor_tensor(out=dst, in0=c1, scalar=e1, in1=tmp, op0=ALU.mult, op1=ALU.add),
                                 ["qa", "esel", "qtmp"], ["QTa%d" % hb])
                        else:
                            for jj in range(NSL):
                                a0 = max(0, 1024 * jj - 2)
                                c0 = qa[64:70, a0:a0 + 2]
                                c1 = qa[64:70, 1024 * jj + 510:1024 * jj + 512]
                                e0 = eselh[64:70, 2 * jj:2 * jj + 1]
                                e1 = eselh[64:70, 2 * jj + 1:2 * jj + 2]
                                dst = QT[hb][64:70, NOWN + 2 * jj:NOWN + 2 * jj + 2]
                                tmp = qtmp[64:70, 0:2]
                                P.op("dve", lambda E, c0=c0, e0=e0, tmp=tmp: E.tensor_scalar(out=tmp, in0=c0, scalar1=e0, scalar2=None, op0=ALU.mult),
                                     ["qa", "eselh"], ["qtmp"])
                                P.op("dve", lambda E, c1=c1, e1=e1, tmp=tmp, dst=dst: E.scalar_tensor_tensor(out=dst, in0=c1, scalar=e1, in1=tmp, op0=ALU.mult, op1=ALU.add),
                                     ["qa", "eselh", "qtmp"], ["QTa%d" % hb])

            units = []
            slot_ctr = 0
            for hh in range(16):
                fox = hh < 8
                if "noSB" in OPT and not fox:
                    continue
                if "noFOX" in OPT and fox:
                    continue
                for sl in range(NSL + 1):
                    halo = sl == NSL
                    if halo and "noHalo" in OPT:
                        continue
                    W = 16 if halo else 512
                    nkb = NB if halo else 8 * (sl + 1)
                    order = list(range(nkb)) if fox else list(range(nkb - 1, -1, -1))

                    def mask_of(kb):
                        if halo:
                            return ("H", kb)
                        if 8 * sl <= kb < 8 * sl + 4:
                            return ("E", kb - 8 * sl)
                        if 8 * sl + 4 <= kb < 8 * sl + 8:
                            return ("NE", kb - 8 * sl - 4)
                        return None
                    npairs = nkb // 2
                    for idx in range(npairs):
                        kbs = (order[2 * idx], order[2 * idx + 1])
                        units.append(dict(hh=hh, fox=fox, sl=sl, W=W, kbs=kbs, first=idx == 0, last=idx == npairs - 1,
                                          mks=(mask_of(kbs[0]), mask_of(kbs[1])), so=slot_ctr, qc=NOWN if halo else 512 * sl))
                    slot_ctr += 1
            for i, u in enumerate(units):
                u["i"] = i
                u["head_first"] = (i == 0 or units[i - 1]["hh"] != u["hh"])
                u["head_last"] = (i == len(units) - 1 or units[i + 1]["hh"] != u["hh"])

            def stage_A(u):
                hb = u["hh"] % 2
                KD = 70
                W, i = u["W"], u["i"]
                zt = psZ[i % 3]
                zk = "psZ%d" % (i % 3)
                mi = 0 if u["fox"] else 1
                rd = ["KTm%d" % hb, "QTm%d" % hb, "KTa%d" % hb, "QTa%d" % hb]
                sl = u["sl"]
                for a in range(2):
                    kb = u["kbs"][a]
                    z = zt[:, a, :W]
                    mk = u["mks"][a] if "noMask" not in OPT else None
                    MM(P, z, KT[hb][0:KD, kb * 128:(kb + 1) * 128], QT[hb][0:KD, u["qc"]:u["qc"] + W], True, mk is None and u["fox"], rd, [zk])
                    if mk is not None:
                        typ, m = mk
                        if typ == "H":
                            MM(P, z, ident[:, :], mh[mi][:, m, :], False, u["fox"], CONSTS, [zk])
                        elif typ == "E":
                            MM(P, z, idE[:, sl, :], mt[mi][:, m, :], False, u["fox"], CONSTS, [zk])
                        else:
                            MM(P, z, idNE[:, sl, :], mt[mi][:, m, :], False, False, CONSTS, [zk])
                            MM(P, z, idE[:, sl, :], allneg[:, :W], False, u["fox"], CONSTS, [zk])

            def stage_B_fox(u):
                W, i = u["W"], u["i"]
                ACTF(P, PT[i % 3][:, :, :W], psZ[i % 3][:, :, :W], AF.Exp, ["psZ%d" % (i % 3)], ["PT%d" % (i % 3)])

            def stage_B1(u):
                W, i = u["W"], u["i"]
                ACTF(P, UU[i % 2][:, :, :W], psZ[i % 3][:, :, :W], AF.Exp, ["psZ%d" % (i % 3)], ["UU%d" % (i % 2)])
                ACTF(P, LL[i % 3][:, :, :W], UU[i % 2][:, :, :W], AF.Ln, ["UU%d" % (i % 2)], ["LL%d" % (i % 3)], bias=1.0)
                if not u["last"]:
                    lk, nk = "LL%d" % (i % 3), "LS%d" % ((i + 1) % 3)
                    if u["first"]:
                        P.op("pool", lambda E: E.tensor_tensor(out=LS[(i + 1) % 3][:, :W], in0=LL[i % 3][:, 0, :W], in1=LL[i % 3][:, 1, :W], op=ALU.add), [lk], [nk])
                    else:
                        P.op("pool", lambda E: E.tensor_tensor(out=LS[(i + 1) % 3][:, :W], in0=LS[i % 3][:, :W], in1=LL[i % 3][:, 0, :W], op=ALU.add),
                             [lk, "LS%d" % (i % 3)], [nk])
                        P.op("pool", lambda E: E.tensor_tensor(out=LS[(i + 1) % 3][:, :W], in0=LS[(i + 1) % 3][:, :W], in1=LL[i % 3][:, 1, :W], op=ALU.add),
                             [lk, nk], [nk])

            def stage_C(u):
                W, i = u["W"], u["i"]
                zt = psZ[i % 3]
                zk = "psZ%d" % (i % 3)
                lk = "LL%d" % (i % 3)
                MM(P, zt[:, 0, :W], negtri[:, :], LL[i % 3][:, 0, :W], False, u["first"], ["negtri", lk], [zk])
                if not u["first"]:
                    MM(P, zt[:, 0, :W], negones[:, :], LS[i % 3][:, :W], False, True, ["negones", "LS%d" % (i % 3)], [zk])
                MM(P, zt[:, 1, :W], negtri[:, :], LL[i % 3][:, 1, :W], False, False, ["negtri", lk], [zk])
                MM(P, zt[:, 1, :W], negones[:, :], LL[i % 3][:, 0, :W], False, u["first"], ["negones", lk], [zk])
                if not u["first"]:
                    MM(P, zt[:, 1, :W], negones[:, :], LS[i % 3][:, :W], False, True, ["negones", "LS%d" % (i % 3)], [zk])

            def stage_B2(u):
                W, i = u["W"], u["i"]
                ACTF(P, PT[i % 3][:, :, :W], psZ[i % 3][:, :, :W], AF.Exp, ["psZ%d" % (i % 3)], ["PT%d" % (i % 3)])

            fin_ctr = [0]

            def stage_D(u):
                hb = u["hh"] % 2
                W, i = u["W"], u["i"]
                o = psO[u["so"] % 2]
                ok = "psO%d" % (u["so"] % 2)
                NV = 65 if u["fox"] else 64
                nsub = max(1, W // 128)
                M = min(128, W)
                for a in range(2):
                    kb = u["kbs"][a]
                    for s_ in range(nsub):
                        MM(P, o[:, s_ * NV:(s_ + 1) * NV], PT[i % 3][:, a, s_ * 128:s_ * 128 + 128], VV[hb][:, kb, 0:NV],
                           u["first"] and s_ == 0 and a == 0, u["last"] and s_ == nsub - 1 and a == 1, ["PT%d" % (i % 3), "VV%d" % hb], [ok], skip=True)
                if u["last"]:
                    f = fin_ctr[0]
                    fin_ctr[0] += 1
                    obuf = ob[f % 3]
                    obk = "ob%d" % (f % 3)
                    ov = o[:M, 0:nsub * NV].rearrange("p (s v) -> p s v", s=nsub)
                    if u["fox"]:
                        rk = "rden%d" % (f % 2)
                        P.op("dve", lambda E: E.reciprocal(out=rden[:M, f % 2, 0:nsub], in_=ov[:, :, 64]), [ok], [rk])
                        for s_ in range(nsub):
                            P.op("dve", lambda E, s_=s_: E.tensor_scalar(out=obuf[:M, s_, :], in0=ov[:, s_, 0:64], scalar1=rden[:M, f % 2, s_:s_ + 1],
                                                                         scalar2=None, op0=ALU.mult), [ok, rk], [obk])
                    else:
                        P.op("dve", lambda E: E.tensor_copy(out=obuf[:M, 0:nsub, :], in_=ov[:, :, 0:64]), [ok], [obk])
                    r0 = u["qc"]
                    hh = u["hh"]
                    if nsub == 4:
                        dst = T["Od"][r0:r0 + 512, hh * 64:(hh + 1) * 64].rearrange("(s p) d -> p s d", p=128)
                        DMA(P, "sp", dst, obuf[:, :, :], [obk], ["Od"])
                    else:
                        DMA(P, "sp", T["Od"][r0:r0 + M, hh * 64:(hh + 1) * 64], obuf[:M, 0, :], [obk], ["Od"])

            n = len(units)
            hh0 = units[0]["hh"]
            load_head(hh0)
            stage_A(units[0])
            job_every = max(1, n // (len(jobs) + 2))
            for i in range(n):
                u = units[i]
                if u["head_first"] and u["hh"] + 1 < 16 and u["hh"] == hh0:
                    load_head(hh0 + 1)
                if i % job_every == job_every - 1:
                    conv_job()
                if i + 1 < n:
                    stage_A(units[i + 1])
                if u["fox"]:
                    stage_B_fox(u)
                else:
                    stage_B1(u)
                    stage_C(u)
                if i >= 1:
                    pu = units[i - 1]
                    if not pu["fox"]:
                        stage_B2(pu)
                    stage_D(pu)
                    if pu["head_last"] and pu["hh"] + 2 < 16:
                        load_head(pu["hh"] + 2)
            pu = units[n - 1]
            if not pu["fox"]:
                stage_B2(pu)
            stage_D(pu)
            while jobn[0] < len(jobs):
                conv_job()
            P.emit()
            stats.append(P.stats)
        if stop_after == "c":
            return nc, stats

        with ExitStack() as es:
            P = Prog(nc, gs, "d")
            wout = sbt(es, "wout", [128, 8, D], BF16)
            fox_g = sbt(es, "fox_g", [128, 512], F32)
            sb_g = sbt(es, "sb_g", [128, 512], F32)
            ffn_g = sbt(es, "ffn_g", [128, D], F32)
            fin_g = sbt(es, "fin_g", [128, D], F32)
            cw = sbt(es, "cw", [128, NCH, 3], F32)
            cb = sbt(es, "cb", [128, NCH], F32)
            hmask = sbt(es, "hmask", [128, 16], F32)
            ident = sbt(es, "ident3", [128, 128], BF16)
            identf = sbt(es, "identf3", [128, 128], F32)
            o_s = [sbt(es, "o_s%d" % i, [128, 4, D], BF16) for i in range(2)]
            xr = [sbt(es, "xr%d" % i, [128, 4, D], F32) for i in range(2)]
            junk = sbt(es, "junk3", [128, D], BF16)
            stat = sbt(es, "stat3", [128, 3, 16], F32)
            on = sbt(es, "on", [128, 4, D], BF16)
            onT = sbt(es, "onT", [128, 8, 512], BF16)
            h2T = sbt(es, "h2T", [128, 8, 512], BF16)
            wu = [sbt(es, "wu%d" % i, [128, 8, 2, 128], BF16) for i in range(3)]
            cv = [sbt(es, "cv%d" % i, [128, 512], F32) for i in range(6)]
            sg = [sbt(es, "sg%d" % i, [128, 512], F32) for i in range(2)]
            aT = sbt(es, "aT", [128, 22, 512], BF16)
            uh = sbt(es, "uh", [128, NCH, 16], F32)
            wd = [sbt(es, "wd%d" % i, [128, 22, 128], BF16) for i in range(3)]
            yT = [sbt(es, "yT%d" % i, [128, 512], F32) for i in range(2)]
            pT = [pst(es, "p3T%d" % i, [128, 8, 128], BF16) for i in range(2)]
            psa = [pst(es, "psa%d" % i, [128, 512], F32) for i in range(2)]
            psu = [pst(es, "psu%d" % i, [128, 512], F32) for i in range(2)]
            psy = pst(es, "psy", [128, 512], F32)
            pst_ = pst(es, "pst", [128, 4, 128], F32)
            rr = RR(["act", "dve"])
            PSU = [psa[0], psa[1], psu[0], psu[1]]
            PSUK = ["psa0", "psa1", "psu0", "psu1"]
            for nm, t_, src in (("wout", wout, T["Wout"][:, :, :]), ("fox_g", fox_g, T["fox_g"][:, :]), ("sb_g", sb_g, T["sb_g"][:, :]),
                                ("ffn_g", ffn_g, T["ffn_g"][:, :]), ("fin_g", fin_g, T["fin_g"][:, :]), ("cw", cw, T["cw"][:, :, :]),
                                ("cb", cb, T["cb"][:, :]), ("hmask", hmask, T["hmask"][:, :]), ("ident", ident, T["ident_bf"][:, :]),
                                ("identf", identf, T["ident_f"][:, :])):
                DMA(P, "sp", t_[:], src, [], [nm])

            P.op("pool", lambda E: E.memset(on[:], 0.0), [], ["on"])
            P.op("pool", lambda E: E.memset(onT[:], 0.0), [], ["onT.%d" % b for b in range(4)])
            P.op("pool", lambda E: E.memset(h2T[:], 0.0), [], ["h2T.%d" % b for b in range(4)])
            sctr = [0]
            pctr = [0]
            wuc = [0]
            wdc = [0]

            def rstd_of(src_ap, rows, width):
                c = sctr[0] % 16
                sctr[0] += 1
                sk = "st%d" % c
                ACTF(P, junk[:rows, :width], src_ap, AF.Square, src_ap_keys[0], ["junk", sk], accum=stat[:rows, 0, c:c + 1])
                ACTF(P, stat[:rows, 1, c:c + 1], stat[:rows, 0, c:c + 1], AF.Ln, [sk], [sk], bias=EPS, scale=1.0 / width)
                ACTF(P, stat[:rows, 2, c:c + 1], stat[:rows, 1, c:c + 1], AF.Exp, [sk], [sk], scale=-0.5)
                return stat[:rows, 2, c:c + 1], sk
            src_ap_keys = [None]

            def transpose_to(src_tile, src_key, blocks, dstT, dst_key):
                for (bi, rows) in blocks:
                    n = pctr[0]
                    pctr[0] += 1
                    pt = pT[n % 2]
                    pk = "p3T%d" % (n % 2)
                    for k in range(8):
                        TR(P, pt[:, k, :], src_tile[:, bi, k * 128:(k + 1) * 128], ident[:, :], [src_key, "ident"], [pk])
                    rr.copy(P, dstT[:, :, bi * 128:bi * 128 + rows], pt[:, :, :rows], [pk], [dst_key + ".%d" % bi])

            slots = [NSL] + list(range(NSL))

            def load_slot(si):
                sl = slots[si]
                halo = sl == NSL
                r0 = NOWN if halo else 512 * sl
                sb_ = si % 2
                os_, xr_ = o_s[sb_], xr[sb_]
                osk, xrk = "o_s%d" % sb_, "xr%d" % sb_
                if halo:
                    DMA(P, "sp", os_[:16, 0, :], T["Od"][r0:r0 + 16, :], [], [osk])
                    DMA(P, "sp", xr_[:16, 0, :], T["xo"][r0:r0 + 16, :], [], [xrk])
                else:
                    DMA(P, "sp", os_[:, :, :], T["Od"][r0:r0 + 512, :].rearrange("(b p) d -> p b d", p=128), [], [osk])
                    DMA(P, "sp", xr_[:, :, :], T["xo"][r0:r0 + 512, :].rearrange("(b p) d -> p b d", p=128), [], [xrk])

            deferred = []

            def do_slot(si, sl):
                halo = sl == NSL
                W = 16 if halo else 512
                r0 = NOWN if halo else 512 * sl
                blocks = [(0, 16)] if halo else [(b, 128) for b in range(4)]
                sb_ = si % 2
                os_, xr_ = o_s[sb_], xr[sb_]
                osk, xrk = "o_s%d" % sb_, "xr%d" % sb_
                if si == 0:
                    load_slot(si)
                if si + 1 < len(slots):
                    load_slot(si + 1)
                for (bi, rows) in blocks:
                    for g in range(2):
                        src = os_[:rows, bi, g * 512:(g + 1) * 512]
                        src_ap_keys[0] = [osk]
                        rs, sk = rstd_of(src, rows, 512)
                        gt = fox_g if g == 0 else sb_g
                        P.op("dve", lambda E, src=src, rs=rs, gt=gt, rows=rows, bi=bi, g=g: E.scalar_tensor_tensor(
                            out=on[:rows, bi, g * 512:(g + 1) * 512], in0=src, scalar=rs, in1=gt[:rows, :], op0=ALU.mult, op1=ALU.mult),
                            [osk, sk, "fox_g", "sb_g"], ["on"])
                transpose_to(on, "on", blocks, onT, "onT")
                onk = ["onT.%d" % bi for (bi, _) in blocks]
                for (bi, rows) in blocks:
                    for ch in range(2):
                        n = pctr[0]
                        pctr[0] += 1
                        pa = psa[n % 2]
                        pk = "psa%d" % (n % 2)
                        for k in range(8):
                            MM(P, pa[:, :], onT[:, k, bi * 128:bi * 128 + 128], wout[:, k, ch * 512:(ch + 1) * 512], k == 0, k == 7,
                               ["wout", "onT.%d" % bi], [pk])
                        P.op("dve", lambda E, pa=pa, rows=rows, bi=bi, ch=ch: E.tensor_tensor(
                            out=xr_[:rows, bi, ch * 512:(ch + 1) * 512], in0=pa[:rows, :], in1=xr_[:rows, bi, ch * 512:(ch + 1) * 512], op=ALU.add),
                            [pk, xrk], [xrk])
                for (bi, rows) in blocks:
                    src = xr_[:rows, bi, :]
                    src_ap_keys[0] = [xrk]
                    rs, sk = rstd_of(src, rows, D)
                    P.op("dve", lambda E, src=src, rs=rs, rows=rows, bi=bi: E.scalar_tensor_tensor(
                        out=on[:rows, bi, :], in0=src, scalar=rs, in1=ffn_g[:rows, :], op0=ALU.mult, op1=ALU.mult),
                        [xrk, sk, "ffn_g"] + onk, ["on"])
                transpose_to(on, "on", blocks, h2T, "h2T")
                h2k = ["h2T.%d" % bi for (bi, _) in blocks]
                for c in range(22):
                    wn = wuc[0]
                    wuc[0] += 1
                    wt = wu[wn % 3]
                    wk = "wu%d" % (wn % 3)
                    DMA(P, "sp", wt[:, :, 0, :], T["Wup"][:, :, c * 128:(c + 1) * 128], [], [wk + "g"])
                    DMA(P, "sp", wt[:, :, 1, :], T["Wup"][:, :, DFF + c * 128:DFF + (c + 1) * 128], [], [wk + "v"])
                    pend = []
                    prev_fin = deferred[:]
                    del deferred[:]
                    for gv in range(2):
                        un = (2 * wn + gv) % 4
                        pu_ = PSU[un]
                        pk = PSUK[un]
                        for k in range(8):
                            MM(P, pu_[:, :W], wt[:, k, gv, :], h2T[:, k, :W], k == 0, k == 7, [wk + ("g" if gv == 0 else "v")] + h2k, [pk])
                        cc = c + 22 * gv
                        if halo:
                            P.op("dve", lambda E, pu_=pu_, cc=cc: E.tensor_tensor(out=uh[:, cc, :], in0=pu_[:, :16], in1=hmask[:, :], op=ALU.mult),
                                 [pk, "hmask"], ["uh"])
                            continue
                        cn = (2 * wn + gv) % 6
                        ct, ck = cv[cn], "cv%d" % cn
                        ACTF(P, ct[:, :], pu_[:, :512], AF.Identity, [pk, "cw", "cb"], [ck], bias=cb[:, cc:cc + 1], scale=cw[:, cc, 2:3])
                        pend.append((pu_, pk, ct, ck, cc))
                    if halo:
                        continue
                    for f_ in prev_fin:
                        f_()
                    for (pu_, pk, ct, ck, cc) in pend:
                        P.op("dve", lambda E, pu_=pu_, ct=ct, cc=cc: E.scalar_tensor_tensor(out=ct[:, 1:512], in0=pu_[:, 0:511], scalar=cw[:, cc, 1:2], in1=ct[:, 1:512],
                                                                                         op0=ALU.mult, op1=ALU.add), [pk, "cw", ck], [ck])
                    for (pu_, pk, ct, ck, cc) in pend:
                        P.op("dve", lambda E, pu_=pu_, ct=ct, cc=cc: E.scalar_tensor_tensor(out=ct[:, 2:512], in0=pu_[:, 0:510], scalar=cw[:, cc, 0:1], in1=ct[:, 2:512],
                                                                                         op0=ALU.mult, op1=ALU.add), [pk, "cw", ck], [ck])
                    for (pu_, pk, ct, ck, cc) in pend:
                        P.op("dve", lambda E, ct=ct, cc=cc: E.scalar_tensor_tensor(out=ct[:, 0:1], in0=uh[:, cc, 2 * sl + 1:2 * sl + 2], scalar=cw[:, cc, 1:2], in1=ct[:, 0:1],
                                                                                op0=ALU.mult, op1=ALU.add), ["uh", "cw", ck], [ck])
                    for (pu_, pk, ct, ck, cc) in pend:
                        P.op("dve", lambda E, ct=ct, cc=cc: E.scalar_tensor_tensor(out=ct[:, 0:2], in0=uh[:, cc, 2 * sl:2 * sl + 2], scalar=cw[:, cc, 0:1], in1=ct[:, 0:2],
                                                                                op0=ALU.mult, op1=ALU.add), ["uh", "cw", ck], [ck])
                    (_, _, ctg, ckg, _), (_, _, ctv, ckv, _) = pend
                    st_, stk = sg[wn % 2], "sg%d" % (wn % 2)

                    def fin(ctg=ctg, ckg=ckg, ctv=ctv, ckv=ckv, st_=st_, stk=stk, c=c):
                        ACTF(P, st_[:, :], ctg[:, :], AF.Silu, [ckg], [stk])
                        P.op("pool", lambda E: E.tensor_tensor(out=aT[:, c, :], in0=st_[:, :], in1=ctv[:, :], op=ALU.mult),
                             [stk, ckv], ["aT.%d" % c])
                    deferred.append(fin)
                for f_ in deferred:
                    f_()
                del deferred[:]
                if halo:
                    return
                for cc in range(8):
                    wn = wdc[0]
                    wdc[0] += 1
                    wt = wd[wn % 3]
                    wk = "wd%d" % (wn % 3)
                    DMA(P, "sp", wt[:, :, :], T["Wdn"][:, :, cc * 128:(cc + 1) * 128], [], [wk])
                    for c in range(22):
                        MM(P, psy[:, :], wt[:, c, :], aT[:, c, :], c == 0, c == 21, [wk, "aT.%d" % c], ["psy"])
                    yt, yk = yT[wn % 2], "yT%d" % (wn % 2)
                    ACTF(P, yt[:, :], psy[:, :], AF.Copy, ["psy"], [yk])
                    for b in range(4):
                        P.op("pe", lambda E, yt=yt, b=b: E.transpose(pst_[:, b, :], yt[:, b * 128:(b + 1) * 128], identf[:, :]), [yk, "identf"], ["pst"])
                    P.op("dve", lambda E, cc=cc: E.tensor_tensor(out=xr_[:, :, cc * 128:(cc + 1) * 128], in0=pst_[:, :, :],
                                                               in1=xr_[:, :, cc * 128:(cc + 1) * 128], op=ALU.add), ["pst", xrk], [xrk])
                for (bi, rows) in blocks:
                    src = xr_[:rows, bi, :]
                    src_ap_keys[0] = [xrk]
                    rs, sk = rstd_of(src, rows, D)
                    P.op("dve", lambda E, src=src, rs=rs: E.scalar_tensor_tensor(out=src, in0=src, scalar=rs, in1=fin_g[:, :], op0=ALU.mult, op1=ALU.mult),
                         [xrk, sk, "fin_g"], [xrk])
                DMA(P, "pool", T["out"][r0:r0 + 512, :].rearrange("(b p) d -> p b d", p=128), xr_[:, :, :], [xrk], ["out"])

            for si, sl in enumerate(slots):
                do_slot(si, sl)
            P.emit()
            stats.append(P.stats)
    return nc, stats


_CACHE = {}


def _consts(S):
    NB = S // 128
    bf = ml_dtypes.bfloat16
    p = np.arange(128)
    c = {}
    c["ident_bf"] = np.eye(128, dtype=np.float32).astype(bf)
    c["ident_f"] = np.eye(128, dtype=np.float32)
    c["triu_f"] = (p[:, None] <= p[None, :]).astype(np.float32)
    c["ones_f"] = np.ones((128, 128), np.float32)
    c["negtri"] = (-(p[:, None] >= p[None, :]).astype(np.float32)).astype(bf)
    c["negones"] = (-np.ones((128, 128), np.float32)).astype(bf)
    t = np.arange(512)
    key = (np.arange(4)[None, :, None] * 128 + p[:, None, None])
    c["mt_f"] = np.where(key <= t[None, None, :], 0.0, NEG).astype(np.float32).astype(bf)
    c["mt_s"] = np.where(key < t[None, None, :], 0.0, NEG).astype(np.float32).astype(bf)
    c["negrow"] = np.full((1, 512), NEG, np.float32).astype(bf)
    c["cm1"] = np.full((64, 3, 128), -1.0, np.float32).astype(bf)
    c["cp1"] = np.full((64, 3, 128), 1.0, np.float32).astype(bf)
    return c


def _core_consts(S, par):
    NT = S // 512
    NSL = NT // 2
    NB = S // 128
    bf = ml_dtypes.bfloat16
    own = own_tiles(par, NSL)
    esel = np.zeros((128, 2 * NSL), np.float32)
    eselh = np.zeros((128, 2 * NSL), np.float32)
    hmask = np.ones((128, 16), np.float32)
    idE = np.zeros((128, NSL, 128), np.float32)
    idNE = np.zeros((128, NSL, 128), np.float32)
    erow = np.zeros((1, NSL, 128), np.float32)
    I = np.eye(128, dtype=np.float32)
    p = np.arange(128)
    keypos = (np.arange(NB)[None, :] * 128 + p[:, None])
    mh_f = np.zeros((128, NB, 16), np.float32)
    mh_s = np.zeros((128, NB, 16), np.float32)
    for j in range(NSL):
        e0 = 1.0 if own[j] == 2 * j else 0.0
        esel[:, 2 * j] = e0
        esel[:, 2 * j + 1] = 1.0 - e0
        eselh[:, 2 * j] = 1.0 if (own[j] == 2 * j and j > 0) else 0.0
        eselh[:, 2 * j + 1] = 1.0 if own[j] == 2 * j + 1 else 0.0
        idE[:, j, :] = e0 * I
        idNE[:, j, :] = (1.0 - e0) * I
        erow[0, j, :] = e0
        for r in range(2):
            col = 2 * j + r
            pos = 512 * own[j] - 2 + r
            if own[j] == 0:
                hmask[:, col] = 0.0
                mh_f[:, :, col] = np.where(keypos == 0, 0.0, NEG)
                mh_s[:, :, col] = np.where(keypos == 0, 0.0, NEG)
            else:
                mh_f[:, :, col] = np.where(keypos <= pos, 0.0, NEG)
                mh_s[:, :, col] = np.where(keypos < pos, 0.0, NEG)
    return dict(esel=esel, eselh=eselh, hmask=hmask, idE=idE.astype(bf), idNE=idNE.astype(bf), erow=erow.astype(bf),
                mh_f=mh_f.astype(bf), mh_s=mh_s.astype(bf)), own


def _prepare(inputs, debug=False):
    x = np.asarray(inputs["x"], np.float32)
    B, S, _ = x.shape
    NT = S // 512
    NSL = NT // 2
    rep = lambda v: np.ascontiguousarray(np.broadcast_to(np.asarray(v, np.float32).reshape(1, -1), (128, np.asarray(v).size)))
    shared = dict(_consts(S))
    shared["w_in"] = np.ascontiguousarray(np.asarray(inputs["w_in"], np.float32)[0])
    shared["w_out"] = np.ascontiguousarray(np.asarray(inputs["w_out"], np.float32)[0])
    shared["w_up"] = np.ascontiguousarray(np.asarray(inputs["w_up"], np.float32)[0])
    shared["w_down"] = np.ascontiguousarray(np.asarray(inputs["w_down"], np.float32)[0])
    shared["attn_g"] = rep(inputs["attn_norm_g"][0])
    shared["ffn_g"] = rep(inputs["ffn_norm_g"][0])
    shared["fin_g"] = rep(inputs["final_norm_g"])
    shared["fox_g"] = rep(inputs["fox_out_g"][0])
    shared["sb_g"] = rep(inputs["sb_out_g"][0])
    shared["fb"] = rep(inputs["forget_bias"][0])
    cwv = np.asarray(inputs["conv_w"], np.float32)[0]
    shared["cw"] = np.ascontiguousarray(cwv.reshape(3, NCH, 128).transpose(2, 1, 0))
    shared["cb"] = np.ascontiguousarray(np.asarray(inputs["conv_b"], np.float32)[0].reshape(NCH, 128).T)
    in_maps = []
    owns = []
    for c in range(8):
        b, par = c // 2, c % 2
        cc, own = _core_consts(S, par)
        owns.append(own)
        xo = np.zeros((NSL * 512 + 16, D), np.float32)
        for j, t in enumerate(own):
            xo[j * 512:(j + 1) * 512] = x[b, t * 512:(t + 1) * 512]
            if t > 0:
                xo[NSL * 512 + 2 * j:NSL * 512 + 2 * j + 2] = x[b, t * 512 - 2:t * 512]
        m = dict(shared)
        m.update(cc)
        m["xn"] = np.ascontiguousarray(x[b])
        m["xo"] = xo
        in_maps.append(m)
    return in_maps, owns, (B, S)


def kernel(**inputs):
    in_maps, owns, (B, S) = _prepare(inputs)
    if S not in _CACHE:
        _CACHE[S] = build_program(S)
    nc, _ = _CACHE[S]
    res = run_bass_kernel_spmd(nc, in_maps, core_ids=list(range(8)))
    out = np.zeros((B, S, D), np.float32)
    for c in range(8):
        b = c // 2
        o = np.asarray(res.results[c]["out"], np.float32)
        for j, t in enumerate(owns[c]):
            out[b, t * 512:(t + 1) * 512] = o[j * 512:(j + 1) * 512]
    return out
```

```python
import numpy as np
import ml_dtypes
from contextlib import ExitStack
import concourse.bass as bass
import concourse.mybir as mybir
from concourse.bass_utils import run_bass_kernel_spmd

F32 = mybir.dt.float32
BF16 = mybir.dt.bfloat16
AF = mybir.ActivationFunctionType
ALU = mybir.AluOpType

D = 1024
DH = 64
DFF = 2816
INC = 3080
EPS = 1e-6
NEG = -30000.0
CQ_F, CK_F, CV_F, CL, CQ_S, CK_S, CV_S = 0, 512, 1024, 1536, 1544, 2056, 2568
NCH = 2 * DFF // 128

import os
OPT = set(os.environ.get("KOPT", "").split(","))
COMPUTE = ("pe", "act", "dve", "pool")
CH = 16000
NDMA = 8


class Prog:
    def __init__(self, nc, gs, tag):
        self.nc = nc
        self.gs = gs
        self.tag = tag
        self.ops = []
        self.last_w = {}
        self.readers = {}

    def op(self, eng, fn, reads=(), writes=(), dma=False):
        j = len(self.ops)
        deps = set()
        for r in reads:
            if r in self.last_w:
                deps.add(self.last_w[r])
        for w in writes:
            if w in self.last_w:
                deps.add(self.last_w[w])
            for rd in self.readers.get(w, ()):
                deps.add(rd)
        deps.discard(j)
        self.ops.append(dict(eng=eng, fn=fn, deps=deps, dma=dma, sig=False))
        for r in reads:
            self.readers.setdefault(r, []).append(j)
        for w in writes:
            self.last_w[w] = j
            self.readers[w] = []
        return j

    def dma(self, q, fn, reads=(), writes=()):
        return self.op(q, fn, reads, writes, dma=True)

    def emit(self):
        nc, ops = self.nc, self.ops
        for j, o in enumerate(ops):
            nd = set()
            for d in o["deps"]:
                p = ops[d]
                if not p["dma"] and not o["dma"] and p["eng"] == o["eng"] == "pe":
                    continue
                nd.add(d)
            o["deps"] = nd
            for d in nd:
                ops[d]["sig"] = True
        cnt = {e: 0 for e in COMPUTE}
        sems = {}

        def getsem(key):
            if key not in sems:
                sems[key] = self.gs.enter_context(nc.semaphore("s%s_%s_%s" % (self.tag, key[0], key[1])))
            return sems[key]

        dcount = {}
        dma_i = {}
        for j, o in enumerate(ops):
            if o["dma"]:
                q = o["eng"]
                k = dma_i.get(q, 0)
                dma_i[q] = k + 1
                key = ("d" + q, k % NDMA)
                dcount[key] = dcount.get(key, 0) + 16
                o["semkey"], o["semval"] = key, dcount[key]
                o["prev"] = (key, dcount[key] - 16)
            elif o["sig"]:
                e = o["eng"]
                c = cnt[e]
                cnt[e] = c + 1
                o["semkey"], o["semval"] = (e, c // CH), c % CH + 1
        last_dma = {}
        for o in ops:
            if o["dma"]:
                last_dma[o["semkey"]] = max(last_dma.get(o["semkey"], 0), o["semval"])
            if "semkey" in o:
                getsem(o["semkey"])
        per = {e: [] for e in ("sp", "act", "dve", "pe", "pool")}
        for j, o in enumerate(ops):
            per[o["eng"]].append(j)
        nwc = [0]

        def run_engine(e, E):
            known = {}
            for j in per[e]:
                o = ops[j]
                need = {}
                for d in o["deps"]:
                    p = ops[d]
                    k, v = p["semkey"], p["semval"]
                    if need.get(k, 0) < v:
                        need[k] = v
                if o["dma"] and o["prev"][1] > 0:
                    k, v = o["prev"]
                    if need.get(k, 0) < v:
                        need[k] = v
                for k, v in need.items():
                    if known.get(k, 0) >= v:
                        continue
                    known[k] = v
                    E.wait_ge(sems[k], v)
                    nwc[0] += 1
                inst = o["fn"](E)
                if o["dma"]:
                    inst.then_inc(sems[o["semkey"]], 16)
                elif o["sig"]:
                    inst.then_inc(sems[o["semkey"]], 1)
            if e == "sp":
                for k, v in last_dma.items():
                    if known.get(k, 0) < v:
                        E.wait_ge(sems[k], v)

        with nc.Block() as block:
            @block.sync
            def _(E):
                run_engine("sp", E)

            @block.scalar
            def _(E):
                run_engine("act", E)

            @block.vector
            def _(E):
                run_engine("dve", E)

            @block.tensor
            def _(E):
                run_engine("pe", E)

            @block.gpsimd
            def _(E):
                run_engine("pool", E)
        self.stats = dict(tag=self.tag, nops=len(ops), nwaits=nwc[0], nsems=len(sems), cnt=cnt)


def MM(P, out, lhsT, rhs, start, stop, rd, wr, skip=False):
    P.op("pe", lambda E: E.matmul(out, lhsT=lhsT, rhs=rhs, start=start, stop=stop, skip_group_check=skip), rd, wr)


def TR(P, out, in_, ident, rd, wr):
    P.op("pe", lambda E: E.transpose(out, in_, ident), rd, wr)


def ACTF(P, out, in_, func, rd, wr, bias=None, scale=None, accum=None):
    kw = {}
    if bias is not None:
        kw["bias"] = bias
    if scale is not None:
        kw["scale"] = scale
    if accum is not None:
        kw["accum_out"] = accum
    P.op("act", lambda E: E.activation(out=out, in_=in_, func=func, **kw), rd, wr)


def DMA(P, q, out, in_, rd, wr):
    P.dma(q, lambda E: E.dma_start(out=out, in_=in_), rd, wr)


class RR:
    def __init__(self, engs):
        self.engs = engs
        self.i = 0

    def copy(self, P, out, in_, rd, wr, scale=None):
        e = self.engs[self.i % len(self.engs)]
        self.i += 1
        if e == "act":
            ACTF(P, out, in_, AF.Copy, rd, wr, scale=scale)
        elif scale is None:
            P.op(e, lambda E: E.tensor_copy(out=out, in_=in_), rd, wr)
        else:
            P.op(e, lambda E: E.tensor_scalar(out=out, in0=in_, scalar1=float(scale), scalar2=None, op0=ALU.mult), rd, wr)


def own_tiles(p, NSL):
    res = []
    for j in range(NSL):
        first = (j % 2 == 0) if p == 0 else (j % 2 == 1)
        res.append(2 * j if first else 2 * j + 1)
    return res


def build_program(S, debug=False, stop_after="d"):
    NT = S // 512
    NSL = NT // 2
    NB = S // 128
    NOWN = NSL * 512
    NQ = NOWN + 16
    nc = bass.Bass("TRN2", target_bir_lowering=False)
    T = {}

    def din(name, shape, dt=F32):
        T[name] = nc.dram_tensor(name, list(shape), dt, kind="ExternalInput").ap()

    def dscr(name, shape, dt=BF16):
        kind = "ExternalOutput" if debug else "Internal"
        T[name] = nc.dram_tensor(name, list(shape), dt, kind=kind).ap()

    din("xn", [S, D]); din("xo", [NQ, D])
    din("w_in", [D, INC]); din("w_out", [D, D]); din("w_up", [D, 2 * DFF]); din("w_down", [DFF, D])
    din("attn_g", [128, D]); din("ffn_g", [128, D]); din("fin_g", [128, D])
    din("fox_g", [128, 512]); din("sb_g", [128, 512]); din("fb", [128, 8])
    din("cw", [128, NCH, 3]); din("cb", [128, NCH])
    din("esel", [128, 2 * NSL]); din("eselh", [128, 2 * NSL]); din("hmask", [128, 16])
    din("idE", [128, NSL, 128], BF16); din("idNE", [128, NSL, 128], BF16); din("erow", [1, NSL, 128], BF16)
    din("mh_f", [128, NB, 16], BF16); din("mh_s", [128, NB, 16], BF16)
    din("ident_bf", [128, 128], BF16); din("ident_f", [128, 128]); din("triu_f", [128, 128]); din("ones_f", [128, 128])
    din("negtri", [128, 128], BF16); din("negones", [128, 128], BF16)
    din("mt_f", [128, 4, 512], BF16); din("mt_s", [128, 4, 512], BF16); din("negrow", [1, 512], BF16)
    din("cm1", [64, 3, 128], BF16); din("cp1", [64, 3, 128], BF16)
    T["out"] = nc.dram_tensor("out", [NOWN, D], F32, kind="ExternalOutput").ap()
    dscr("Kd", [16, 64, S]); dscr("Vd", [NT, 128, 16, 4, 65]); dscr("Qd", [16, 64, NQ])
    dscr("KA", [8, 6, S]); dscr("QA", [8, 6, S]); dscr("Od", [NQ + 112, D])
    dscr("Wup", [128, 8, 2 * DFF]); dscr("Wdn", [128, 22, D]); dscr("Wout", [128, 8, D])
    if debug:
        dscr("dbg_nF", [128, 8, NB], F32)

    stats = []
    with ExitStack() as gs:
        def sbt(es, name, shape, dt):
            return es.enter_context(nc.sbuf_tensor("sb_" + name, list(shape), dt))

        def pst(es, name, shape, dt):
            return es.enter_context(nc.psum_tensor("pp_" + name, list(shape), dt))

        with ExitStack() as es:
            P = Prog(nc, gs, "a")
            win = sbt(es, "win", [128, 8, INC], BF16)
            wstg = [sbt(es, "wstg%d" % i, [128, INC], F32) for i in range(2)]
            g_r = sbt(es, "g_r", [128, D], F32)
            fb_r = sbt(es, "fb_r", [128, 8], F32)
            ident = sbt(es, "ident", [128, 128], BF16)
            xb = [sbt(es, "xb%d" % i, [128, D], F32) for i in range(3)]
            junk = sbt(es, "junk", [128, D], BF16)
            stat = sbt(es, "stat", [128, 3, 8], F32)
            xnb = [sbt(es, "xnb%d" % i, [128, D], BF16) for i in range(2)]
            hT = [sbt(es, "hT%d" % i, [128, 8, 512], BF16) for i in range(2)]
            kts = [sbt(es, "kts%d" % i, [128, 512], BF16) for i in range(3)]
            vs = [sbt(es, "vs%d" % i, [128, 16, 4, 65], BF16) for i in range(2)]
            flog = sbt(es, "flog", [128, 8, NB], F32)
            esp = ExitStack()
            pT = [pst(esp, "pT%d" % i, [128, 8, 128], BF16) for i in range(2)]
            psk = [pst(esp, "psk%d" % i, [128, 512], F32) for i in range(2)]
            psv = [pst(esp, "psv%d" % i, [128, 512], F32) for i in range(2)]
            psf = [pst(esp, "psf%d" % i, [128, 512], F32) for i in range(2)]
            rr = RR(["act", "dve"])

            DMA(P, "sp", g_r[:], T["attn_g"][:, :], [], ["g_r"])
            DMA(P, "sp", fb_r[:], T["fb"][:, :], [], ["fb_r"])
            DMA(P, "sp", ident[:], T["ident_bf"][:, :], [], ["ident"])
            for k in range(8):
                DMA(P, "sp", wstg[k % 2][:], T["w_in"][k * 128:(k + 1) * 128, :], [], ["wstg%d" % (k % 2)])
                P.op("pool" if k % 2 == 0 else "dve", lambda E, k=k: E.tensor_copy(out=win[:, k, :], in_=wstg[k % 2][:]), ["wstg%d" % (k % 2)], ["win.%d" % k])
            for i in range(2):
                P.op("pool", lambda E, i=i: E.memset(vs[i][:], 1.0), [], ["vs%d.%d.%d" % (i, b, g) for b in range(4) for g in range(2)])
                P.op("pool", lambda E, i=i: E.memset(xnb[i][:], 0.0), [], ["xnb%d" % i])
            blkn = [0]

            def norm_block(src_rows, rows, hbuf, col0):
                st = {}

                def part1():
                    n = blkn[0]
                    blkn[0] += 1
                    st["n"] = n
                    x = xb[n % 3]
                    xs = "xb%d" % (n % 3)
                    sk = "stat%d" % (n % 8)
                    DMA(P, "sp", x[:rows, :], src_rows, [], [xs])
                    ACTF(P, junk[:rows, :], x[:rows, :], AF.Square, [xs], ["junk", sk], accum=stat[:rows, 0, n % 8:n % 8 + 1])
                    ACTF(P, stat[:rows, 1, n % 8:n % 8 + 1], stat[:rows, 0, n % 8:n % 8 + 1], AF.Ln, [sk], [sk], bias=EPS, scale=1.0 / D)
                    ACTF(P, stat[:rows, 2, n % 8:n % 8 + 1], stat[:rows, 1, n % 8:n % 8 + 1], AF.Exp, [sk], [sk], scale=-0.5)
                    xq = xnb[n % 2]
                    qs = "xnb%d" % (n % 2)
                    P.op("dve", lambda E: E.scalar_tensor_tensor(out=xq[:rows, :], in0=x[:rows, :], scalar=stat[:rows, 2, n % 8:n % 8 + 1],
                                                                 in1=g_r[:rows, :], op0=ALU.mult, op1=ALU.mult), [xs, sk, "g_r"], [qs])

                def part2():
                    n = st["n"]
                    xq = xnb[n % 2]
                    qs = "xnb%d" % (n % 2)
                    pt = pT[n % 2]
                    ps = "pT%d" % (n % 2)
                    for k in range(8):
                        TR(P, pt[:, k, :], xq[:, k * 128:(k + 1) * 128], ident[:, :], [qs, "ident"], [ps])
                    rr.copy(P, hT[hbuf][:, :, col0:col0 + rows], pt[:, :, :rows], [ps], ["hT%d.%d" % (hbuf, col0 // 128)])
                return part1, part2

            def proj_T(hbuf, width, col_w, dst, hkeys, scale=None):
                n = proj_T.n
                proj_T.n += 1
                ps = psk[n % 2]
                pk = "psk%d" % (n % 2)
                for k in range(8):
                    MM(P, ps[:, :width], win[:, k, col_w:col_w + 128], hT[hbuf][:, k, :width], k == 0, k == 7, ["win.%d" % k] + hkeys, [pk])
                ks = kts[n % 3]
                kk = "kts%d" % (n % 3)
                rr.copy(P, ks[:, :width], ps[:, :width], [pk], [kk], scale=scale)
                DMA(P, "pool", dst, ks[:, :width], [kk], ["KQscr"])
            proj_T.n = 0

            vcount = [0]

            def make_tile_A(Tn, hb):
                blocks = [norm_block(T["xn"][Tn * 512 + b * 128:Tn * 512 + (b + 1) * 128, :], 128, hb, b * 128) for b in range(4)]
                hk = ["hT%d.%d" % (hb, b) for b in range(4)]
                vb = Tn % 2
                groups = []

                def kgrp(c):
                    col = (CK_F + c * 128) if c < 4 else (CK_S + (c - 4) * 128)
                    h0 = 2 * c if c < 4 else 8 + 2 * (c - 4)
                    proj_T(hb, 512, col, T["Kd"][h0:h0 + 2, :, Tn * 512:(Tn + 1) * 512].rearrange("h r t -> (h r) t"), hk)

                def vgrp(b, g):
                    n = vcount[0]
                    vcount[0] += 1
                    ps = psv[n % 2]
                    pk = "psv%d" % (n % 2)
                    vc = CV_F if g == 0 else CV_S
                    for k in range(8):
                        MM(P, ps[:, :], hT[hb][:, k, b * 128:(b + 1) * 128], win[:, k, vc:vc + 512], k == 0, k == 7,
                           ["win.%d" % k, "hT%d.%d" % (hb, b)], [pk])
                    rr.copy(P, vs[vb][:, g * 8:(g + 1) * 8, b, 0:64], ps[:, :].rearrange("p (h d) -> p h d", h=8), [pk],
                            ["vs%d.%d.%d" % (vb, b, g)])
                    if g == 1:
                        pf = psf[b % 2]
                        for k in range(8):
                            MM(P, pf[:, 0:8], hT[hb][:, k, b * 128:(b + 1) * 128], win[:, k, CL:CL + 8], k == 0, k == 7,
                               ["win.%d" % k, "hT%d.%d" % (hb, b)], ["psf%d" % (b % 2)])
                        P.op("dve", lambda E: E.tensor_tensor(out=flog[:, :, Tn * 4 + b], in0=pf[:, 0:8], in1=fb_r[:, :], op=ALU.add),
                             ["psf%d" % (b % 2), "fb_r"], ["flog"])

                def vstore():
                    DMA(P, "sp", T["Vd"][Tn, :, :, :, :], vs[vb][:, :, :, :],
                        ["vs%d.%d.%d" % (vb, b, g) for b in range(4) for g in range(2)], ["Vscr"])
                for c in range(8):
                    groups.append(lambda c=c: kgrp(c))
                for b in range(4):
                    for g in range(2):
                        groups.append(lambda b=b, g=g: vgrp(b, g))
                groups.append(vstore)
                return blocks, groups

            def make_tile_Q(row0, width, col0, hb):
                nb = (width + 127) // 128
                blocks = []
                for b in range(nb):
                    rows = min(128, width - b * 128)
                    blocks.append(norm_block(T["xo"][row0 + b * 128:row0 + b * 128 + rows, :], rows, hb, b * 128))
                hk = ["hT%d.%d" % (hb, b) for b in range(nb)]

                def qgrp(c):
                    col = (CQ_F + c * 128) if c < 4 else (CQ_S + (c - 4) * 128)
                    h0 = 2 * c if c < 4 else 8 + 2 * (c - 4)
                    proj_T(hb, width, col, T["Qd"][h0:h0 + 2, :, col0:col0 + width].rearrange("h r t -> (h r) t"), hk, scale=0.125)
                groups = [(lambda c=c: qgrp(c)) for c in range(8)]
                return blocks, groups

            seq = []
            qi = 0
            for Tn in range(NT):
                seq.append(("A", Tn))
                if Tn % 2 == 1 and qi < NSL:
                    seq.append(("Q", qi))
                    qi += 1
            seq.append(("H", 0))
            tiles = []
            for i, (kind, idx) in enumerate(seq):
                hb = i % 2
                if kind == "A":
                    tiles.append(make_tile_A(idx, hb))
                elif kind == "Q":
                    tiles.append(make_tile_Q(idx * 512, 512, idx * 512, hb))
                else:
                    tiles.append(make_tile_Q(NOWN, 16, NOWN, hb))
            for (p1, p2) in tiles[0][0]:
                p1()
                p2()
            for i, (blocks, groups) in enumerate(tiles):
                nxt = tiles[i + 1][0] if i + 1 < len(tiles) else []
                G = len(groups)
                nb_ = max(1, len(nxt))
                ev = {}
                for b, (p1, p2) in enumerate(nxt):
                    ev.setdefault(int(b * G / nb_), []).append(p1)
                    ev.setdefault(min(G - 1, int((b + 0.7) * G / nb_)), []).append(p2)
                for gi, g in enumerate(groups):
                    g()
                    for f_ in ev.get(gi, []):
                        f_()
            P.emit()
            stats.append(P.stats)
            esp.close()
            if stop_after == "a":
                return nc, stats

            with ExitStack() as es2:
                P = Prog(nc, gs, "b")
                nlf = sbt(es2, "nlf", [128, 8 * NB], F32)
                ee = sbt(es2, "ee", [128, 8 * NB], F32)
                sc = [sbt(es2, "sc%d" % i, [128, 8, NB], F32) for i in range(2)]
                tot = sbt(es2, "tot", [128, 8, NB], F32)
                nF = sbt(es2, "nF", [128, 8, NB], F32)
                nFp = sbt(es2, "nFp", [128, 8, 128], F32)
                nFT = sbt(es2, "nFT", [NB, 8, 128], F32)
                r1 = sbt(es2, "r1", [NB, 8, 128], F32)
                parts = [sbt(es2, "part%d" % i, [NB, 8, 128], BF16) for i in range(3)]
                triu = sbt(es2, "triu", [128, 128], F32)
                ones = sbt(es2, "ones", [128, 128], F32)
                identf = sbt(es2, "identf", [128, 128], F32)
                cm1 = sbt(es2, "cm1", [64, 3, 128], BF16)
                cp1 = sbt(es2, "cp1", [64, 3, 128], BF16)
                ps_c = pst(es2, "ps_c", [128, 512], F32)
                ps_t = pst(es2, "ps_t", [128, 512], F32)
                ps_x = [pst(es2, "ps_x%d" % i, [128, 4, 128], F32) for i in range(2)]
                W8 = 8 * NB
                DMA(P, "sp", triu[:], T["triu_f"][:, :], [], ["triu"])
                DMA(P, "sp", ones[:], T["ones_f"][:, :], [], ["ones"])
                DMA(P, "sp", identf[:], T["ident_f"][:, :], [], ["identf"])
                DMA(P, "sp", cm1[:], T["cm1"][:, :, :], [], ["cm1"])
                DMA(P, "sp", cp1[:], T["cp1"][:, :, :], [], ["cp1"])
                fl2 = flog[:, :, :].rearrange("p h b -> p (h b)")
                ACTF(P, ee[:, :], fl2, AF.Exp, [], ["ee"], scale=-1.0)
                ACTF(P, nlf[:, :], ee[:, :], AF.Ln, ["ee"], ["nlf"], bias=1.0)
                MM(P, ps_c[:, :W8], triu[:, :], nlf[:, :], True, True, ["triu", "nlf"], ["ps_c"])
                MM(P, ps_t[:, :W8], ones[:, :], nlf[:, :], True, True, ["ones", "nlf"], ["ps_t"])
                P.op("dve", lambda E: E.tensor_copy(out=tot[:, :, :], in_=ps_t[:, :W8].rearrange("p (h b) -> p h b", h=8)), ["ps_t"], ["tot"])
                P.op("dve", lambda E: E.tensor_copy(out=sc[0][:, :, :], in_=tot[:, :, :]), ["tot"], ["sc0"])
                cur = 0
                d = 1
                while d < NB:
                    nxt = 1 - cur
                    P.op("dve", lambda E, cur=cur, nxt=nxt, d=d: E.tensor_copy(out=sc[nxt][:, :, 0:d], in_=sc[cur][:, :, 0:d]), ["sc%d" % cur], ["sc%d" % nxt])
                    P.op("dve", lambda E, cur=cur, nxt=nxt, d=d: E.tensor_tensor(out=sc[nxt][:, :, d:NB], in0=sc[cur][:, :, d:NB], in1=sc[cur][:, :, 0:NB - d], op=ALU.add),
                         ["sc%d" % cur], ["sc%d" % nxt])
                    cur = nxt
                    d *= 2
                P.op("dve", lambda E, cur=cur: E.tensor_tensor(out=tot[:, :, :], in0=sc[cur][:, :, :], in1=tot[:, :, :], op=ALU.subtract), ["sc%d" % cur, "tot"], ["tot"])
                P.op("dve", lambda E: E.tensor_tensor(out=nF[:, :, :], in0=ps_c[:, :W8].rearrange("p (h b) -> p h b", h=8), in1=tot[:, :, :], op=ALU.add), ["ps_c", "tot"], ["nF"])
                if debug:
                    DMA(P, "sp", T["dbg_nF"][:, :, :], nF[:, :, :], ["nF"], ["dbg"])
                P.op("dve", lambda E: E.memset(nFp[:], 0.0), [], ["nFp"])
                P.op("dve", lambda E: E.tensor_copy(out=nFp[:, :, 0:NB], in_=nF[:, :, :]), ["nF", "nFp"], ["nFp"])
                for h in range(8):
                    px = ps_x[h // 4]
                    P.op("pe", lambda E, h=h, px=px: E.transpose(px[:, h % 4, :], nFp[:, h, :], identf[:, :]), ["nFp", "identf"], ["ps_x%d" % (h // 4)])
                for q in range(2):
                    P.op("dve", lambda E, q=q: E.tensor_copy(out=nFT[:, q * 4:(q + 1) * 4, :], in_=ps_x[q][:NB, :, :]), ["ps_x%d" % q], ["nFT"])
                P.op("dve", lambda E: E.tensor_copy(out=parts[0][:, :, :], in_=nFT[:, :, :]), ["nFT"], ["part0"])
                P.op("dve", lambda E: E.tensor_tensor(out=r1[:, :, :], in0=nFT[:, :, :], in1=parts[0][:, :, :], op=ALU.subtract), ["nFT", "part0"], ["r1"])
                P.op("dve", lambda E: E.tensor_copy(out=parts[1][:, :, :], in_=r1[:, :, :]), ["r1"], ["part1"])
                P.op("dve", lambda E: E.tensor_tensor(out=nFT[:, :, :], in0=r1[:, :, :], in1=parts[1][:, :, :], op=ALU.subtract), ["r1", "part1"], ["nFT"])
                P.op("dve", lambda E: E.tensor_copy(out=parts[2][:, :, :], in_=nFT[:, :, :]), ["nFT"], ["part2"])
                for i in range(3):
                    DMA(P, "sp", T["KA"][:, i, :].rearrange("h (b t) -> b h t", t=128), parts[i][:, :, :], ["part%d" % i], ["KAs"])
                    DMA(P, "sp", T["QA"][:, 3 + i, :].rearrange("h (b t) -> b h t", t=128), parts[i][:, :, :], ["part%d" % i], ["QAs"])
                for h in range(8):
                    DMA(P, "sp", T["KA"][h, 3:6, :].rearrange("r (b t) -> b r t", t=128), cm1[:NB, :, :], ["cm1"], ["KAs"])
                    DMA(P, "sp", T["QA"][h, 0:3, :].rearrange("r (b t) -> b r t", t=128), cp1[:NB, :, :], ["cp1"], ["QAs"])
                P.emit()
                stats.append(P.stats)
        if stop_after == "b":
            return nc, stats

        with ExitStack() as es:
            P = Prog(nc, gs, "c")
            KT = [sbt(es, "KT%d" % i, [70, S], BF16) for i in range(2)]
            QT = [sbt(es, "QT%d" % i, [70, NQ], BF16) for i in range(2)]
            VV = [sbt(es, "VV%d" % i, [128, NB, 65], BF16) for i in range(2)]
            qa = sbt(es, "qa", [70, S], BF16)
            qtmp = sbt(es, "qtmp", [70, 512], BF16)
            esel = sbt(es, "esel", [128, 2 * NSL], F32)
            eselh = sbt(es, "eselh", [128, 2 * NSL], F32)
            idE = sbt(es, "idE", [128, NSL, 128], BF16)
            idNE = sbt(es, "idNE", [128, NSL, 128], BF16)
            erow = sbt(es, "erow", [1, NSL, 128], BF16)
            negrow = sbt(es, "negrow", [1, 512], BF16)
            allneg = sbt(es, "allneg", [128, 512], BF16)
            ident = sbt(es, "ident2", [128, 128], BF16)
            mt = [sbt(es, "mt%d" % i, [128, 4, 512], BF16) for i in range(2)]
            mh = [sbt(es, "mh%d" % i, [128, NB, 16], BF16) for i in range(2)]
            negtri = sbt(es, "negtri", [128, 128], BF16)
            negones = sbt(es, "negones", [128, 128], BF16)
            PT = [sbt(es, "PT%d" % i, [128, 2, 512], BF16) for i in range(3)]
            UU = [sbt(es, "UU%d" % i, [128, 2, 512], F32) for i in range(2)]
            LL = [sbt(es, "LL%d" % i, [128, 2, 512], BF16) for i in range(3)]
            LS = [sbt(es, "LS%d" % i, [128, 512], BF16) for i in range(3)]
            rden = sbt(es, "rden", [128, 2, 4], F32)
            ob = [sbt(es, "ob%d" % i, [128, 4, 64], BF16) for i in range(3)]
            cst = [sbt(es, "cst%d" % i, [128, 2816], F32) for i in range(2)]
            cbf = [sbt(es, "cbf%d" % i, [128, 2816], BF16) for i in range(2)]
            jobs = []
            for k in range(8):
                for hf in range(2):
                    jobs.append((T["w_up"][k * 128:(k + 1) * 128, hf * 2816:(hf + 1) * 2816], T["Wup"][:, k, hf * 2816:(hf + 1) * 2816], 2816, None))
            for c in range(0, 22, 2):
                jobs.append((T["w_down"][c * 128:(c + 2) * 128, :].rearrange("(c p) n -> p c n", p=128), T["Wdn"][:, c:c + 2, :], 2048, 2))
            for k in range(0, 8, 2):
                jobs.append((T["w_out"][k * 128:(k + 2) * 128, :].rearrange("(c p) n -> p c n", p=128), T["Wout"][:, k:k + 2, :], 2048, 2))
            jobn = [0]

            def conv_job():
                n = jobn[0]
                if n >= len(jobs):
                    return
                jobn[0] += 1
                src, dst, width, sub = jobs[n]
                b = n % 2
                if sub is None:
                    s_ap, b_ap = cst[b][:, :width], cbf[b][:, :width]
                else:
                    s_ap = cst[b][:, :width].rearrange("p (c n) -> p c n", c=sub)
                    b_ap = cbf[b][:, :width].rearrange("p (c n) -> p c n", c=sub)
                DMA(P, "sp", s_ap, src, [], ["cst%d" % b])
                P.op("dve", lambda E: E.tensor_copy(out=cbf[b][:, :width], in_=cst[b][:, :width]), ["cst%d" % b], ["cbf%d" % b])
                DMA(P, "sp", dst, b_ap, ["cbf%d" % b], ["Wscr"])
            psZ = [pst(es, "psZ%d" % i, [128, 2, 512], F32) for i in range(3)]
            psO = [pst(es, "psO%d" % i, [128, 512], F32) for i in range(2)]
            for nm, t_, src in (("esel", esel, T["esel"][:, :]), ("eselh", eselh, T["eselh"][:, :]), ("idE", idE, T["idE"][:, :, :]),
                                ("idNE", idNE, T["idNE"][:, :, :]), ("erow", erow, T["erow"][:, :, :]), ("negrow", negrow, T["negrow"][:, :]),
                                ("ident", ident, T["ident_bf"][:, :]), ("mt0", mt[0], T["mt_f"][:, :, :]), ("mt1", mt[1], T["mt_s"][:, :, :]),
                                ("mh0", mh[0], T["mh_f"][:, :, :]), ("mh1", mh[1], T["mh_s"][:, :, :]),
                                ("negtri", negtri, T["negtri"][:, :]), ("negones", negones, T["negones"][:, :])):
                DMA(P, "sp", t_[:], src, [], [nm])
            P.op("pool", lambda E: E.memset(allneg[:], NEG), [], ["allneg"])
            for i in range(3):
                P.op("pool", lambda E, i=i: E.memset(PT[i][:], 0.0), [], ["PT%d" % i])
            CONSTS = ["idE", "idNE", "erow", "negrow", "ident", "mt0", "mt1", "mh0", "mh1", "allneg"]

            def load_head(hh):
                hb = hh % 2
                fox = hh < 8
                DMA(P, "sp", KT[hb][0:64, :], T["Kd"][hh, :, :], [], ["KTm%d" % hb])
                DMA(P, "sp", QT[hb][0:64, :], T["Qd"][hh, :, :], [], ["QTm%d" % hb])
                DMA(P, "sp", VV[hb][:, :, :].rearrange("p (t b) d -> p t b d", b=4), T["Vd"][:, :, hh, :, :].rearrange("t p b d -> p t b d"), [], ["VV%d" % hb])
                if not fox:
                    P.op("pool", lambda E: E.memset(KT[hb][64:70, :], 0.0), [], ["KTa%d" % hb])
                    P.op("pool", lambda E: E.memset(QT[hb][64:70, :], 0.0), [], ["QTa%d" % hb])
                if fox:
                    DMA(P, "sp", KT[hb][64:70, :], T["KA"][hh, :, :], [], ["KTa%d" % hb])
                    DMA(P, "sp", qa[64:70, :], T["QA"][hh, :, :], [], ["qa"])
                    for j in range(NSL + 1):
                        if j < NSL:
                            c0 = qa[64:70, 1024 * j:1024 * j + 512]
                            c1 = qa[64:70, 1024 * j + 512:1024 * j + 1024]
                            e0 = esel[64:70, 2 * j:2 * j + 1]
                            e1 = esel[64:70, 2 * j + 1:2 * j + 2]
                            dst = QT[hb][64:70, 512 * j:512 * j + 512]
                            tmp = qtmp[64:70, 0:512]
                            P.op("dve", lambda E, c0=c0, e0=e0, tmp=tmp: E.tensor_scalar(out=tmp, in0=c0, scalar1=e0, scalar2=None, op0=ALU.mult),
                                 ["qa", "esel"], ["qtmp"])
                            P.op("dve", lambda E, c1=c1, e1=e1, tmp=tmp, dst=dst: E.scalar_tensor_tensor(out=dst, in0=c1, scalar=e1, in1=tmp, op0=ALU.mult, op1=ALU.add),
                                 ["qa", "esel", "qtmp"], ["QTa%d" % hb])
                        else:
                            for jj in range(NSL):
                                a0 = max(0, 1024 * jj - 2)
                                c0 = qa[64:70, a0:a0 + 2]
                                c1 = qa[64:70, 1024 * jj + 510:1024 * jj + 512]
                                e0 = eselh[64:70, 2 * jj:2 * jj + 1]
                                e1 = eselh[64:70, 2 * jj + 1:2 * jj + 2]
                                dst = QT[hb][64:70, NOWN + 2 * jj:NOWN + 2 * jj + 2]
                                tmp = qtmp[64:70, 0:2]
                                P.op("dve", lambda E, c0=c0, e0=e0, tmp=tmp: E.tensor_scalar(out=tmp, in0=c0, scalar1=e0, scalar2=None, op0=ALU.mult),
                                     ["qa", "eselh"], ["qtmp"])
                                P.op("dve", lambda E, c1=c1, e1=e1, tmp=tmp, dst=dst: E.scalar_tensor_tensor(out=dst, in0=c1, scalar=e1, in1=tmp, op0=ALU.mult, op1=ALU.add),
                                     ["qa", "eselh", "qtmp"], ["QTa%d" % hb])

            units = []
            slot_ctr = 0
            for hh in range(16):
                fox = hh < 8
                if "noSB" in OPT and not fox:
                    continue
                if "noFOX" in OPT and fox:
                    continue
                for sl in range(NSL + 1):
                    halo = sl == NSL
                    if halo and "noHalo" in OPT:
                        continue
                    W = 16 if halo else 512
                    nkb = NB if halo else 8 * (sl + 1)
                    order = list(range(nkb)) if fox else list(range(nkb - 1, -1, -1))

                    def mask_of(kb):
                        if halo:
                            return ("H", kb)
                        if 8 * sl <= kb < 8 * sl + 4:
                            return ("E", kb - 8 * sl)
                        if 8 * sl + 4 <= kb < 8 * sl + 8:
                            return ("NE", kb - 8 * sl - 4)
                        return None
                    npairs = nkb // 2
                    for idx in range(npairs):
                        kbs = (order[2 * idx], order[2 * idx + 1])
                        units.append(dict(hh=hh, fox=fox, sl=sl, W=W, kbs=kbs, first=idx == 0, last=idx == npairs - 1,
                                          mks=(mask_of(kbs[0]), mask_of(kbs[1])), so=slot_ctr, qc=NOWN if halo else 512 * sl))
                    slot_ctr += 1
            for i, u in enumerate(units):
                u["i"] = i
                u["head_first"] = (i == 0 or units[i - 1]["hh"] != u["hh"])
                u["head_last"] = (i == len(units) - 1 or units[i + 1]["hh"] != u["hh"])

            def stage_A(u):
                hb = u["hh"] % 2
                KD = 70
                W, i = u["W"], u["i"]
                zt = psZ[i % 3]
                zk = "psZ%d" % (i % 3)
                mi = 0 if u["fox"] else 1
                rd = ["KTm%d" % hb, "QTm%d" % hb, "KTa%d" % hb, "QTa%d" % hb]
                sl = u["sl"]
                for a in range(2):
                    kb = u["kbs"][a]
                    z = zt[:, a, :W]
                    mk = u["mks"][a] if "noMask" not in OPT else None
                    MM(P, z, KT[hb][0:KD, kb * 128:(kb + 1) * 128], QT[hb][0:KD, u["qc"]:u["qc"] + W], True, mk is None, rd, [zk])
                    if mk is not None:
                        typ, m = mk
                        if typ == "H":
                            MM(P, z, ident[:, :], mh[mi][:, m, :], False, True, CONSTS, [zk])
                        elif typ == "E":
                            MM(P, z, idE[:, sl, :], mt[mi][:, m, :], False, True, CONSTS, [zk])
                        else:
                            MM(P, z, idNE[:, sl, :], mt[mi][:, m, :], False, False, CONSTS, [zk])
                            MM(P, z, idE[:, sl, :], allneg[:, :W], False, True, CONSTS, [zk])

            def stage_B_fox(u):
                W, i = u["W"], u["i"]
                ACTF(P, PT[i % 3][:, :, :W], psZ[i % 3][:, :, :W], AF.Exp, ["psZ%d" % (i % 3)], ["PT%d" % (i % 3)])

            def stage_B1(u):
                W, i = u["W"], u["i"]
                ACTF(P, UU[i % 2][:, :, :W], psZ[i % 3][:, :, :W], AF.Exp, ["psZ%d" % (i % 3)], ["UU%d" % (i % 2)])
                ACTF(P, LL[i % 3][:, :, :W], UU[i % 2][:, :, :W], AF.Ln, ["UU%d" % (i % 2)], ["LL%d" % (i % 3)], bias=1.0)
                if not u["last"]:
                    lk, nk = "LL%d" % (i % 3), "LS%d" % ((i + 1) % 3)
                    if u["first"]:
                        P.op("pool", lambda E: E.tensor_tensor(out=LS[(i + 1) % 3][:, :W], in0=LL[i % 3][:, 0, :W], in1=LL[i % 3][:, 1, :W], op=ALU.add), [lk], [nk])
                    else:
                        P.op("pool", lambda E: E.tensor_tensor(out=LS[(i + 1) % 3][:, :W], in0=LS[i % 3][:, :W], in1=LL[i % 3][:, 0, :W], op=ALU.add),
                             [lk, "LS%d" % (i % 3)], [nk])
                        P.op("pool", lambda E: E.tensor_tensor(out=LS[(i + 1) % 3][:, :W], in0=LS[(i + 1) % 3][:, :W], in1=LL[i % 3][:, 1, :W], op=ALU.add),
                             [lk, nk], [nk])

            def stage_C(u):
                W, i = u["W"], u["i"]
                zt = psZ[i % 3]
                zk = "psZ%d" % (i % 3)
                lk = "LL%d" % (i % 3)
                MM(P, zt[:, 0, :W], negtri[:, :], LL[i % 3][:, 0, :W], False, u["first"], ["negtri", lk], [zk], skip=True)
                if not u["first"]:
                    MM(P, zt[:, 0, :W], negones[:, :], LS[i % 3][:, :W], False, True, ["negones", "LS%d" % (i % 3)], [zk], skip=True)
                MM(P, zt[:, 1, :W], negtri[:, :], LL[i % 3][:, 1, :W], False, False, ["negtri", lk], [zk], skip=True)
                MM(P, zt[:, 1, :W], negones[:, :], LL[i % 3][:, 0, :W], False, u["first"], ["negones", lk], [zk], skip=True)
                if not u["first"]:
                    MM(P, zt[:, 1, :W], negones[:, :], LS[i % 3][:, :W], False, True, ["negones", "LS%d" % (i % 3)], [zk], skip=True)

            def stage_B2(u):
                W, i = u["W"], u["i"]
                ACTF(P, PT[i % 3][:, :, :W], psZ[i % 3][:, :, :W], AF.Exp, ["psZ%d" % (i % 3)], ["PT%d" % (i % 3)])

            fin_ctr = [0]

            def stage_D(u):
                hb = u["hh"] % 2
                W, i = u["W"], u["i"]
                o = psO[u["so"] % 2]
                ok = "psO%d" % (u["so"] % 2)
                NV = 65 if u["fox"] else 64
                nsub = max(1, W // 128)
                M = min(128, W)
                for a in range(2):
                    kb = u["kbs"][a]
                    for s_ in range(nsub):
                        MM(P, o[:, s_ * NV:(s_ + 1) * NV], PT[i % 3][:, a, s_ * 128:s_ * 128 + 128], VV[hb][:, kb, 0:NV],
                           u["first"] and s_ == 0 and a == 0, u["last"] and s_ == nsub - 1 and a == 1, ["PT%d" % (i % 3), "VV%d" % hb], [ok], skip=True)
                if u["last"]:
                    f = fin_ctr[0]
                    fin_ctr[0] += 1
                    obuf = ob[f % 3]
                    obk = "ob%d" % (f % 3)
                    ov = o[:M, 0:nsub * NV].rearrange("p (s v) -> p s v", s=nsub)
                    if u["fox"]:
                        rk = "rden%d" % (f % 2)
                        P.op("dve", lambda E: E.reciprocal(out=rden[:M, f % 2, 0:nsub], in_=ov[:, :, 64]), [ok], [rk])
                        for s_ in range(nsub):
                            P.op("dve", lambda E, s_=s_: E.tensor_scalar(out=obuf[:M, s_, :], in0=ov[:, s_, 0:64], scalar1=rden[:M, f % 2, s_:s_ + 1],
                                                                         scalar2=None, op0=ALU.mult), [ok, rk], [obk])
                    else:
                        P.op("dve", lambda E: E.tensor_copy(out=obuf[:M, 0:nsub, :], in_=ov[:, :, 0:64]), [ok], [obk])
                    r0 = u["qc"]
                    hh = u["hh"]
                    if nsub == 4:
                        dst = T["Od"][r0:r0 + 512, hh * 64:(hh + 1) * 64].rearrange("(s p) d -> p s d", p=128)
                        DMA(P, "sp", dst, obuf[:, :, :], [obk], ["Od"])
                    else:
                        DMA(P, "sp", T["Od"][r0:r0 + M, hh * 64:(hh + 1) * 64], obuf[:M, 0, :], [obk], ["Od"])

            n = len(units)
            hh0 = units[0]["hh"]
            load_head(hh0)
            stage_A(units[0])
            job_every = max(1, n // (len(jobs) + 2))
            for i in range(n):
                u = units[i]
                if u["head_first"] and u["hh"] + 1 < 16 and u["hh"] == hh0:
                    load_head(hh0 + 1)
                if i % job_every == job_every - 1:
                    conv_job()
                if i + 1 < n:
                    stage_A(units[i + 1])
                if u["fox"]:
                    stage_B_fox(u)
                else:
                    stage_B1(u)
                    stage_C(u)
                if i >= 1:
                    pu = units[i - 1]
                    if not pu["fox"]:
                        stage_B2(pu)
                    stage_D(pu)
                    if pu["head_last"] and pu["hh"] + 2 < 16:
                        load_head(pu["hh"] + 2)
            pu = units[n - 1]
            if not pu["fox"]:
                stage_B2(pu)
            stage_D(pu)
            while jobn[0] < len(jobs):
                conv_job()
            P.emit()
            stats.append(P.stats)
        if stop_after == "c":
            return nc, stats

        with ExitStack() as es:
            P = Prog(nc, gs, "d")
            wout = sbt(es, "wout", [128, 8, D], BF16)
            fox_g = sbt(es, "fox_g", [128, 512], F32)
            sb_g = sbt(es, "sb_g", [128, 512], F32)
            ffn_g = sbt(es, "ffn_g", [128, D], F32)
            fin_g = sbt(es, "fin_g", [128, D], F32)
            cw = sbt(es, "cw", [128, NCH, 3], F32)
            cb = sbt(es, "cb", [128, NCH], F32)
            hmask = sbt(es, "hmask", [128, 16], F32)
            ident = sbt(es, "ident3", [128, 128], BF16)
            identf = sbt(es, "identf3", [128, 128], F32)
            o_s = [sbt(es, "o_s%d" % i, [128, 4, D], BF16) for i in range(2)]
            xr = [sbt(es, "xr%d" % i, [128, 4, D], F32) for i in range(2)]
            junk = sbt(es, "junk3", [128, D], BF16)
            stat = sbt(es, "stat3", [128, 3, 16], F32)
            on = sbt(es, "on", [128, 4, D], BF16)
            onT = sbt(es, "onT", [128, 8, 512], BF16)
            h2T = sbt(es, "h2T", [128, 8, 512], BF16)
            wu = [sbt(es, "wu%d" % i, [128, 8, 2, 128], BF16) for i in range(3)]
            cv = [sbt(es, "cv%d" % i, [128, 512], F32) for i in range(6)]
            sg = [sbt(es, "sg%d" % i, [128, 512], F32) for i in range(2)]
            aT = sbt(es, "aT", [128, 22, 512], BF16)
            uh = sbt(es, "uh", [128, NCH, 16], F32)
            wd = [sbt(es, "wd%d" % i, [128, 22, 128], BF16) for i in range(3)]
            yT = [sbt(es, "yT%d" % i, [128, 512], F32) for i in range(2)]
            pT = [pst(es, "p3T%d" % i, [128, 8, 128], BF16) for i in range(2)]
            psa = [pst(es, "psa%d" % i, [128, 512], F32) for i in range(2)]
            psu = [pst(es, "psu%d" % i, [128, 512], F32) for i in range(2)]
            psy = pst(es, "psy", [128, 512], F32)
            pst_ = pst(es, "pst", [128, 4, 128], F32)
            rr = RR(["act", "dve"])
            PSU = [psa[0], psa[1], psu[0], psu[1]]
            PSUK = ["psa0", "psa1", "psu0", "psu1"]
            for nm, t_, src in (("wout", wout, T["Wout"][:, :, :]), ("fox_g", fox_g, T["fox_g"][:, :]), ("sb_g", sb_g, T["sb_g"][:, :]),
                                ("ffn_g", ffn_g, T["ffn_g"][:, :]), ("fin_g", fin_g, T["fin_g"][:, :]), ("cw", cw, T["cw"][:, :, :]),
                                ("cb", cb, T["cb"][:, :]), ("hmask", hmask, T["hmask"][:, :]), ("ident", ident, T["ident_bf"][:, :]),
                                ("identf", identf, T["ident_f"][:, :])):
                DMA(P, "sp", t_[:], src, [], [nm])

            P.op("pool", lambda E: E.memset(on[:], 0.0), [], ["on"])
            P.op("pool", lambda E: E.memset(onT[:], 0.0), [], ["onT.%d" % b for b in range(4)])
            P.op("pool", lambda E: E.memset(h2T[:], 0.0), [], ["h2T.%d" % b for b in range(4)])
            sctr = [0]
            pctr = [0]
            wuc = [0]
            wdc = [0]

            def rstd_of(src_ap, rows, width):
                c = sctr[0] % 16
                sctr[0] += 1
                sk = "st%d" % c
                ACTF(P, junk[:rows, :width], src_ap, AF.Square, src_ap_keys[0], ["junk", sk], accum=stat[:rows, 0, c:c + 1])
                ACTF(P, stat[:rows, 1, c:c + 1], stat[:rows, 0, c:c + 1], AF.Ln, [sk], [sk], bias=EPS, scale=1.0 / width)
                ACTF(P, stat[:rows, 2, c:c + 1], stat[:rows, 1, c:c + 1], AF.Exp, [sk], [sk], scale=-0.5)
                return stat[:rows, 2, c:c + 1], sk
            src_ap_keys = [None]

            def transpose_to(src_tile, src_key, blocks, dstT, dst_key):
                for (bi, rows) in blocks:
                    n = pctr[0]
                    pctr[0] += 1
                    pt = pT[n % 2]
                    pk = "p3T%d" % (n % 2)
                    for k in range(8):
                        TR(P, pt[:, k, :], src_tile[:, bi, k * 128:(k + 1) * 128], ident[:, :], [src_key, "ident"], [pk])
                    rr.copy(P, dstT[:, :, bi * 128:bi * 128 + rows], pt[:, :, :rows], [pk], [dst_key + ".%d" % bi])

            slots = [NSL] + list(range(NSL))

            def load_slot(si):
                sl = slots[si]
                halo = sl == NSL
                r0 = NOWN if halo else 512 * sl
                sb_ = si % 2
                os_, xr_ = o_s[sb_], xr[sb_]
                osk, xrk = "o_s%d" % sb_, "xr%d" % sb_
                if halo:
                    DMA(P, "sp", os_[:16, 0, :], T["Od"][r0:r0 + 16, :], [], [osk])
                    DMA(P, "sp", xr_[:16, 0, :], T["xo"][r0:r0 + 16, :], [], [xrk])
                else:
                    DMA(P, "sp", os_[:, :, :], T["Od"][r0:r0 + 512, :].rearrange("(b p) d -> p b d", p=128), [], [osk])
                    DMA(P, "sp", xr_[:, :, :], T["xo"][r0:r0 + 512, :].rearrange("(b p) d -> p b d", p=128), [], [xrk])

            deferred = []

            def do_slot(si, sl):
                halo = sl == NSL
                W = 16 if halo else 512
                r0 = NOWN if halo else 512 * sl
                blocks = [(0, 16)] if halo else [(b, 128) for b in range(4)]
                sb_ = si % 2
                os_, xr_ = o_s[sb_], xr[sb_]
                osk, xrk = "o_s%d" % sb_, "xr%d" % sb_
                if si == 0:
                    load_slot(si)
                if si + 1 < len(slots):
                    load_slot(si + 1)
                for (bi, rows) in blocks:
                    for g in range(2):
                        src = os_[:rows, bi, g * 512:(g + 1) * 512]
                        src_ap_keys[0] = [osk]
                        rs, sk = rstd_of(src, rows, 512)
                        gt = fox_g if g == 0 else sb_g
                        P.op("dve", lambda E, src=src, rs=rs, gt=gt, rows=rows, bi=bi, g=g: E.scalar_tensor_tensor(
                            out=on[:rows, bi, g * 512:(g + 1) * 512], in0=src, scalar=rs, in1=gt[:rows, :], op0=ALU.mult, op1=ALU.mult),
                            [osk, sk, "fox_g", "sb_g"], ["on"])
                transpose_to(on, "on", blocks, onT, "onT")
                onk = ["onT.%d" % bi for (bi, _) in blocks]
                for (bi, rows) in blocks:
                    for ch in range(2):
                        n = pctr[0]
                        pctr[0] += 1
                        pa = psa[n % 2]
                        pk = "psa%d" % (n % 2)
                        for k in range(8):
                            MM(P, pa[:, :], onT[:, k, bi * 128:bi * 128 + 128], wout[:, k, ch * 512:(ch + 1) * 512], k == 0, k == 7,
                               ["wout", "onT.%d" % bi], [pk])
                        P.op("dve", lambda E, pa=pa, rows=rows, bi=bi, ch=ch: E.tensor_tensor(
                            out=xr_[:rows, bi, ch * 512:(ch + 1) * 512], in0=pa[:rows, :], in1=xr_[:rows, bi, ch * 512:(ch + 1) * 512], op=ALU.add),
                            [pk, xrk], [xrk])
                for (bi, rows) in blocks:
                    src = xr_[:rows, bi, :]
                    src_ap_keys[0] = [xrk]
                    rs, sk = rstd_of(src, rows, D)
                    P.op("dve", lambda E, src=src, rs=rs, rows=rows, bi=bi: E.scalar_tensor_tensor(
                        out=on[:rows, bi, :], in0=src, scalar=rs, in1=ffn_g[:rows, :], op0=ALU.mult, op1=ALU.mult),
                        [xrk, sk, "ffn_g"] + onk, ["on"])
                transpose_to(on, "on", blocks, h2T, "h2T")
                h2k = ["h2T.%d" % bi for (bi, _) in blocks]
                for c in range(22):
                    wn = wuc[0]
                    wuc[0] += 1
                    wt = wu[wn % 3]
                    wk = "wu%d" % (wn % 3)
                    DMA(P, "sp", wt[:, :, 0, :], T["Wup"][:, :, c * 128:(c + 1) * 128], [], [wk + "g"])
                    DMA(P, "sp", wt[:, :, 1, :], T["Wup"][:, :, DFF + c * 128:DFF + (c + 1) * 128], [], [wk + "v"])
                    pend = []
                    prev_fin = deferred[:]
                    del deferred[:]
                    for gv in range(2):
                        un = (2 * wn + gv) % 4
                        pu_ = PSU[un]
                        pk = PSUK[un]
                        for k in range(8):
                            MM(P, pu_[:, :W], wt[:, k, gv, :], h2T[:, k, :W], k == 0, k == 7, [wk + ("g" if gv == 0 else "v")] + h2k, [pk])
                        cc = c + 22 * gv
                        if halo:
                            P.op("dve", lambda E, pu_=pu_, cc=cc: E.tensor_tensor(out=uh[:, cc, :], in0=pu_[:, :16], in1=hmask[:, :], op=ALU.mult),
                                 [pk, "hmask"], ["uh"])
                            continue
                        cn = (2 * wn + gv) % 6
                        ct, ck = cv[cn], "cv%d" % cn
                        ACTF(P, ct[:, :], pu_[:, :512], AF.Identity, [pk, "cw", "cb"], [ck], bias=cb[:, cc:cc + 1], scale=cw[:, cc, 2:3])
                        pend.append((pu_, pk, ct, ck, cc))
                    if halo:
                        continue
                    for f_ in prev_fin:
                        f_()
                    for (pu_, pk, ct, ck, cc) in pend:
                        P.op("dve", lambda E, pu_=pu_, ct=ct, cc=cc: E.scalar_tensor_tensor(out=ct[:, 1:512], in0=pu_[:, 0:511], scalar=cw[:, cc, 1:2], in1=ct[:, 1:512],
                                                                                         op0=ALU.mult, op1=ALU.add), [pk, "cw", ck], [ck])
                    for (pu_, pk, ct, ck, cc) in pend:
                        P.op("dve", lambda E, pu_=pu_, ct=ct, cc=cc: E.scalar_tensor_tensor(out=ct[:, 2:512], in0=pu_[:, 0:510], scalar=cw[:, cc, 0:1], in1=ct[:, 2:512],
                                                                                         op0=ALU.mult, op1=ALU.add), [pk, "cw", ck], [ck])
                    for (pu_, pk, ct, ck, cc) in pend:
                        P.op("dve", lambda E, ct=ct, cc=cc: E.scalar_tensor_tensor(out=ct[:, 0:1], in0=uh[:, cc, 2 * sl + 1:2 * sl + 2], scalar=cw[:, cc, 1:2], in1=ct[:, 0:1],
                                                                                op0=ALU.mult, op1=ALU.add), ["uh", "cw", ck], [ck])
                    for (pu_, pk, ct, ck, cc) in pend:
                        P.op("dve", lambda E, ct=ct, cc=cc: E.scalar_tensor_tensor(out=ct[:, 0:2], in0=uh[:, cc, 2 * sl:2 * sl + 2], scalar=cw[:, cc, 0:1], in1=ct[:, 0:2],
                                                                                op0=ALU.mult, op1=ALU.add), ["uh", "cw", ck], [ck])
                    (_, _, ctg, ckg, _), (_, _, ctv, ckv, _) = pend
                    st_, stk = sg[wn % 2], "sg%d" % (wn % 2)

                    def fin(ctg=ctg, ckg=ckg, ctv=ctv, ckv=ckv, st_=st_, stk=stk, c=c):
                        ACTF(P, st_[:, :], ctg[:, :], AF.Silu, [ckg], [stk])
                        P.op("pool", lambda E: E.tensor_tensor(out=aT[:, c, :], in0=st_[:, :], in1=ctv[:, :], op=ALU.mult),
                             [stk, ckv], ["aT.%d" % c])
                    deferred.append(fin)
                for f_ in deferred:
                    f_()
                del deferred[:]
                if halo:
                    return
                for cc in range(8):
                    wn = wdc[0]
                    wdc[0] += 1
                    wt = wd[wn % 3]
                    wk = "wd%d" % (wn % 3)
                    DMA(P, "sp", wt[:, :, :], T["Wdn"][:, :, cc * 128:(cc + 1) * 128], [], [wk])
                    for c in range(22):
                        MM(P, psy[:, :], wt[:, c, :], aT[:, c, :], c == 0, c == 21, [wk, "aT.%d" % c], ["psy"])
                    yt, yk = yT[wn % 2], "yT%d" % (wn % 2)
                    ACTF(P, yt[:, :], psy[:, :], AF.Copy, ["psy"], [yk])
                    for b in range(4):
                        P.op("pe", lambda E, yt=yt, b=b: E.transpose(pst_[:, b, :], yt[:, b * 128:(b + 1) * 128], identf[:, :]), [yk, "identf"], ["pst"])
                    P.op("dve", lambda E, cc=cc: E.tensor_tensor(out=xr_[:, :, cc * 128:(cc + 1) * 128], in0=pst_[:, :, :],
                                                               in1=xr_[:, :, cc * 128:(cc + 1) * 128], op=ALU.add), ["pst", xrk], [xrk])
                for (bi, rows) in blocks:
                    src = xr_[:rows, bi, :]
                    src_ap_keys[0] = [xrk]
                    rs, sk = rstd_of(src, rows, D)
                    P.op("dve", lambda E, src=src, rs=rs: E.scalar_tensor_tensor(out=src, in0=src, scalar=rs, in1=fin_g[:, :], op0=ALU.mult, op1=ALU.mult),
                         [xrk, sk, "fin_g"], [xrk])
                DMA(P, "pool", T["out"][r0:r0 + 512, :].rearrange("(b p) d -> p b d", p=128), xr_[:, :, :], [xrk], ["out"])

            for si, sl in enumerate(slots):
                do_slot(si, sl)
            P.emit()
            stats.append(P.stats)
    return nc, stats


_CACHE = {}


def _consts(S):
    NB = S // 128
    bf = ml_dtypes.bfloat16
    p = np.arange(128)
    c = {}
    c["ident_bf"] = np.eye(128, dtype=np.float32).astype(bf)
    c["ident_f"] = np.eye(128, dtype=np.float32)
    c["triu_f"] = (p[:, None] <= p[None, :]).astype(np.float32)
    c["ones_f"] = np.ones((128, 128), np.float32)
    c["negtri"] = (-(p[:, None] >= p[None, :]).astype(np.float32)).astype(bf)
    c["negones"] = (-np.ones((128, 128), np.float32)).astype(bf)
    t = np.arange(512)
    key = (np.arange(4)[None, :, None] * 128 + p[:, None, None])
    c["mt_f"] = np.where(key <= t[None, None, :], 0.0, NEG).astype(np.float32).astype(bf)
    c["mt_s"] = np.where(key < t[None, None, :], 0.0, NEG).astype(np.float32).astype(bf)
    c["negrow"] = np.full((1, 512), NEG, np.float32).astype(bf)
    c["cm1"] = np.full((64, 3, 128), -1.0, np.float32).astype(bf)
    c["cp1"] = np.full((64, 3, 128), 1.0, np.float32).astype(bf)
    return c


def _core_consts(S, par):
    NT = S // 512
    NSL = NT // 2
    NB = S // 128
    bf = ml_dtypes.bfloat16
    own = own_tiles(par, NSL)
    esel = np.zeros((128, 2 * NSL), np.float32)
    eselh = np.zeros((128, 2 * NSL), np.float32)
    hmask = np.ones((128, 16), np.float32)
    idE = np.zeros((128, NSL, 128), np.float32)
    idNE = np.zeros((128, NSL, 128), np.float32)
    erow = np.zeros((1, NSL, 128), np.float32)
    I = np.eye(128, dtype=np.float32)
    p = np.arange(128)
    keypos = (np.arange(NB)[None, :] * 128 + p[:, None])
    mh_f = np.zeros((128, NB, 16), np.float32)
    mh_s = np.zeros((128, NB, 16), np.float32)
    for j in range(NSL):
        e0 = 1.0 if own[j] == 2 * j else 0.0
        esel[:, 2 * j] = e0
        esel[:, 2 * j + 1] = 1.0 - e0
        eselh[:, 2 * j] = 1.0 if (own[j] == 2 * j and j > 0) else 0.0
        eselh[:, 2 * j + 1] = 1.0 if own[j] == 2 * j + 1 else 0.0
        idE[:, j, :] = e0 * I
        idNE[:, j, :] = (1.0 - e0) * I
        erow[0, j, :] = e0
        for r in range(2):
            col = 2 * j + r
            pos = 512 * own[j] - 2 + r
            if own[j] == 0:
                hmask[:, col] = 0.0
                mh_f[:, :, col] = np.where(keypos == 0, 0.0, NEG)
                mh_s[:, :, col] = np.where(keypos == 0, 0.0, NEG)
            else:
                mh_f[:, :, col] = np.where(keypos <= pos, 0.0, NEG)
                mh_s[:, :, col] = np.where(keypos < pos, 0.0, NEG)
    return dict(esel=esel, eselh=eselh, hmask=hmask, idE=idE.astype(bf), idNE=idNE.astype(bf), erow=erow.astype(bf),
                mh_f=mh_f.astype(bf), mh_s=mh_s.astype(bf)), own


def _prepare(inputs, debug=False):
    x = np.asarray(inputs["x"], np.float32)
    B, S, _ = x.shape
    NT = S // 512
    NSL = NT // 2
    rep = lambda v: np.ascontiguousarray(np.broadcast_to(np.asarray(v, np.float32).reshape(1, -1), (128, np.asarray(v).size)))
    shared = dict(_consts(S))
    shared["w_in"] = np.ascontiguousarray(np.asarray(inputs["w_in"], np.float32)[0])
    shared["w_out"] = np.ascontiguousarray(np.asarray(inputs["w_out"], np.float32)[0])
    shared["w_up"] = np.ascontiguousarray(np.asarray(inputs["w_up"], np.float32)[0])
    shared["w_down"] = np.ascontiguousarray(np.asarray(inputs["w_down"], np.float32)[0])
    shared["attn_g"] = rep(inputs["attn_norm_g"][0])
    shared["ffn_g"] = rep(inputs["ffn_norm_g"][0])
    shared["fin_g"] = rep(inputs["final_norm_g"])
    shared["fox_g"] = rep(inputs["fox_out_g"][0])
    shared["sb_g"] = rep(inputs["sb_out_g"][0])
    shared["fb"] = rep(inputs["forget_bias"][0])
    cwv = np.asarray(inputs["conv_w"], np.float32)[0]
    shared["cw"] = np.ascontiguousarray(cwv.reshape(3, NCH, 128).transpose(2, 1, 0))
    shared["cb"] = np.ascontiguousarray(np.asarray(inputs["conv_b"], np.float32)[0].reshape(NCH, 128).T)
    in_maps = []
    owns = []
    for c in range(8):
        b, par = c // 2, c % 2
        cc, own = _core_consts(S, par)
        owns.append(own)
        xo = np.zeros((NSL * 512 + 16, D), np.float32)
        for j, t in enumerate(own):
            xo[j * 512:(j + 1) * 512] = x[b, t * 512:(t + 1) * 512]
            if t > 0:
                xo[NSL * 512 + 2 * j:NSL * 512 + 2 * j + 2] = x[b, t * 512 - 2:t * 512]
        m = dict(shared)
        m.update(cc)
        m["xn"] = np.ascontiguousarray(x[b])
        m["xo"] = xo
        in_maps.append(m)
    return in_maps, owns, (B, S)


def kernel(**inputs):
    in_maps, owns, (B, S) = _prepare(inputs)
    if S not in _CACHE:
        _CACHE[S] = build_program(S)
    nc, _ = _CACHE[S]
    res = run_bass_kernel_spmd(nc, in_maps, core_ids=list(range(8)))
    out = np.zeros((B, S, D), np.float32)
    for c in range(8):
        b = c // 2
        o = np.asarray(res.results[c]["out"], np.float32)
        for j, t in enumerate(owns[c]):
            out[b, t * 512:(t + 1) * 512] = o[j * 512:(j + 1) * 512]
    return out
```

```python
import numpy as np
import ml_dtypes
from contextlib import ExitStack
import concourse.bass as bass
import concourse.mybir as mybir
from concourse.bass_utils import run_bass_kernel_spmd

F32 = mybir.dt.float32
BF16 = mybir.dt.bfloat16
AF = mybir.ActivationFunctionType
ALU = mybir.AluOpType

D = 1024
DH = 64
DFF = 2816
INC = 3080
EPS = 1e-6
NEG = -30000.0
CQ_F, CK_F, CV_F, CL, CQ_S, CK_S, CV_S = 0, 512, 1024, 1536, 1544, 2056, 2568
NCH = 2 * DFF // 128

import os
OPT = set(os.environ.get("KOPT", "").split(","))
COMPUTE = ("pe", "act", "dve", "pool")
CH = 16000
NDMA = 8


class Prog:
    def __init__(self, nc, gs, tag):
        self.nc = nc
        self.gs = gs
        self.tag = tag
        self.ops = []
        self.last_w = {}
        self.readers = {}

    def op(self, eng, fn, reads=(), writes=(), dma=False):
        j = len(self.ops)
        deps = set()
        for r in reads:
            if r in self.last_w:
                deps.add(self.last_w[r])
        for w in writes:
            if w in self.last_w:
                deps.add(self.last_w[w])
            for rd in self.readers.get(w, ()):
                deps.add(rd)
        deps.discard(j)
        self.ops.append(dict(eng=eng, fn=fn, deps=deps, dma=dma, sig=False))
        for r in reads:
            self.readers.setdefault(r, []).append(j)
        for w in writes:
            self.last_w[w] = j
            self.readers[w] = []
        return j

    def dma(self, q, fn, reads=(), writes=()):
        return self.op(q, fn, reads, writes, dma=True)

    def emit(self):
        nc, ops = self.nc, self.ops
        for j, o in enumerate(ops):
            nd = set()
            for d in o["deps"]:
                p = ops[d]
                if not p["dma"] and not o["dma"] and p["eng"] == o["eng"] == "pe":
                    continue
                nd.add(d)
            o["deps"] = nd
            for d in nd:
                ops[d]["sig"] = True
        cnt = {e: 0 for e in COMPUTE}
        sems = {}

        def getsem(key):
            if key not in sems:
                sems[key] = self.gs.enter_context(nc.semaphore("s%s_%s_%s" % (self.tag, key[0], key[1])))
            return sems[key]

        dcount = {}
        dma_i = {}
        for j, o in enumerate(ops):
            if o["dma"]:
                q = o["eng"]
                k = dma_i.get(q, 0)
                dma_i[q] = k + 1
                key = ("d" + q, k % NDMA)
                dcount[key] = dcount.get(key, 0) + 16
                o["semkey"], o["semval"] = key, dcount[key]
                o["prev"] = (key, dcount[key] - 16)
            elif o["sig"]:
                e = o["eng"]
                c = cnt[e]
                cnt[e] = c + 1
                o["semkey"], o["semval"] = (e, c // CH), c % CH + 1
        last_dma = {}
        for o in ops:
            if o["dma"]:
                last_dma[o["semkey"]] = max(last_dma.get(o["semkey"], 0), o["semval"])
            if "semkey" in o:
                getsem(o["semkey"])
        per = {e: [] for e in ("sp", "act", "dve", "pe", "pool")}
        for j, o in enumerate(ops):
            per[o["eng"]].append(j)
        nwc = [0]

        def run_engine(e, E):
            known = {}
            for j in per[e]:
                o = ops[j]
                need = {}
                for d in o["deps"]:
                    p = ops[d]
                    k, v = p["semkey"], p["semval"]
                    if need.get(k, 0) < v:
                        need[k] = v
                if o["dma"] and o["prev"][1] > 0:
                    k, v = o["prev"]
                    if need.get(k, 0) < v:
                        need[k] = v
                for k, v in need.items():
                    if known.get(k, 0) >= v:
                        continue
                    known[k] = v
                    E.wait_ge(sems[k], v)
                    nwc[0] += 1
                inst = o["fn"](E)
                if o["dma"]:
                    inst.then_inc(sems[o["semkey"]], 16)
                elif o["sig"]:
                    inst.then_inc(sems[o["semkey"]], 1)
            if e == "sp":
                for k, v in last_dma.items():
                    if known.get(k, 0) < v:
                        E.wait_ge(sems[k], v)

        with nc.Block() as block:
            @block.sync
            def _(E):
                run_engine("sp", E)

            @block.scalar
            def _(E):
                run_engine("act", E)

            @block.vector
            def _(E):
                run_engine("dve", E)

            @block.tensor
            def _(E):
                run_engine("pe", E)

            @block.gpsimd
            def _(E):
                run_engine("pool", E)
        self.stats = dict(tag=self.tag, nops=len(ops), nwaits=nwc[0], nsems=len(sems), cnt=cnt)


def MM(P, out, lhsT, rhs, start, stop, rd, wr, skip=False):
    P.op("pe", lambda E: E.matmul(out, lhsT=lhsT, rhs=rhs, start=start, stop=stop, skip_group_check=skip), rd, wr)


def TR(P, out, in_, ident, rd, wr):
    P.op("pe", lambda E: E.transpose(out, in_, ident), rd, wr)


def ACTF(P, out, in_, func, rd, wr, bias=None, scale=None, accum=None):
    kw = {}
    if bias is not None:
        kw["bias"] = bias
    if scale is not None:
        kw["scale"] = scale
    if accum is not None:
        kw["accum_out"] = accum
    P.op("act", lambda E: E.activation(out=out, in_=in_, func=func, **kw), rd, wr)


def DMA(P, q, out, in_, rd, wr):
    P.dma(q, lambda E: E.dma_start(out=out, in_=in_), rd, wr)


class RR:
    def __init__(self, engs):
        self.engs = engs
        self.i = 0

    def copy(self, P, out, in_, rd, wr, scale=None):
        e = self.engs[self.i % len(self.engs)]
        self.i += 1
        if e == "act":
            ACTF(P, out, in_, AF.Copy, rd, wr, scale=scale)
        elif scale is None:
            P.op(e, lambda E: E.tensor_copy(out=out, in_=in_), rd, wr)
        else:
            P.op(e, lambda E: E.tensor_scalar(out=out, in0=in_, scalar1=float(scale), scalar2=None, op0=ALU.mult), rd, wr)


def own_tiles(p, NSL):
    res = []
    for j in range(NSL):
        first = (j % 2 == 0) if p == 0 else (j % 2 == 1)
        res.append(2 * j if first else 2 * j + 1)
    return res


def build_program(S, debug=False, stop_after="d"):
    NT = S // 512
    NSL = NT // 2
    NB = S // 128
    NOWN = NSL * 512
    NQ = NOWN + 16
    nc = bass.Bass("TRN2", target_bir_lowering=False)
    T = {}

    def din(name, shape, dt=F32):
        T[name] = nc.dram_tensor(name, list(shape), dt, kind="ExternalInput").ap()

    def dscr(name, shape, dt=BF16):
        kind = "ExternalOutput" if debug else "Internal"
        T[name] = nc.dram_tensor(name, list(shape), dt, kind=kind).ap()

    din("xn", [S, D]); din("xo", [NQ, D])
    din("w_in", [D, INC]); din("w_out", [D, D]); din("w_up", [D, 2 * DFF]); din("w_down", [DFF, D])
    din("attn_g", [128, D]); din("ffn_g", [128, D]); din("fin_g", [128, D])
    din("fox_g", [128, 512]); din("sb_g", [128, 512]); din("fb", [128, 8])
    din("cw", [128, NCH, 3]); din("cb", [128, NCH])
    din("esel", [128, 2 * NSL]); din("eselh", [128, 2 * NSL]); din("hmask", [128, 16])
    din("idE", [128, NSL, 128], BF16); din("idNE", [128, NSL, 128], BF16); din("erow", [1, NSL, 128], BF16)
    din("mh_f", [128, NB, 16], BF16); din("mh_s", [128, NB, 16], BF16)
    din("ident_bf", [128, 128], BF16); din("ident_f", [128, 128]); din("triu_f", [128, 128]); din("ones_f", [128, 128])
    din("negtri", [128, 128], BF16); din("negones", [128, 128], BF16)
    din("mt_f", [128, 4, 512], BF16); din("mt_s", [128, 4, 512], BF16); din("negrow", [1, 512], BF16)
    din("cm1", [64, 3, 128], BF16); din("cp1", [64, 3, 128], BF16)
    T["out"] = nc.dram_tensor("out", [NOWN, D], F32, kind="ExternalOutput").ap()
    dscr("Kd", [16, 64, S]); dscr("Vd", [NT, 128, 16, 4, 65]); dscr("Qd", [16, 64, NQ])
    dscr("KA", [8, 6, S]); dscr("QA", [8, 6, S]); dscr("Od", [NQ + 112, D])
    dscr("Wup", [128, 8, 2 * DFF]); dscr("Wdn", [128, 22, D]); dscr("Wout", [128, 8, D])
    if debug:
        dscr("dbg_nF", [128, 8, NB], F32)

    stats = []
    with ExitStack() as gs:
        def sbt(es, name, shape, dt):
            return es.enter_context(nc.sbuf_tensor("sb_" + name, list(shape), dt))

        def pst(es, name, shape, dt):
            return es.enter_context(nc.psum_tensor("pp_" + name, list(shape), dt))

        with ExitStack() as es:
            P = Prog(nc, gs, "a")
            win = sbt(es, "win", [128, 8, INC], BF16)
            wstg = [sbt(es, "wstg%d" % i, [128, INC], F32) for i in range(2)]
            g_r = sbt(es, "g_r", [128, D], F32)
            fb_r = sbt(es, "fb_r", [128, 8], F32)
            ident = sbt(es, "ident", [128, 128], BF16)
            xb = [sbt(es, "xb%d" % i, [128, D], F32) for i in range(3)]
            junk = sbt(es, "junk", [128, D], BF16)
            stat = sbt(es, "stat", [128, 3, 8], F32)
            xnb = [sbt(es, "xnb%d" % i, [128, D], BF16) for i in range(2)]
            hT = [sbt(es, "hT%d" % i, [128, 8, 512], BF16) for i in range(2)]
            kts = [sbt(es, "kts%d" % i, [128, 512], BF16) for i in range(3)]
            vs = [sbt(es, "vs%d" % i, [128, 16, 4, 65], BF16) for i in range(2)]
            flog = sbt(es, "flog", [128, 8, NB], F32)
            esp = ExitStack()
            pT = [pst(esp, "pT%d" % i, [128, 8, 128], BF16) for i in range(2)]
            psk = [pst(esp, "psk%d" % i, [128, 512], F32) for i in range(2)]
            psv = [pst(esp, "psv%d" % i, [128, 512], F32) for i in range(2)]
            psf = [pst(esp, "psf%d" % i, [128, 512], F32) for i in range(2)]
            rr = RR(["act", "dve"])

            DMA(P, "sp", g_r[:], T["attn_g"][:, :], [], ["g_r"])
            DMA(P, "sp", fb_r[:], T["fb"][:, :], [], ["fb_r"])
            DMA(P, "sp", ident[:], T["ident_bf"][:, :], [], ["ident"])
            for k in range(8):
                DMA(P, "sp" if k % 2 == 0 else "pool", wstg[k % 2][:], T["w_in"][k * 128:(k + 1) * 128, :], [], ["wstg%d" % (k % 2)])
                if k % 2 == 0:
                    P.op("dve", lambda E, k=k: E.tensor_copy(out=win[:, k, :], in_=wstg[k % 2][:]), ["wstg%d" % (k % 2)], ["win.%d" % k])
                else:
                    ACTF(P, win[:, k, :], wstg[k % 2][:], AF.Copy, ["wstg%d" % (k % 2)], ["win.%d" % k])
            for i in range(2):
                P.op("pool", lambda E, i=i: E.memset(vs[i][:], 1.0), [], ["vs%d.%d.%d" % (i, b, g) for b in range(4) for g in range(2)])
                P.op("pool", lambda E, i=i: E.memset(xnb[i][:], 0.0), [], ["xnb%d" % i])
            blkn = [0]

            def norm_block(src_rows, rows, hbuf, col0):
                st = {}

                def part1():
                    n = blkn[0]
                    blkn[0] += 1
                    st["n"] = n
                    x = xb[n % 3]
                    xs = "xb%d" % (n % 3)
                    sk = "stat%d" % (n % 8)
                    DMA(P, "sp", x[:rows, :], src_rows, [], [xs])
                    ACTF(P, junk[:rows, :], x[:rows, :], AF.Square, [xs], ["junk", sk], accum=stat[:rows, 0, n % 8:n % 8 + 1])
                    ACTF(P, stat[:rows, 1, n % 8:n % 8 + 1], stat[:rows, 0, n % 8:n % 8 + 1], AF.Ln, [sk], [sk], bias=EPS, scale=1.0 / D)
                    ACTF(P, stat[:rows, 2, n % 8:n % 8 + 1], stat[:rows, 1, n % 8:n % 8 + 1], AF.Exp, [sk], [sk], scale=-0.5)
                    xq = xnb[n % 2]
                    qs = "xnb%d" % (n % 2)
                    P.op("dve", lambda E: E.scalar_tensor_tensor(out=xq[:rows, :], in0=x[:rows, :], scalar=stat[:rows, 2, n % 8:n % 8 + 1],
                                                                 in1=g_r[:rows, :], op0=ALU.mult, op1=ALU.mult), [xs, sk, "g_r"], [qs])

                def part2():
                    n = st["n"]
                    xq = xnb[n % 2]
                    qs = "xnb%d" % (n % 2)
                    pt = pT[n % 2]
                    ps = "pT%d" % (n % 2)
                    for k in range(8):
                        TR(P, pt[:, k, :], xq[:, k * 128:(k + 1) * 128], ident[:, :], [qs, "ident"], [ps])
                    rr.copy(P, hT[hbuf][:, :, col0:col0 + rows], pt[:, :, :rows], [ps], ["hT%d.%d" % (hbuf, col0 // 128)])
                return part1, part2

            def proj_T(hbuf, width, col_w, dst, hkeys, scale=None):
                n = proj_T.n
                proj_T.n += 1
                ps = psk[n % 2]
                pk = "psk%d" % (n % 2)
                for k in range(8):
                    MM(P, ps[:, :width], win[:, k, col_w:col_w + 128], hT[hbuf][:, k, :width], k == 0, k == 7, ["win.%d" % k] + hkeys, [pk])
                ks = kts[n % 3]
                kk = "kts%d" % (n % 3)
                rr.copy(P, ks[:, :width], ps[:, :width], [pk], [kk], scale=scale)
                DMA(P, "pool", dst, ks[:, :width], [kk], ["KQscr"])
            proj_T.n = 0

            vcount = [0]

            def make_tile_A(Tn, hb):
                blocks = [norm_block(T["xn"][Tn * 512 + b * 128:Tn * 512 + (b + 1) * 128, :], 128, hb, b * 128) for b in range(4)]
                hk = ["hT%d.%d" % (hb, b) for b in range(4)]
                vb = Tn % 2
                groups = []

                def kgrp(c):
                    col = (CK_F + c * 128) if c < 4 else (CK_S + (c - 4) * 128)
                    h0 = 2 * c if c < 4 else 8 + 2 * (c - 4)
                    proj_T(hb, 512, col, T["Kd"][h0:h0 + 2, :, Tn * 512:(Tn + 1) * 512].rearrange("h r t -> (h r) t"), hk)

                def vgrp(b, g):
                    n = vcount[0]
                    vcount[0] += 1
                    ps = psv[n % 2]
                    pk = "psv%d" % (n % 2)
                    vc = CV_F if g == 0 else CV_S
                    for k in range(8):
                        MM(P, ps[:, :], hT[hb][:, k, b * 128:(b + 1) * 128], win[:, k, vc:vc + 512], k == 0, k == 7,
                           ["win.%d" % k, "hT%d.%d" % (hb, b)], [pk])
                    rr.copy(P, vs[vb][:, g * 8:(g + 1) * 8, b, 0:64], ps[:, :].rearrange("p (h d) -> p h d", h=8), [pk],
                            ["vs%d.%d.%d" % (vb, b, g)])
                    if g == 1:
                        pf = psf[b % 2]
                        for k in range(8):
                            MM(P, pf[:, 0:8], hT[hb][:, k, b * 128:(b + 1) * 128], win[:, k, CL:CL + 8], k == 0, k == 7,
                               ["win.%d" % k, "hT%d.%d" % (hb, b)], ["psf%d" % (b % 2)])
                        P.op("dve", lambda E: E.tensor_tensor(out=flog[:, :, Tn * 4 + b], in0=pf[:, 0:8], in1=fb_r[:, :], op=ALU.add),
                             ["psf%d" % (b % 2), "fb_r"], ["flog"])

                def vstore():
                    DMA(P, "sp", T["Vd"][Tn, :, :, :, :], vs[vb][:, :, :, :],
                        ["vs%d.%d.%d" % (vb, b, g) for b in range(4) for g in range(2)], ["Vscr"])
                for c in range(8):
                    groups.append(lambda c=c: kgrp(c))
                for b in range(4):
                    for g in range(2):
                        groups.append(lambda b=b, g=g: vgrp(b, g))
                groups.append(vstore)
                return blocks, groups

            def make_tile_Q(row0, width, col0, hb):
                nb = (width + 127) // 128
                blocks = []
                for b in range(nb):
                    rows = min(128, width - b * 128)
                    blocks.append(norm_block(T["xo"][row0 + b * 128:row0 + b * 128 + rows, :], rows, hb, b * 128))
                hk = ["hT%d.%d" % (hb, b) for b in range(nb)]

                def qgrp(c):
                    col = (CQ_F + c * 128) if c < 4 else (CQ_S + (c - 4) * 128)
                    h0 = 2 * c if c < 4 else 8 + 2 * (c - 4)
                    proj_T(hb, width, col, T["Qd"][h0:h0 + 2, :, col0:col0 + width].rearrange("h r t -> (h r) t"), hk, scale=0.125)
                groups = [(lambda c=c: qgrp(c)) for c in range(8)]
                return blocks, groups

            seq = []
            qi = 0
            for Tn in range(NT):
                seq.append(("A", Tn))
                if Tn % 2 == 1 and qi < NSL:
                    seq.append(("Q", qi))
                    qi += 1
            seq.append(("H", 0))
            tiles = []
            for i, (kind, idx) in enumerate(seq):
                hb = i % 2
                if kind == "A":
                    tiles.append(make_tile_A(idx, hb))
                elif kind == "Q":
                    tiles.append(make_tile_Q(idx * 512, 512, idx * 512, hb))
                else:
                    tiles.append(make_tile_Q(NOWN, 16, NOWN, hb))
            for (p1, p2) in tiles[0][0]:
                p1()
                p2()
            for i, (blocks, groups) in enumerate(tiles):
                nxt = tiles[i + 1][0] if i + 1 < len(tiles) else []
                G = len(groups)
                nb_ = max(1, len(nxt))
                ev = {}
                for b, (p1, p2) in enumerate(nxt):
                    ev.setdefault(int(b * G / nb_), []).append(p1)
                    ev.setdefault(min(G - 1, int((b + 0.7) * G / nb_)), []).append(p2)
                for gi, g in enumerate(groups):
                    g()
                    for f_ in ev.get(gi, []):
                        f_()
            P.emit()
            stats.append(P.stats)
            esp.close()
            if stop_after == "a":
                return nc, stats

            with ExitStack() as es2:
                P = Prog(nc, gs, "b")
                nlf = sbt(es2, "nlf", [128, 8 * NB], F32)
                ee = sbt(es2, "ee", [128, 8 * NB], F32)
                sc = [sbt(es2, "sc%d" % i, [128, 8, NB], F32) for i in range(2)]
                tot = sbt(es2, "tot", [128, 8, NB], F32)
                nF = sbt(es2, "nF", [128, 8, NB], F32)
                nFp = sbt(es2, "nFp", [128, 8, 128], F32)
                nFT = sbt(es2, "nFT", [NB, 8, 128], F32)
                r1 = sbt(es2, "r1", [NB, 8, 128], F32)
                parts = [sbt(es2, "part%d" % i, [NB, 8, 128], BF16) for i in range(3)]
                triu = sbt(es2, "triu", [128, 128], F32)
                ones = sbt(es2, "ones", [128, 128], F32)
                identf = sbt(es2, "identf", [128, 128], F32)
                cm1 = sbt(es2, "cm1", [64, 3, 128], BF16)
                cp1 = sbt(es2, "cp1", [64, 3, 128], BF16)
                ps_c = pst(es2, "ps_c", [128, 512], F32)
                ps_t = pst(es2, "ps_t", [128, 512], F32)
                ps_x = [pst(es2, "ps_x%d" % i, [128, 4, 128], F32) for i in range(2)]
                W8 = 8 * NB
                DMA(P, "sp", triu[:], T["triu_f"][:, :], [], ["triu"])
                DMA(P, "sp", ones[:], T["ones_f"][:, :], [], ["ones"])
                DMA(P, "sp", identf[:], T["ident_f"][:, :], [], ["identf"])
                DMA(P, "sp", cm1[:], T["cm1"][:, :, :], [], ["cm1"])
                DMA(P, "sp", cp1[:], T["cp1"][:, :, :], [], ["cp1"])
                fl2 = flog[:, :, :].rearrange("p h b -> p (h b)")
                ACTF(P, ee[:, :], fl2, AF.Exp, [], ["ee"], scale=-1.0)
                ACTF(P, nlf[:, :], ee[:, :], AF.Ln, ["ee"], ["nlf"], bias=1.0)
                MM(P, ps_c[:, :W8], triu[:, :], nlf[:, :], True, True, ["triu", "nlf"], ["ps_c"])
                MM(P, ps_t[:, :W8], ones[:, :], nlf[:, :], True, True, ["ones", "nlf"], ["ps_t"])
                P.op("dve", lambda E: E.tensor_copy(out=tot[:, :, :], in_=ps_t[:, :W8].rearrange("p (h b) -> p h b", h=8)), ["ps_t"], ["tot"])
                P.op("dve", lambda E: E.tensor_copy(out=sc[0][:, :, :], in_=tot[:, :, :]), ["tot"], ["sc0"])
                cur = 0
                d = 1
                while d < NB:
                    nxt = 1 - cur
                    P.op("dve", lambda E, cur=cur, nxt=nxt, d=d: E.tensor_copy(out=sc[nxt][:, :, 0:d], in_=sc[cur][:, :, 0:d]), ["sc%d" % cur], ["sc%d" % nxt])
                    P.op("dve", lambda E, cur=cur, nxt=nxt, d=d: E.tensor_tensor(out=sc[nxt][:, :, d:NB], in0=sc[cur][:, :, d:NB], in1=sc[cur][:, :, 0:NB - d], op=ALU.add),
                         ["sc%d" % cur], ["sc%d" % nxt])
                    cur = nxt
                    d *= 2
                P.op("dve", lambda E, cur=cur: E.tensor_tensor(out=tot[:, :, :], in0=sc[cur][:, :, :], in1=tot[:, :, :], op=ALU.subtract), ["sc%d" % cur, "tot"], ["tot"])
                P.op("dve", lambda E: E.tensor_tensor(out=nF[:, :, :], in0=ps_c[:, :W8].rearrange("p (h b) -> p h b", h=8), in1=tot[:, :, :], op=ALU.add), ["ps_c", "tot"], ["nF"])
                if debug:
                    DMA(P, "sp", T["dbg_nF"][:, :, :], nF[:, :, :], ["nF"], ["dbg"])
                P.op("dve", lambda E: E.memset(nFp[:], 0.0), [], ["nFp"])
                P.op("dve", lambda E: E.tensor_copy(out=nFp[:, :, 0:NB], in_=nF[:, :, :]), ["nF", "nFp"], ["nFp"])
                for h in range(8):
                    px = ps_x[h // 4]
                    P.op("pe", lambda E, h=h, px=px: E.transpose(px[:, h % 4, :], nFp[:, h, :], identf[:, :]), ["nFp", "identf"], ["ps_x%d" % (h // 4)])
                for q in range(2):
                    P.op("dve", lambda E, q=q: E.tensor_copy(out=nFT[:, q * 4:(q + 1) * 4, :], in_=ps_x[q][:NB, :, :]), ["ps_x%d" % q], ["nFT"])
                P.op("dve", lambda E: E.tensor_copy(out=parts[0][:, :, :], in_=nFT[:, :, :]), ["nFT"], ["part0"])
                P.op("dve", lambda E: E.tensor_tensor(out=r1[:, :, :], in0=nFT[:, :, :], in1=parts[0][:, :, :], op=ALU.subtract), ["nFT", "part0"], ["r1"])
                P.op("dve", lambda E: E.tensor_copy(out=parts[1][:, :, :], in_=r1[:, :, :]), ["r1"], ["part1"])
                P.op("dve", lambda E: E.tensor_tensor(out=nFT[:, :, :], in0=r1[:, :, :], in1=parts[1][:, :, :], op=ALU.subtract), ["r1", "part1"], ["nFT"])
                P.op("dve", lambda E: E.tensor_copy(out=parts[2][:, :, :], in_=nFT[:, :, :]), ["nFT"], ["part2"])
                for i in range(3):
                    DMA(P, "sp", T["KA"][:, i, :].rearrange("h (b t) -> b h t", t=128), parts[i][:, :, :], ["part%d" % i], ["KAs"])
                    DMA(P, "sp", T["QA"][:, 3 + i, :].rearrange("h (b t) -> b h t", t=128), parts[i][:, :, :], ["part%d" % i], ["QAs"])
                for h in range(8):
                    DMA(P, "sp", T["KA"][h, 3:6, :].rearrange("r (b t) -> b r t", t=128), cm1[:NB, :, :], ["cm1"], ["KAs"])
                    DMA(P, "sp", T["QA"][h, 0:3, :].rearrange("r (b t) -> b r t", t=128), cp1[:NB, :, :], ["cp1"], ["QAs"])
                P.emit()
                stats.append(P.stats)
        if stop_after == "b":
            return nc, stats

        with ExitStack() as es:
            P = Prog(nc, gs, "c")
            KT = [sbt(es, "KT%d" % i, [70, S], BF16) for i in range(2)]
            QT = [sbt(es, "QT%d" % i, [70, NQ], BF16) for i in range(2)]
            VV = [sbt(es, "VV%d" % i, [128, NB, 65], BF16) for i in range(2)]
            qa = sbt(es, "qa", [70, S], BF16)
            qtmp = sbt(es, "qtmp", [70, 512], BF16)
            esel = sbt(es, "esel", [128, 2 * NSL], F32)
            eselh = sbt(es, "eselh", [128, 2 * NSL], F32)
            idE = sbt(es, "idE", [128, NSL, 128], BF16)
            idNE = sbt(es, "idNE", [128, NSL, 128], BF16)
            erow = sbt(es, "erow", [1, NSL, 128], BF16)
            negrow = sbt(es, "negrow", [1, 512], BF16)
            allneg = sbt(es, "allneg", [128, 512], BF16)
            ident = sbt(es, "ident2", [128, 128], BF16)
            mt = [sbt(es, "mt%d" % i, [128, 4, 512], BF16) for i in range(2)]
            mh = [sbt(es, "mh%d" % i, [128, NB, 16], BF16) for i in range(2)]
            negtri = sbt(es, "negtri", [128, 128], BF16)
            negones = sbt(es, "negones", [128, 128], BF16)
            PT = [sbt(es, "PT%d" % i, [128, 2, 512], BF16) for i in range(3)]
            UU = [sbt(es, "UU%d" % i, [128, 2, 512], F32) for i in range(2)]
            LL = [sbt(es, "LL%d" % i, [128, 2, 512], BF16) for i in range(3)]
            LS = [sbt(es, "LS%d" % i, [128, 512], BF16) for i in range(3)]
            rden = sbt(es, "rden", [128, 2, 4], F32)
            ob = [sbt(es, "ob%d" % i, [128, 4, 64], BF16) for i in range(3)]
            cst = [sbt(es, "cst%d" % i, [128, 2816], F32) for i in range(2)]
            cbf = [sbt(es, "cbf%d" % i, [128, 2816], BF16) for i in range(2)]
            jobs = []
            for k in range(8):
                for hf in range(2):
                    jobs.append((T["w_up"][k * 128:(k + 1) * 128, hf * 2816:(hf + 1) * 2816], T["Wup"][:, k, hf * 2816:(hf + 1) * 2816], 2816, None))
            for c in range(0, 22, 2):
                jobs.append((T["w_down"][c * 128:(c + 2) * 128, :].rearrange("(c p) n -> p c n", p=128), T["Wdn"][:, c:c + 2, :], 2048, 2))
            for k in range(0, 8, 2):
                jobs.append((T["w_out"][k * 128:(k + 2) * 128, :].rearrange("(c p) n -> p c n", p=128), T["Wout"][:, k:k + 2, :], 2048, 2))
            jobn = [0]

            def conv_job():
                n = jobn[0]
                if n >= len(jobs):
                    return
                jobn[0] += 1
                src, dst, width, sub = jobs[n]
                b = n % 2
                if sub is None:
                    s_ap, b_ap = cst[b][:, :width], cbf[b][:, :width]
                else:
                    s_ap = cst[b][:, :width].rearrange("p (c n) -> p c n", c=sub)
                    b_ap = cbf[b][:, :width].rearrange("p (c n) -> p c n", c=sub)
                DMA(P, "sp", s_ap, src, [], ["cst%d" % b])
                P.op("dve", lambda E: E.tensor_copy(out=cbf[b][:, :width], in_=cst[b][:, :width]), ["cst%d" % b], ["cbf%d" % b])
                DMA(P, "sp", dst, b_ap, ["cbf%d" % b], ["Wscr"])
            psZ = [pst(es, "psZ%d" % i, [128, 2, 512], F32) for i in range(3)]
            psO = [pst(es, "psO%d" % i, [128, 512], F32) for i in range(2)]
            for nm, t_, src in (("esel", esel, T["esel"][:, :]), ("eselh", eselh, T["eselh"][:, :]), ("idE", idE, T["idE"][:, :, :]),
                                ("idNE", idNE, T["idNE"][:, :, :]), ("erow", erow, T["erow"][:, :, :]), ("negrow", negrow, T["negrow"][:, :]),
                                ("ident", ident, T["ident_bf"][:, :]), ("mt0", mt[0], T["mt_f"][:, :, :]), ("mt1", mt[1], T["mt_s"][:, :, :]),
                                ("mh0", mh[0], T["mh_f"][:, :, :]), ("mh1", mh[1], T["mh_s"][:, :, :]),
                                ("negtri", negtri, T["negtri"][:, :]), ("negones", negones, T["negones"][:, :])):
                DMA(P, "sp", t_[:], src, [], [nm])
            P.op("pool", lambda E: E.memset(allneg[:], NEG), [], ["allneg"])
            for i in range(3):
                P.op("pool", lambda E, i=i: E.memset(PT[i][:], 0.0), [], ["PT%d" % i])
            CONSTS = ["idE", "idNE", "erow", "negrow", "ident", "mt0", "mt1", "mh0", "mh1", "allneg"]

            def load_head(hh):
                hb = hh % 2
                fox = hh < 8
                DMA(P, "sp", KT[hb][0:64, :], T["Kd"][hh, :, :], [], ["KTm%d" % hb])
                DMA(P, "sp", QT[hb][0:64, :], T["Qd"][hh, :, :], [], ["QTm%d" % hb])
                DMA(P, "sp", VV[hb][:, :, :].rearrange("p (t b) d -> p t b d", b=4), T["Vd"][:, :, hh, :, :].rearrange("t p b d -> p t b d"), [], ["VV%d" % hb])
                if not fox:
                    P.op("pool", lambda E: E.memset(KT[hb][64:70, :], 0.0), [], ["KTa%d" % hb])
                    P.op("pool", lambda E: E.memset(QT[hb][64:70, :], 0.0), [], ["QTa%d" % hb])
                if fox:
                    DMA(P, "sp", KT[hb][64:70, :], T["KA"][hh, :, :], [], ["KTa%d" % hb])
                    DMA(P, "sp", qa[64:70, :], T["QA"][hh, :, :], [], ["qa"])
                    for j in range(NSL + 1):
                        if j < NSL:
                            c0 = qa[64:70, 1024 * j:1024 * j + 512]
                            c1 = qa[64:70, 1024 * j + 512:1024 * j + 1024]
                            e0 = esel[64:70, 2 * j:2 * j + 1]
                            e1 = esel[64:70, 2 * j + 1:2 * j + 2]
                            dst = QT[hb][64:70, 512 * j:512 * j + 512]
                            tmp = qtmp[64:70, 0:512]
                            P.op("dve", lambda E, c0=c0, e0=e0, tmp=tmp: E.tensor_scalar(out=tmp, in0=c0, scalar1=e0, scalar2=None, op0=ALU.mult),
                                 ["qa", "esel"], ["qtmp"])
                            P.op("dve", lambda E, c1=c1, e1=e1, tmp=tmp, dst=dst: E.scalar_tensor_tensor(out=dst, in0=c1, scalar=e1, in1=tmp, op0=ALU.mult, op1=ALU.add),
                                 ["qa", "esel", "qtmp"], ["QTa%d" % hb])
                        else:
                            for jj in range(NSL):
                                a0 = max(0, 1024 * jj - 2)
                                c0 = qa[64:70, a0:a0 + 2]
                                c1 = qa[64:70, 1024 * jj + 510:1024 * jj + 512]
                                e0 = eselh[64:70, 2 * jj:2 * jj + 1]
                                e1 = eselh[64:70, 2 * jj + 1:2 * jj + 2]
                                dst = QT[hb][64:70, NOWN + 2 * jj:NOWN + 2 * jj + 2]
                                tmp = qtmp[64:70, 0:2]
                                P.op("dve", lambda E, c0=c0, e0=e0, tmp=tmp: E.tensor_scalar(out=tmp, in0=c0, scalar1=e0, scalar2=None, op0=ALU.mult),
                                     ["qa", "eselh"], ["qtmp"])
                                P.op("dve", lambda E, c1=c1, e1=e1, tmp=tmp, dst=dst: E.scalar_tensor_tensor(out=dst, in0=c1, scalar=e1, in1=tmp, op0=ALU.mult, op1=ALU.add),
                                     ["qa", "eselh", "qtmp"], ["QTa%d" % hb])

            units = []
            slot_ctr = 0
            for hh in range(16):
                fox = hh < 8
                if "noSB" in OPT and not fox:
                    continue
                if "noFOX" in OPT and fox:
                    continue
                for sl in range(NSL + 1):
                    halo = sl == NSL
                    if halo and "noHalo" in OPT:
                        continue
                    W = 16 if halo else 512
                    nkb = NB if halo else 8 * (sl + 1)
                    order = list(range(nkb)) if fox else list(range(nkb - 1, -1, -1))

                    def mask_of(kb):
                        if halo:
                            return ("H", kb)
                        if 8 * sl <= kb < 8 * sl + 4:
                            return ("E", kb - 8 * sl)
                        if 8 * sl + 4 <= kb < 8 * sl + 8:
                            return ("NE", kb - 8 * sl - 4)
                        return None
                    npairs = nkb // 2
                    for idx in range(npairs):
                        kbs = (order[2 * idx], order[2 * idx + 1])
                        units.append(dict(hh=hh, fox=fox, sl=sl, W=W, kbs=kbs, first=idx == 0, last=idx == npairs - 1,
                                          mks=(mask_of(kbs[0]), mask_of(kbs[1])), so=slot_ctr, qc=NOWN if halo else 512 * sl))
                    slot_ctr += 1
            for i, u in enumerate(units):
                u["i"] = i
                u["head_first"] = (i == 0 or units[i - 1]["hh"] != u["hh"])
                u["head_last"] = (i == len(units) - 1 or units[i + 1]["hh"] != u["hh"])

            def stage_A(u):
                hb = u["hh"] % 2
                KD = 70
                W, i = u["W"], u["i"]
                zt = psZ[i % 3]
                zk = "psZ%d" % (i % 3)
                mi = 0 if u["fox"] else 1
                rd = ["KTm%d" % hb, "QTm%d" % hb, "KTa%d" % hb, "QTa%d" % hb]
                sl = u["sl"]
                for a in range(2):
                    kb = u["kbs"][a]
                    z = zt[:, a, :W]
                    mk = u["mks"][a] if "noMask" not in OPT else None
                    MM(P, z, KT[hb][0:KD, kb * 128:(kb + 1) * 128], QT[hb][0:KD, u["qc"]:u["qc"] + W], True, mk is None, rd, [zk])
                    if mk is not None:
                        typ, m = mk
                        if typ == "H":
                            MM(P, z, ident[:, :], mh[mi][:, m, :], False, True, CONSTS, [zk])
                        elif typ == "E":
                            MM(P, z, idE[:, sl, :], mt[mi][:, m, :], False, True, CONSTS, [zk])
                        else:
                            MM(P, z, idNE[:, sl, :], mt[mi][:, m, :], False, False, CONSTS, [zk])
                            MM(P, z, idE[:, sl, :], allneg[:, :W], False, True, CONSTS, [zk])

            def stage_B_fox(u):
                W, i = u["W"], u["i"]
                ACTF(P, PT[i % 3][:, :, :W], psZ[i % 3][:, :, :W], AF.Exp, ["psZ%d" % (i % 3)], ["PT%d" % (i % 3)])

            def stage_B1(u):
                W, i = u["W"], u["i"]
                ACTF(P, UU[i % 2][:, :, :W], psZ[i % 3][:, :, :W], AF.Exp, ["psZ%d" % (i % 3)], ["UU%d" % (i % 2)])
                ACTF(P, LL[i % 3][:, :, :W], UU[i % 2][:, :, :W], AF.Ln, ["UU%d" % (i % 2)], ["LL%d" % (i % 3)], bias=1.0)
                if not u["last"]:
                    lk, nk = "LL%d" % (i % 3), "LS%d" % ((i + 1) % 3)
                    if u["first"]:
                        P.op("pool", lambda E: E.tensor_tensor(out=LS[(i + 1) % 3][:, :W], in0=LL[i % 3][:, 0, :W], in1=LL[i % 3][:, 1, :W], op=ALU.add), [lk], [nk])
                    else:
                        P.op("pool", lambda E: E.tensor_tensor(out=LS[(i + 1) % 3][:, :W], in0=LS[i % 3][:, :W], in1=LL[i % 3][:, 0, :W], op=ALU.add),
                             [lk, "LS%d" % (i % 3)], [nk])
                        P.op("pool", lambda E: E.tensor_tensor(out=LS[(i + 1) % 3][:, :W], in0=LS[(i + 1) % 3][:, :W], in1=LL[i % 3][:, 1, :W], op=ALU.add),
                             [lk, nk], [nk])

            def stage_C(u):
                W, i = u["W"], u["i"]
                zt = psZ[i % 3]
                zk = "psZ%d" % (i % 3)
                lk = "LL%d" % (i % 3)
                MM(P, zt[:, 0, :W], negtri[:, :], LL[i % 3][:, 0, :W], False, u["first"], ["negtri", lk], [zk], skip=True)
                if not u["first"]:
                    MM(P, zt[:, 0, :W], negones[:, :], LS[i % 3][:, :W], False, True, ["negones", "LS%d" % (i % 3)], [zk], skip=True)
                MM(P, zt[:, 1, :W], negtri[:, :], LL[i % 3][:, 1, :W], False, False, ["negtri", lk], [zk], skip=True)
                MM(P, zt[:, 1, :W], negones[:, :], LL[i % 3][:, 0, :W], False, u["first"], ["negones", lk], [zk], skip=True)
                if not u["first"]:
                    MM(P, zt[:, 1, :W], negones[:, :], LS[i % 3][:, :W], False, True, ["negones", "LS%d" % (i % 3)], [zk], skip=True)

            def stage_B2(u):
                W, i = u["W"], u["i"]
                ACTF(P, PT[i % 3][:, :, :W], psZ[i % 3][:, :, :W], AF.Exp, ["psZ%d" % (i % 3)], ["PT%d" % (i % 3)])

            fin_ctr = [0]

            def stage_D(u):
                hb = u["hh"] % 2
                W, i = u["W"], u["i"]
                o = psO[u["so"] % 2]
                ok = "psO%d" % (u["so"] % 2)
                NV = 65 if u["fox"] else 64
                nsub = max(1, W // 128)
                M = min(128, W)
                for a in range(2):
                    kb = u["kbs"][a]
                    for s_ in range(nsub):
                        MM(P, o[:, s_ * NV:(s_ + 1) * NV], PT[i % 3][:, a, s_ * 128:s_ * 128 + 128], VV[hb][:, kb, 0:NV],
                           u["first"] and s_ == 0 and a == 0, u["last"] and s_ == nsub - 1 and a == 1, ["PT%d" % (i % 3), "VV%d" % hb], [ok], skip=True)
                if u["last"]:
                    f = fin_ctr[0]
                    fin_ctr[0] += 1
                    obuf = ob[f % 3]
                    obk = "ob%d" % (f % 3)
                    ov = o[:M, 0:nsub * NV].rearrange("p (s v) -> p s v", s=nsub)
                    if u["fox"]:
                        rk = "rden%d" % (f % 2)
                        P.op("dve", lambda E: E.reciprocal(out=rden[:M, f % 2, 0:nsub], in_=ov[:, :, 64]), [ok], [rk])
                        for s_ in range(nsub):
                            P.op("dve", lambda E, s_=s_: E.tensor_scalar(out=obuf[:M, s_, :], in0=ov[:, s_, 0:64], scalar1=rden[:M, f % 2, s_:s_ + 1],
                                                                         scalar2=None, op0=ALU.mult), [ok, rk], [obk])
                    else:
                        P.op("dve", lambda E: E.tensor_copy(out=obuf[:M, 0:nsub, :], in_=ov[:, :, 0:64]), [ok], [obk])
                    r0 = u["qc"]
                    hh = u["hh"]
                    if nsub == 4:
                        dst = T["Od"][r0:r0 + 512, hh * 64:(hh + 1) * 64].rearrange("(s p) d -> p s d", p=128)
                        DMA(P, "sp", dst, obuf[:, :, :], [obk], ["Od"])
                    else:
                        DMA(P, "sp", T["Od"][r0:r0 + M, hh * 64:(hh + 1) * 64], obuf[:M, 0, :], [obk], ["Od"])

            n = len(units)
            hh0 = units[0]["hh"]
            load_head(hh0)
            stage_A(units[0])
            job_every = max(1, n // (len(jobs) + 2))
            for i in range(n):
                u = units[i]
                if u["head_first"] and u["hh"] + 1 < 16 and u["hh"] == hh0:
                    load_head(hh0 + 1)
                if i % job_every == job_every - 1:
                    conv_job()
                if i + 1 < n:
                    stage_A(units[i + 1])
                if u["fox"]:
                    stage_B_fox(u)
                else:
                    stage_B1(u)
                    stage_C(u)
                if i >= 1:
                    pu = units[i - 1]
                    if not pu["fox"]:
                        stage_B2(pu)
                    stage_D(pu)
                    if pu["head_last"] and pu["hh"] + 2 < 16:
                        load_head(pu["hh"] + 2)
            pu = units[n - 1]
            if not pu["fox"]:
                stage_B2(pu)
            stage_D(pu)
            while jobn[0] < len(jobs):
                conv_job()
            P.emit()
            stats.append(P.stats)
        if stop_after == "c":
            return nc, stats

        with ExitStack() as es:
            P = Prog(nc, gs, "d")
            wout = sbt(es, "wout", [128, 8, D], BF16)
            fox_g = sbt(es, "fox_g", [128, 512], F32)
            sb_g = sbt(es, "sb_g", [128, 512], F32)
            ffn_g = sbt(es, "ffn_g", [128, D], F32)
            fin_g = sbt(es, "fin_g", [128, D], F32)
            cw = sbt(es, "cw", [128, NCH, 3], F32)
            cb = sbt(es, "cb", [128, NCH], F32)
            hmask = sbt(es, "hmask", [128, 16], F32)
            ident = sbt(es, "ident3", [128, 128], BF16)
            identf = sbt(es, "identf3", [128, 128], F32)
            o_s = [sbt(es, "o_s%d" % i, [128, 4, D], BF16) for i in range(2)]
            xr = [sbt(es, "xr%d" % i, [128, 4, D], F32) for i in range(2)]
            junk = sbt(es, "junk3", [128, D], BF16)
            stat = sbt(es, "stat3", [128, 3, 16], F32)
            on = sbt(es, "on", [128, 4, D], BF16)
            onT = sbt(es, "onT", [128, 8, 512], BF16)
            h2T = sbt(es, "h2T", [128, 8, 512], BF16)
            wu = [sbt(es, "wu%d" % i, [128, 8, 2, 128], BF16) for i in range(3)]
            cv = [sbt(es, "cv%d" % i, [128, 512], F32) for i in range(6)]
            sg = [sbt(es, "sg%d" % i, [128, 512], F32) for i in range(2)]
            aT = sbt(es, "aT", [128, 22, 512], BF16)
            uh = sbt(es, "uh", [128, NCH, 16], F32)
            wd = [sbt(es, "wd%d" % i, [128, 22, 128], BF16) for i in range(3)]
            yT = [sbt(es, "yT%d" % i, [128, 512], F32) for i in range(2)]
            pT = [pst(es, "p3T%d" % i, [128, 8, 128], BF16) for i in range(2)]
            psa = [pst(es, "psa%d" % i, [128, 512], F32) for i in range(2)]
            psu = [pst(es, "psu%d" % i, [128, 512], F32) for i in range(2)]
            psy = pst(es, "psy", [128, 512], F32)
            pst_ = pst(es, "pst", [128, 4, 128], F32)
            rr = RR(["act", "dve"])
            PSU = [psa[0], psa[1], psu[0], psu[1]]
            PSUK = ["psa0", "psa1", "psu0", "psu1"]
            for nm, t_, src in (("wout", wout, T["Wout"][:, :, :]), ("fox_g", fox_g, T["fox_g"][:, :]), ("sb_g", sb_g, T["sb_g"][:, :]),
                                ("ffn_g", ffn_g, T["ffn_g"][:, :]), ("fin_g", fin_g, T["fin_g"][:, :]), ("cw", cw, T["cw"][:, :, :]),
                                ("cb", cb, T["cb"][:, :]), ("hmask", hmask, T["hmask"][:, :]), ("ident", ident, T["ident_bf"][:, :]),
                                ("identf", identf, T["ident_f"][:, :])):
                DMA(P, "sp", t_[:], src, [], [nm])

            P.op("pool", lambda E: E.memset(on[:], 0.0), [], ["on"])
            P.op("pool", lambda E: E.memset(onT[:], 0.0), [], ["onT.%d" % b for b in range(4)])
            P.op("pool", lambda E: E.memset(h2T[:], 0.0), [], ["h2T.%d" % b for b in range(4)])
            sctr = [0]
            pctr = [0]
            wuc = [0]
            wdc = [0]

            def rstd_of(src_ap, rows, width):
                c = sctr[0] % 16
                sctr[0] += 1
                sk = "st%d" % c
                ACTF(P, junk[:rows, :width], src_ap, AF.Square, src_ap_keys[0], ["junk", sk], accum=stat[:rows, 0, c:c + 1])
                ACTF(P, stat[:rows, 1, c:c + 1], stat[:rows, 0, c:c + 1], AF.Ln, [sk], [sk], bias=EPS, scale=1.0 / width)
                ACTF(P, stat[:rows, 2, c:c + 1], stat[:rows, 1, c:c + 1], AF.Exp, [sk], [sk], scale=-0.5)
                return stat[:rows, 2, c:c + 1], sk
            src_ap_keys = [None]

            def transpose_to(src_tile, src_key, blocks, dstT, dst_key):
                for (bi, rows) in blocks:
                    n = pctr[0]
                    pctr[0] += 1
                    pt = pT[n % 2]
                    pk = "p3T%d" % (n % 2)
                    for k in range(8):
                        TR(P, pt[:, k, :], src_tile[:, bi, k * 128:(k + 1) * 128], ident[:, :], [src_key, "ident"], [pk])
                    rr.copy(P, dstT[:, :, bi * 128:bi * 128 + rows], pt[:, :, :rows], [pk], [dst_key + ".%d" % bi])

            slots = [NSL] + list(range(NSL))

            def load_slot(si):
                sl = slots[si]
                halo = sl == NSL
                r0 = NOWN if halo else 512 * sl
                sb_ = si % 2
                os_, xr_ = o_s[sb_], xr[sb_]
                osk, xrk = "o_s%d" % sb_, "xr%d" % sb_
                if halo:
                    DMA(P, "sp", os_[:16, 0, :], T["Od"][r0:r0 + 16, :], [], [osk])
                    DMA(P, "sp", xr_[:16, 0, :], T["xo"][r0:r0 + 16, :], [], [xrk])
                else:
                    DMA(P, "sp", os_[:, :, :], T["Od"][r0:r0 + 512, :].rearrange("(b p) d -> p b d", p=128), [], [osk])
                    DMA(P, "sp", xr_[:, :, :], T["xo"][r0:r0 + 512, :].rearrange("(b p) d -> p b d", p=128), [], [xrk])

            deferred = []

            def do_slot(si, sl):
                halo = sl == NSL
                W = 16 if halo else 512
                r0 = NOWN if halo else 512 * sl
                blocks = [(0, 16)] if halo else [(b, 128) for b in range(4)]
                sb_ = si % 2
                os_, xr_ = o_s[sb_], xr[sb_]
                osk, xrk = "o_s%d" % sb_, "xr%d" % sb_
                if si == 0:
                    load_slot(si)
                if si + 1 < len(slots):
                    load_slot(si + 1)
                for (bi, rows) in blocks:
                    for g in range(2):
                        src = os_[:rows, bi, g * 512:(g + 1) * 512]
                        src_ap_keys[0] = [osk]
                        rs, sk = rstd_of(src, rows, 512)
                        gt = fox_g if g == 0 else sb_g
                        P.op("dve", lambda E, src=src, rs=rs, gt=gt, rows=rows, bi=bi, g=g: E.scalar_tensor_tensor(
                            out=on[:rows, bi, g * 512:(g + 1) * 512], in0=src, scalar=rs, in1=gt[:rows, :], op0=ALU.mult, op1=ALU.mult),
                            [osk, sk, "fox_g", "sb_g"], ["on"])
                transpose_to(on, "on", blocks, onT, "onT")
                onk = ["onT.%d" % bi for (bi, _) in blocks]
                for (bi, rows) in blocks:
                    for ch in range(2):
                        n = pctr[0]
                        pctr[0] += 1
                        pa = psa[n % 2]
                        pk = "psa%d" % (n % 2)
                        for k in range(8):
                            MM(P, pa[:, :], onT[:, k, bi * 128:bi * 128 + 128], wout[:, k, ch * 512:(ch + 1) * 512], k == 0, k == 7,
                               ["wout", "onT.%d" % bi], [pk])
                        P.op("dve", lambda E, pa=pa, rows=rows, bi=bi, ch=ch: E.tensor_tensor(
                            out=xr_[:rows, bi, ch * 512:(ch + 1) * 512], in0=pa[:rows, :], in1=xr_[:rows, bi, ch * 512:(ch + 1) * 512], op=ALU.add),
                            [pk, xrk], [xrk])
                for (bi, rows) in blocks:
                    src = xr_[:rows, bi, :]
                    src_ap_keys[0] = [xrk]
                    rs, sk = rstd_of(src, rows, D)
                    P.op("dve", lambda E, src=src, rs=rs, rows=rows, bi=bi: E.scalar_tensor_tensor(
                        out=on[:rows, bi, :], in0=src, scalar=rs, in1=ffn_g[:rows, :], op0=ALU.mult, op1=ALU.mult),
                        [xrk, sk, "ffn_g"] + onk, ["on"])
                transpose_to(on, "on", blocks, h2T, "h2T")
                h2k = ["h2T.%d" % bi for (bi, _) in blocks]
                for c in range(22):
                    wn = wuc[0]
                    wuc[0] += 1
                    wt = wu[wn % 3]
                    wk = "wu%d" % (wn % 3)
                    DMA(P, "sp", wt[:, :, 0, :], T["Wup"][:, :, c * 128:(c + 1) * 128], [], [wk + "g"])
                    DMA(P, "sp", wt[:, :, 1, :], T["Wup"][:, :, DFF + c * 128:DFF + (c + 1) * 128], [], [wk + "v"])
                    pend = []
                    prev_fin = deferred[:]
                    del deferred[:]
                    for gv in range(2):
                        un = (2 * wn + gv) % 4
                        pu_ = PSU[un]
                        pk = PSUK[un]
                        for k in range(8):
                            MM(P, pu_[:, :W], wt[:, k, gv, :], h2T[:, k, :W], k == 0, k == 7, [wk + ("g" if gv == 0 else "v")] + h2k, [pk])
                        cc = c + 22 * gv
                        if halo:
                            P.op("dve", lambda E, pu_=pu_, cc=cc: E.tensor_tensor(out=uh[:, cc, :], in0=pu_[:, :16], in1=hmask[:, :], op=ALU.mult),
                                 [pk, "hmask"], ["uh"])
                            continue
                        cn = (2 * wn + gv) % 6
                        ct, ck = cv[cn], "cv%d" % cn
                        ACTF(P, ct[:, :], pu_[:, :512], AF.Identity, [pk, "cw", "cb"], [ck], bias=cb[:, cc:cc + 1], scale=cw[:, cc, 2:3])
                        pend.append((pu_, pk, ct, ck, cc))
                    if halo:
                        continue
                    for f_ in prev_fin:
                        f_()
                    for (pu_, pk, ct, ck, cc) in pend:
                        P.op("dve", lambda E, pu_=pu_, ct=ct, cc=cc: E.scalar_tensor_tensor(out=ct[:, 1:512], in0=pu_[:, 0:511], scalar=cw[:, cc, 1:2], in1=ct[:, 1:512],
                                                                                         op0=ALU.mult, op1=ALU.add), [pk, "cw", ck], [ck])
                    for (pu_, pk, ct, ck, cc) in pend:
                        P.op("dve", lambda E, pu_=pu_, ct=ct, cc=cc: E.scalar_tensor_tensor(out=ct[:, 2:512], in0=pu_[:, 0:510], scalar=cw[:, cc, 0:1], in1=ct[:, 2:512],
                                                                                         op0=ALU.mult, op1=ALU.add), [pk, "cw", ck], [ck])
                    for (pu_, pk, ct, ck, cc) in pend:
                        P.op("dve", lambda E, ct=ct, cc=cc: E.scalar_tensor_tensor(out=ct[:, 0:1], in0=uh[:, cc, 2 * sl + 1:2 * sl + 2], scalar=cw[:, cc, 1:2], in1=ct[:, 0:1],
                                                                                op0=ALU.mult, op1=ALU.add), ["uh", "cw", ck], [ck])
                    for (pu_, pk, ct, ck, cc) in pend:
                        P.op("dve", lambda E, ct=ct, cc=cc: E.scalar_tensor_tensor(out=ct[:, 0:2], in0=uh[:, cc, 2 * sl:2 * sl + 2], scalar=cw[:, cc, 0:1], in1=ct[:, 0:2],
                                                                                op0=ALU.mult, op1=ALU.add), ["uh", "cw", ck], [ck])
                    (_, _, ctg, ckg, _), (_, _, ctv, ckv, _) = pend
                    st_, stk = sg[wn % 2], "sg%d" % (wn % 2)

                    def fin(ctg=ctg, ckg=ckg, ctv=ctv, ckv=ckv, st_=st_, stk=stk, c=c):
                        ACTF(P, st_[:, :], ctg[:, :], AF.Silu, [ckg], [stk])
                        P.op("pool", lambda E: E.tensor_tensor(out=aT[:, c, :], in0=st_[:, :], in1=ctv[:, :], op=ALU.mult),
                             [stk, ckv], ["aT.%d" % c])
                    deferred.append(fin)
                for f_ in deferred:
                    f_()
                del deferred[:]
                if halo:
                    return
                for cc in range(8):
                    wn = wdc[0]
                    wdc[0] += 1
                    wt = wd[wn % 3]
                    wk = "wd%d" % (wn % 3)
                    DMA(P, "sp", wt[:, :, :], T["Wdn"][:, :, cc * 128:(cc + 1) * 128], [], [wk])
                    py_, pyk = (psy, "psy") if cc % 2 == 0 else (psu[0], "psu0")
                    pt_ = pst_[:, :, :] if cc % 2 == 0 else psu[1][:, :].rearrange("p (b c) -> p b c", b=4)
                    ptk = "pst" if cc % 2 == 0 else "psu1"
                    for c in range(22):
                        MM(P, py_[:, :], wt[:, c, :], aT[:, c, :], c == 0, c == 21, [wk, "aT.%d" % c], [pyk])
                    yt, yk = yT[wn % 2], "yT%d" % (wn % 2)
                    ACTF(P, yt[:, :], py_[:, :], AF.Copy, [pyk], [yk])
                    for b in range(4):
                        P.op("pe", lambda E, yt=yt, b=b, pt_=pt_: E.transpose(pt_[:, b, :], yt[:, b * 128:(b + 1) * 128], identf[:, :]), [yk, "identf"], [ptk])
                    P.op("dve", lambda E, cc=cc, pt_=pt_: E.tensor_tensor(out=xr_[:, :, cc * 128:(cc + 1) * 128], in0=pt_,
                                                                         in1=xr_[:, :, cc * 128:(cc + 1) * 128], op=ALU.add), [ptk, xrk], [xrk])
                for (bi, rows) in blocks:
                    src = xr_[:rows, bi, :]
                    src_ap_keys[0] = [xrk]
                    rs, sk = rstd_of(src, rows, D)
                    P.op("dve", lambda E, src=src, rs=rs: E.scalar_tensor_tensor(out=src, in0=src, scalar=rs, in1=fin_g[:, :], op0=ALU.mult, op1=ALU.mult),
                         [xrk, sk, "fin_g"], [xrk])
                DMA(P, "pool", T["out"][r0:r0 + 512, :].rearrange("(b p) d -> p b d", p=128), xr_[:, :, :], [xrk], ["out"])

            for si, sl in enumerate(slots):
                do_slot(si, sl)
            P.emit()
            stats.append(P.stats)
    return nc, stats


_CACHE = {}


def _consts(S):
    NB = S // 128
    bf = ml_dtypes.bfloat16
    p = np.arange(128)
    c = {}
    c["ident_bf"] = np.eye(128, dtype=np.float32).astype(bf)
    c["ident_f"] = np.eye(128, dtype=np.float32)
    c["triu_f"] = (p[:, None] <= p[None, :]).astype(np.float32)
    c["ones_f"] = np.ones((128, 128), np.float32)
    c["negtri"] = (-(p[:, None] >= p[None, :]).astype(np.float32)).astype(bf)
    c["negones"] = (-np.ones((128, 128), np.float32)).astype(bf)
    t = np.arange(512)
    key = (np.arange(4)[None, :, None] * 128 + p[:, None, None])
    c["mt_f"] = np.where(key <= t[None, None, :], 0.0, NEG).astype(np.float32).astype(bf)
    c["mt_s"] = np.where(key < t[None, None, :], 0.0, NEG).astype(np.float32).astype(bf)
    c["negrow"] = np.full((1, 512), NEG, np.float32).astype(bf)
    c["cm1"] = np.full((64, 3, 128), -1.0, np.float32).astype(bf)
    c["cp1"] = np.full((64, 3, 128), 1.0, np.float32).astype(bf)
    return c


def _core_consts(S, par):
    NT = S // 512
    NSL = NT // 2
    NB = S // 128
    bf = ml_dtypes.bfloat16
    own = own_tiles(par, NSL)
    esel = np.zeros((128, 2 * NSL), np.float32)
    eselh = np.zeros((128, 2 * NSL), np.float32)
    hmask = np.ones((128, 16), np.float32)
    idE = np.zeros((128, NSL, 128), np.float32)
    idNE = np.zeros((128, NSL, 128), np.float32)
    erow = np.zeros((1, NSL, 128), np.float32)
    I = np.eye(128, dtype=np.float32)
    p = np.arange(128)
    keypos = (np.arange(NB)[None, :] * 128 + p[:, None])
    mh_f = np.zeros((128, NB, 16), np.float32)
    mh_s = np.zeros((128, NB, 16), np.float32)
    for j in range(NSL):
        e0 = 1.0 if own[j] == 2 * j else 0.0
        esel[:, 2 * j] = e0
        esel[:, 2 * j + 1] = 1.0 - e0
        eselh[:, 2 * j] = 1.0 if (own[j] == 2 * j and j > 0) else 0.0
        eselh[:, 2 * j + 1] = 1.0 if own[j] == 2 * j + 1 else 0.0
        idE[:, j, :] = e0 * I
        idNE[:, j, :] = (1.0 - e0) * I
        erow[0, j, :] = e0
        for r in range(2):
            col = 2 * j + r
            pos = 512 * own[j] - 2 + r
            if own[j] == 0:
                hmask[:, col] = 0.0
                mh_f[:, :, col] = np.where(keypos == 0, 0.0, NEG)
                mh_s[:, :, col] = np.where(keypos == 0, 0.0, NEG)
            else:
                mh_f[:, :, col] = np.where(keypos <= pos, 0.0, NEG)
                mh_s[:, :, col] = np.where(keypos < pos, 0.0, NEG)
    return dict(esel=esel, eselh=eselh, hmask=hmask, idE=idE.astype(bf), idNE=idNE.astype(bf), erow=erow.astype(bf),
                mh_f=mh_f.astype(bf), mh_s=mh_s.astype(bf)), own


def _prepare(inputs, debug=False):
    x = np.asarray(inputs["x"], np.float32)
    B, S, _ = x.shape
    NT = S // 512
    NSL = NT // 2
    rep = lambda v: np.ascontiguousarray(np.broadcast_to(np.asarray(v, np.float32).reshape(1, -1), (128, np.asarray(v).size)))
    shared = dict(_consts(S))
    shared["w_in"] = np.ascontiguousarray(np.asarray(inputs["w_in"], np.float32)[0])
    shared["w_out"] = np.ascontiguousarray(np.asarray(inputs["w_out"], np.float32)[0])
    shared["w_up"] = np.ascontiguousarray(np.asarray(inputs["w_up"], np.float32)[0])
    shared["w_down"] = np.ascontiguousarray(np.asarray(inputs["w_down"], np.float32)[0])
    shared["attn_g"] = rep(inputs["attn_norm_g"][0])
    shared["ffn_g"] = rep(inputs["ffn_norm_g"][0])
    shared["fin_g"] = rep(inputs["final_norm_g"])
    shared["fox_g"] = rep(inputs["fox_out_g"][0])
    shared["sb_g"] = rep(inputs["sb_out_g"][0])
    shared["fb"] = rep(inputs["forget_bias"][0])
    cwv = np.asarray(inputs["conv_w"], np.float32)[0]
    shared["cw"] = np.ascontiguousarray(cwv.reshape(3, NCH, 128).transpose(2, 1, 0))
    shared["cb"] = np.ascontiguousarray(np.asarray(inputs["conv_b"], np.float32)[0].reshape(NCH, 128).T)
    in_maps = []
    owns = []
    for c in range(8):
        b, par = c // 2, c % 2
        cc, own = _core_consts(S, par)
        owns.append(own)
        xo = np.zeros((NSL * 512 + 16, D), np.float32)
        for j, t in enumerate(own):
            xo[j * 512:(j + 1) * 512] = x[b, t * 512:(t + 1) * 512]
            if t > 0:
                xo[NSL * 512 + 2 * j:NSL * 512 + 2 * j + 2] = x[b, t * 512 - 2:t * 512]
        m = dict(shared)
        m.update(cc)
        m["xn"] = np.ascontiguousarray(x[b])
        m["xo"] = xo
        in_maps.append(m)
    return in_maps, owns, (B, S)


def kernel(**inputs):
    in_maps, owns, (B, S) = _prepare(inputs)
    if S not in _CACHE:
        _CACHE[S] = build_program(S)
    nc, _ = _CACHE[S]
    res = run_bass_kernel_spmd(nc, in_maps, core_ids=list(range(8)))
    out = np.zeros((B, S, D), np.float32)
    for c in range(8):
        b = c // 2
        o = np.asarray(res.results[c]["out"], np.float32)
        for j, t in enumerate(owns[c]):
            out[b, t * 512:(t + 1) * 512] = o[j * 512:(j + 1) * 512]
    return out
```

```python
import numpy as np
import ml_dtypes
from contextlib import ExitStack
import concourse.bass as bass
import concourse.mybir as mybir
from concourse.bass_utils import run_bass_kernel_spmd

F32 = mybir.dt.float32
BF16 = mybir.dt.bfloat16
AF = mybir.ActivationFunctionType
ALU = mybir.AluOpType

D = 1024
DH = 64
DFF = 2816
INC = 3080
EPS = 1e-6
NEG = -30000.0
CQ_F, CK_F, CV_F, CL, CQ_S, CK_S, CV_S = 0, 512, 1024, 1536, 1544, 2056, 2568
NCH = 2 * DFF // 128

import os
OPT = set(os.environ.get("KOPT", "").split(","))
COMPUTE = ("pe", "act", "dve", "pool")
CH = 16000
NDMA = 8


class Prog:
    def __init__(self, nc, gs, tag):
        self.nc = nc
        self.gs = gs
        self.tag = tag
        self.ops = []
        self.last_w = {}
        self.readers = {}

    def op(self, eng, fn, reads=(), writes=(), dma=False):
        j = len(self.ops)
        deps = set()
        for r in reads:
            if r in self.last_w:
                deps.add(self.last_w[r])
        for w in writes:
            if w in self.last_w:
                deps.add(self.last_w[w])
            for rd in self.readers.get(w, ()):
                deps.add(rd)
        deps.discard(j)
        self.ops.append(dict(eng=eng, fn=fn, deps=deps, dma=dma, sig=False))
        for r in reads:
            self.readers.setdefault(r, []).append(j)
        for w in writes:
            self.last_w[w] = j
            self.readers[w] = []
        return j

    def dma(self, q, fn, reads=(), writes=()):
        return self.op(q, fn, reads, writes, dma=True)

    def emit(self):
        nc, ops = self.nc, self.ops
        for j, o in enumerate(ops):
            nd = set()
            for d in o["deps"]:
                p = ops[d]
                if not p["dma"] and not o["dma"] and p["eng"] == o["eng"] == "pe":
                    continue
                nd.add(d)
            o["deps"] = nd
            for d in nd:
                ops[d]["sig"] = True
        cnt = {e: 0 for e in COMPUTE}
        sems = {}

        def getsem(key):
            if key not in sems:
                sems[key] = self.gs.enter_context(nc.semaphore("s%s_%s_%s" % (self.tag, key[0], key[1])))
            return sems[key]

        dcount = {}
        dma_i = {}
        for j, o in enumerate(ops):
            if o["dma"]:
                q = o["eng"]
                k = dma_i.get(q, 0)
                dma_i[q] = k + 1
                key = ("d" + q, k % NDMA)
                dcount[key] = dcount.get(key, 0) + 16
                o["semkey"], o["semval"] = key, dcount[key]
                o["prev"] = (key, dcount[key] - 16)
            elif o["sig"]:
                e = o["eng"]
                c = cnt[e]
                cnt[e] = c + 1
                o["semkey"], o["semval"] = (e, c // CH), c % CH + 1
        last_dma = {}
        for o in ops:
            if o["dma"]:
                last_dma[o["semkey"]] = max(last_dma.get(o["semkey"], 0), o["semval"])
            if "semkey" in o:
                getsem(o["semkey"])
        per = {e: [] for e in ("sp", "act", "dve", "pe", "pool")}
        for j, o in enumerate(ops):
            per[o["eng"]].append(j)
        nwc = [0]

        def run_engine(e, E):
            known = {}
            for j in per[e]:
                o = ops[j]
                need = {}
                for d in o["deps"]:
                    p = ops[d]
                    k, v = p["semkey"], p["semval"]
                    if need.get(k, 0) < v:
                        need[k] = v
                if o["dma"] and o["prev"][1] > 0:
                    k, v = o["prev"]
                    if need.get(k, 0) < v:
                        need[k] = v
                for k, v in need.items():
                    if known.get(k, 0) >= v:
                        continue
                    known[k] = v
                    E.wait_ge(sems[k], v)
                    nwc[0] += 1
                inst = o["fn"](E)
                if o["dma"]:
                    inst.then_inc(sems[o["semkey"]], 16)
                elif o["sig"]:
                    inst.then_inc(sems[o["semkey"]], 1)
            if e == "sp":
                for k, v in last_dma.items():
                    if known.get(k, 0) < v:
                        E.wait_ge(sems[k], v)

        with nc.Block() as block:
            @block.sync
            def _(E):
                run_engine("sp", E)

            @block.scalar
            def _(E):
                run_engine("act", E)

            @block.vector
            def _(E):
                run_engine("dve", E)

            @block.tensor
            def _(E):
                run_engine("pe", E)

            @block.gpsimd
            def _(E):
                run_engine("pool", E)
        self.stats = dict(tag=self.tag, nops=len(ops), nwaits=nwc[0], nsems=len(sems), cnt=cnt)


def MM(P, out, lhsT, rhs, start, stop, rd, wr, skip=False):
    P.op("pe", lambda E: E.matmul(out, lhsT=lhsT, rhs=rhs, start=start, stop=stop, skip_group_check=skip), rd, wr)


def TR(P, out, in_, ident, rd, wr):
    P.op("pe", lambda E: E.transpose(out, in_, ident), rd, wr)


def ACTF(P, out, in_, func, rd, wr, bias=None, scale=None, accum=None):
    kw = {}
    if bias is not None:
        kw["bias"] = bias
    if scale is not None:
        kw["scale"] = scale
    if accum is not None:
        kw["accum_out"] = accum
    P.op("act", lambda E: E.activation(out=out, in_=in_, func=func, **kw), rd, wr)


def DMA(P, q, out, in_, rd, wr):
    P.dma(q, lambda E: E.dma_start(out=out, in_=in_), rd, wr)


class RR:
    def __init__(self, engs):
        self.engs = engs
        self.i = 0

    def copy(self, P, out, in_, rd, wr, scale=None):
        e = self.engs[self.i % len(self.engs)]
        self.i += 1
        if e == "act":
            ACTF(P, out, in_, AF.Copy, rd, wr, scale=scale)
        elif scale is None:
            P.op(e, lambda E: E.tensor_copy(out=out, in_=in_), rd, wr)
        else:
            P.op(e, lambda E: E.tensor_scalar(out=out, in0=in_, scalar1=float(scale), scalar2=None, op0=ALU.mult), rd, wr)


def own_tiles(p, NSL):
    res = []
    for j in range(NSL):
        first = (j % 2 == 0) if p == 0 else (j % 2 == 1)
        res.append(2 * j if first else 2 * j + 1)
    return res


def build_program(S, debug=False, stop_after="d"):
    NT = S // 512
    NSL = NT // 2
    NB = S // 128
    NOWN = NSL * 512
    NQ = NOWN + 16
    nc = bass.Bass("TRN2", target_bir_lowering=False)
    T = {}

    def din(name, shape, dt=F32):
        T[name] = nc.dram_tensor(name, list(shape), dt, kind="ExternalInput").ap()

    def dscr(name, shape, dt=BF16):
        kind = "ExternalOutput" if debug else "Internal"
        T[name] = nc.dram_tensor(name, list(shape), dt, kind=kind).ap()

    din("xn", [S, D]); din("xo", [NQ, D])
    din("w_in", [D, INC]); din("w_out", [D, D]); din("w_up", [D, 2 * DFF]); din("w_down", [DFF, D])
    din("attn_g", [128, D]); din("ffn_g", [128, D]); din("fin_g", [128, D])
    din("fox_g", [128, 512]); din("sb_g", [128, 512]); din("fb", [128, 8])
    din("cw", [128, NCH, 3]); din("cb", [128, NCH])
    din("esel", [128, 2 * NSL]); din("eselh", [128, 2 * NSL]); din("hmask", [128, 16])
    din("idE", [128, NSL, 128], BF16); din("idNE", [128, NSL, 128], BF16); din("erow", [1, NSL, 128], BF16)
    din("mh_f", [128, NB, 16], BF16); din("mh_s", [128, NB, 16], BF16)
    din("ident_bf", [128, 128], BF16); din("ident_f", [128, 128]); din("triu_f", [128, 128]); din("ones_f", [128, 128])
    din("negtri", [128, 128], BF16); din("negones", [128, 128], BF16)
    din("mt_f", [128, 4, 512], BF16); din("mt_s", [128, 4, 512], BF16); din("negrow", [1, 512], BF16)
    din("cm1", [64, 3, 128], BF16); din("cp1", [64, 3, 128], BF16)
    T["out"] = nc.dram_tensor("out", [NOWN, D], F32, kind="ExternalOutput").ap()
    dscr("Kd", [16, 64, S]); dscr("Vd", [NT, 128, 16, 4, 65]); dscr("Qd", [16, 64, NQ])
    dscr("KA", [8, 6, S]); dscr("QA", [8, 6, S]); dscr("Od", [NQ + 112, D])
    dscr("Wup", [128, 8, 2 * DFF]); dscr("Wdn", [128, 22, D]); dscr("Wout", [128, 8, D])
    if debug:
        dscr("dbg_nF", [128, 8, NB], F32)

    stats = []
    with ExitStack() as gs:
        def sbt(es, name, shape, dt):
            return es.enter_context(nc.sbuf_tensor("sb_" + name, list(shape), dt))

        def pst(es, name, shape, dt):
            return es.enter_context(nc.psum_tensor("pp_" + name, list(shape), dt))

        with ExitStack() as es:
            P = Prog(nc, gs, "a")
            win = sbt(es, "win", [128, 8, INC], BF16)
            wstg = [sbt(es, "wstg%d" % i, [128, INC], F32) for i in range(2)]
            g_r = sbt(es, "g_r", [128, D], F32)
            fb_r = sbt(es, "fb_r", [128, 8], F32)
            ident = sbt(es, "ident", [128, 128], BF16)
            xb = [sbt(es, "xb%d" % i, [128, D], F32) for i in range(3)]
            junk = sbt(es, "junk", [128, D], BF16)
            stat = sbt(es, "stat", [128, 3, 8], F32)
            xnb = [sbt(es, "xnb%d" % i, [128, D], BF16) for i in range(2)]
            hT = [sbt(es, "hT%d" % i, [128, 8, 512], BF16) for i in range(2)]
            kts = [sbt(es, "kts%d" % i, [128, 512], BF16) for i in range(3)]
            vs = [sbt(es, "vs%d" % i, [128, 16, 4, 65], BF16) for i in range(2)]
            flog = sbt(es, "flog", [128, 8, NB], F32)
            esp = ExitStack()
            pT = [pst(esp, "pT%d" % i, [128, 8, 128], BF16) for i in range(2)]
            psk = [pst(esp, "psk%d" % i, [128, 512], F32) for i in range(2)]
            psv = [pst(esp, "psv%d" % i, [128, 512], F32) for i in range(2)]
            psf = [pst(esp, "psf%d" % i, [128, 512], F32) for i in range(2)]
            rr = RR(["act", "dve"])

            DMA(P, "sp", g_r[:], T["attn_g"][:, :], [], ["g_r"])
            DMA(P, "sp", fb_r[:], T["fb"][:, :], [], ["fb_r"])
            DMA(P, "sp", ident[:], T["ident_bf"][:, :], [], ["ident"])
            for k in range(8):
                DMA(P, "sp" if k % 2 == 0 else "pool", wstg[k % 2][:], T["w_in"][k * 128:(k + 1) * 128, :], [], ["wstg%d" % (k % 2)])
                if k % 2 == 0:
                    P.op("dve", lambda E, k=k: E.tensor_copy(out=win[:, k, :], in_=wstg[k % 2][:]), ["wstg%d" % (k % 2)], ["win.%d" % k])
                else:
                    ACTF(P, win[:, k, :], wstg[k % 2][:], AF.Copy, ["wstg%d" % (k % 2)], ["win.%d" % k])
            for i in range(2):
                P.op("pool", lambda E, i=i: E.memset(vs[i][:], 1.0), [], ["vs%d.%d.%d" % (i, b, g) for b in range(4) for g in range(2)])
                P.op("pool", lambda E, i=i: E.memset(xnb[i][:], 0.0), [], ["xnb%d" % i])
            blkn = [0]

            def norm_block(src_rows, rows, hbuf, col0):
                st = {}

                def part1():
                    n = blkn[0]
                    blkn[0] += 1
                    st["n"] = n
                    x = xb[n % 3]
                    xs = "xb%d" % (n % 3)
                    sk = "stat%d" % (n % 8)
                    DMA(P, "sp", x[:rows, :], src_rows, [], [xs])
                    ACTF(P, junk[:rows, :], x[:rows, :], AF.Square, [xs], ["junk", sk], accum=stat[:rows, 0, n % 8:n % 8 + 1])
                    ACTF(P, stat[:rows, 1, n % 8:n % 8 + 1], stat[:rows, 0, n % 8:n % 8 + 1], AF.Ln, [sk], [sk], bias=EPS, scale=1.0 / D)
                    ACTF(P, stat[:rows, 2, n % 8:n % 8 + 1], stat[:rows, 1, n % 8:n % 8 + 1], AF.Exp, [sk], [sk], scale=-0.5)
                    xq = xnb[n % 2]
                    qs = "xnb%d" % (n % 2)
                    P.op("dve", lambda E: E.scalar_tensor_tensor(out=xq[:rows, :], in0=x[:rows, :], scalar=stat[:rows, 2, n % 8:n % 8 + 1],
                                                                 in1=g_r[:rows, :], op0=ALU.mult, op1=ALU.mult), [xs, sk, "g_r"], [qs])

                def part2():
                    n = st["n"]
                    xq = xnb[n % 2]
                    qs = "xnb%d" % (n % 2)
                    pt = pT[n % 2]
                    ps = "pT%d" % (n % 2)
                    for k in range(8):
                        TR(P, pt[:, k, :], xq[:, k * 128:(k + 1) * 128], ident[:, :], [qs, "ident"], [ps])
                    rr.copy(P, hT[hbuf][:, :, col0:col0 + rows], pt[:, :, :rows], [ps], ["hT%d.%d" % (hbuf, col0 // 128)])
                return part1, part2

            def proj_T(hbuf, width, col_w, dst, hkeys, scale=None):
                n = proj_T.n
                proj_T.n += 1
                ps = psk[n % 2]
                pk = "psk%d" % (n % 2)
                for k in range(8):
                    MM(P, ps[:, :width], win[:, k, col_w:col_w + 128], hT[hbuf][:, k, :width], k == 0, k == 7, ["win.%d" % k] + hkeys, [pk])
                ks = kts[n % 3]
                kk = "kts%d" % (n % 3)
                rr.copy(P, ks[:, :width], ps[:, :width], [pk], [kk], scale=scale)
                DMA(P, "pool", dst, ks[:, :width], [kk], ["KQscr"])
            proj_T.n = 0

            vcount = [0]

            def make_tile_A(Tn, hb):
                blocks = [norm_block(T["xn"][Tn * 512 + b * 128:Tn * 512 + (b + 1) * 128, :], 128, hb, b * 128) for b in range(4)]
                hk = ["hT%d.%d" % (hb, b) for b in range(4)]
                vb = Tn % 2
                groups = []

                def kgrp(c):
                    col = (CK_F + c * 128) if c < 4 else (CK_S + (c - 4) * 128)
                    h0 = 2 * c if c < 4 else 8 + 2 * (c - 4)
                    proj_T(hb, 512, col, T["Kd"][h0:h0 + 2, :, Tn * 512:(Tn + 1) * 512].rearrange("h r t -> (h r) t"), hk)

                def vgrp(b, g):
                    n = vcount[0]
                    vcount[0] += 1
                    ps = psv[n % 2]
                    pk = "psv%d" % (n % 2)
                    vc = CV_F if g == 0 else CV_S
                    for k in range(8):
                        MM(P, ps[:, :], hT[hb][:, k, b * 128:(b + 1) * 128], win[:, k, vc:vc + 512], k == 0, k == 7,
                           ["win.%d" % k, "hT%d.%d" % (hb, b)], [pk])
                    rr.copy(P, vs[vb][:, g * 8:(g + 1) * 8, b, 0:64], ps[:, :].rearrange("p (h d) -> p h d", h=8), [pk],
                            ["vs%d.%d.%d" % (vb, b, g)])
                    if g == 1:
                        pf = psf[b % 2]
                        for k in range(8):
                            MM(P, pf[:, 0:8], hT[hb][:, k, b * 128:(b + 1) * 128], win[:, k, CL:CL + 8], k == 0, k == 7,
                               ["win.%d" % k, "hT%d.%d" % (hb, b)], ["psf%d" % (b % 2)])
                        P.op("dve", lambda E: E.tensor_tensor(out=flog[:, :, Tn * 4 + b], in0=pf[:, 0:8], in1=fb_r[:, :], op=ALU.add),
                             ["psf%d" % (b % 2), "fb_r"], ["flog"])

                def vstore():
                    DMA(P, "sp", T["Vd"][Tn, :, :, :, :], vs[vb][:, :, :, :],
                        ["vs%d.%d.%d" % (vb, b, g) for b in range(4) for g in range(2)], ["Vscr"])
                for c in range(8):
                    groups.append(lambda c=c: kgrp(c))
                for b in range(4):
                    for g in range(2):
                        groups.append(lambda b=b, g=g: vgrp(b, g))
                groups.append(vstore)
                return blocks, groups

            def make_tile_Q(row0, width, col0, hb):
                nb = (width + 127) // 128
                blocks = []
                for b in range(nb):
                    rows = min(128, width - b * 128)
                    blocks.append(norm_block(T["xo"][row0 + b * 128:row0 + b * 128 + rows, :], rows, hb, b * 128))
                hk = ["hT%d.%d" % (hb, b) for b in range(nb)]

                def qgrp(c):
                    col = (CQ_F + c * 128) if c < 4 else (CQ_S + (c - 4) * 128)
                    h0 = 2 * c if c < 4 else 8 + 2 * (c - 4)
                    proj_T(hb, width, col, T["Qd"][h0:h0 + 2, :, col0:col0 + width].rearrange("h r t -> (h r) t"), hk, scale=0.125)
                groups = [(lambda c=c: qgrp(c)) for c in range(8)]
                return blocks, groups

            seq = []
            qi = 0
            for Tn in range(NT):
                seq.append(("A", Tn))
                if Tn % 2 == 1 and qi < NSL:
                    seq.append(("Q", qi))
                    qi += 1
            seq.append(("H", 0))
            tiles = []
            for i, (kind, idx) in enumerate(seq):
                hb = i % 2
                if kind == "A":
                    tiles.append(make_tile_A(idx, hb))
                elif kind == "Q":
                    tiles.append(make_tile_Q(idx * 512, 512, idx * 512, hb))
                else:
                    tiles.append(make_tile_Q(NOWN, 16, NOWN, hb))
            for (p1, p2) in tiles[0][0]:
                p1()
                p2()
            for i, (blocks, groups) in enumerate(tiles):
                nxt = tiles[i + 1][0] if i + 1 < len(tiles) else []
                G = len(groups)
                nb_ = max(1, len(nxt))
                ev = {}
                for b, (p1, p2) in enumerate(nxt):
                    ev.setdefault(int(b * G / nb_), []).append(p1)
                    ev.setdefault(min(G - 1, int((b + 0.7) * G / nb_)), []).append(p2)
                for gi, g in enumerate(groups):
                    g()
                    for f_ in ev.get(gi, []):
                        f_()
            P.emit()
            stats.append(P.stats)
            esp.close()
            if stop_after == "a":
                return nc, stats

            with ExitStack() as es2:
                P = Prog(nc, gs, "b")
                nlf = sbt(es2, "nlf", [128, 8 * NB], F32)
                ee = sbt(es2, "ee", [128, 8 * NB], F32)
                sc = [sbt(es2, "sc%d" % i, [128, 8, NB], F32) for i in range(2)]
                tot = sbt(es2, "tot", [128, 8, NB], F32)
                nF = sbt(es2, "nF", [128, 8, NB], F32)
                nFp = sbt(es2, "nFp", [128, 8, 128], F32)
                nFT = sbt(es2, "nFT", [NB, 8, 128], F32)
                r1 = sbt(es2, "r1", [NB, 8, 128], F32)
                parts = [sbt(es2, "part%d" % i, [NB, 8, 128], BF16) for i in range(3)]
                triu = sbt(es2, "triu", [128, 128], F32)
                ones = sbt(es2, "ones", [128, 128], F32)
                identf = sbt(es2, "identf", [128, 128], F32)
                cm1 = sbt(es2, "cm1", [64, 3, 128], BF16)
                cp1 = sbt(es2, "cp1", [64, 3, 128], BF16)
                ps_c = pst(es2, "ps_c", [128, 512], F32)
                ps_t = pst(es2, "ps_t", [128, 512], F32)
                ps_x = [pst(es2, "ps_x%d" % i, [128, 4, 128], F32) for i in range(2)]
                W8 = 8 * NB
                DMA(P, "sp", triu[:], T["triu_f"][:, :], [], ["triu"])
                DMA(P, "sp", ones[:], T["ones_f"][:, :], [], ["ones"])
                DMA(P, "sp", identf[:], T["ident_f"][:, :], [], ["identf"])
                DMA(P, "sp", cm1[:], T["cm1"][:, :, :], [], ["cm1"])
                DMA(P, "sp", cp1[:], T["cp1"][:, :, :], [], ["cp1"])
                fl2 = flog[:, :, :].rearrange("p h b -> p (h b)")
                ACTF(P, ee[:, :], fl2, AF.Exp, [], ["ee"], scale=-1.0)
                ACTF(P, nlf[:, :], ee[:, :], AF.Ln, ["ee"], ["nlf"], bias=1.0)
                MM(P, ps_c[:, :W8], triu[:, :], nlf[:, :], True, True, ["triu", "nlf"], ["ps_c"])
                MM(P, ps_t[:, :W8], ones[:, :], nlf[:, :], True, True, ["ones", "nlf"], ["ps_t"])
                P.op("dve", lambda E: E.tensor_copy(out=tot[:, :, :], in_=ps_t[:, :W8].rearrange("p (h b) -> p h b", h=8)), ["ps_t"], ["tot"])
                P.op("dve", lambda E: E.tensor_copy(out=sc[0][:, :, :], in_=tot[:, :, :]), ["tot"], ["sc0"])
                cur = 0
                d = 1
                while d < NB:
                    nxt = 1 - cur
                    P.op("dve", lambda E, cur=cur, nxt=nxt, d=d: E.tensor_copy(out=sc[nxt][:, :, 0:d], in_=sc[cur][:, :, 0:d]), ["sc%d" % cur], ["sc%d" % nxt])
                    P.op("dve", lambda E, cur=cur, nxt=nxt, d=d: E.tensor_tensor(out=sc[nxt][:, :, d:NB], in0=sc[cur][:, :, d:NB], in1=sc[cur][:, :, 0:NB - d], op=ALU.add),
                         ["sc%d" % cur], ["sc%d" % nxt])
                    cur = nxt
                    d *= 2
                P.op("dve", lambda E, cur=cur: E.tensor_tensor(out=tot[:, :, :], in0=sc[cur][:, :, :], in1=tot[:, :, :], op=ALU.subtract), ["sc%d" % cur, "tot"], ["tot"])
                P.op("dve", lambda E: E.tensor_tensor(out=nF[:, :, :], in0=ps_c[:, :W8].rearrange("p (h b) -> p h b", h=8), in1=tot[:, :, :], op=ALU.add), ["ps_c", "tot"], ["nF"])
                if debug:
                    DMA(P, "sp", T["dbg_nF"][:, :, :], nF[:, :, :], ["nF"], ["dbg"])
                P.op("dve", lambda E: E.memset(nFp[:], 0.0), [], ["nFp"])
                P.op("dve", lambda E: E.tensor_copy(out=nFp[:, :, 0:NB], in_=nF[:, :, :]), ["nF", "nFp"], ["nFp"])
                for h in range(8):
                    px = ps_x[h // 4]
                    P.op("pe", lambda E, h=h, px=px: E.transpose(px[:, h % 4, :], nFp[:, h, :], identf[:, :]), ["nFp", "identf"], ["ps_x%d" % (h // 4)])
                for q in range(2):
                    P.op("dve", lambda E, q=q: E.tensor_copy(out=nFT[:, q * 4:(q + 1) * 4, :], in_=ps_x[q][:NB, :, :]), ["ps_x%d" % q], ["nFT"])
                P.op("dve", lambda E: E.tensor_copy(out=parts[0][:, :, :], in_=nFT[:, :, :]), ["nFT"], ["part0"])
                P.op("dve", lambda E: E.tensor_tensor(out=r1[:, :, :], in0=nFT[:, :, :], in1=parts[0][:, :, :], op=ALU.subtract), ["nFT", "part0"], ["r1"])
                P.op("dve", lambda E: E.tensor_copy(out=parts[1][:, :, :], in_=r1[:, :, :]), ["r1"], ["part1"])
                P.op("dve", lambda E: E.tensor_tensor(out=nFT[:, :, :], in0=r1[:, :, :], in1=parts[1][:, :, :], op=ALU.subtract), ["r1", "part1"], ["nFT"])
                P.op("dve", lambda E: E.tensor_copy(out=parts[2][:, :, :], in_=nFT[:, :, :]), ["nFT"], ["part2"])
                for i in range(3):
                    DMA(P, "sp", T["KA"][:, i, :].rearrange("h (b t) -> b h t", t=128), parts[i][:, :, :], ["part%d" % i], ["KAs"])
                    DMA(P, "sp", T["QA"][:, 3 + i, :].rearrange("h (b t) -> b h t", t=128), parts[i][:, :, :], ["part%d" % i], ["QAs"])
                for h in range(8):
                    DMA(P, "sp", T["KA"][h, 3:6, :].rearrange("r (b t) -> b r t", t=128), cm1[:NB, :, :], ["cm1"], ["KAs"])
                    DMA(P, "sp", T["QA"][h, 0:3, :].rearrange("r (b t) -> b r t", t=128), cp1[:NB, :, :], ["cp1"], ["QAs"])
                P.emit()
                stats.append(P.stats)
        if stop_after == "b":
            return nc, stats

        with ExitStack() as es:
            P = Prog(nc, gs, "c")
            KT = [sbt(es, "KT%d" % i, [70, S], BF16) for i in range(2)]
            QT = [sbt(es, "QT%d" % i, [70, NQ], BF16) for i in range(2)]
            VV = [sbt(es, "VV%d" % i, [128, NB, 65], BF16) for i in range(2)]
            qa = sbt(es, "qa", [70, S], BF16)
            qtmp = sbt(es, "qtmp", [70, 512], BF16)
            esel = sbt(es, "esel", [128, 2 * NSL], F32)
            eselh = sbt(es, "eselh", [128, 2 * NSL], F32)
            idE = sbt(es, "idE", [128, NSL, 128], BF16)
            idNE = sbt(es, "idNE", [128, NSL, 128], BF16)
            erow = sbt(es, "erow", [1, NSL, 128], BF16)
            negrow = sbt(es, "negrow", [1, 512], BF16)
            allneg = sbt(es, "allneg", [128, 512], BF16)
            ident = sbt(es, "ident2", [128, 128], BF16)
            mt = [sbt(es, "mt%d" % i, [128, 4, 512], BF16) for i in range(2)]
            mh = [sbt(es, "mh%d" % i, [128, NB, 16], BF16) for i in range(2)]
            negtri = sbt(es, "negtri", [128, 128], BF16)
            negones = sbt(es, "negones", [128, 128], BF16)
            PT = [sbt(es, "PT%d" % i, [128, 2, 512], BF16) for i in range(3)]
            UU = [sbt(es, "UU%d" % i, [128, 2, 512], F32) for i in range(2)]
            LL = [sbt(es, "LL%d" % i, [128, 2, 512], BF16) for i in range(3)]
            LS = [sbt(es, "LS%d" % i, [128, 512], BF16) for i in range(3)]
            rden = sbt(es, "rden", [128, 2, 4], F32)
            ob = [sbt(es, "ob%d" % i, [128, 4, 64], BF16) for i in range(3)]
            cst = [sbt(es, "cst%d" % i, [128, 2816], F32) for i in range(2)]
            cbf = [sbt(es, "cbf%d" % i, [128, 2816], BF16) for i in range(2)]
            jobs = []
            for k in range(8):
                for hf in range(2):
                    jobs.append((T["w_up"][k * 128:(k + 1) * 128, hf * 2816:(hf + 1) * 2816], T["Wup"][:, k, hf * 2816:(hf + 1) * 2816], 2816, None))
            for c in range(0, 22, 2):
                jobs.append((T["w_down"][c * 128:(c + 2) * 128, :].rearrange("(c p) n -> p c n", p=128), T["Wdn"][:, c:c + 2, :], 2048, 2))
            for k in range(0, 8, 2):
                jobs.append((T["w_out"][k * 128:(k + 2) * 128, :].rearrange("(c p) n -> p c n", p=128), T["Wout"][:, k:k + 2, :], 2048, 2))
            jobn = [0]

            def conv_job():
                n = jobn[0]
                if n >= len(jobs):
                    return
                jobn[0] += 1
                src, dst, width, sub = jobs[n]
                b = n % 2
                if sub is None:
                    s_ap, b_ap = cst[b][:, :width], cbf[b][:, :width]
                else:
                    s_ap = cst[b][:, :width].rearrange("p (c n) -> p c n", c=sub)
                    b_ap = cbf[b][:, :width].rearrange("p (c n) -> p c n", c=sub)
                DMA(P, "sp", s_ap, src, [], ["cst%d" % b])
                P.op("dve", lambda E: E.tensor_copy(out=cbf[b][:, :width], in_=cst[b][:, :width]), ["cst%d" % b], ["cbf%d" % b])
                DMA(P, "sp", dst, b_ap, ["cbf%d" % b], ["Wscr"])
            psZ = [pst(es, "psZ%d" % i, [128, 2, 512], F32) for i in range(3)]
            psO = [pst(es, "psO%d" % i, [128, 512], F32) for i in range(2)]
            for nm, t_, src in (("esel", esel, T["esel"][:, :]), ("eselh", eselh, T["eselh"][:, :]), ("idE", idE, T["idE"][:, :, :]),
                                ("idNE", idNE, T["idNE"][:, :, :]), ("erow", erow, T["erow"][:, :, :]), ("negrow", negrow, T["negrow"][:, :]),
                                ("ident", ident, T["ident_bf"][:, :]), ("mt0", mt[0], T["mt_f"][:, :, :]), ("mt1", mt[1], T["mt_s"][:, :, :]),
                                ("mh0", mh[0], T["mh_f"][:, :, :]), ("mh1", mh[1], T["mh_s"][:, :, :]),
                                ("negtri", negtri, T["negtri"][:, :]), ("negones", negones, T["negones"][:, :])):
                DMA(P, "sp", t_[:], src, [], [nm])
            P.op("pool", lambda E: E.memset(allneg[:], NEG), [], ["allneg"])
            for i in range(3):
                P.op("pool", lambda E, i=i: E.memset(PT[i][:], 0.0), [], ["PT%d" % i])
            CONSTS = ["idE", "idNE", "erow", "negrow", "ident", "mt0", "mt1", "mh0", "mh1", "allneg"]

            def load_head(hh):
                hb = hh % 2
                fox = hh < 8
                DMA(P, "sp", KT[hb][0:64, :], T["Kd"][hh, :, :], [], ["KTm%d" % hb])
                DMA(P, "sp", QT[hb][0:64, :], T["Qd"][hh, :, :], [], ["QTm%d" % hb])
                DMA(P, "sp", VV[hb][:, :, :].rearrange("p (t b) d -> p t b d", b=4), T["Vd"][:, :, hh, :, :].rearrange("t p b d -> p t b d"), [], ["VV%d" % hb])
                if not fox:
                    P.op("pool", lambda E: E.memset(KT[hb][64:70, :], 0.0), [], ["KTa%d" % hb])
                    P.op("pool", lambda E: E.memset(QT[hb][64:70, :], 0.0), [], ["QTa%d" % hb])
                if fox:
                    DMA(P, "sp", KT[hb][64:70, :], T["KA"][hh, :, :], [], ["KTa%d" % hb])
                    DMA(P, "sp", qa[64:70, :], T["QA"][hh, :, :], [], ["qa"])
                    for j in range(NSL + 1):
                        if j < NSL:
                            c0 = qa[64:70, 1024 * j:1024 * j + 512]
                            c1 = qa[64:70, 1024 * j + 512:1024 * j + 1024]
                            e0 = esel[64:70, 2 * j:2 * j + 1]
                            e1 = esel[64:70, 2 * j + 1:2 * j + 2]
                            dst = QT[hb][64:70, 512 * j:512 * j + 512]
                            tmp = qtmp[64:70, 0:512]
                            P.op("dve", lambda E, c0=c0, e0=e0, tmp=tmp: E.tensor_scalar(out=tmp, in0=c0, scalar1=e0, scalar2=None, op0=ALU.mult),
                                 ["qa", "esel"], ["qtmp"])
                            P.op("dve", lambda E, c1=c1, e1=e1, tmp=tmp, dst=dst: E.scalar_tensor_tensor(out=dst, in0=c1, scalar=e1, in1=tmp, op0=ALU.mult, op1=ALU.add),
                                 ["qa", "esel", "qtmp"], ["QTa%d" % hb])
                        else:
                            for jj in range(NSL):
                                a0 = max(0, 1024 * jj - 2)
                                c0 = qa[64:70, a0:a0 + 2]
                                c1 = qa[64:70, 1024 * jj + 510:1024 * jj + 512]
                                e0 = eselh[64:70, 2 * jj:2 * jj + 1]
                                e1 = eselh[64:70, 2 * jj + 1:2 * jj + 2]
                                dst = QT[hb][64:70, NOWN + 2 * jj:NOWN + 2 * jj + 2]
                                tmp = qtmp[64:70, 0:2]
                                P.op("dve", lambda E, c0=c0, e0=e0, tmp=tmp: E.tensor_scalar(out=tmp, in0=c0, scalar1=e0, scalar2=None, op0=ALU.mult),
                                     ["qa", "eselh"], ["qtmp"])
                                P.op("dve", lambda E, c1=c1, e1=e1, tmp=tmp, dst=dst: E.scalar_tensor_tensor(out=dst, in0=c1, scalar=e1, in1=tmp, op0=ALU.mult, op1=ALU.add),
                                     ["qa", "eselh", "qtmp"], ["QTa%d" % hb])

            units = []
            slot_ctr = 0
            for hh in range(16):
                fox = hh < 8
                if "noSB" in OPT and not fox:
                    continue
                if "noFOX" in OPT and fox:
                    continue
                for sl in range(NSL + 1):
                    halo = sl == NSL
                    if halo and "noHalo" in OPT:
                        continue
                    W = 16 if halo else 512
                    nkb = NB if halo else 8 * (sl + 1)
                    order = list(range(nkb)) if fox else list(range(nkb - 1, -1, -1))

                    def mask_of(kb):
                        if halo:
                            return ("H", kb)
                        if 8 * sl <= kb < 8 * sl + 4:
                            return ("E", kb - 8 * sl)
                        if 8 * sl + 4 <= kb < 8 * sl + 8:
                            return ("NE", kb - 8 * sl - 4)
                        return None
                    npairs = nkb // 2
                    for idx in range(npairs):
                        kbs = (order[2 * idx], order[2 * idx + 1])
                        units.append(dict(hh=hh, fox=fox, sl=sl, W=W, kbs=kbs, first=idx == 0, last=idx == npairs - 1,
                                          mks=(mask_of(kbs[0]), mask_of(kbs[1])), so=slot_ctr, qc=NOWN if halo else 512 * sl))
                    slot_ctr += 1
            for i, u in enumerate(units):
                u["i"] = i
                u["head_first"] = (i == 0 or units[i - 1]["hh"] != u["hh"])
                u["head_last"] = (i == len(units) - 1 or units[i + 1]["hh"] != u["hh"])

            def stage_A(u):
                hb = u["hh"] % 2
                KD = 70
                W, i = u["W"], u["i"]
                zt = psZ[i % 3]
                zk = "psZ%d" % (i % 3)
                mi = 0 if u["fox"] else 1
                rd = ["KTm%d" % hb, "QTm%d" % hb, "KTa%d" % hb, "QTa%d" % hb]
                sl = u["sl"]
                for a in range(2):
                    kb = u["kbs"][a]
                    z = zt[:, a, :W]
                    mk = u["mks"][a] if "noMask" not in OPT else None
                    MM(P, z, KT[hb][0:KD, kb * 128:(kb + 1) * 128], QT[hb][0:KD, u["qc"]:u["qc"] + W], True, mk is None, rd, [zk])
                    if mk is not None:
                        typ, m = mk
                        if typ == "H":
                            MM(P, z, ident[:, :], mh[mi][:, m, :], False, True, CONSTS, [zk])
                        elif typ == "E":
                            MM(P, z, idE[:, sl, :], mt[mi][:, m, :], False, True, CONSTS, [zk])
                        else:
                            MM(P, z, idNE[:, sl, :], mt[mi][:, m, :], False, False, CONSTS, [zk])
                            MM(P, z, idE[:, sl, :], allneg[:, :W], False, True, CONSTS, [zk])

            def stage_B_fox(u):
                W, i = u["W"], u["i"]
                ACTF(P, PT[i % 3][:, :, :W], psZ[i % 3][:, :, :W], AF.Exp, ["psZ%d" % (i % 3)], ["PT%d" % (i % 3)])

            def stage_B1(u):
                W, i = u["W"], u["i"]
                ACTF(P, UU[i % 2][:, :, :W], psZ[i % 3][:, :, :W], AF.Exp, ["psZ%d" % (i % 3)], ["UU%d" % (i % 2)])
                ACTF(P, LL[i % 3][:, :, :W], UU[i % 2][:, :, :W], AF.Ln, ["UU%d" % (i % 2)], ["LL%d" % (i % 3)], bias=1.0)
                if not u["last"]:
                    lk, nk = "LL%d" % (i % 3), "LS%d" % ((i + 1) % 3)
                    if u["first"]:
                        P.op("pool", lambda E: E.tensor_tensor(out=LS[(i + 1) % 3][:, :W], in0=LL[i % 3][:, 0, :W], in1=LL[i % 3][:, 1, :W], op=ALU.add), [lk], [nk])
                    else:
                        P.op("pool", lambda E: E.tensor_tensor(out=LS[(i + 1) % 3][:, :W], in0=LS[i % 3][:, :W], in1=LL[i % 3][:, 0, :W], op=ALU.add),
                             [lk, "LS%d" % (i % 3)], [nk])
                        P.op("pool", lambda E: E.tensor_tensor(out=LS[(i + 1) % 3][:, :W], in0=LS[(i + 1) % 3][:, :W], in1=LL[i % 3][:, 1, :W], op=ALU.add),
                             [lk, nk], [nk])

            def stage_C(u):
                W, i = u["W"], u["i"]
                zt = psZ[i % 3]
                zk = "psZ%d" % (i % 3)
                lk = "LL%d" % (i % 3)
                MM(P, zt[:, 0, :W], negtri[:, :], LL[i % 3][:, 0, :W], False, u["first"], ["negtri", lk], [zk], skip=True)
                if not u["first"]:
                    MM(P, zt[:, 0, :W], negones[:, :], LS[i % 3][:, :W], False, True, ["negones", "LS%d" % (i % 3)], [zk], skip=True)
                MM(P, zt[:, 1, :W], negtri[:, :], LL[i % 3][:, 1, :W], False, False, ["negtri", lk], [zk], skip=True)
                MM(P, zt[:, 1, :W], negones[:, :], LL[i % 3][:, 0, :W], False, u["first"], ["negones", lk], [zk], skip=True)
                if not u["first"]:
                    MM(P, zt[:, 1, :W], negones[:, :], LS[i % 3][:, :W], False, True, ["negones", "LS%d" % (i % 3)], [zk], skip=True)

            def stage_B2(u):
                W, i = u["W"], u["i"]
                ACTF(P, PT[i % 3][:, :, :W], psZ[i % 3][:, :, :W], AF.Exp, ["psZ%d" % (i % 3)], ["PT%d" % (i % 3)])

            fin_ctr = [0]

            def stage_D(u):
                hb = u["hh"] % 2
                W, i = u["W"], u["i"]
                o = psO[u["so"] % 2]
                ok = "psO%d" % (u["so"] % 2)
                NV = 65 if u["fox"] else 64
                nsub = max(1, W // 128)
                M = min(128, W)
                for a in range(2):
                    kb = u["kbs"][a]
                    for s_ in range(nsub):
                        MM(P, o[:, s_ * NV:(s_ + 1) * NV], PT[i % 3][:, a, s_ * 128:s_ * 128 + 128], VV[hb][:, kb, 0:NV],
                           u["first"] and s_ == 0 and a == 0, u["last"] and s_ == nsub - 1 and a == 1, ["PT%d" % (i % 3), "VV%d" % hb], [ok], skip=True)
                if u["last"]:
                    f = fin_ctr[0]
                    fin_ctr[0] += 1
                    obuf = ob[f % 3]
                    obk = "ob%d" % (f % 3)
                    ov = o[:M, 0:nsub * NV].rearrange("p (s v) -> p s v", s=nsub)
                    if u["fox"]:
                        rk = "rden%d" % (f % 2)
                        P.op("dve", lambda E: E.reciprocal(out=rden[:M, f % 2, 0:nsub], in_=ov[:, :, 64]), [ok], [rk])
                        for s_ in range(nsub):
                            P.op("dve", lambda E, s_=s_: E.tensor_scalar(out=obuf[:M, s_, :], in0=ov[:, s_, 0:64], scalar1=rden[:M, f % 2, s_:s_ + 1],
                                                                         scalar2=None, op0=ALU.mult), [ok, rk], [obk])
                    else:
                        P.op("dve", lambda E: E.tensor_copy(out=obuf[:M, 0:nsub, :], in_=ov[:, :, 0:64]), [ok], [obk])
                    r0 = u["qc"]
                    hh = u["hh"]
                    if nsub == 4:
                        dst = T["Od"][r0:r0 + 512, hh * 64:(hh + 1) * 64].rearrange("(s p) d -> p s d", p=128)
                        DMA(P, "sp", dst, obuf[:, :, :], [obk], ["Od"])
                    else:
                        DMA(P, "sp", T["Od"][r0:r0 + M, hh * 64:(hh + 1) * 64], obuf[:M, 0, :], [obk], ["Od"])

            n = len(units)
            hh0 = units[0]["hh"]
            load_head(hh0)
            stage_A(units[0])
            job_every = max(1, n // (len(jobs) + 2))
            for i in range(n):
                u = units[i]
                if u["head_first"] and u["hh"] + 1 < 16 and u["hh"] == hh0:
                    load_head(hh0 + 1)
                if i % job_every == job_every - 1:
                    conv_job()
                if i + 1 < n:
                    stage_A(units[i + 1])
                if u["fox"]:
                    stage_B_fox(u)
                else:
                    stage_B1(u)
                    stage_C(u)
                if i >= 1:
                    pu = units[i - 1]
                    if not pu["fox"]:
                        stage_B2(pu)
                    stage_D(pu)
                    if pu["head_last"] and pu["hh"] + 2 < 16:
                        load_head(pu["hh"] + 2)
            pu = units[n - 1]
            if not pu["fox"]:
                stage_B2(pu)
            stage_D(pu)
            while jobn[0] < len(jobs):
                conv_job()
            P.emit()
            stats.append(P.stats)
        if stop_after == "c":
            return nc, stats

        with ExitStack() as es:
            P = Prog(nc, gs, "d")
            wout = sbt(es, "wout", [128, 8, D], BF16)
            fox_g = sbt(es, "fox_g", [128, 512], F32)
            sb_g = sbt(es, "sb_g", [128, 512], F32)
            ffn_g = sbt(es, "ffn_g", [128, D], F32)
            fin_g = sbt(es, "fin_g", [128, D], F32)
            cw = sbt(es, "cw", [128, NCH, 3], F32)
            cb = sbt(es, "cb", [128, NCH], F32)
            hmask = sbt(es, "hmask", [128, 16], F32)
            ident = sbt(es, "ident3", [128, 128], BF16)
            identf = sbt(es, "identf3", [128, 128], F32)
            o_s = [sbt(es, "o_s%d" % i, [128, 4, D], BF16) for i in range(2)]
            xr = [sbt(es, "xr%d" % i, [128, 4, D], F32) for i in range(2)]
            junk = sbt(es, "junk3", [128, D], BF16)
            stat = sbt(es, "stat3", [128, 3, 16], F32)
            on = sbt(es, "on", [128, 4, D], BF16)
            onT = sbt(es, "onT", [128, 8, 512], BF16)
            h2T = sbt(es, "h2T", [128, 8, 512], BF16)
            wu = [sbt(es, "wu%d" % i, [128, 8, 2, 128], BF16) for i in range(3)]
            cv = [sbt(es, "cv%d" % i, [128, 512], F32) for i in range(6)]
            sg = [sbt(es, "sg%d" % i, [128, 512], F32) for i in range(2)]
            aT = sbt(es, "aT", [128, 22, 512], BF16)
            uh = sbt(es, "uh", [128, NCH, 16], F32)
            wd = [sbt(es, "wd%d" % i, [128, 22, 128], BF16) for i in range(3)]
            yT = [sbt(es, "yT%d" % i, [128, 512], F32) for i in range(2)]
            pT = [pst(es, "p3T%d" % i, [128, 8, 128], BF16) for i in range(2)]
            psa = [pst(es, "psa%d" % i, [128, 512], F32) for i in range(2)]
            psu = [pst(es, "psu%d" % i, [128, 512], F32) for i in range(2)]
            psy = pst(es, "psy", [128, 512], F32)
            pst_ = pst(es, "pst", [128, 4, 128], F32)
            rr = RR(["act", "dve"])
            PSU = [psa[0], psa[1], psu[0], psu[1]]
            PSUK = ["psa0", "psa1", "psu0", "psu1"]
            for nm, t_, src in (("wout", wout, T["Wout"][:, :, :]), ("fox_g", fox_g, T["fox_g"][:, :]), ("sb_g", sb_g, T["sb_g"][:, :]),
                                ("ffn_g", ffn_g, T["ffn_g"][:, :]), ("fin_g", fin_g, T["fin_g"][:, :]), ("cw", cw, T["cw"][:, :, :]),
                                ("cb", cb, T["cb"][:, :]), ("hmask", hmask, T["hmask"][:, :]), ("ident", ident, T["ident_bf"][:, :]),
                                ("identf", identf, T["ident_f"][:, :])):
                DMA(P, "sp", t_[:], src, [], [nm])

            P.op("pool", lambda E: E.memset(on[:], 0.0), [], ["on"])
            P.op("pool", lambda E: E.memset(onT[:], 0.0), [], ["onT.%d" % b for b in range(4)])
            P.op("pool", lambda E: E.memset(h2T[:], 0.0), [], ["h2T.%d" % b for b in range(4)])
            sctr = [0]
            pctr = [0]
            wuc = [0]
            wdc = [0]

            def rstd_of(src_ap, rows, width):
                c = sctr[0] % 16
                sctr[0] += 1
                sk = "st%d" % c
                ACTF(P, junk[:rows, :width], src_ap, AF.Square, src_ap_keys[0], ["junk", sk], accum=stat[:rows, 0, c:c + 1])
                ACTF(P, stat[:rows, 1, c:c + 1], stat[:rows, 0, c:c + 1], AF.Ln, [sk], [sk], bias=EPS, scale=1.0 / width)
                ACTF(P, stat[:rows, 2, c:c + 1], stat[:rows, 1, c:c + 1], AF.Exp, [sk], [sk], scale=-0.5)
                return stat[:rows, 2, c:c + 1], sk
            src_ap_keys = [None]

            def transpose_to(src_tile, src_key, blocks, dstT, dst_key):
                for (bi, rows) in blocks:
                    n = pctr[0]
                    pctr[0] += 1
                    pt = pT[n % 2]
                    pk = "p3T%d" % (n % 2)
                    for k in range(8):
                        TR(P, pt[:, k, :], src_tile[:, bi, k * 128:(k + 1) * 128], ident[:, :], [src_key, "ident"], [pk])
                    rr.copy(P, dstT[:, :, bi * 128:bi * 128 + rows], pt[:, :, :rows], [pk], [dst_key + ".%d" % bi])

            slots = [NSL] + list(range(NSL))

            def load_slot(si):
                sl = slots[si]
                halo = sl == NSL
                r0 = NOWN if halo else 512 * sl
                sb_ = si % 2
                os_, xr_ = o_s[sb_], xr[sb_]
                osk, xrk = "o_s%d" % sb_, "xr%d" % sb_
                if halo:
                    DMA(P, "sp", os_[:16, 0, :], T["Od"][r0:r0 + 16, :], [], [osk])
                    DMA(P, "sp", xr_[:16, 0, :], T["xo"][r0:r0 + 16, :], [], [xrk])
                else:
                    DMA(P, "sp", os_[:, :, :], T["Od"][r0:r0 + 512, :].rearrange("(b p) d -> p b d", p=128), [], [osk])
                    DMA(P, "sp", xr_[:, :, :], T["xo"][r0:r0 + 512, :].rearrange("(b p) d -> p b d", p=128), [], [xrk])

            deferred = []

            def gnorm_block(si, bi, rows):
                sb_ = si % 2
                os_ = o_s[sb_]
                osk = "o_s%d" % sb_
                for g in range(2):
                    src = os_[:rows, bi, g * 512:(g + 1) * 512]
                    src_ap_keys[0] = [osk]
                    rs, sk = rstd_of(src, rows, 512)
                    gt = fox_g if g == 0 else sb_g
                    P.op("dve", lambda E, src=src, rs=rs, gt=gt, g=g: E.scalar_tensor_tensor(
                        out=on[:rows, bi, g * 512:(g + 1) * 512], in0=src, scalar=rs, in1=gt[:rows, :], op0=ALU.mult, op1=ALU.mult),
                        [osk, sk, "fox_g", "sb_g"], ["on"])

            def do_slot(si, sl):
                halo = sl == NSL
                W = 16 if halo else 512
                r0 = NOWN if halo else 512 * sl
                blocks = [(0, 16)] if halo else [(b, 128) for b in range(4)]
                sb_ = si % 2
                os_, xr_ = o_s[sb_], xr[sb_]
                osk, xrk = "o_s%d" % sb_, "xr%d" % sb_
                if si == 0:
                    load_slot(si)
                if si + 1 < len(slots):
                    load_slot(si + 1)
                if si == 0:
                    for (bi, rows) in blocks:
                        gnorm_block(si, bi, rows)
                transpose_to(on, "on", blocks, onT, "onT")
                onk = ["onT.%d" % bi for (bi, _) in blocks]
                for (bi, rows) in blocks:
                    for ch in range(2):
                        n = pctr[0]
                        pctr[0] += 1
                        pa = psa[n % 2]
                        pk = "psa%d" % (n % 2)
                        for k in range(8):
                            MM(P, pa[:, :], onT[:, k, bi * 128:bi * 128 + 128], wout[:, k, ch * 512:(ch + 1) * 512], k == 0, k == 7,
                               ["wout", "onT.%d" % bi], [pk])
                        P.op("dve", lambda E, pa=pa, rows=rows, bi=bi, ch=ch: E.tensor_tensor(
                            out=xr_[:rows, bi, ch * 512:(ch + 1) * 512], in0=pa[:rows, :], in1=xr_[:rows, bi, ch * 512:(ch + 1) * 512], op=ALU.add),
                            [pk, xrk], [xrk])
                for (bi, rows) in blocks:
                    src = xr_[:rows, bi, :]
                    src_ap_keys[0] = [xrk]
                    rs, sk = rstd_of(src, rows, D)
                    P.op("dve", lambda E, src=src, rs=rs, rows=rows, bi=bi: E.scalar_tensor_tensor(
                        out=on[:rows, bi, :], in0=src, scalar=rs, in1=ffn_g[:rows, :], op0=ALU.mult, op1=ALU.mult),
                        [xrk, sk, "ffn_g"] + onk, ["on"])
                transpose_to(on, "on", blocks, h2T, "h2T")
                h2k = ["h2T.%d" % bi for (bi, _) in blocks]
                for c in range(22):
                    wn = wuc[0]
                    wuc[0] += 1
                    wt = wu[wn % 3]
                    wk = "wu%d" % (wn % 3)
                    DMA(P, "sp", wt[:, :, 0, :], T["Wup"][:, :, c * 128:(c + 1) * 128], [], [wk + "g"])
                    DMA(P, "sp", wt[:, :, 1, :], T["Wup"][:, :, DFF + c * 128:DFF + (c + 1) * 128], [], [wk + "v"])
                    pend = []
                    prev_fin = deferred[:]
                    del deferred[:]
                    for gv in range(2):
                        un = (2 * wn + gv) % 4
                        pu_ = PSU[un]
                        pk = PSUK[un]
                        for k in range(8):
                            MM(P, pu_[:, :W], wt[:, k, gv, :], h2T[:, k, :W], k == 0, k == 7, [wk + ("g" if gv == 0 else "v")] + h2k, [pk])
                        cc = c + 22 * gv
                        if halo:
                            P.op("dve", lambda E, pu_=pu_, cc=cc: E.tensor_tensor(out=uh[:, cc, :], in0=pu_[:, :16], in1=hmask[:, :], op=ALU.mult),
                                 [pk, "hmask"], ["uh"])
                            continue
                        cn = (2 * wn + gv) % 6
                        ct, ck = cv[cn], "cv%d" % cn
                        ACTF(P, ct[:, :], pu_[:, :512], AF.Identity, [pk, "cw", "cb"], [ck], bias=cb[:, cc:cc + 1], scale=cw[:, cc, 2:3])
                        pend.append((pu_, pk, ct, ck, cc))
                    if halo:
                        continue
                    for f_ in prev_fin:
                        f_()
                    for (pu_, pk, ct, ck, cc) in pend:
                        P.op("dve", lambda E, pu_=pu_, ct=ct, cc=cc: E.scalar_tensor_tensor(out=ct[:, 1:512], in0=pu_[:, 0:511], scalar=cw[:, cc, 1:2], in1=ct[:, 1:512],
                                                                                         op0=ALU.mult, op1=ALU.add), [pk, "cw", ck], [ck])
                    for (pu_, pk, ct, ck, cc) in pend:
                        P.op("dve", lambda E, pu_=pu_, ct=ct, cc=cc: E.scalar_tensor_tensor(out=ct[:, 2:512], in0=pu_[:, 0:510], scalar=cw[:, cc, 0:1], in1=ct[:, 2:512],
                                                                                         op0=ALU.mult, op1=ALU.add), [pk, "cw", ck], [ck])
                    for (pu_, pk, ct, ck, cc) in pend:
                        P.op("dve", lambda E, ct=ct, cc=cc: E.scalar_tensor_tensor(out=ct[:, 0:1], in0=uh[:, cc, 2 * sl + 1:2 * sl + 2], scalar=cw[:, cc, 1:2], in1=ct[:, 0:1],
                                                                                op0=ALU.mult, op1=ALU.add), ["uh", "cw", ck], [ck])
                    for (pu_, pk, ct, ck, cc) in pend:
                        P.op("dve", lambda E, ct=ct, cc=cc: E.scalar_tensor_tensor(out=ct[:, 0:2], in0=uh[:, cc, 2 * sl:2 * sl + 2], scalar=cw[:, cc, 0:1], in1=ct[:, 0:2],
                                                                                op0=ALU.mult, op1=ALU.add), ["uh", "cw", ck], [ck])
                    (_, _, ctg, ckg, _), (_, _, ctv, ckv, _) = pend
                    st_, stk = sg[wn % 2], "sg%d" % (wn % 2)

                    def fin(ctg=ctg, ckg=ckg, ctv=ctv, ckv=ckv, st_=st_, stk=stk, c=c):
                        ACTF(P, st_[:, :], ctg[:, :], AF.Silu, [ckg], [stk])
                        P.op("pool", lambda E: E.tensor_tensor(out=aT[:, c, :], in0=st_[:, :], in1=ctv[:, :], op=ALU.mult),
                             [stk, ckv], ["aT.%d" % c])
                    deferred.append(fin)
                for f_ in deferred:
                    f_()
                del deferred[:]
                if halo:
                    if si + 1 < len(slots):
                        for b in range(4):
                            gnorm_block(si + 1, b, 128)
                    return
                for cc in range(8):
                    wn = wdc[0]
                    wdc[0] += 1
                    wt = wd[wn % 3]
                    wk = "wd%d" % (wn % 3)
                    DMA(P, "sp", wt[:, :, :], T["Wdn"][:, :, cc * 128:(cc + 1) * 128], [], [wk])
                    py_, pyk = (psy, "psy") if cc % 2 == 0 else (psu[0], "psu0")
                    pt_ = pst_[:, :, :] if cc % 2 == 0 else psu[1][:, :].rearrange("p (b c) -> p b c", b=4)
                    ptk = "pst" if cc % 2 == 0 else "psu1"
                    for c in range(22):
                        MM(P, py_[:, :], wt[:, c, :], aT[:, c, :], c == 0, c == 21, [wk, "aT.%d" % c], [pyk])
                    yt, yk = yT[wn % 2], "yT%d" % (wn % 2)
                    ACTF(P, yt[:, :], py_[:, :], AF.Copy, [pyk], [yk])
                    for b in range(4):
                        P.op("pe", lambda E, yt=yt, b=b, pt_=pt_: E.transpose(pt_[:, b, :], yt[:, b * 128:(b + 1) * 128], identf[:, :]), [yk, "identf"], [ptk])
                    P.op("dve", lambda E, cc=cc, pt_=pt_: E.tensor_tensor(out=xr_[:, :, cc * 128:(cc + 1) * 128], in0=pt_,
                                                                         in1=xr_[:, :, cc * 128:(cc + 1) * 128], op=ALU.add), [ptk, xrk], [xrk])
                    if cc < 4 and si + 1 < len(slots):
                        gnorm_block(si + 1, cc, 128)
                for (bi, rows) in blocks:
                    src = xr_[:rows, bi, :]
                    src_ap_keys[0] = [xrk]
                    rs, sk = rstd_of(src, rows, D)
                    P.op("dve", lambda E, src=src, rs=rs: E.scalar_tensor_tensor(out=src, in0=src, scalar=rs, in1=fin_g[:, :], op0=ALU.mult, op1=ALU.mult),
                         [xrk, sk, "fin_g"], [xrk])
                DMA(P, "pool", T["out"][r0:r0 + 512, :].rearrange("(b p) d -> p b d", p=128), xr_[:, :, :], [xrk], ["out"])

            for si, sl in enumerate(slots):
                do_slot(si, sl)
            P.emit()
            stats.append(P.stats)
    return nc, stats


_CACHE = {}


def _consts(S):
    NB = S // 128
    bf = ml_dtypes.bfloat16
    p = np.arange(128)
    c = {}
    c["ident_bf"] = np.eye(128, dtype=np.float32).astype(bf)
    c["ident_f"] = np.eye(128, dtype=np.float32)
    c["triu_f"] = (p[:, None] <= p[None, :]).astype(np.float32)
    c["ones_f"] = np.ones((128, 128), np.float32)
    c["negtri"] = (-(p[:, None] >= p[None, :]).astype(np.float32)).astype(bf)
    c["negones"] = (-np.ones((128, 128), np.float32)).astype(bf)
    t = np.arange(512)
    key = (np.arange(4)[None, :, None] * 128 + p[:, None, None])
    c["mt_f"] = np.where(key <= t[None, None, :], 0.0, NEG).astype(np.float32).astype(bf)
    c["mt_s"] = np.where(key < t[None, None, :], 0.0, NEG).astype(np.float32).astype(bf)
    c["negrow"] = np.full((1, 512), NEG, np.float32).astype(bf)
    c["cm1"] = np.full((64, 3, 128), -1.0, np.float32).astype(bf)
    c["cp1"] = np.full((64, 3, 128), 1.0, np.float32).astype(bf)
    return c


def _core_consts(S, par):
    NT = S // 512
    NSL = NT // 2
    NB = S // 128
    bf = ml_dtypes.bfloat16
    own = own_tiles(par, NSL)
    esel = np.zeros((128, 2 * NSL), np.float32)
    eselh = np.zeros((128, 2 * NSL), np.float32)
    hmask = np.ones((128, 16), np.float32)
    idE = np.zeros((128, NSL, 128), np.float32)
    idNE = np.zeros((128, NSL, 128), np.float32)
    erow = np.zeros((1, NSL, 128), np.float32)
    I = np.eye(128, dtype=np.float32)
    p = np.arange(128)
    keypos = (np.arange(NB)[None, :] * 128 + p[:, None])
    mh_f = np.zeros((128, NB, 16), np.float32)
    mh_s = np.zeros((128, NB, 16), np.float32)
    for j in range(NSL):
        e0 = 1.0 if own[j] == 2 * j else 0.0
        esel[:, 2 * j] = e0
        esel[:, 2 * j + 1] = 1.0 - e0
        eselh[:, 2 * j] = 1.0 if (own[j] == 2 * j and j > 0) else 0.0
        eselh[:, 2 * j + 1] = 1.0 if own[j] == 2 * j + 1 else 0.0
        idE[:, j, :] = e0 * I
        idNE[:, j, :] = (1.0 - e0) * I
        erow[0, j, :] = e0
        for r in range(2):
            col = 2 * j + r
            pos = 512 * own[j] - 2 + r
            if own[j] == 0:
                hmask[:, col] = 0.0
                mh_f[:, :, col] = np.where(keypos == 0, 0.0, NEG)
                mh_s[:, :, col] = np.where(keypos == 0, 0.0, NEG)
            else:
                mh_f[:, :, col] = np.where(keypos <= pos, 0.0, NEG)
                mh_s[:, :, col] = np.where(keypos < pos, 0.0, NEG)
    return dict(esel=esel, eselh=eselh, hmask=hmask, idE=idE.astype(bf), idNE=idNE.astype(bf), erow=erow.astype(bf),
                mh_f=mh_f.astype(bf), mh_s=mh_s.astype(bf)), own


def _prepare(inputs, debug=False):
    x = np.asarray(inputs["x"], np.float32)
    B, S, _ = x.shape
    NT = S // 512
    NSL = NT // 2
    rep = lambda v: np.ascontiguousarray(np.broadcast_to(np.asarray(v, np.float32).reshape(1, -1), (128, np.asarray(v).size)))
    shared = dict(_consts(S))
    shared["w_in"] = np.ascontiguousarray(np.asarray(inputs["w_in"], np.float32)[0])
    shared["w_out"] = np.ascontiguousarray(np.asarray(inputs["w_out"], np.float32)[0])
    shared["w_up"] = np.ascontiguousarray(np.asarray(inputs["w_up"], np.float32)[0])
    shared["w_down"] = np.ascontiguousarray(np.asarray(inputs["w_down"], np.float32)[0])
    shared["attn_g"] = rep(inputs["attn_norm_g"][0])
    shared["ffn_g"] = rep(inputs["ffn_norm_g"][0])
    shared["fin_g"] = rep(inputs["final_norm_g"])
    shared["fox_g"] = rep(inputs["fox_out_g"][0])
    shared["sb_g"] = rep(inputs["sb_out_g"][0])
    shared["fb"] = rep(inputs["forget_bias"][0])
    cwv = np.asarray(inputs["conv_w"], np.float32)[0]
    shared["cw"] = np.ascontiguousarray(cwv.reshape(3, NCH, 128).transpose(2, 1, 0))
    shared["cb"] = np.ascontiguousarray(np.asarray(inputs["conv_b"], np.float32)[0].reshape(NCH, 128).T)
    in_maps = []
    owns = []
    for c in range(8):
        b, par = c // 2, c % 2
        cc, own = _core_consts(S, par)
        owns.append(own)
        xo = np.zeros((NSL * 512 + 16, D), np.float32)
        for j, t in enumerate(own):
            xo[j * 512:(j + 1) * 512] = x[b, t * 512:(t + 1) * 512]
            if t > 0:
                xo[NSL * 512 + 2 * j:NSL * 512 + 2 * j + 2] = x[b, t * 512 - 2:t * 512]
        m = dict(shared)
        m.update(cc)
        m["xn"] = np.ascontiguousarray(x[b])
        m["xo"] = xo
        in_maps.append(m)
    return in_maps, owns, (B, S)


def kernel(**inputs):
    in_maps, owns, (B, S) = _prepare(inputs)
    if S not in _CACHE:
        _CACHE[S] = build_program(S)
    nc, _ = _CACHE[S]
    res = run_bass_kernel_spmd(nc, in_maps, core_ids=list(range(8)))
    out = np.zeros((B, S, D), np.float32)
    for c in range(8):
        b = c // 2
        o = np.asarray(res.results[c]["out"], np.float32)
        for j, t in enumerate(owns[c]):
            out[b, t * 512:(t + 1) * 512] = o[j * 512:(j + 1) * 512]
    return out
```
